# Optimizing a Trainium2 kernel written in Bass

```python
import math
import jax
import jax.numpy as jnp
from jax import lax
import numpy as np

D_MODEL = 2048
BATCH = 4
SEQ = 4096
DEPTH = 4

CTX_LEN = 256
GRID_W = 64
D_MIX = D_MODEL
ATTN_W = D_MIX // 2
SSM_W = D_MIX // 4
GMLP_W = D_MIX - ATTN_W - SSM_W
HEAD_DIM = 64
N_HEADS = ATTN_W // HEAD_DIM
GQA_RATIO = 8
N_KV_HEADS = N_HEADS // GQA_RATIO
KV_W = N_KV_HEADS * HEAD_DIM
WINDOW = 128
ATTN_BLOCK = 128
ROPE_BASE = 10000.0
SSM_GROUP = 16
SSM_GROUPS = SSM_W // SSM_GROUP
SSM_STATE = 64
DT_MIN = 0.001
DT_MAX = 0.1
GMLP_CHUNK = 128
GMLP_GROUP_W = 128
GMLP_GROUPS = GMLP_W // GMLP_GROUP_W
D_FF = ((8 * D_MODEL // 3 + 255) // 256) * 256
CONV_W = 3
N_MOD = 6
NORM_EPS = 1e-6
OFF_K = ATTN_W
OFF_V = OFF_K + KV_W
OFF_S = OFF_V + KV_W
OFF_GU = OFF_S + SSM_W
OFF_GV = OFF_GU + GMLP_W
N_IN = OFF_GV + GMLP_W

kernel_name = 'hymba_style_s5_swa_gmlp_convffn_dit'


def rms_norm(x, g):
    xf = x.astype(jnp.float32)
    y = xf * lax.rsqrt(jnp.mean(xf * xf, axis=-1, keepdims=True) + NORM_EPS)
    return (y * g.astype(jnp.float32)).astype(x.dtype)


def layer_norm(x, g, b):
    xf = x.astype(jnp.float32)
    mu = jnp.mean(xf, axis=-1, keepdims=True)
    var = jnp.mean(jnp.square(xf - mu), axis=-1, keepdims=True)
    return ((xf - mu) * lax.rsqrt(var + NORM_EPS) * g.astype(jnp.float32) + b.astype(jnp.float32)).astype(x.dtype)


def modulate(h, shift, scale):
    return h * (1 + scale) + shift


def axial_rope_angles(rows):
    row = jnp.repeat(jnp.arange(rows), GRID_W)
    col = jnp.tile(jnp.arange(GRID_W), rows)
    n_freq = HEAD_DIM // 4
    inv_freq = ROPE_BASE ** (-jnp.arange(n_freq, dtype=jnp.float32) / n_freq)
    ang = jnp.stack([row, col], axis=-1).astype(jnp.float32)[:, :, None] * inv_freq
    return jnp.cos(ang), jnp.sin(ang)


def apply_rope(x, cos, sin):
    b, l, h, _ = x.shape
    xr = x.astype(jnp.float32).reshape(b, l, h, 2, 2, HEAD_DIM // 4)
    x1, x2 = xr[..., 0, :], xr[..., 1, :]
    cs, sn = cos[None, :, None], sin[None, :, None]
    out = jnp.stack([x1 * cs - x2 * sn, x1 * sn + x2 * cs], axis=-2)
    return out.reshape(b, l, h, HEAD_DIM).astype(x.dtype)


def window_attention(q, k, v, kc, vc, sink):
    b, l, _, _ = q.shape
    nb = l // ATTN_BLOCK
    n_c = kc.shape[1]
    scale = HEAD_DIM ** -0.5
    qb = q.reshape(b, nb, ATTN_BLOCK, N_KV_HEADS, GQA_RATIO, HEAD_DIM)
    pad = ((0, 0), (ATTN_BLOCK, ATTN_BLOCK), (0, 0), (0, 0))

    def band(t):
        tb = jnp.pad(t, pad).reshape(b, nb + 2, ATTN_BLOCK, N_KV_HEADS, HEAD_DIM)
        return jnp.concatenate([tb[:, :-2], tb[:, 1:-1], tb[:, 2:]], axis=2)

    kw, vw = band(k), band(v)
    s_win = jnp.einsum('bnqkgd,bnjkd->bnkgqj', qb, kw).astype(jnp.float32) * scale
    blk = jnp.arange(nb)[:, None, None]
    qpos = blk * ATTN_BLOCK + jnp.arange(ATTN_BLOCK)[None, :, None]
    kpos = (blk - 1) * ATTN_BLOCK + jnp.arange(3 * ATTN_BLOCK)[None, None, :]
    mask = (jnp.abs(kpos - qpos) <= WINDOW) & (kpos >= 0) & (kpos < l)
    s_win = jnp.where(mask[None, :, None, None], s_win, -jnp.inf)
    s_ctx = jnp.einsum('bnqkgd,bckd->bnkgqc', qb, kc).astype(jnp.float32) * scale
    s_sink = jnp.broadcast_to(sink.astype(jnp.float32).reshape(1, 1, N_KV_HEADS, GQA_RATIO, 1, 1),
                              s_win.shape[:-1] + (1,))
    p = jax.nn.softmax(jnp.concatenate([s_win, s_ctx, s_sink], axis=-1), axis=-1).astype(v.dtype)
    nw = 3 * ATTN_BLOCK
    o = (jnp.einsum('bnkgqj,bnjkd->bnqkgd', p[..., :nw], vw)
         + jnp.einsum('bnkgqc,bckd->bnqkgd', p[..., nw:nw + n_c], vc))
    return o.reshape(b, l, N_HEADS * HEAD_DIM)


def context_attention(qc, kc, vc, sink):
    b, n, _, _ = qc.shape
    qg = qc.reshape(b, n, N_KV_HEADS, GQA_RATIO, HEAD_DIM)
    s = jnp.einsum('bqkgd,bjkd->bkgqj', qg, kc).astype(jnp.float32) * HEAD_DIM ** -0.5
    s_sink = jnp.broadcast_to(sink.astype(jnp.float32).reshape(1, N_KV_HEADS, GQA_RATIO, 1, 1), s.shape[:-1] + (1,))
    p = jax.nn.softmax(jnp.concatenate([s, s_sink], axis=-1), axis=-1).astype(vc.dtype)
    o = jnp.einsum('bkgqj,bjkd->bqkgd', p[..., :n], vc)
    return o.reshape(b, n, N_HEADS * HEAD_DIM)


def s5_discretize(lam_re, lam_im, log_dt, b_re, b_im):
    lam_re = lam_re.astype(jnp.float32)
    lam_im = lam_im.astype(jnp.float32)
    dt = jnp.exp(log_dt.astype(jnp.float32))[:, None]
    mag = jnp.exp(lam_re * dt)
    a_re = mag * jnp.cos(lam_im * dt)
    a_im = mag * jnp.sin(lam_im * dt)
    den = lam_re * lam_re + lam_im * lam_im
    n_re = a_re - 1.0
    f_re = (n_re * lam_re + a_im * lam_im) / den
    f_im = (a_im * lam_re - n_re * lam_im) / den
    b_re = b_re.astype(jnp.float32)
    b_im = b_im.astype(jnp.float32)
    bb_re = f_re[..., None] * b_re - f_im[..., None] * b_im
    bb_im = f_re[..., None] * b_im + f_im[..., None] * b_re
    return a_re, a_im, bb_re, bb_im


def complex_affine_combine(e1, e2):
    a1r, a1i, b1r, b1i = e1
    a2r, a2i, b2r, b2i = e2
    return (a2r * a1r - a2i * a1i, a2r * a1i + a2i * a1r,
            a2r * b1r - a2i * b1i + b2r, a2r * b1i + a2i * b1r + b2i)


def s5_states(u, disc, h0, reverse):
    a_re, a_im, bb_re, bb_im = disc
    bu_re = jnp.einsum('gph,bngh->bngp', bb_re, u)
    bu_im = jnp.einsum('gph,bngh->bngp', bb_im, u)
    if h0 is not None:
        h_re, h_im = h0
        first = -1 if reverse else 0
        bu_re = bu_re.at[:, first].add(a_re * h_re - a_im * h_im)
        bu_im = bu_im.at[:, first].add(a_re * h_im + a_im * h_re)
    shape = bu_re.shape
    elems = (jnp.broadcast_to(a_re, shape), jnp.broadcast_to(a_im, shape), bu_re, bu_im)
    _, _, s_re, s_im = lax.associative_scan(complex_affine_combine, elems, reverse=reverse, axis=1)
    return s_re, s_im


def s5_readout(s_re, s_im, c_re, c_im):
    return jnp.einsum('ghp,bngp->bngh', c_re, s_re) - jnp.einsum('ghp,bngp->bngh', c_im, s_im)


def s5_glu(y, w_glu, b_glu):
    g = jax.nn.gelu(y)
    return g * jax.nn.sigmoid(g @ w_glu + b_glu)


def s5_mixer(u, uc, lam_re, lam_im, log_dt, b_re, b_im, c_re, c_im, d_skip, w_glu, b_glu, ctx_out):
    dtype = u.dtype
    bsz, n, _ = u.shape
    n_c = uc.shape[1]
    u4 = u.astype(jnp.float32).reshape(bsz, n, SSM_GROUPS, SSM_GROUP)
    uc4 = uc.astype(jnp.float32).reshape(bsz, n_c, SSM_GROUPS, SSM_GROUP)
    d4 = d_skip.astype(jnp.float32).reshape(SSM_GROUPS, SSM_GROUP)
    y = d4 * u4
    yc = d4 * uc4 if ctx_out else None
    for direction, rev in enumerate((False, True)):
        disc = s5_discretize(lam_re[direction], lam_im[direction], log_dt[direction], b_re[direction], b_im[direction])
        cr = c_re[direction].astype(jnp.float32)
        ci = c_im[direction].astype(jnp.float32)
        sc_re, sc_im = s5_states(uc4, disc, None, rev)
        end = 0 if rev else -1
        s_re, s_im = s5_states(u4, disc, (sc_re[:, end], sc_im[:, end]), rev)
        y = y + s5_readout(s_re, s_im, cr, ci)
        if ctx_out:
            yc = yc + s5_readout(sc_re, sc_im, cr, ci)
    out = s5_glu(y.reshape(bsz, n, SSM_W).astype(dtype), w_glu, b_glu)
    out_c = s5_glu(yc.reshape(bsz, n_c, SSM_W).astype(dtype), w_glu, b_glu) if ctx_out else None
    return out, out_c


def gmlp_spatial_gate(gu, gv, ln_g, ln_b, w_s, b_s):
    bsz, n, _ = gu.shape
    u = jax.nn.gelu(gu)
    v = layer_norm(jax.nn.gelu(gv), ln_g, ln_b)
    vch = v.reshape(bsz, n // GMLP_CHUNK, GMLP_CHUNK, GMLP_GROUPS, GMLP_GROUP_W)
    mixed = jnp.einsum('gij,bnjgc->bnigc', w_s, vch) + b_s.T[None, None, :, :, None]
    return u * mixed.reshape(bsz, n, GMLP_W)


def conv_ffn(h, w_up, conv_w, conv_b, w_down):
    n = h.shape[1]
    up = h @ w_up
    gate, val = up[..., :D_FF], up[..., D_FF:]
    half = CONV_W // 2
    gp = jnp.pad(gate, ((0, 0), (half, half), (0, 0)))
    gate = sum(gp[:, j:j + n] * conv_w[j] for j in range(CONV_W)) + conv_b
    return (jax.nn.silu(gate) * val) @ w_down


def setup_inputs(seed: int = 0) -> dict:
    key = jax.random.key(seed)
    ks = iter(jax.random.split(key, 40))

    def nrm(shape, std):
        return std * jax.random.normal(next(ks), shape, jnp.float32)

    x = nrm((BATCH, SEQ, D_MODEL), 1.0)
    c = nrm((BATCH, D_MODEL), 1.0)
    ctx = nrm((BATCH, CTX_LEN, D_MODEL), 1.0)
    c_ctx = nrm((D_MODEL,), 1.0)
    w_ada = nrm((DEPTH, D_MODEL, N_MOD * D_MODEL), 0.5 * D_MODEL ** -0.5)
    b_ada = nrm((DEPTH, N_MOD * D_MODEL), 0.02)
    g_mix = 1.0 + nrm((DEPTH, D_MODEL), 0.02)
    g_ffn = 1.0 + nrm((DEPTH, D_MODEL), 0.02)
    w_in = nrm((DEPTH, D_MODEL, N_IN), D_MODEL ** -0.5)
    w_out = nrm((DEPTH, D_MIX, D_MODEL), D_MIX ** -0.5)
    attn_sink = nrm((DEPTH, N_HEADS), 0.5)
    ssm_shape = (DEPTH, 2, SSM_GROUPS, SSM_STATE)
    ssm_lambda_re = -0.5 + nrm(ssm_shape, 0.01)
    ssm_lambda_im = math.pi * jnp.arange(SSM_STATE, dtype=jnp.float32) + nrm(ssm_shape, 0.01)
    ssm_log_dt = jax.random.uniform(next(ks), (DEPTH, 2, SSM_GROUPS), jnp.float32,
                                    minval=math.log(DT_MIN), maxval=math.log(DT_MAX))
    ssm_b_re = nrm((DEPTH, 2, SSM_GROUPS, SSM_STATE, SSM_GROUP), SSM_GROUP ** -0.5)
    ssm_b_im = nrm((DEPTH, 2, SSM_GROUPS, SSM_STATE, SSM_GROUP), SSM_GROUP ** -0.5)
    ssm_c_re = nrm((DEPTH, 2, SSM_GROUPS, SSM_GROUP, SSM_STATE), SSM_STATE ** -0.5)
    ssm_c_im = nrm((DEPTH, 2, SSM_GROUPS, SSM_GROUP, SSM_STATE), SSM_STATE ** -0.5)
    ssm_d = nrm((DEPTH, SSM_W), 1.0)
    ssm_w_glu = nrm((DEPTH, SSM_W, SSM_W), SSM_W ** -0.5)
    ssm_b_glu = nrm((DEPTH, SSM_W), 0.02)
    gmlp_ln_g = 1.0 + nrm((DEPTH, GMLP_W), 0.02)
    gmlp_ln_b = nrm((DEPTH, GMLP_W), 0.02)
    gmlp_w_s = nrm((DEPTH, GMLP_GROUPS, GMLP_CHUNK, GMLP_CHUNK), 0.5 * GMLP_CHUNK ** -0.5)
    gmlp_b_s = 1.0 + nrm((DEPTH, GMLP_GROUPS, GMLP_CHUNK), 0.02)
    ffn_w_up = nrm((DEPTH, D_MODEL, 2 * D_FF), D_MODEL ** -0.5)
    ffn_conv_w = nrm((DEPTH, CONV_W, D_FF), CONV_W ** -0.5)
    ffn_conv_b = nrm((DEPTH, D_FF), 0.02)
    ffn_w_down = nrm((DEPTH, D_FF, D_MODEL), D_FF ** -0.5)
    g_final = 1.0 + nrm((D_MODEL,), 0.02)
    return {'x': x, 'c': c, 'ctx': ctx, 'c_ctx': c_ctx, 'w_ada': w_ada, 'b_ada': b_ada,
            'g_mix': g_mix, 'g_ffn': g_ffn, 'w_in': w_in, 'w_out': w_out, 'attn_sink': attn_sink,
            'ssm_lambda_re': ssm_lambda_re, 'ssm_lambda_im': ssm_lambda_im, 'ssm_log_dt': ssm_log_dt,
            'ssm_b_re': ssm_b_re, 'ssm_b_im': ssm_b_im, 'ssm_c_re': ssm_c_re, 'ssm_c_im': ssm_c_im,
            'ssm_d': ssm_d, 'ssm_w_glu': ssm_w_glu, 'ssm_b_glu': ssm_b_glu,
            'gmlp_ln_g': gmlp_ln_g, 'gmlp_ln_b': gmlp_ln_b, 'gmlp_w_s': gmlp_w_s, 'gmlp_b_s': gmlp_b_s,
            'ffn_w_up': ffn_w_up, 'ffn_conv_w': ffn_conv_w, 'ffn_conv_b': ffn_conv_b, 'ffn_w_down': ffn_w_down,
            'g_final': g_final}


def reference(x, c, ctx, c_ctx, w_ada, b_ada, g_mix, g_ffn, w_in, w_out, attn_sink,
              ssm_lambda_re, ssm_lambda_im, ssm_log_dt, ssm_b_re, ssm_b_im, ssm_c_re, ssm_c_im,
              ssm_d, ssm_w_glu, ssm_b_glu, gmlp_ln_g, gmlp_ln_b, gmlp_w_s, gmlp_b_s,
              ffn_w_up, ffn_conv_w, ffn_conv_b, ffn_w_down, g_final):
    b, l, _ = x.shape
    n_c = ctx.shape[1]
    rows = l // GRID_W
    cos, sin = axial_rope_angles(rows)
    xc = ctx
    act_c = jax.nn.silu(c)
    act_cc = jax.nn.silu(c_ctx)
    for i in range(DEPTH):
        last = i == DEPTH - 1
        mod = (act_c @ w_ada[i] + b_ada[i])[:, None, :]
        mod_c = (act_cc @ w_ada[i] + b_ada[i])[None, None, :]
        shift_a, scale_a, gate_a, shift_f, scale_f, gate_f = jnp.split(mod, N_MOD, axis=-1)
        shift_ac, scale_ac, gate_ac, shift_fc, scale_fc, gate_fc = jnp.split(mod_c, N_MOD, axis=-1)

        h = modulate(rms_norm(x, g_mix[i]), shift_a, scale_a)
        hc = modulate(rms_norm(xc, g_mix[i]), shift_ac, scale_ac)
        z = h @ w_in[i]
        base = OFF_K if last else 0
        zc = hc @ (w_in[i][:, OFF_K:OFF_GU] if last else w_in[i])

        q = apply_rope(z[..., :OFF_K].reshape(b, l, N_HEADS, HEAD_DIM), cos, sin)
        k = apply_rope(z[..., OFF_K:OFF_V].reshape(b, l, N_KV_HEADS, HEAD_DIM), cos, sin)
        v = z[..., OFF_V:OFF_S].reshape(b, l, N_KV_HEADS, HEAD_DIM)
        kc = zc[..., OFF_K - base:OFF_V - base].reshape(b, n_c, N_KV_HEADS, HEAD_DIM)
        vc = zc[..., OFF_V - base:OFF_S - base].reshape(b, n_c, N_KV_HEADS, HEAD_DIM)
        o_attn = window_attention(q, k, v, kc, vc, attn_sink[i])

        o_ssm, o_ssm_c = s5_mixer(z[..., OFF_S:OFF_GU], zc[..., OFF_S - base:OFF_GU - base],
                                  ssm_lambda_re[i], ssm_lambda_im[i], ssm_log_dt[i], ssm_b_re[i], ssm_b_im[i],
                                  ssm_c_re[i], ssm_c_im[i], ssm_d[i], ssm_w_glu[i], ssm_b_glu[i], not last)

        o_gmlp = gmlp_spatial_gate(z[..., OFF_GU:OFF_GV], z[..., OFF_GV:], gmlp_ln_g[i], gmlp_ln_b[i],
                                   gmlp_w_s[i], gmlp_b_s[i])

        x = x + gate_a * (jnp.concatenate([o_attn, o_ssm, o_gmlp], axis=-1) @ w_out[i])
        x = x + gate_f * conv_ffn(modulate(rms_norm(x, g_ffn[i]), shift_f, scale_f),
                                  ffn_w_up[i], ffn_conv_w[i], ffn_conv_b[i], ffn_w_down[i])

        if not last:
            qc = zc[..., :OFF_K].reshape(b, n_c, N_HEADS, HEAD_DIM)
            o_attn_c = context_attention(qc, kc, vc, attn_sink[i])
            o_gmlp_c = gmlp_spatial_gate(zc[..., OFF_GU:OFF_GV], zc[..., OFF_GV:], gmlp_ln_g[i], gmlp_ln_b[i],
                                         gmlp_w_s[i], gmlp_b_s[i])
            xc = xc + gate_ac * (jnp.concatenate([o_attn_c, o_ssm_c, o_gmlp_c], axis=-1) @ w_out[i])
            xc = xc + gate_fc * conv_ffn(modulate(rms_norm(xc, g_ffn[i]), shift_fc, scale_fc),
                                         ffn_w_up[i], ffn_conv_w[i], ffn_conv_b[i], ffn_w_down[i])
    return rms_norm(x, g_final)
```

```python
import numpy as np
from contextlib import ExitStack
import concourse.bass as bass
import concourse.mybir as mybir
from concourse.bass_utils import run_bass_kernel_spmd

F32 = mybir.dt.float32
BF16 = mybir.dt.bfloat16
I32 = mybir.dt.int32
AF = mybir.ActivationFunctionType
ALU = mybir.AluOpType
AX = mybir.AxisListType

D = 2048
NLAT = 2048
NCTX = 256
NTOK = NLAT + NCTX
NT = NTOK // 128
DEPTH = 4
DFF = 5632
NFF = DFF // 128
EPS = 1e-6
TWO_PI = 6.283185307179586
PI = 3.141592653589793
BLOCKS = [(0, 256)] + [(256 + 512 * i, 512) for i in range(4)]


class Tok:
    __slots__ = ("w", "r", "name")

    def __init__(self, name=""):
        self.w = None
        self.r = []
        self.name = name


class Eng:
    def __init__(self, name, h, selfsync):
        self.name = name
        self.h = h
        self.sem = None
        self.cnt = 0
        self.seen = {}
        self.selfsync = selfsync


class Prog:
    SEM_EPOCH = 12000
    NDMA = 12

    def __init__(self, nc):
        self.nc = nc
        self.stack = ExitStack()
        self.nsem = 0
        self.eng = {
            "pe": Eng("pe", nc.tensor, False),
            "act": Eng("act", nc.scalar, True),
            "dve": Eng("dve", nc.vector, True),
            "pool": Eng("pool", nc.gpsimd, True),
            "sp": Eng("sp", nc.sync, False),
        }
        for e in self.eng.values():
            self._new_sem(e)
        self.dq = {}
        for q in ("sp", "pool", "act"):
            sems = [self._sem("d%s%d" % (q, i)) for i in range(self.NDMA)]
            self.dq[q] = {"sems": sems, "cnt": [0] * self.NDMA, "i": 0}
        self.out_stamps = []

    def _sem(self, name):
        self.nsem += 1
        return self.stack.enter_context(self.nc.semaphore("%s_%d" % (name, self.nsem)))

    def _new_sem(self, e):
        e.sem = self._sem("e" + e.name)
        e.cnt = 0

    def close(self):
        self.stack.close()

    def _wait(self, e, stamp):
        sem, val = stamp
        key = id(sem)
        if e.seen.get(key, 0) >= val:
            return
        e.h.wait_ge(sem, val)
        e.seen[key] = val

    def _deps(self, e, r, w):
        for t in r:
            if t.w is not None:
                if t.w[0] is e.sem and not e.selfsync:
                    continue
                self._wait(e, t.w)
        for t in w:
            if t.w is not None:
                if not (t.w[0] is e.sem and not e.selfsync):
                    self._wait(e, t.w)
            for st in t.r:
                if st[0] is e.sem:
                    continue
                self._wait(e, st)

    def _stamp(self, st, r, w):
        for t in r:
            t.r = [s for s in t.r if s[0] is not st[0]] + [st]
        for t in w:
            t.w = st
            t.r = []

    def op(self, eng, fn, r=(), w=()):
        e = self.eng[eng]
        self._deps(e, r, w)
        ins = fn(e.h)
        if e.cnt >= self.SEM_EPOCH:
            self._new_sem(e)
        e.cnt += 1
        ins.then_inc(e.sem, 1)
        self._stamp((e.sem, e.cnt), r, w)

    def mm(self, fns, r=(), w=()):
        e = self.eng["pe"]
        self._deps(e, r, w)
        ins = None
        for fn in fns:
            ins = fn(e.h)
        if e.cnt >= self.SEM_EPOCH:
            self._new_sem(e)
        e.cnt += 1
        ins.then_inc(e.sem, 1)
        self._stamp((e.sem, e.cnt), r, w)

    def dma(self, q, out, in_, r=(), w=(), is_out=False, **kw):
        e = self.eng[q]
        dq = self.dq[q]
        slot = dq["i"] % self.NDMA
        dq["i"] += 1
        sem = dq["sems"][slot]
        prev = dq["cnt"][slot]
        if prev > 0:
            self._wait(e, (sem, prev))
        self._deps(e, r, w)
        ins = e.h.dma_start(out=out, in_=in_, **kw)
        ins.then_inc(sem, 16)
        dq["cnt"][slot] = prev + 16
        st = (sem, prev + 16)
        self._stamp(st, r, w)
        if is_out:
            self.out_stamps.append(st)

    def collective(self, ins_ap, outs_ap, groups, r=(), w=()):
        e = self.eng["pool"]
        if not hasattr(self, "cc_sem"):
            self.cc_sem = self._sem("cc")
            self.cc_cnt = 0
        self._deps(e, r, w)
        ins = self.nc.gpsimd.collective_compute("AllGather", ALU.bypass, replica_groups=groups, ins=[ins_ap], outs=[outs_ap])
        self.cc_cnt += 1
        ins.then_inc(self.cc_sem, 1)
        self._stamp((self.cc_sem, self.cc_cnt), r, w)

    def barrier(self):
        for e in self.eng.values():
            for o in self.eng.values():
                if o is e or o.cnt == 0:
                    continue
                self._wait(e, (o.sem, o.cnt))
            for q, dq in self.dq.items():
                for s_, c in zip(dq["sems"], dq["cnt"]):
                    if c > 0:
                        self._wait(e, (s_, c))
            if getattr(self, "cc_cnt", 0) > 0:
                self._wait(e, (self.cc_sem, self.cc_cnt))

    def finish(self):
        self.barrier()
        e = self.eng["sp"]
        for st in self.out_stamps:
            self._wait(e, st)
        for q, dq in self.dq.items():
            for s, c in zip(dq["sems"], dq["cnt"]):
                if c > 0:
                    self._wait(e, (s, c))


class Ctx:
    def __init__(self, P):
        self.P = P
        self.nc = P.nc
        self.stack = ExitStack()
        self.n = 0

    UID = [0]

    def sb(self, shape, dt, name="t"):
        Ctx.UID[0] += 1
        return self.stack.enter_context(self.nc.sbuf_tensor("%s_%d" % (name, Ctx.UID[0]), list(shape), dt))

    def ps(self, shape, dt, name="p"):
        Ctx.UID[0] += 1
        return self.stack.enter_context(self.nc.psum_tensor("%s_%d" % (name, Ctx.UID[0]), list(shape), dt))

    def close(self):
        self.P.barrier()
        self.stack.close()


def bcast_rows(ap1d, nrows):
    return ap1d.rearrange("(o n) -> o n", o=1).broadcast(0, nrows)


class Reg:
    def __init__(self, nc, ins=(), outs=(), per_layer=False):
        self.nc = nc
        self.nl = 1 if per_layer else DEPTH
        self.per_layer = per_layer
        self.ins = set(ins)
        self.outs = set(outs)
        self.t = {}
        self.tok = {}

    def get(self, name, shape=None, dt=F32):
        if name not in self.t:
            kind = "ExternalInput" if name in self.ins else ("ExternalOutput" if name in self.outs else "Internal")
            self.t[name] = self.nc.dram_tensor(name, list(shape), dt, kind=kind).ap()
            self.tok[name] = Tok(name)
        return self.t[name]


def build_consts(P, C):
    nc = P.nc
    K = {}
    idf = C.sb([128, 128], F32, "idf")
    idb = C.sb([128, 128], BF16, "idb")
    io = C.sb([128, 128], I32, "io")
    t_id = Tok("ident")
    P.op("pool", lambda g: g.iota(io[:], [[1, 128]], base=0, channel_multiplier=-1), w=[t_id])
    P.op("dve", lambda v: v.tensor_copy(out=idf[:], in_=io[:]), r=[t_id], w=[t_id])
    P.op("dve", lambda v: v.tensor_scalar(out=idf[:], in0=idf[:], scalar1=0.0, scalar2=None, op0=ALU.is_equal), r=[t_id], w=[t_id])
    P.op("dve", lambda v: v.tensor_copy(out=idb[:], in_=idf[:]), r=[t_id], w=[t_id])
    K["idf"] = idf
    K["idb"] = idb
    K["tok"] = t_id
    return K


def reg_li(R, l):
    return 0 if R.per_layer else l


Reg.li = reg_li


def reg_tk(R, name, i=0):
    key = (name, i)
    if key not in R.tok:
        R.tok[key] = Tok("%s[%s]" % (name, i))
    return R.tok[key]


Reg.tk = reg_tk


def phase_mod(P, R, layers):
    C = Ctx(P)
    for _ in phase_mod_gen(P, R, layers, C):
        pass
    C.close()


def phase_mod_gen(P, R, layers, C):
    cvec = R.get("cvec", [2, D])
    w_ada = R.get("w_ada", [R.nl, D, 6 * D])
    b_ada = R.get("b_ada", [R.nl, 6 * D])
    mod = R.get("mod", [DEPTH, 2, 6 * D])
    cc = C.sb([128, 16, 2], F32, "cc")
    t_cc = Tok()
    for j in range(2):
        P.dma("sp", cc[:, :, j], cvec[j].rearrange("(k p) -> p k", p=128), w=[t_cc], allow_slow_non_contiguous=True)
    act = C.sb([128, 16, 2], F32, "act")
    P.op("act", lambda a: a.activation(out=act[:], in_=cc[:], func=AF.Silu), r=[t_cc], w=[t_cc])
    NB = 2
    wb = [C.sb([128, 16, 512], BF16, "wada") for _ in range(NB)]
    actb = C.sb([128, 16, 2], BF16, "actb")
    P.op("dve", lambda v: v.tensor_copy(out=actb[:], in_=act[:]), r=[t_cc], w=[t_cc])
    t_wb = [Tok() for _ in range(NB)]
    bb = [C.sb([2, 512], F32, "bada") for _ in range(NB)]
    ob = [C.sb([2, 512], F32, "oada") for _ in range(NB)]
    t_ob = [Tok() for _ in range(NB)]
    ps = [C.ps([2, 512], F32, "psada") for _ in range(NB)]
    t_ps = [Tok() for _ in range(NB)]
    it = 0
    for l in layers:
        wv = w_ada[R.li(l)].rearrange("(k p) c -> p k c", p=128)
        for cb in range(24):
            i = it % NB
            it += 1
            cs = slice(cb * 512, (cb + 1) * 512)
            P.dma("pool", wb[i][:], wv[:, :, cs], w=[t_wb[i]])
            P.dma("sp", bb[i][:], b_ada[R.li(l), cs].partition_broadcast(2), w=[t_wb[i]])
            P.mm([(lambda pe, k=k, i=i: pe.matmul(ps[i][:], actb[:, k, :], wb[i][:, k, :], start=(k == 0), stop=(k == 15)))
                  for k in range(16)], r=[t_wb[i], t_cc], w=[t_ps[i]])
            P.op("dve", lambda v, i=i: v.tensor_tensor(out=ob[i][:], in0=ps[i][:], in1=bb[i][:], op=ALU.add),
                 r=[t_ps[i], t_wb[i]], w=[t_ob[i]])
            P.dma("sp", mod[l, :, cs], ob[i][:], r=[t_ob[i]], w=[R.tk("mod", l)])
            yield


NQ = 1024
OFFX_Q, OFFX_QS, OFFX_K, OFFX_KS, OFFX_V, OFFX_U, OFFX_GU, OFFX_GV = 0, 1024, 2048, 2176, 2304, 2432, 2944, 3456
NINX = 3968


def range_reduce_sin(P, C, out, ang, shape, tk, extra=0.0):
    ti = C.sb(shape, I32, "rr_i")
    tf = C.sb(shape, F32, "rr_f")
    tr = C.sb(shape, F32, "rr_r")
    t = Tok()
    P.op("dve", lambda v: v.tensor_scalar(out=ti[:], in0=ang, scalar1=extra, scalar2=1.0 / TWO_PI, op0=ALU.add, op1=ALU.mult), r=[tk], w=[t])
    P.op("dve", lambda v: v.tensor_copy(out=tf[:], in_=ti[:]), r=[t], w=[t])
    P.op("dve", lambda v: v.scalar_tensor_tensor(out=tr[:], in0=tf[:], scalar=-TWO_PI, in1=ang, op0=ALU.mult, op1=ALU.add), r=[t, tk], w=[t])
    if extra != 0.0:
        P.op("dve", lambda v: v.tensor_scalar(out=tr[:], in0=tr[:], scalar1=extra, scalar2=None, op0=ALU.add), r=[t], w=[t])
    P.op("dve", lambda v: v.tensor_scalar(out=tf[:], in0=tr[:], scalar1=PI, scalar2=-TWO_PI, op0=ALU.is_gt, op1=ALU.mult), r=[t], w=[t])
    P.op("dve", lambda v: v.tensor_tensor(out=tr[:], in0=tr[:], in1=tf[:], op=ALU.add), r=[t], w=[t])
    P.op("dve", lambda v: v.tensor_scalar(out=tf[:], in0=tr[:], scalar1=-PI, scalar2=TWO_PI, op0=ALU.is_lt, op1=ALU.mult), r=[t], w=[t])
    P.op("dve", lambda v: v.tensor_tensor(out=tr[:], in0=tr[:], in1=tf[:], op=ALU.add), r=[t], w=[t])
    P.op("act", lambda a: a.activation(out=out, in_=tr[:], func=AF.Sin), r=[t], w=[tk])


def phase_rope(P, R):
    C = Ctx(P)
    rowcol = R.get("rowcol", [2, NLAT])
    ropeT = R.get("ropeT", [2, 128, NTOK])
    pos = C.sb([128, NLAT], F32, "pos")
    t = Tok()
    for base in (0, 64):
        P.dma("sp", pos[base:base + 32, :], rowcol[0].partition_broadcast(32), w=[t])
        P.dma("sp", pos[base + 32:base + 64, :], rowcol[1].partition_broadcast(32), w=[t])
    pi_ = C.sb([128, 1], I32, "pidx")
    pf = C.sb([128, 4], F32, "pf")
    tp = Tok()
    P.op("pool", lambda g: g.iota(pi_[:], [[0, 1]], base=0, channel_multiplier=1), w=[tp])
    P.op("dve", lambda v: v.tensor_copy(out=pf[:, 0:1], in_=pi_[:]), r=[tp], w=[tp])
    pj = C.sb([128, 1], I32, "pj")
    P.op("dve", lambda v: v.tensor_scalar(out=pj[:], in0=pi_[:], scalar1=15, scalar2=None, op0=ALU.bitwise_and), r=[tp], w=[tp])
    P.op("dve", lambda v: v.tensor_copy(out=pf[:, 1:2], in_=pj[:]), r=[tp], w=[tp])
    P.op("act", lambda a: a.activation(out=pf[:, 2:3], in_=pf[:, 1:2], func=AF.Exp, scale=-float(np.log(10000.0)) / 16.0), r=[tp], w=[tp])
    P.op("dve", lambda v: v.tensor_scalar(out=pj[:], in0=pi_[:], scalar1=16, scalar2=None, op0=ALU.bitwise_and), r=[tp], w=[tp])
    P.op("dve", lambda v: v.tensor_copy(out=pf[:, 3:4], in_=pj[:]), r=[tp], w=[tp])
    P.op("dve", lambda v: v.tensor_scalar(out=pf[:, 3:4], in0=pf[:, 3:4], scalar1=1.0 / 8.0, scalar2=-1.0, op0=ALU.mult, op1=ALU.add), r=[tp], w=[tp])
    ang = C.sb([128, NLAT], F32, "ang")
    P.op("dve", lambda v: v.tensor_scalar(out=ang[:], in0=pos[:], scalar1=pf[:, 2:3], scalar2=None, op0=ALU.mult), r=[t, tp], w=[t])
    cs = C.sb([128, NTOK], F32, "cs")
    sn = C.sb([128, NTOK], F32, "sn")
    tc_, ts_ = Tok(), Tok()
    P.op("pool", lambda g: g.memset(cs[:, 0:NCTX], 1.0), w=[tc_])
    P.op("pool", lambda g: g.memset(sn[:, 0:NCTX], 0.0), w=[ts_])
    C2 = Ctx(P)
    range_reduce_sin(P, C2, sn[:, NCTX:], ang[:], [128, NLAT], t)
    P.op("dve", lambda v: v.tensor_scalar(out=sn[:, NCTX:], in0=sn[:, NCTX:], scalar1=pf[:, 3:4], scalar2=None, op0=ALU.mult), r=[t, tp], w=[t])
    P.dma("sp", ropeT[1], sn[:], r=[t, ts_], w=[R.tk("ropeT", 1)])
    t2 = Tok()
    t2.w = t.w
    range_reduce_sin(P, C2, cs[:, NCTX:], ang[:], [128, NLAT], t, extra=PI / 2)
    P.dma("sp", ropeT[0], cs[:], r=[t, tc_], w=[R.tk("ropeT", 0)])
    C2.close()
    C.close()


def load_bcast_mod(P, C, R, l, r, sec, name):
    mod = R.get("mod", [DEPTH, 2, 6 * D])
    t = C.sb([128, D], F32, name)
    tk = Tok(name)
    P.dma("sp", t[:], mod[l, r, sec * D:(sec + 1) * D].partition_broadcast(128), r=[R.tk("mod", l)], w=[tk])
    return t, tk


def norm_mod_transpose(P, R, K, C, l, xname, gname, sec_shift, sec_scale, hT_all, dst_fn=None, tok_fn=None, after=None, extra_row=None):
    x = R.get(xname, [NTOK, D])
    gvec = R.get(gname, [R.nl, D])
    C1 = Ctx(P)
    gb = C1.sb([128, D], F32, "gb")
    t_gb = Tok()
    P.dma("sp", gb[:], gvec[R.li(l)].partition_broadcast(128), w=[t_gb])
    GS, SH, tGS, tSH = [], [], [], []
    for r in range(2):
        sc, tsc = load_bcast_mod(P, C1, R, l, r, sec_scale, "gs%d" % r)
        sh, tsh = load_bcast_mod(P, C1, R, l, r, sec_shift, "sh%d" % r)
        P.op("dve", lambda v, sc=sc: v.scalar_tensor_tensor(out=sc[:], in0=sc[:], scalar=1.0, in1=gb[:], op0=ALU.add, op1=ALU.mult),
             r=[tsc, t_gb], w=[tsc])
        GS.append(sc); SH.append(sh); tGS.append(tsc); tSH.append(tsh)
    NB = 3
    xt = [C1.sb([128, D], F32, "xt") for _ in range(NB)]
    t_xt = [Tok() for _ in range(NB)]
    junk = C1.sb([128, D], BF16, "junk")
    t_junk = Tok()
    st = [C1.sb([128, 4], F32, "st") for _ in range(NB)]
    t_st = [Tok() for _ in range(NB)]
    t1 = [C1.sb([128, D], F32, "t1") for _ in range(NB)]
    t_t1 = [Tok() for _ in range(NB)]
    hb = [C1.sb([128, D], BF16, "hb") for _ in range(NB)]
    t_hb = [Tok() for _ in range(NB)]
    pT = [C1.ps([128, 1024], BF16, "pT") for _ in range(2)]
    t_pT = [Tok() for _ in range(2)]
    t_h = [(Tok("hTa%d" % i), Tok("hTb%d" % i)) for i in range(NT + 1)]
    if tok_fn is not None:
        t_h = [tok_fn(i) for i in range(NT + 1)]
    for ti in range(NT + (1 if extra_row is not None else 0)):
        i = ti % NB
        r = 1 if ti < NCTX // 128 else 0
        if ti < NT:
            P.dma("sp", xt[i][:], x[ti * 128:(ti + 1) * 128, :], r=[R.tk(xname, ti)], w=[t_xt[i]])
        else:
            P.op("pool", lambda g, i=i: g.memset(xt[i][:], 1.0), w=[t_xt[i]])
            P.dma("sp", xt[i][0:1, :], extra_row[0], r=[extra_row[1]], w=[t_xt[i]])
        P.op("act", lambda a, i=i: a.activation(out=junk[:], in_=xt[i][:], func=AF.Square, accum_out=st[i][:, 0:1]),
             r=[t_xt[i]], w=[t_junk, t_st[i]])
        P.op("act", lambda a, i=i: a.activation(out=st[i][:, 1:2], in_=st[i][:, 0:1], func=AF.Sqrt, scale=1.0 / D, bias=EPS),
             r=[t_st[i]], w=[t_st[i]])
        P.op("dve", lambda v, i=i: v.reciprocal(out=st[i][:, 2:3], in_=st[i][:, 1:2]), r=[t_st[i]], w=[t_st[i]])
        P.op("pool", lambda g, i=i, r=r: g.tensor_tensor(out=t1[i][:], in0=xt[i][:], in1=GS[r][:], op=ALU.mult),
             r=[t_xt[i], tGS[r]], w=[t_t1[i]])
        P.op("dve", lambda v, i=i, r=r: v.scalar_tensor_tensor(out=hb[i][:], in0=t1[i][:], scalar=st[i][:, 2:3], in1=SH[r][:],
                                                              op0=ALU.mult, op1=ALU.add),
             r=[t_t1[i], t_st[i], tSH[r]], w=[t_hb[i]])
        for h in range(2):
            P.mm([(lambda pe, k=k, h=h, i=i: pe.transpose(pT[h][:, (k % 8) * 128:(k % 8 + 1) * 128], hb[i][:, k * 128:(k + 1) * 128], K["idb"][:]))
                  for k in range(8 * h, 8 * h + 8)], r=[t_hb[i], K["tok"]], w=[t_pT[h]])
            dst = hT_all[:, 8 * h:8 * h + 8, ti * 128:(ti + 1) * 128] if dst_fn is None else dst_fn(ti, h)
            src = pT[h][:].rearrange("p (k t) -> p k t", k=8)
            if h == 0:
                P.op("act", lambda a, dst=dst, src=src: a.copy(out=dst, in_=src), r=[t_pT[h]], w=[t_h[ti][h]])
            else:
                P.op("dve", lambda v, dst=dst, src=src: v.tensor_copy(out=dst, in_=src), r=[t_pT[h]], w=[t_h[ti][h]])
        if after is not None:
            after(ti, t_h[ti])
    C1.close()
    return t_h


def toks_for(t_h, t0, n):
    out = []
    for ti in range(t0 // 128, (t0 + n + 127) // 128):
        out += list(t_h[ti])
    return out


def load_w_cols(P, C, wsrc, col_ranges, buf, tk, q="pool"):
    wv = wsrc.rearrange("(k p) c -> p k c", p=128)
    o = 0
    for (c0, n) in col_ranges:
        P.dma(q, buf[:, :, o:o + n], wv[:, :, c0:c0 + n], w=[tk])
        o += n


def phase_A(P, R, K, l, xname, ctx_full=True, parts="abc"):
    w_in = R.get("w_inx", [R.nl, D, NINX])[R.li(l):R.li(l) + 1]
    qT = R.get("qT", [NQ, NTOK], BF16)
    kT = R.get("kT", [128, NTOK + 128], BF16)
    vv = R.get("v", [NTOK + 128, 128], BF16)
    uT = R.get("uT", [512, NTOK])
    oT = R.get("oT_g", [512, NTOK], BF16)
    ropeT = R.get("ropeT", [2, 128, NTOK])
    C = Ctx(P)
    hT = C.sb([128, 16, NTOK], BF16, "hT_all")
    t_h = norm_mod_transpose(P, R, K, C, l, xname, "g_mix", 0, 1, hT)

    if "a" not in parts:
        C.close(); return
    Ca = Ctx(P)
    cs = Ca.sb([128, NTOK], F32, "cs")
    sn = Ca.sb([128, NTOK], F32, "sn")
    t_cs = Tok()
    P.dma("sp", cs[:], ropeT[0], r=[R.tk("ropeT", 0)], w=[t_cs])
    P.dma("sp", sn[:], ropeT[1], r=[R.tk("ropeT", 1)], w=[t_cs])
    NB = 2
    wq = [Ca.sb([128, 16, 1024], BF16, "wq") for _ in range(NB)]
    t_wq = [Tok() for _ in range(NB)]
    psA = [Ca.ps([128, 512], F32, "psA") for _ in range(4)]
    t_psA = [Tok() for _ in range(4)]
    r1 = [Ca.sb([128, 512], F32, "r1") for _ in range(2)]
    r2 = [Ca.sb([128, 512], F32, "r2") for _ in range(2)]
    ro = [Ca.sb([128, 512], BF16, "ro") for _ in range(2)]
    t_r1 = [Tok() for _ in range(2)]
    t_r2 = [Tok() for _ in range(2)]
    t_ro = [Tok() for _ in range(2)]
    it = 0
    wvA = w_in[0].rearrange("(k p) c -> p k c", p=128)
    for j in range(9):
        wi = (j // 4) % NB
        fo = (j % 4) * 128
        if j % 4 == 0 and j < 8:
            P.dma("pool", wq[wi][:, :, 0:512], wvA[:, :, OFFX_Q + j * 128:OFFX_Q + j * 128 + 512], w=[t_wq[wi]])
            P.dma("pool", wq[wi][:, :, 512:1024], wvA[:, :, OFFX_QS + j * 128:OFFX_QS + j * 128 + 512], w=[t_wq[wi]])
        elif j == 8:
            P.dma("pool", wq[wi][:, :, 0:128], wvA[:, :, OFFX_K:OFFX_K + 128], w=[t_wq[wi]])
            P.dma("pool", wq[wi][:, :, 512:640], wvA[:, :, OFFX_KS:OFFX_KS + 128], w=[t_wq[wi]])
        for (t0, n) in BLOCKS:
            i = it % 2
            it += 1
            pa, pb = psA[2 * i], psA[2 * i + 1]
            hs = toks_for(t_h, t0, n)
            P.mm([(lambda pe, k=k, pa=pa, wi=wi, t0=t0, n=n, fo=fo: pe.matmul(pa[:, :n], wq[wi][:, k, fo:fo + 128], hT[:, k, t0:t0 + n], start=(k == 0), stop=(k == 15)))
                  for k in range(16)], r=[t_wq[wi]] + hs, w=[t_psA[2 * i]])
            P.mm([(lambda pe, k=k, pb=pb, wi=wi, t0=t0, n=n, fo=fo: pe.matmul(pb[:, :n], wq[wi][:, k, 512 + fo:512 + fo + 128], hT[:, k, t0:t0 + n], start=(k == 0), stop=(k == 15)))
                  for k in range(16)], r=[t_wq[wi]] + hs, w=[t_psA[2 * i + 1]])
            P.op("dve", lambda v, i=i, pa=pa, t0=t0, n=n: v.tensor_tensor(out=r1[i][:, :n], in0=pa[:, :n], in1=cs[:, t0:t0 + n], op=ALU.mult),
                 r=[t_psA[2 * i], t_cs], w=[t_r1[i]])
            P.op("dve", lambda v, i=i, pb=pb, t0=t0, n=n: v.tensor_tensor(out=r2[i][:, :n], in0=pb[:, :n], in1=sn[:, t0:t0 + n], op=ALU.mult),
                 r=[t_psA[2 * i + 1], t_cs], w=[t_r2[i]])
            P.op("pool", lambda g, i=i, n=n: g.tensor_tensor(out=ro[i][:, :n], in0=r1[i][:, :n], in1=r2[i][:, :n], op=ALU.add),
                 r=[t_r1[i], t_r2[i]], w=[t_ro[i]])
            if j < 8:
                P.dma("sp", qT[j * 128:(j + 1) * 128, t0:t0 + n], ro[i][:, :n], r=[t_ro[i]], w=[R.tk("qT", (j, t0))])
            else:
                P.dma("sp", kT[:, t0:t0 + n], ro[i][:, :n], r=[t_ro[i]], w=[R.tk("kT", t0)])
    Ca.close()

    if "b" not in parts:
        C.close(); return
    Cb = Ctx(P)
    wuA = Cb.sb([128, 16, 512], BF16, "wu")
    t_wuA = Tok()
    load_w_cols(P, Cb, w_in[0], [(OFFX_U, 512)], wuA, t_wuA)
    psB = [Cb.ps([128, 512], F32, "psB") for _ in range(2)]
    t_psB = [Tok() for _ in range(2)]
    ub = [Cb.sb([128, 512], F32, "ub") for _ in range(2)]
    t_ub = [Tok() for _ in range(2)]
    it = 0
    for j in range(4):
        for (t0, n) in BLOCKS:
            i = it % 2
            it += 1
            hs = toks_for(t_h, t0, n)
            P.mm([(lambda pe, k=k, i=i, j=j, t0=t0, n=n: pe.matmul(psB[i][:, :n], wuA[:, k, j * 128:(j + 1) * 128], hT[:, k, t0:t0 + n], start=(k == 0), stop=(k == 15)))
                  for k in range(16)], r=[t_wuA] + hs, w=[t_psB[i]])
            P.op("act", lambda a, i=i, n=n: a.copy(out=ub[i][:, :n], in_=psB[i][:, :n]), r=[t_psB[i]], w=[t_ub[i]])
            P.dma("sp", uT[j * 128:(j + 1) * 128, t0:t0 + n], ub[i][:, :n], r=[t_ub[i]], w=[R.tk("uT", (j, t0))])
    wv_ = Cb.sb([128, 16, 128], BF16, "wv")
    t_wv = Tok()
    load_w_cols(P, Cb, w_in[0], [(OFFX_V, 128)], wv_, t_wv)
    vb = [Cb.sb([128, 128], BF16, "vb") for _ in range(2)]
    t_vb = [Tok() for _ in range(2)]
    for ti in range(NT):
        i = ti % 2
        P.mm([(lambda pe, k=k, i=i, ti=ti: pe.matmul(psB[i][:, :128], hT[:, k, ti * 128:(ti + 1) * 128], wv_[:, k, :], start=(k == 0), stop=(k == 15)))
              for k in range(16)], r=[t_wv] + list(t_h[ti]), w=[t_psB[i]])
        P.op("act", lambda a, i=i: a.copy(out=vb[i][:], in_=psB[i][:, :128]), r=[t_psB[i]], w=[t_vb[i]])
        P.dma("sp", vv[ti * 128:(ti + 1) * 128, :], vb[i][:], r=[t_vb[i]], w=[R.tk("v", ti)])
    Cb.close()

    if "c" not in parts:
        C.close(); return
    Cc = Ctx(P)
    lng = R.get("gmlp_ln_g", [R.nl, 512])[R.li(l):R.li(l) + 1]
    lnb = R.get("gmlp_ln_b", [R.nl, 512])[R.li(l):R.li(l) + 1]
    wsT = R.get("gmlp_wsT", [R.nl, 4, 128, 128])[R.li(l):R.li(l) + 1]
    bs = R.get("gmlp_b_s", [R.nl, 4, 128])[R.li(l):R.li(l) + 1]
    tiles = list(range(NT)) if ctx_full else list(range(NCTX // 128, NT))
    blocks = BLOCKS if ctx_full else BLOCKS[1:]
    guT = Cc.sb([128, 4, NTOK], F32, "guT")
    t_gu = {}
    wgu = Cc.sb([128, 16, 512], BF16, "wgu")
    t_wgu = Tok()
    load_w_cols(P, Cc, w_in[0], [(OFFX_GU, 512)], wgu, t_wgu)
    psC = [Cc.ps([128, 512], F32, "psC") for _ in range(2)]
    t_psC = [Tok() for _ in range(2)]
    it = 0
    for j in range(4):
        for (t0, n) in blocks:
            i = it % 2
            it += 1
            hs = toks_for(t_h, t0, n)
            P.mm([(lambda pe, k=k, i=i, j=j, t0=t0, n=n: pe.matmul(psC[i][:, :n], wgu[:, k, j * 128:(j + 1) * 128], hT[:, k, t0:t0 + n], start=(k == 0), stop=(k == 15)))
                  for k in range(16)], r=[t_wgu] + hs, w=[t_psC[i]])
            t_gu[(j, t0)] = Tok()
            P.op("act", lambda a, i=i, j=j, t0=t0, n=n: a.activation(out=guT[:, j, t0:t0 + n], in_=psC[i][:, :n], func=AF.Gelu_apprx_tanh),
                 r=[t_psC[i]], w=[t_gu[(j, t0)]])
    wgv = Cc.sb([128, 16, 512], BF16, "wgv")
    t_wgv = Tok()
    load_w_cols(P, Cc, w_in[0], [(OFFX_GV, 512)], wgv, t_wgv)
    LNG = Cc.sb([128, 512], F32, "LNG")
    LNB = Cc.sb([128, 512], F32, "LNB")
    BS = Cc.sb([128, 4, 128], F32, "BS")
    WS = Cc.sb([128, 4, 128], BF16, "WS")
    t_par = Tok()
    P.dma("sp", LNG[:], lng[0].partition_broadcast(128), w=[t_par])
    P.dma("sp", LNB[:], lnb[0].partition_broadcast(128), w=[t_par])
    P.dma("sp", BS[:].rearrange("p g i -> p (g i)"), bs[0].rearrange("g i -> (g i)").partition_broadcast(128), w=[t_par])
    P.dma("pool", WS[:], wsT[0].rearrange("g j i -> j g i"), w=[t_par])
    gg = [Cc.sb([128, 512], F32, "gg") for _ in range(2)]
    t_gg = [Tok() for _ in range(2)]
    sq = Cc.sb([128, 512], BF16, "sq")
    t_sq = Tok()
    stt = [Cc.sb([128, 8], F32, "stt") for _ in range(2)]
    t_stt = [Tok() for _ in range(2)]
    vn = [Cc.sb([128, 512], BF16, "vn") for _ in range(2)]
    t_vn = [Tok() for _ in range(2)]
    pm = [Cc.ps([128, 512], F32, "pm") for _ in range(2)]
    t_pm = [Tok() for _ in range(2)]
    mm_ = [Cc.sb([128, 512], F32, "mm") for _ in range(2)]
    t_mm = [Tok() for _ in range(2)]
    og = [Cc.sb([128, 4, 128], BF16, "og") for _ in range(2)]
    t_og = [Tok() for _ in range(2)]
    for n_, ti in enumerate(tiles):
        i = n_ % 2
        s = stt[i]
        P.mm([(lambda pe, k=k, i=i, ti=ti: pe.matmul(psC[i][:], hT[:, k, ti * 128:(ti + 1) * 128], wgv[:, k, :], start=(k == 0), stop=(k == 15)))
              for k in range(16)], r=[t_wgv] + list(t_h[ti]), w=[t_psC[i]])
        P.op("act", lambda a, i=i, s=s: a.activation(out=gg[i][:], in_=psC[i][:], func=AF.Gelu_apprx_tanh, accum_out=s[:, 0:1]),
             r=[t_psC[i]], w=[t_gg[i], t_stt[i]])
        P.op("act", lambda a, i=i, s=s: a.activation(out=sq[:], in_=gg[i][:], func=AF.Square, accum_out=s[:, 1:2]),
             r=[t_gg[i]], w=[t_sq, t_stt[i]])
        P.op("dve", lambda v, s=s: v.tensor_scalar(out=s[:, 2:3], in0=s[:, 0:1], scalar1=1.0 / 512, scalar2=None, op0=ALU.mult), r=[t_stt[i]], w=[t_stt[i]])
        P.op("dve", lambda v, s=s: v.tensor_tensor(out=s[:, 3:4], in0=s[:, 2:3], in1=s[:, 2:3], op=ALU.mult), r=[t_stt[i]], w=[t_stt[i]])
        P.op("dve", lambda v, s=s: v.scalar_tensor_tensor(out=s[:, 4:5], in0=s[:, 1:2], scalar=1.0 / 512, in1=s[:, 3:4], op0=ALU.mult, op1=ALU.subtract),
             r=[t_stt[i]], w=[t_stt[i]])
        P.op("act", lambda a, s=s: a.activation(out=s[:, 5:6], in_=s[:, 4:5], func=AF.Sqrt, bias=EPS), r=[t_stt[i]], w=[t_stt[i]])
        P.op("dve", lambda v, s=s: v.reciprocal(out=s[:, 6:7], in_=s[:, 5:6]), r=[t_stt[i]], w=[t_stt[i]])
        P.op("dve", lambda v, i=i, s=s: v.tensor_scalar(out=gg[i][:], in0=gg[i][:], scalar1=s[:, 2:3], scalar2=s[:, 6:7], op0=ALU.subtract, op1=ALU.mult),
             r=[t_stt[i], t_gg[i]], w=[t_gg[i]])
        P.op("pool", lambda g, i=i: g.tensor_tensor(out=gg[i][:], in0=gg[i][:], in1=LNG[:], op=ALU.mult), r=[t_gg[i], t_par], w=[t_gg[i]])
        P.op("pool", lambda g, i=i: g.tensor_tensor(out=vn[i][:], in0=gg[i][:], in1=LNB[:], op=ALU.add), r=[t_gg[i], t_par], w=[t_vn[i]])
        P.mm([(lambda pe, g=g, i=i: pe.matmul(pm[i][:, g * 128:(g + 1) * 128], vn[i][:, g * 128:(g + 1) * 128], WS[:, g, :], start=True, stop=True))
              for g in range(4)], r=[t_vn[i], t_par], w=[t_pm[i]])
        P.op("dve", lambda v, i=i: v.tensor_tensor(out=mm_[i][:], in0=pm[i][:], in1=BS[:].rearrange("p g i -> p (g i)"), op=ALU.add),
             r=[t_pm[i], t_par], w=[t_mm[i]])
        blk0 = [b for b in BLOCKS if b[0] <= ti * 128 < b[0] + b[1]][0][0]
        P.op("pool", lambda g, i=i, ti=ti: g.tensor_tensor(out=og[i][:], in0=mm_[i][:].rearrange("p (g i) -> p g i", g=4),
                                                           in1=guT[:, :, ti * 128:(ti + 1) * 128], op=ALU.mult),
             r=[t_mm[i]] + [t_gu[(j, blk0)] for j in range(4)], w=[t_og[i]])
        P.dma("sp", oT[:, ti * 128:(ti + 1) * 128].rearrange("(g p) t -> p g t", p=128), og[i][:], r=[t_og[i]], w=[R.tk("oT_g", ti)])
    Cc.close()
    C.close()


def _qsw_perm():
    def perm(nheads):
        idx = np.arange(nheads * 64).reshape(nheads, 2, 2, 16)
        return idx[:, :, ::-1, :].reshape(-1)
    return perm(16), perm(2)


def prep_shared(inputs):
    w_in = np.asarray(inputs["w_in"])
    pq, pk = _qsw_perm()
    q = w_in[:, :, 0:1024]
    k = w_in[:, :, 1024:1152]
    w_inx = np.concatenate([q, q[:, :, pq], k, k[:, :, pk], w_in[:, :, 1152:]], axis=2)
    sh = {"w_inx": np.ascontiguousarray(w_inx, dtype=np.float32)}
    for name in ("w_ada", "b_ada", "g_mix", "g_ffn", "gmlp_ln_g", "gmlp_ln_b"):
        sh[name] = np.ascontiguousarray(inputs[name], dtype=np.float32)
    return sh


def prep_core(inputs, sh, b, s):
    x = np.asarray(inputs["x"])[b]
    ctx = np.asarray(inputs["ctx"])[b]
    pos = np.arange(4096)
    if s == 0:
        xl = x[:NLAT]
        pl = pos[:NLAT]
        cl = ctx
    else:
        xl = x[NLAT:][::-1]
        pl = pos[NLAT:][::-1]
        cl = ctx[::-1]
    m = dict(sh)
    m["x_in"] = np.ascontiguousarray(np.concatenate([cl, xl], axis=0), dtype=np.float32)
    m["rowcol"] = np.stack([pl // 64, pl % 64]).astype(np.float32)
    m["cvec"] = np.stack([np.asarray(inputs["c"])[b], np.asarray(inputs["c_ctx"])]).astype(np.float32)
    ws = np.asarray(inputs["gmlp_w_s"])
    bs = np.asarray(inputs["gmlp_b_s"])
    if s == 1:
        ws = ws[:, :, ::-1, ::-1]
        bs = bs[:, :, ::-1]
    m["gmlp_wsT"] = np.ascontiguousarray(ws.transpose(0, 1, 3, 2), dtype=np.float32)
    m["gmlp_b_s"] = np.ascontiguousarray(bs, dtype=np.float32)
    return m


def phase_D(P, R, K, l, xname, xout, ctx_full=True, side=None):
    w_out = R.get("w_out", [R.nl, D, D])[R.li(l):R.li(l) + 1]
    oTa = R.get("oT_a", [NQ, NTOK], BF16)
    oTs = R.get("oT_s", [512, NTOK], BF16)
    oTg = R.get("oT_g", [512, NTOK], BF16)
    x = R.get(xname, [NTOK, D])
    xo = R.get(xout, [NTOK, D])
    C = Ctx(P)
    W = C.sb([128, 16, D], BF16, "wout")
    t_W = Tok()
    wv = w_out[0].rearrange("(k p) c -> p k c", p=128)
    for k4 in range(4):
        P.dma("pool", W[:, 4 * k4:4 * k4 + 4, :], wv[:, 4 * k4:4 * k4 + 4, :], w=[t_W])
    GA, tGA = [], []
    for r in range(2):
        g, tg = load_bcast_mod(P, C, R, l, r, 2, "ga%d" % r)
        GA.append(g); tGA.append(tg)
    NB = 2
    ot = [C.sb([128, 16, 512], BF16, "ot") for _ in range(NB)]
    t_ot = [Tok() for _ in range(NB)]
    xt = [C.sb([128, D], F32, "xt") for _ in range(NB)]
    t_xt = [Tok() for _ in range(NB)]
    tmp = [C.sb([128, 512], F32, "tmp") for _ in range(2)]
    t_tmp = [Tok() for _ in range(2)]
    ps = [C.ps([128, 512], F32, "psD") for _ in range(4)]
    t_ps = [Tok() for _ in range(4)]
    tiles = list(range(NT)) if ctx_full else list(range(NCTX // 128, NT))
    it = 0
    for n_, ti in enumerate(tiles):
        i = n_ % NB
        r = 1 if ti < NCTX // 128 else 0
        gi_ = (ti + 2) // 4
        oi = gi_ % NB
        g_lo = max(0, gi_ * 4 - 2)
        g_n = min(NT, gi_ * 4 + 2) - g_lo
        oo = (ti - g_lo) * 128
        if ti == g_lo or n_ == 0:
            tsl = slice(g_lo * 128, (g_lo + g_n) * 128)
            P.dma("sp", ot[oi][:, 0:8, 0:128 * g_n], oTa[:, tsl].rearrange("(k p) t -> p k t", p=128), w=[t_ot[oi]])
            P.dma("sp", ot[oi][:, 8:12, 0:128 * g_n], oTs[:, tsl].rearrange("(k p) t -> p k t", p=128), w=[t_ot[oi]])
            P.dma("sp", ot[oi][:, 12:16, 0:128 * g_n], oTg[:, tsl].rearrange("(k p) t -> p k t", p=128), w=[t_ot[oi]])
        P.dma("act", xt[i][:], x[ti * 128:(ti + 1) * 128, :], r=[R.tk(xname, ti)], w=[t_xt[i]])
        for cb in range(4):
            j = it % 4
            it += 1
            cs = slice(cb * 512, (cb + 1) * 512)
            P.mm([(lambda pe, k=k, j=j, oi=oi, oo=oo, cs=cs: pe.matmul(ps[j][:], ot[oi][:, k, oo:oo + 128], W[:, k, cs], start=(k == 0), stop=(k == 15)))
                  for k in range(16)], r=[t_ot[oi], t_W], w=[t_ps[j]])
            P.op("dve", lambda v, j=j, r=r, cs=cs: v.tensor_tensor(out=tmp[j % 2][:], in0=ps[j][:], in1=GA[r][:, cs], op=ALU.mult),
                 r=[t_ps[j], tGA[r]], w=[t_tmp[j % 2]])
            P.op("dve", lambda g, j=j, i=i, cs=cs: g.tensor_tensor(out=xt[i][:, cs], in0=xt[i][:, cs], in1=tmp[j % 2][:], op=ALU.add),
                 r=[t_tmp[j % 2], t_xt[i]], w=[t_xt[i]])
            if side is not None and cb % 2 == 1:
                next(side, None)
        P.dma("sp", xo[ti * 128:(ti + 1) * 128, :], xt[i][:], r=[t_xt[i]], w=[R.tk(xout, ti)])
    if side is not None:
        for _ in side:
            pass
    C.close()


H2C = NTOK + 3
FFN_BLOCKS = [(1, 769, [0, 1, 2, 3, 4, 5]), (770, 768, [6, 7, 8, 9, 10, 11]), (1538, 768, [12, 13, 14, 15, 16, 17])]


def tile_col(ti):
    return 1 + ti * 128 if ti < 2 else 258 + (ti - 2) * 128


def phase_E(P, R, K, l, xname, xout, dbg=""):
    w_up = R.get("ffn_w_up", [R.nl, D, 2 * DFF])[R.li(l):R.li(l) + 1]
    w_dn = R.get("ffn_w_down", [R.nl, DFF, D])[R.li(l):R.li(l) + 1]
    cw = R.get("ffn_conv_w", [R.nl, 3, DFF])[R.li(l):R.li(l) + 1]
    cb_ = R.get("ffn_conv_b", [R.nl, DFF])[R.li(l):R.li(l) + 1]
    h2T = R.get("h2T", [D, H2C], BF16)
    xb = R.get("xb_recv", [1, D])
    x = R.get(xname, [NTOK, D])
    xo = R.get(xout, [NTOK, D])
    h2v = h2T.rearrange("(k p) c -> p k c", p=128)
    C0 = Ctx(P)
    groups = [[0, 1], [2, 3, 4, 5], [6, 7, 8, 9], [10, 11, 12, 13], [14, 15, 16, 17], [NT]]
    gof = {}
    for gi, gl in enumerate(groups):
        for k_, ti in enumerate(gl):
            gof[ti] = (gi, k_, len(gl))
    stage = [C0.sb([128, 16, 512], BF16, "stage") for _ in range(2)]
    t_stage = [(Tok(), Tok()) for _ in range(2)]
    zt = C0.sb([128, 16, 1], BF16, "zt")
    t_z = Tok()
    P.op("pool", lambda g: g.memset(zt[:], 0.0), w=[t_z])
    P.dma("sp", h2v[:, :, 0:1], zt[:], r=[t_z], w=[R.tk("h2T", "z0")], allow_slow_non_contiguous=True)
    P.dma("sp", h2v[:, :, 257:258], zt[:], r=[t_z], w=[R.tk("h2T", "z1")], allow_slow_non_contiguous=True)

    def after(ti, toks):
        gi, k_, gn = gof[ti]
        if k_ != gn - 1:
            return
        if ti < NT:
            c0 = tile_col(groups[gi][0])
            P.dma("sp", h2v[:, :, c0:c0 + 128 * gn], stage[gi % 2][:, :, 0:128 * gn], r=list(toks), w=[R.tk("h2T", ti)])
        else:
            P.dma("sp", h2v[:, :, H2C - 1:H2C], stage[gi % 2][:, :, 0:1], r=list(toks), w=[R.tk("h2T", ti)], allow_slow_non_contiguous=True)

    norm_mod_transpose(P, R, K, C0, l, xname, "g_ffn", 3, 4, None,
                       dst_fn=lambda ti, h: stage[gof[ti][0] % 2][:, 8 * h:8 * h + 8, 128 * gof[ti][1]:128 * gof[ti][1] + 128],
                       tok_fn=lambda ti: t_stage[gof[ti][0] % 2], after=after, extra_row=(xb, R.tk("xb_recv")))
    C0.close()
    if dbg == "e0":
        return
    h2_toks = [R.tk("h2T", k) for k in ["z0", "z1"] + list(range(NT + 1))]

    C = Ctx(P)
    CW = C.sb([128, NFF, 3], F32, "CW")
    CB = C.sb([128, NFF], F32, "CB")
    t_cw = Tok()
    for j in range(3):
        P.dma("sp", CW[:, :, j], cw[0, j].rearrange("(f p) -> p f", p=128), w=[t_cw], allow_slow_non_contiguous=True)
    P.dma("sp", CB[:], cb_[0].rearrange("(f p) -> p f", p=128), w=[t_cw], allow_slow_non_contiguous=True)
    aT = C.sb([128, NFF, 769], BF16, "aT")
    wuv = w_up[0].rearrange("(k p) c -> p k c", p=128)
    wdv = w_dn[0].rearrange("(f p) c -> p f c", p=128)
    for (a, n, tiles) in FFN_BLOCKS:
        Cu = Ctx(P)
        hb = Cu.sb([128, 16, 771], BF16, "h2blk")
        t_hb = Tok()
        P.dma("sp", hb[:, 0:8, 0:n + 2], h2v[:, 0:8, a - 1:a + n + 1], r=h2_toks, w=[t_hb])
        P.dma("act", hb[:, 8:16, 0:n + 2], h2v[:, 8:16, a - 1:a + n + 1], r=h2_toks, w=[t_hb])
        wu = [Cu.sb([128, 16, 1024], BF16, "wup") for _ in range(2)]
        t_wu = [Tok() for _ in range(2)]
        G = [[Cu.ps([128, 512], F32, "G") for _ in range(2)] for _ in range(2)]
        V = [[Cu.ps([128, 512], F32, "V") for _ in range(2)] for _ in range(2)]
        t_G = [[Tok() for _ in range(2)] for _ in range(2)]
        t_V = [[Tok() for _ in range(2)] for _ in range(2)]
        g1 = [Cu.sb([128, 385], F32, "g1") for _ in range(2)]
        t_g1 = [Tok() for _ in range(2)]
        sl = [Cu.sb([128, 385], F32, "sl") for _ in range(2)]
        t_sl = [Tok() for _ in range(2)]
        n0 = n - 384
        chunks = [(0, n0), (n0, 384)]
        t_aT = Tok()
        for f in range(NFF):
            wi = (f // 4) % 2
            pb = f % 2
            fo = (f % 4) * 128
            if f % 4 == 0:
                P.dma("pool", wu[wi][:, :, 0:512], wuv[:, :, f * 128:f * 128 + 512], w=[t_wu[wi]])
                P.dma("pool", wu[wi][:, :, 512:1024], wuv[:, :, DFF + f * 128:DFF + f * 128 + 512], w=[t_wu[wi]])
            for c, (s0, ln) in enumerate(chunks):
                Gp, Vp = G[pb][c], V[pb][c]
                P.mm([(lambda pe, k=k, Gp=Gp, wi=wi, s0=s0, ln=ln, fo=fo: pe.matmul(Gp[:, 0:ln + 2], wu[wi][:, k, fo:fo + 128], hb[:, k, s0:s0 + ln + 2], start=(k == 0), stop=(k == 15)))
                      for k in range(16)], r=[t_wu[wi], t_hb], w=[t_G[pb][c]])
                P.mm([(lambda pe, k=k, Vp=Vp, wi=wi, s0=s0, ln=ln, fo=fo: pe.matmul(Vp[:, 0:ln], wu[wi][:, k, 512 + fo:512 + fo + 128], hb[:, k, s0 + 1:s0 + ln + 1], start=(k == 0), stop=(k == 15)))
                      for k in range(16)], r=[t_wu[wi], t_hb], w=[t_V[pb][c]])
                P.op("act", lambda a_, Gp=Gp, c=c, f=f, ln=ln: a_.activation(out=g1[c][:, 0:ln], in_=Gp[:, 1:ln + 1], func=AF.Identity,
                                                                           scale=CW[:, f, 1:2], bias=CB[:, f:f + 1]),
                     r=[t_G[pb][c], t_cw], w=[t_g1[c]])
                P.op("dve", lambda v, Gp=Gp, c=c, f=f, ln=ln: v.scalar_tensor_tensor(out=g1[c][:, 0:ln], in0=Gp[:, 0:ln], scalar=CW[:, f, 0:1], in1=g1[c][:, 0:ln],
                                                                                  op0=ALU.mult, op1=ALU.add),
                     r=[t_G[pb][c], t_cw, t_g1[c]], w=[t_g1[c]])
                P.op("dve", lambda v, Gp=Gp, c=c, f=f, ln=ln: v.scalar_tensor_tensor(out=g1[c][:, 0:ln], in0=Gp[:, 2:ln + 2], scalar=CW[:, f, 2:3], in1=g1[c][:, 0:ln],
                                                                                  op0=ALU.mult, op1=ALU.add),
                     r=[t_G[pb][c], t_cw, t_g1[c]], w=[t_g1[c]])
                P.op("act", lambda a_, c=c, ln=ln: a_.activation(out=sl[c][:, 0:ln], in_=g1[c][:, 0:ln], func=AF.Silu), r=[t_g1[c]], w=[t_sl[c]])
                P.op("dve", lambda v, Vp=Vp, c=c, f=f, s0=s0, ln=ln: v.tensor_tensor(out=aT[:, f, s0:s0 + ln], in0=sl[c][:, 0:ln], in1=Vp[:, 0:ln], op=ALU.mult),
                     r=[t_sl[c], t_V[pb][c]], w=[t_aT])
        Cu.close()
        if dbg == "up":
            continue
        Cd = Ctx(P)
        GF, tGF = [], []
        for r in range(2):
            g, tg = load_bcast_mod(P, Cd, R, l, r, 5, "gf%d" % r)
            GF.append(g); tGF.append(tg)
        wd = [Cd.sb([128, 11, 512], BF16, "wd") for _ in range(8)]
        t_wd = [Tok() for _ in range(8)]
        ps = [Cd.ps([128, 512], F32, "psd") for _ in range(6)]
        t_ps = [Tok() for _ in range(6)]
        xt = [Cd.sb([128, 512], F32, "xt") for _ in range(4)]
        t_xt = [Tok() for _ in range(4)]
        tmp = [Cd.sb([128, 512], F32, "tmp") for _ in range(2)]
        t_tmp = [Tok() for _ in range(2)]
        x_it = 0
        for cb in range(4):
            cs = slice(cb * 512, (cb + 1) * 512)
            for fc in range(4):
                wi = (cb % 2) * 4 + fc
                P.dma("pool", wd[wi][:], wdv[:, fc * 11:(fc + 1) * 11, cs], w=[t_wd[wi]])
            for hf in range(2):
                tl = list(enumerate(tiles))[3 * hf:3 * hf + 3]
                for fc in range(4):
                    wi = (cb % 2) * 4 + fc
                    fns = []
                    for ff in range(11):
                        f = fc * 11 + ff
                        for q, ti in tl:
                            lc = tile_col(ti) - a
                            fns.append(lambda pe, q=q, f=f, ff=ff, lc=lc, wi=wi: pe.matmul(ps[q][:], aT[:, f, lc:lc + 128], wd[wi][:, ff, :], start=(f == 0), stop=(f == NFF - 1)))
                    P.mm(fns, r=[t_wd[wi], t_aT], w=[t_ps[q] for q, _ in tl])
                for q, ti in tl:
                    r = 1 if ti < 2 else 0
                    xi = x_it % 4
                    x_it += 1
                    P.dma("sp", xt[xi][:], x[ti * 128:(ti + 1) * 128, cs], r=[R.tk(xname, ti)], w=[t_xt[xi]])
                    P.op("dve", lambda v, q=q, r=r, cs=cs, xi=xi: v.tensor_tensor(out=tmp[xi % 2][:], in0=ps[q][:], in1=GF[r][:, cs], op=ALU.mult),
                         r=[t_ps[q], tGF[r]], w=[t_tmp[xi % 2]])
                    P.op("dve", lambda v, xi=xi: v.tensor_tensor(out=xt[xi][:], in0=xt[xi][:], in1=tmp[xi % 2][:], op=ALU.add),
                         r=[t_tmp[xi % 2], t_xt[xi]], w=[t_xt[xi]])
                    P.dma("sp", xo[ti * 128:(ti + 1) * 128, cs], xt[xi][:], r=[t_xt[xi]], w=[R.tk(xout, (ti, cb))])
        Cd.close()
    C.close()


NEG = -30000.0


def phase_C(P, R, K, l, ctx_full=True, dbg_qtiles=None, dbg_skip=()):
    qT = R.get("qT", [NQ, NTOK], BF16)
    kT = R.get("kT", [128, NTOK + 128], BF16)
    vv = R.get("v", [NTOK + 128, 128], BF16)
    oT = R.get("oT_a", [NQ, NTOK], BF16)
    sink = R.get("attn_sink", [R.nl, 16])[R.li(l):R.li(l) + 1]
    C = Ctx(P)
    io = C.sb([128, 128], I32, "mio")
    mf = C.sb([128, 128], F32, "mf")
    masks = {}
    t_m = Tok()
    for name, cm, op, thr in (("prev", -1, ALU.is_le, 0.0), ("next", -1, ALU.is_ge, 0.0), ("halo", 1, ALU.is_ge, 127.0)):
        mk = C.sb([128, 8, 128], BF16, "mask_" + name)
        P.op("pool", lambda g, cm=cm: g.iota(io[:], [[1, 128]], base=0, channel_multiplier=cm), r=[t_m], w=[t_m])
        P.op("dve", lambda v: v.tensor_copy(out=mf[:], in_=io[:]), r=[t_m], w=[t_m])
        for h8 in range(8):
            P.op("dve", lambda v, mk=mk, op=op, thr=thr, h8=h8: v.tensor_scalar(out=mk[:, h8, :], in0=mf[:], scalar1=thr, scalar2=None, op0=op), r=[t_m], w=[t_m])
        masks[name] = mk
    onesA = C.sb([128, 128], BF16, "onesA")
    onesB = C.sb([128, 128], BF16, "onesB")
    P.op("pool", lambda g: g.memset(onesA[:], 0.0), w=[t_m])
    P.op("pool", lambda g: g.memset(onesB[:], 0.0), w=[t_m])
    P.op("pool", lambda g: g.memset(onesA[:, 0:64], 1.0), r=[t_m], w=[t_m])
    P.op("pool", lambda g: g.memset(onesB[:, 64:128], 1.0), r=[t_m], w=[t_m])
    es = C.sb([128, 8], F32, "esink")
    t_es = Tok()
    sv = sink[0].rearrange("(m h) -> h m", h=2)
    P.dma("sp", es[0:64, :], sv[0].partition_broadcast(64), w=[t_es], allow_slow_non_contiguous=True)
    P.dma("sp", es[64:128, :], sv[1].partition_broadcast(64), w=[t_es], allow_slow_non_contiguous=True)
    P.op("act", lambda a: a.activation(out=es[:], in_=es[:], func=AF.Exp), r=[t_es], w=[t_es])
    NKB = NT + 1
    KT2 = [C.sb([128, NTOK + 128], BF16, "KT2_%d" % j) for j in range(2)]
    t_k = Tok()
    for j in range(2):
        P.dma("sp", KT2[j][0:64, :], kT[64 * j:64 * j + 64, :], w=[t_k])
        P.dma("act", KT2[j][64:128, :], kT[64 * j:64 * j + 64, :], w=[t_k])
    VP = C.sb([128, NKB, 2, 256], BF16, "VP")
    t_v = Tok()
    P.op("pool", lambda g: g.memset(VP[:], 1.0), w=[t_v])
    v3 = vv.rearrange("(kb p) c -> p kb c", p=128)
    for j in range(2):
        P.dma("sp", VP[:, :, j, 0:64], v3[:, :, 64 * j:64 * j + 64], r=[t_v], w=[t_v])
        P.dma("act", VP[:, :, j, 192:256], v3[:, :, 64 * j:64 * j + 64], r=[t_v], w=[t_v])
    qs = [C.sb([128, 8, 128], BF16, "qs") for _ in range(2)]
    t_qs = [Tok() for _ in range(2)]
    S = [C.ps([128, 1024], F32, "S") for _ in range(2)]
    t_S = [Tok() for _ in range(2)]
    PT = [C.sb([128, 5, 1024], BF16, "PT") for _ in range(2)]
    t_PT = [Tok() for _ in range(2)]
    Op = [C.ps([128, 1024], F32, "Op") for _ in range(2)]
    t_O = [Tok() for _ in range(2)]
    dsb = [C.sb([128, 512], F32, "dsb") for _ in range(2)]
    t_dsb = [Tok() for _ in range(2)]
    osb = [C.sb([128, 4, 128], BF16, "osb") for _ in range(2)]
    t_osb = [Tok() for _ in range(2)]
    qtiles = list(range(NT)) if ctx_full else list(range(2, NT))
    if dbg_qtiles is not None:
        qtiles = dbg_qtiles
    s_it = 0
    pj = 0
    for n_, qt in enumerate(qtiles):
        qi = n_ % 2
        P.dma("sp", qs[qi][:], qT[:, qt * 128:(qt + 1) * 128].rearrange("(m p) t -> p m t", p=128), w=[t_qs[qi]])
        if qt < 2:
            kbs = [(0, None), (1, None)]
        else:
            i = qt - 2
            kbs = [(0, None), (1, None), (qt, None)]
            if i > 0:
                kbs.append((qt - 1, "prev"))
            if i < 15:
                kbs.append((qt + 1, "next"))
            else:
                kbs.append((NT, "halo"))
        for j in range(2):
            pi = pj % 2
            pj += 1
            for kbi, (kb, mname) in enumerate(kbs):
                si = s_it % 2
                s_it += 1
                fns = []
                for hh in range(8):
                    m, e = 4 * j + hh // 2, hh % 2
                    cbk = e * 4 + hh // 2
                    fns.append(lambda pe, si=si, hh=cbk, m=m, e=e, kb=kb, qi=qi, j=j, mname=mname: pe.matmul(
                        S[si][:, hh * 128:(hh + 1) * 128], KT2[j][64 * e:64 * e + 64, kb * 128:(kb + 1) * 128],
                        qs[qi][64 * e:64 * e + 64, m, :], start=True, stop=True))
                P.mm(fns, r=[t_k, t_qs[qi]], w=[t_S[si]])
                P.op("act", lambda a, si=si, pi=pi, kbi=kbi: a.activation(out=PT[pi][:, kbi, :], in_=S[si][:], func=AF.Exp, scale=0.125),
                     r=[t_S[si]], w=[t_PT[pi]])
                if mname is not None:
                    P.op("pool", lambda g, pi=pi, kbi=kbi, mname=mname: g.tensor_tensor(out=PT[pi][:, kbi, :], in0=PT[pi][:, kbi, :],
                                                                                       in1=masks[mname][:].rearrange("p h q -> p (h q)"), op=ALU.mult),
                         r=[t_PT[pi], t_m], w=[t_PT[pi]])
            if "pv" in dbg_skip:
                continue
            fns = []
            nk = len(kbs)
            for hh in range(8):
                mm, e = hh // 2, hh % 2
                for kbi, (kb, _) in enumerate(kbs):
                    fns.append(lambda pe, pi=pi, hh=hh, mm=mm, kbi=kbi, kb=kb, e=e, j=j, nk=nk: pe.matmul(
                        Op[pi][:, hh * 128:(hh + 1) * 128], VP[:, kb, j, 128 * e:128 * e + 128],
                        PT[pi][:, kbi, (4 * e + mm) * 128:(4 * e + mm + 1) * 128], start=(kbi == 0), stop=(kbi == nk - 1)))
            P.mm(fns, r=[t_v, t_PT[pi]], w=[t_O[pi]])
            Ov = Op[pi][:].rearrange("p (m e q) -> p m e q", e=2, q=128)
            dv = dsb[pi][:].rearrange("p (m q) -> p m q", q=128)
            P.op("act", lambda a, Ov=Ov, dv=dv: a.copy(out=dv[0:64], in_=Ov[64:128, :, 0, :]), r=[t_O[pi]], w=[t_dsb[pi]])
            P.op("act", lambda a, Ov=Ov, dv=dv: a.copy(out=dv[64:128], in_=Ov[0:64, :, 1, :]), r=[t_O[pi]], w=[t_dsb[pi]])
            for mm in range(4):
                m = 4 * j + mm
                P.op("dve", lambda v, pi=pi, mm=mm, m=m: v.tensor_scalar(out=dsb[pi][:, mm * 128:(mm + 1) * 128], in0=dsb[pi][:, mm * 128:(mm + 1) * 128],
                                                                      scalar1=es[:, m:m + 1], scalar2=None, op0=ALU.add),
                     r=[t_dsb[pi], t_es], w=[t_dsb[pi]])
            P.op("dve", lambda v, pi=pi: v.reciprocal(out=dsb[pi][:], in_=dsb[pi][:]), r=[t_dsb[pi]], w=[t_dsb[pi]])
            P.op("dve", lambda v, pi=pi, Ov=Ov, dv=dv: v.tensor_tensor(out=osb[pi][0:64], in0=Ov[0:64, :, 0, :], in1=dv[0:64], op=ALU.mult),
                 r=[t_O[pi], t_dsb[pi]], w=[t_osb[pi]])
            P.op("dve", lambda v, pi=pi, Ov=Ov, dv=dv: v.tensor_tensor(out=osb[pi][64:128], in0=Ov[64:128, :, 1, :], in1=dv[64:128], op=ALU.mult),
                 r=[t_O[pi], t_dsb[pi]], w=[t_osb[pi]])
            P.dma("sp", oT[512 * j:512 * j + 512, qt * 128:(qt + 1) * 128].rearrange("(m p) t -> p m t", p=128), osb[pi][:],
                  r=[t_osb[pi]], w=[R.tk("oT_a", (qt, j))])
    C.close()


TCH = 16
NCH = NTOK // TCH
NCC = NCTX // TCH


class SSMState:
    pass


def ssm_prep(P, R, K, C, l):
    S = SSMState()
    li = R.li(l)
    lam_re = R.get("ssm_lambda_re", [R.nl, 2, 32, 64])
    lam_im = R.get("ssm_lambda_im", [R.nl, 2, 32, 64])
    log_dt = R.get("ssm_log_dt", [R.nl, 2, 32])
    b_re = R.get("ssm_b_re", [R.nl, 2, 32, 64, 16])
    b_im = R.get("ssm_b_im", [R.nl, 2, 32, 64, 16])
    c_re = R.get("ssm_c_re", [R.nl, 2, 32, 16, 64])
    c_im = R.get("ssm_c_im", [R.nl, 2, 32, 16, 64])
    pf = C.sb([128, 16], F32, "pf")
    ARE = C.sb([128, 64], F32, "ARE")
    AIM = C.sb([128, 64], F32, "AIM")
    LB = C.sb([128, 64, 128], BF16, "LB")
    CW = C.sb([128, 64, 16], F32, "CW")
    CL = C.sb([128, 64], F32, "CL")
    CR = C.sb([128, 64], F32, "CR")
    ID2 = C.sb([128, 64], F32, "ID2")
    ATR = C.sb([64, 64], F32, "ATR")
    ATI = C.sb([64, 64], F32, "ATI")
    Ct = Ctx(P)
    t = Tok()
    LR = Ct.sb([128, 64], F32, "LR")
    LI = Ct.sb([128, 64], F32, "LI")
    DT = Ct.sb([128, 64], F32, "DT")
    for base in (0, 64):
        P.dma("sp", LR[base:base + 64, :], lam_re[li].rearrange("d g p -> p (d g)"), w=[t], allow_slow_non_contiguous=True)
        P.dma("sp", LI[base:base + 64, :], lam_im[li].rearrange("d g p -> p (d g)"), w=[t], allow_slow_non_contiguous=True)
    P.dma("sp", DT[:], log_dt[li].rearrange("d g -> (d g)").partition_broadcast(128), w=[t])
    pi_ = Ct.sb([128, 1], I32, "pi")
    pj_ = Ct.sb([128, 1], I32, "pj")
    tp = Tok()
    P.op("pool", lambda g: g.iota(pi_[:], [[0, 1]], base=0, channel_multiplier=1), w=[tp])
    P.op("dve", lambda v: v.tensor_scalar(out=pj_[:], in0=pi_[:], scalar1=6, scalar2=None, op0=ALU.logical_shift_right), r=[tp], w=[tp])
    P.op("dve", lambda v: v.tensor_copy(out=pf[:, 1:2], in_=pj_[:]), r=[tp], w=[tp])
    P.op("dve", lambda v: v.tensor_scalar(out=pf[:, 0:1], in0=pf[:, 1:2], scalar1=-1.0, scalar2=1.0, op0=ALU.mult, op1=ALU.add), r=[tp], w=[tp])
    P.op("dve", lambda v: v.tensor_scalar(out=pf[:, 2:3], in0=pf[:, 1:2], scalar1=2.0, scalar2=-1.0, op0=ALU.mult, op1=ALU.add), r=[tp], w=[tp])
    P.op("dve", lambda v: v.tensor_scalar(out=pf[:, 3:4], in0=pf[:, 1:2], scalar1=-1.0, scalar2=None, op0=ALU.mult), r=[tp], w=[tp])
    P.op("dve", lambda v: v.tensor_scalar(out=pj_[:], in0=pi_[:], scalar1=4, scalar2=None, op0=ALU.logical_shift_right), r=[tp], w=[tp])
    P.op("dve", lambda v: v.tensor_copy(out=pf[:, 12:13], in_=pj_[:]), r=[tp], w=[tp])
    for i in range(8):
        P.op("dve", lambda v, i=i: v.tensor_scalar(out=pf[:, 4 + i:5 + i], in0=pf[:, 12:13], scalar1=float(i), scalar2=None, op0=ALU.is_equal), r=[tp], w=[tp])
    S.pf, S.t_pf = pf, tp
    P.op("act", lambda a: a.activation(out=DT[:], in_=DT[:], func=AF.Exp), r=[t], w=[t])
    MAG = Ct.sb([128, 64], F32, "MAG")
    ANG = Ct.sb([128, 64], F32, "ANG")
    P.op("dve", lambda v: v.tensor_tensor(out=MAG[:], in0=LR[:], in1=DT[:], op=ALU.mult), r=[t], w=[t])
    P.op("act", lambda a: a.activation(out=MAG[:], in_=MAG[:], func=AF.Exp), r=[t], w=[t])
    P.op("dve", lambda v: v.tensor_tensor(out=ANG[:], in0=LI[:], in1=DT[:], op=ALU.mult), r=[t], w=[t])
    SN = Ct.sb([128, 64], F32, "SN")
    CS = Ct.sb([128, 64], F32, "CS")
    Cr = Ctx(P)
    tsn = Tok(); tsn.w = t.w
    range_reduce_sin(P, Cr, SN[:], ANG[:], [128, 64], t)
    range_reduce_sin(P, Cr, CS[:], ANG[:], [128, 64], t, extra=PI / 2)
    Cr.close()
    t_a = Tok()
    P.op("dve", lambda v: v.tensor_tensor(out=ARE[:], in0=MAG[:], in1=CS[:], op=ALU.mult), r=[t], w=[t_a])
    P.op("dve", lambda v: v.tensor_tensor(out=AIM[:], in0=MAG[:], in1=SN[:], op=ALU.mult), r=[t], w=[t_a])
    S.ARE, S.AIM, S.t_a = ARE, AIM, t_a
    DEN = Ct.sb([128, 64], F32, "DEN")
    T1 = Ct.sb([128, 64], F32, "T1")
    NRE = Ct.sb([128, 64], F32, "NRE")
    FRE = Ct.sb([128, 64], F32, "FRE")
    FIM = Ct.sb([128, 64], F32, "FIM")
    tf = Tok()
    P.op("dve", lambda v: v.tensor_tensor(out=DEN[:], in0=LR[:], in1=LR[:], op=ALU.mult), r=[t], w=[tf])
    P.op("dve", lambda v: v.tensor_tensor(out=T1[:], in0=LI[:], in1=LI[:], op=ALU.mult), r=[t], w=[tf])
    P.op("dve", lambda v: v.tensor_tensor(out=DEN[:], in0=DEN[:], in1=T1[:], op=ALU.add), r=[tf], w=[tf])
    P.op("dve", lambda v: v.reciprocal(out=DEN[:], in_=DEN[:]), r=[tf], w=[tf])
    P.op("dve", lambda v: v.tensor_scalar(out=NRE[:], in0=ARE[:], scalar1=-1.0, scalar2=None, op0=ALU.add), r=[t_a], w=[tf])
    P.op("dve", lambda v: v.tensor_tensor(out=FRE[:], in0=NRE[:], in1=LR[:], op=ALU.mult), r=[tf, t], w=[tf])
    P.op("dve", lambda v: v.tensor_tensor(out=T1[:], in0=AIM[:], in1=LI[:], op=ALU.mult), r=[tf, t, t_a], w=[tf])
    P.op("dve", lambda v: v.tensor_tensor(out=FRE[:], in0=FRE[:], in1=T1[:], op=ALU.add), r=[tf], w=[tf])
    P.op("dve", lambda v: v.tensor_tensor(out=FRE[:], in0=FRE[:], in1=DEN[:], op=ALU.mult), r=[tf], w=[tf])
    P.op("dve", lambda v: v.tensor_tensor(out=FIM[:], in0=AIM[:], in1=LR[:], op=ALU.mult), r=[tf, t, t_a], w=[tf])
    P.op("dve", lambda v: v.tensor_tensor(out=T1[:], in0=NRE[:], in1=LI[:], op=ALU.mult), r=[tf, t], w=[tf])
    P.op("dve", lambda v: v.tensor_tensor(out=FIM[:], in0=FIM[:], in1=T1[:], op=ALU.subtract), r=[tf], w=[tf])
    P.op("dve", lambda v: v.tensor_tensor(out=FIM[:], in0=FIM[:], in1=DEN[:], op=ALU.mult), r=[tf], w=[tf])
    P.op("dve", lambda v: v.tensor_scalar(out=FIM[:], in0=FIM[:], scalar1=pf[:, 2:3], scalar2=None, op0=ALU.mult), r=[tf, tp], w=[tf])
    BX = Ct.sb([128, 64, 16], F32, "BX")
    BY = Ct.sb([128, 64, 16], F32, "BY")
    tb = Tok()
    brv = b_re[li].rearrange("d g p h -> p (d g) h")
    biv = b_im[li].rearrange("d g p h -> p (d g) h")
    P.dma("sp", BX[0:64], brv, w=[tb])
    P.dma("act", BX[64:128], biv, w=[tb])
    P.dma("sp", BY[0:64], biv, w=[tb])
    P.dma("act", BY[64:128], brv, w=[tb])
    for h in range(16):
        P.op("dve", lambda v, h=h: v.tensor_tensor(out=BX[:, :, h], in0=BX[:, :, h], in1=FRE[:], op=ALU.mult), r=[tb, tf], w=[tb])
        P.op("pool", lambda g, h=h: g.tensor_tensor(out=BY[:, :, h], in0=BY[:, :, h], in1=FIM[:], op=ALU.mult), r=[tb, tf], w=[tb])
    P.op("dve", lambda v: v.tensor_tensor(out=BX[:], in0=BX[:], in1=BY[:], op=ALU.add), r=[tb], w=[tb])
    t_LB = Tok()
    psT = Ct.ps([128, 128], F32, "psT")
    t_psT = Tok()
    for b8 in range(8):
        P.mm([lambda pe, b8=b8: pe.transpose(psT[:], BX[:, b8 * 8:(b8 + 1) * 8, :].rearrange("p g h -> p (g h)"), K["idf"][:])],
             r=[tb, K["tok"]], w=[t_psT])
        for i in range(8):
            P.op("dve", lambda v, b8=b8, i=i: v.tensor_scalar(out=LB[:, b8 * 8 + i, :], in0=psT[:], scalar1=pf[:, 4 + i:5 + i], scalar2=None, op0=ALU.mult),
                 r=[t_psT, tp], w=[t_LB])
    S.LB, S.t_LB = LB, t_LB
    t_CW = Tok()
    crv = c_re[li].rearrange("d g h p -> p (d g) h")
    civ = c_im[li].rearrange("d g h p -> p (d g) h")
    for q4 in range(0, 64, 4):
        P.dma("sp", CW[0:64, q4:q4 + 4, :], crv[:, q4:q4 + 4, :], w=[t_CW], allow_slow_non_contiguous=True)
        P.dma("act", CW[64:128, q4:q4 + 4, :], civ[:, q4:q4 + 4, :], w=[t_CW], allow_slow_non_contiguous=True)
    P.op("dve", lambda v: v.tensor_scalar(out=CW[:].rearrange("p g h -> p (g h)"), in0=CW[:].rearrange("p g h -> p (g h)"), scalar1=pf[:, 2:3], scalar2=-1.0,
                                          op0=ALU.mult, op1=ALU.mult), r=[t_CW, tp], w=[t_CW])
    S.CW, S.t_CW = CW, t_CW
    t_c = Tok()
    P.op("dve", lambda v: v.tensor_scalar(out=CL[:], in0=ARE[:], scalar1=pf[:, 0:1], scalar2=None, op0=ALU.mult), r=[t_a, tp], w=[t_c])
    P.op("dve", lambda v: v.scalar_tensor_tensor(out=CL[:], in0=AIM[:], scalar=pf[:, 3:4], in1=CL[:], op0=ALU.mult, op1=ALU.add), r=[t_a, tp, t_c], w=[t_c])
    P.op("dve", lambda v: v.tensor_scalar(out=CR[:], in0=AIM[:], scalar1=pf[:, 0:1], scalar2=None, op0=ALU.mult), r=[t_a, tp], w=[t_c])
    P.op("dve", lambda v: v.scalar_tensor_tensor(out=CR[:], in0=ARE[:], scalar=pf[:, 1:2], in1=CR[:], op0=ALU.mult, op1=ALU.add), r=[t_a, tp, t_c], w=[t_c])
    S.CL, S.CR, S.t_c = CL, CR, t_c
    t_id2 = Tok()
    P.op("dve", lambda v: v.tensor_copy(out=ID2[0:64, :], in_=K["idf"][0:64, 0:64]), r=[K["tok"]], w=[t_id2])
    P.op("dve", lambda v: v.tensor_copy(out=ID2[64:128, :], in_=K["idf"][64:128, 64:128]), r=[K["tok"]], w=[t_id2])
    S.ID2, S.t_id2 = ID2, t_id2
    q1 = Ct.sb([64, 64], F32, "q1")
    q2 = Ct.sb([64, 64], F32, "q2")
    t_at = Tok()
    P.op("dve", lambda v: v.tensor_copy(out=ATR[:], in_=ARE[0:64, :]), r=[t_a], w=[t_at])
    P.op("dve", lambda v: v.tensor_copy(out=ATI[:], in_=AIM[0:64, :]), r=[t_a], w=[t_at])
    for _ in range(4):
        P.op("dve", lambda v: v.tensor_tensor(out=q1[:], in0=ATR[:], in1=ATR[:], op=ALU.mult), r=[t_at], w=[t_at])
        P.op("dve", lambda v: v.tensor_tensor(out=q2[:], in0=ATI[:], in1=ATI[:], op=ALU.mult), r=[t_at], w=[t_at])
        P.op("dve", lambda v: v.tensor_tensor(out=ATI[:], in0=ATR[:], in1=ATI[:], op=ALU.mult), r=[t_at], w=[t_at])
        P.op("dve", lambda v: v.tensor_scalar(out=ATI[:], in0=ATI[:], scalar1=2.0, scalar2=None, op0=ALU.mult), r=[t_at], w=[t_at])
        P.op("dve", lambda v: v.tensor_tensor(out=ATR[:], in0=q1[:], in1=q2[:], op=ALU.subtract), r=[t_at], w=[t_at])
    S.ATR, S.ATI, S.t_at = ATR, ATI, t_at
    Ct.close()
    return S


def ssm_rvm(P, C, S, UT, t_UT, gds, Hs, t_Hs, ps, t_ps, la, t_la, hinit=None, jmap=lambda j: j, act_only=False):
    le = "pool" if act_only else "dve"
    for i, gd in enumerate(gds):
        P.op(le, lambda v, i=i, gd=gd: v.tensor_scalar(out=la[i][:, 0:64], in0=S.ID2[:], scalar1=S.CL[:, gd:gd + 1], scalar2=None, op0=ALU.mult),
             r=[S.t_id2, S.t_c], w=[t_la[i]])
        P.op(le, lambda v, i=i, gd=gd: v.tensor_scalar(out=la[i][:, 64:128], in0=S.ID2[:], scalar1=S.CR[:, gd:gd + 1], scalar2=None, op0=ALU.mult),
             r=[S.t_id2, S.t_c], w=[t_la[i]])
    for s in range(TCH):
        for i, gd in enumerate(gds):
            d, g = gd // 32, gd % 32
            j = s if d == 0 else TCH - 1 - s
            jprev = j - 1 if d == 0 else j + 1
            u_cols = UT[:, g // 8, :].rearrange("p (c j) -> p j c", j=TCH)[:, j, :]
            fns = []
            has2 = (s > 0) or (hinit is not None)
            fns.append(lambda pe, i=i, gd=gd, u_cols=u_cols, has2=has2: pe.matmul(ps[i], S.LB[:, gd, :], u_cols, start=True, stop=(not has2)))
            rd = [t_UT, S.t_LB, t_la[i]]
            if s > 0:
                fns.append(lambda pe, i=i, jprev=jprev: pe.matmul(ps[i], la[i][:], Hs[i][:, :, jmap(jprev)], start=False, stop=True))
                rd.append(t_Hs[i])
            elif hinit is not None:
                fns.append(lambda pe, i=i: pe.matmul(ps[i], la[i][:], hinit[i][0], start=False, stop=True))
                rd.append(hinit[i][1])
            P.mm(fns, r=rd, w=[t_ps[i]])
            if act_only or (i + s) % 2 == 0:
                P.op("act", lambda a, i=i, j=j: a.copy(out=Hs[i][:, :, jmap(j)], in_=ps[i]), r=[t_ps[i]], w=[t_Hs[i]])
            else:
                P.op("dve", lambda v, i=i, j=j: v.tensor_copy(out=Hs[i][:, :, jmap(j)], in_=ps[i]), r=[t_ps[i]], w=[t_Hs[i]])


def ssm_chain(P, C, S, LL, t_L, d, c_list, init, eng="dve"):
    g0 = d * 32
    HH = [C.sb([64, 3, 32], F32, "chH") for _ in range(2)]
    T1 = C.sb([64, 2, 32], F32, "chT1")
    T2 = C.sb([64, 2, 32], F32, "chT2")
    A1 = C.sb([64, 2, 32], F32, "chA1")
    A2 = C.sb([64, 2, 32], F32, "chA2")
    tH = Tok()

    def op(fn, r, w):
        P.op(eng, fn, r=r, w=w)
    op(lambda v: v.tensor_copy(out=A1[:, 0, :], in_=S.ATR[:, g0:g0 + 32]), [S.t_at], [tH])
    op(lambda v: v.tensor_copy(out=A1[:, 1, :], in_=S.ATR[:, g0:g0 + 32]), [S.t_at], [tH])
    op(lambda v: v.tensor_scalar(out=A2[:, 0, :], in0=S.ATI[:, g0:g0 + 32], scalar1=-1.0, scalar2=None, op0=ALU.mult), [S.t_at], [tH])
    op(lambda v: v.tensor_copy(out=A2[:, 1, :], in_=S.ATI[:, g0:g0 + 32]), [S.t_at], [tH])
    if init is None:
        op(lambda v: v.memset(HH[0][:], 0.0), [], [tH])
    else:
        op(lambda v: v.tensor_copy(out=HH[0][:, 0:2, :], in_=init[0]), [init[1]], [tH])
        op(lambda v: v.tensor_copy(out=HH[0][:, 2, :], in_=init[0][:, 0, :]), [init[1]], [tH])
    cur = 0
    for c in c_list:
        h, hn = HH[cur], HH[1 - cur]
        lc = LL[:, :, g0:g0 + 32, c]
        rr = [tH, t_L]
        op(lambda v, h=h: v.tensor_tensor(out=T1[:], in0=A1[:], in1=h[:, 0:2, :], op=ALU.mult), rr, [tH])
        op(lambda v, h=h: v.tensor_tensor(out=T2[:], in0=A2[:], in1=h[:, 1:3, :], op=ALU.mult), rr, [tH])
        op(lambda v: v.tensor_tensor(out=T1[:], in0=T1[:], in1=T2[:], op=ALU.add), rr, [tH])
        op(lambda v, hn=hn, lc=lc: v.tensor_tensor(out=hn[:, 0:2, :], in0=T1[:], in1=lc, op=ALU.add), rr, [tH])
        op(lambda v, hn=hn: v.tensor_copy(out=hn[:, 2, :], in_=hn[:, 0, :]), rr, [tH])
        op(lambda v, h=h, lc=lc: v.tensor_copy(out=lc, in_=h[:, 0:2, :]), rr, [tH, t_L])
        cur = 1 - cur
    return HH[cur], tH


def phase_B(P, R, K, l, part, ctx_full=True, dbg="", S=None):
    li = R.li(l)
    uT = R.get("uT", [512, NTOK])
    send = R.get("ssm_send", [64, 2, 32])
    recv = R.get("ssm_recv", [64, 2, 32])
    hinit0 = R.get("hinit0", [64, 2, 32, NCH])
    oTs = R.get("oT_s", [512, NTOK], BF16)
    C = Ctx(P)
    if S is None:
        S = ssm_prep(P, R, K, C, l)
    if dbg == "prep":
        C.close(); return
    yTM = C.sb([128, NT, 512], F32, "yTM") if part == "b" else None
    t_y = Tok()
    Cs = Ctx(P)
    UT = Cs.sb([128, 4, NTOK], BF16, "UT")
    t_UT = Tok()
    P.dma("pool", UT[:], uT.rearrange("(c p) t -> p c t", p=128), w=[t_UT])
    shared = getattr(S, "LL", None) is not None
    if shared:
        LL, t_Ld = S.LL, S.t_Ld
    else:
        LL = Cs.sb([64, 2, 64, NCH], F32, "LL")
        t_Ld = [Tok(), Tok()]
    LRE = LL[:, 0]
    LIM = LL[:, 1]
    NG = 4
    NG1 = 6
    Hs = [Cs.sb([128, NCH, TCH], F32, "Hs") for _ in range(NG)] if part == "b" else []
    t_Hs = [Tok() for _ in range(NG)]
    Hs1 = [Cs.sb([128, NCH, 2], F32, "Hs1") for _ in range(NG1)]
    t_Hs1 = [Tok() for _ in range(NG1)]
    psb = [Cs.ps([128, 512], F32, "psR") for _ in range(NG1)]
    ps = [psb[i][:, 0:NCH] for i in range(NG1)]
    t_ps = [Tok() for _ in range(NG1)]
    la = [Cs.sb([128, 128], F32, "la") for _ in range(NG1)]
    t_la = [Tok() for _ in range(NG1)]

    def pass1(d, act_only=False):
        for g4 in range(0, 32, NG1):
            gds = [d * 32 + g for g in range(g4, min(32, g4 + NG1))]
            ssm_rvm(P, C, S, UT, t_UT, gds, Hs1, t_Hs1, ps, t_ps, la, t_la, jmap=lambda j: j % 2, act_only=act_only)
            jl = (TCH - 1 if d == 0 else 0) % 2
            for i, gd in enumerate(gds):
                P.op("act", lambda a, i=i, gd=gd: a.copy(out=LRE[:, gd, :], in_=Hs1[i][0:64, :, jl]), r=[t_Hs1[i]], w=[t_Ld[d]])
                if act_only:
                    P.op("act", lambda a, i=i, gd=gd: a.copy(out=LIM[:, gd, :], in_=Hs1[i][64:128, :, jl]), r=[t_Hs1[i]], w=[t_Ld[d]])
                else:
                    P.op("dve", lambda v, i=i, gd=gd: v.tensor_copy(out=LIM[:, gd, :], in_=Hs1[i][64:128, :, jl]), r=[t_Hs1[i]], w=[t_Ld[d]])

    if part == "a":
        pass1(0)
        if dbg == "nochain":
            Cs.close(); C.close(); return
        hh, tH = ssm_chain(P, Cs, S, LL, t_Ld[0], 0, list(range(NCH)), None, eng="dve")
        P.dma("sp", send, hh[:, 0:2, :], r=[tH], w=[R.tk("ssm_send", 0)])
        if shared:
            pass1(1, act_only=True)
        else:
            P.dma("sp", hinit0[:, 0], LRE[:, 0:32, :], r=[t_Ld[0], tH], w=[R.tk("hinit0", 0)])
            P.dma("sp", hinit0[:, 1], LIM[:, 0:32, :], r=[t_Ld[0], tH], w=[R.tk("hinit0", 1)])
        Cs.close()
        C.close()
        return
    if not shared:
        pass1(1)
    if not getattr(S, "chain1_done", False):
        rin = Cs.sb([64, 2, 32], F32, "rin")
        t_rin = Tok()
        P.dma("sp", rin[:], recv, r=[R.tk("ssm_recv", 0)], w=[t_rin])
        ssm_chain(P, Cs, S, LL, t_Ld[1], 1, list(range(NCH - 1, NCC - 1, -1)), (rin[:], t_rin))
        ssm_chain(P, Cs, S, LL, t_Ld[1], 1, list(range(NCC - 1, -1, -1)), None, eng=("pool" if shared else "dve"))
    if not shared:
        P.dma("sp", LRE[:, 0:32, :], hinit0[:, 0], r=[R.tk("hinit0", 0)], w=[t_Ld[0]])
        P.dma("sp", LIM[:, 0:32, :], hinit0[:, 1], r=[R.tk("hinit0", 1)], w=[t_Ld[0]])
    hin = [Cs.sb([128, NCH], F32, "hin") for _ in range(NG)]
    t_hin = [Tok() for _ in range(NG)]
    yps = [Cs.ps([128, 512], F32, "yps") for _ in range(2)]
    t_yps = [Tok() for _ in range(2)]
    for g2 in range(0, 32, 2):
        gds = [g2, 32 + g2, g2 + 1, 32 + g2 + 1]
        hinit = []
        for i, gd in enumerate(gds):
            P.op("act", lambda a, i=i, gd=gd: a.copy(out=hin[i][0:64, :], in_=LRE[:, gd, :]), r=[t_Ld[gd // 32]], w=[t_hin[i]])
            P.op("dve", lambda v, i=i, gd=gd: v.tensor_copy(out=hin[i][64:128, :], in_=LIM[:, gd, :]), r=[t_Ld[gd // 32]], w=[t_hin[i]])
            hinit.append((hin[i][:], t_hin[i]))
        ssm_rvm(P, C, S, UT, t_UT, gds, Hs, t_Hs, ps, t_ps, la, t_la, hinit=hinit)
        for k2 in range(2):
            g = g2 + k2
            yp = yps[k2]
            fns = []
            for ti in range(NT):
                for dd in range(2):
                    i = 2 * k2 + dd
                    gd = gds[i]
                    lhsT = Hs[i][:, ti * 8:(ti + 1) * 8, :].rearrange("p c j -> p (c j)")
                    fns.append(lambda pe, yp=yp, ti=ti, lhsT=lhsT, gd=gd, dd=dd: pe.matmul(yp[:, ti * 16:(ti + 1) * 16], lhsT, S.CW[:, gd, :], start=(dd == 0), stop=(dd == 1)))
            P.mm(fns, r=[t_Hs[2 * k2], t_Hs[2 * k2 + 1], S.t_CW], w=[t_yps[k2]])
            P.op("act" if k2 == 0 else "dve",
                 (lambda a, g=g, yp=yp: a.copy(out=yTM[:, :, 16 * g:16 * g + 16], in_=yp[:, 0:NT * 16].rearrange("p (t h) -> p t h", h=16))) if k2 == 0 else
                 (lambda v, g=g, yp=yp: v.tensor_copy(out=yTM[:, :, 16 * g:16 * g + 16], in_=yp[:, 0:NT * 16].rearrange("p (t h) -> p t h", h=16))),
                 r=[t_yps[k2]], w=[t_y])
    Cs.close()
    Cg = Ctx(P)
    dsk = R.get("ssm_d", [R.nl, 512])
    wglu = R.get("ssm_w_glu", [R.nl, 512, 512])
    bglu = R.get("ssm_b_glu", [R.nl, 512])
    DS = Cg.sb([128, 4], F32, "DS")
    BG = Cg.sb([128, 4], F32, "BG")
    WG = Cg.sb([128, 4, 512], BF16, "WG")
    t_g = Tok()
    P.dma("sp", DS[:], dsk[li].rearrange("(c p) -> p c", p=128), w=[t_g], allow_slow_non_contiguous=True)
    P.dma("sp", BG[:], bglu[li].rearrange("(c p) -> p c", p=128), w=[t_g], allow_slow_non_contiguous=True)
    P.dma("pool", WG[:], wglu[li].rearrange("(c p) n -> p c n", p=128), w=[t_g])
    uf = [Cg.sb([128, 4, 128], F32, "uf") for _ in range(2)]
    t_uf = [Tok() for _ in range(2)]
    pt = [Cg.ps([128, 512], F32, "ptr") for _ in range(2)]
    t_pt = [Tok() for _ in range(2)]
    y2 = [Cg.sb([128, 4, 128], F32, "y2") for _ in range(2)]
    t_y2 = [Tok() for _ in range(2)]
    gf = [Cg.sb([128, 4, 128], F32, "gf") for _ in range(2)]
    gb = [Cg.sb([128, 4, 128], BF16, "gb") for _ in range(2)]
    t_gf = [Tok() for _ in range(2)]
    pg = [Cg.ps([128, 512], F32, "pg") for _ in range(2)]
    t_pg = [Tok() for _ in range(2)]
    sg = [Cg.sb([128, 4, 128], F32, "sg") for _ in range(2)]
    t_sg = [Tok() for _ in range(2)]
    ob = [Cg.sb([128, 4, 128], BF16, "ob") for _ in range(2)]
    t_ob = [Tok() for _ in range(2)]
    tiles = list(range(NT)) if ctx_full else list(range(2, NT))
    for n_, ti in enumerate(tiles):
        i = n_ % 2
        tsl = slice(ti * 128, (ti + 1) * 128)
        P.dma("sp", uf[i][:], uT[:, tsl].rearrange("(c p) t -> p c t", p=128), w=[t_uf[i]])
        P.mm([(lambda pe, c=c, i=i, ti=ti: pe.transpose(pt[i][:, c * 128:(c + 1) * 128], yTM[:, ti, c * 128:(c + 1) * 128], K["idf"][:])) for c in range(4)],
             r=[t_y, K["tok"]], w=[t_pt[i]])
        for c in range(4):
            P.op("dve", lambda v, c=c, i=i: v.scalar_tensor_tensor(out=y2[i][:, c, :], in0=uf[i][:, c, :], scalar=DS[:, c:c + 1], in1=pt[i][:, c * 128:(c + 1) * 128],
                                                                  op0=ALU.mult, op1=ALU.add), r=[t_uf[i], t_pt[i], t_g], w=[t_y2[i]])
        P.op("act", lambda a, i=i: a.activation(out=gf[i][:], in_=y2[i][:], func=AF.Gelu_apprx_tanh), r=[t_y2[i]], w=[t_gf[i]])
        P.op("pool", lambda g_, i=i: g_.tensor_copy(out=gb[i][:], in_=gf[i][:]), r=[t_gf[i]], w=[t_gf[i]])
        fns = []
        for co in range(4):
            for c in range(4):
                fns.append(lambda pe, co=co, c=c, i=i: pe.matmul(pg[i][:, co * 128:(co + 1) * 128], WG[:, c, co * 128:(co + 1) * 128], gb[i][:, c, :], start=(c == 0), stop=(c == 3)))
        P.mm(fns, r=[t_g, t_gf[i]], w=[t_pg[i]])
        for co in range(4):
            P.op("act", lambda a, co=co, i=i: a.activation(out=sg[i][:, co, :], in_=pg[i][:, co * 128:(co + 1) * 128], func=AF.Sigmoid, bias=BG[:, co:co + 1]),
                 r=[t_pg[i], t_g], w=[t_sg[i]])
        P.op("dve", lambda v, i=i: v.tensor_tensor(out=ob[i][:], in0=gf[i][:], in1=sg[i][:], op=ALU.mult), r=[t_gf[i], t_sg[i]], w=[t_ob[i]])
        P.dma("sp", oTs[:, tsl].rearrange("(c p) t -> p c t", p=128), ob[i][:], r=[t_ob[i]], w=[R.tk("oT_s", ti)])
    Cg.close()
    C.close()


def ssm_chain1_async(P, R, Cq, S):
    recv = R.get("ssm_recv", [64, 2, 32])
    rin = Cq.sb([64, 2, 32], F32, "rin")
    t_rin = Tok()
    P.dma("sp", rin[:], recv, r=[R.tk("ssm_recv", 0)], w=[t_rin])
    ssm_chain(P, Cq, S, S.LL, S.t_Ld[1], 1, list(range(NCH - 1, NCC - 1, -1)), (rin[:], t_rin), eng="pool")
    ssm_chain(P, Cq, S, S.LL, S.t_Ld[1], 1, list(range(NCC - 1, -1, -1)), None, eng="pool")
    S.chain1_done = True

def phase_F(P, R, K, xname):
    x = R.get(xname, [NTOK, D])
    y = R.get("y_out", [NLAT, D])
    gf = R.get("g_final", [D])
    C = Ctx(P)
    G = C.sb([128, D], F32, "gfin")
    t_G = Tok()
    P.dma("sp", G[:], gf.partition_broadcast(128), w=[t_G])
    xt = [C.sb([128, D], F32, "xt") for _ in range(2)]
    t_xt = [Tok() for _ in range(2)]
    junk = C.sb([128, D], BF16, "junk")
    t_j = Tok()
    st = [C.sb([128, 4], F32, "st") for _ in range(2)]
    t_st = [Tok() for _ in range(2)]
    yo = [C.sb([128, D], F32, "yo") for _ in range(2)]
    t_yo = [Tok() for _ in range(2)]
    for n_, ti in enumerate(range(2, NT)):
        i = n_ % 2
        P.dma("sp", xt[i][:], x[ti * 128:(ti + 1) * 128, :], r=[R.tk(xname, ti)], w=[t_xt[i]])
        P.op("act", lambda a, i=i: a.activation(out=junk[:], in_=xt[i][:], func=AF.Square, accum_out=st[i][:, 0:1]), r=[t_xt[i]], w=[t_j, t_st[i]])
        P.op("act", lambda a, i=i: a.activation(out=st[i][:, 1:2], in_=st[i][:, 0:1], func=AF.Sqrt, scale=1.0 / D, bias=EPS), r=[t_st[i]], w=[t_st[i]])
        P.op("dve", lambda v, i=i: v.reciprocal(out=st[i][:, 2:3], in_=st[i][:, 1:2]), r=[t_st[i]], w=[t_st[i]])
        P.op("dve", lambda v, i=i: v.scalar_tensor_tensor(out=yo[i][:], in0=xt[i][:], scalar=st[i][:, 2:3], in1=G[:], op0=ALU.mult, op1=ALU.mult),
             r=[t_xt[i], t_st[i], t_G], w=[t_yo[i]])
        P.dma("sp", y[(ti - 2) * 128:(ti - 1) * 128, :], yo[i][:], r=[t_yo[i]], w=[R.tk("y_out", ti)], is_out=True)
    C.close()


CORES = [(b, s) for b in range(4) for s in range(2)]


def prep_core_layer(inputs, l, s):
    g = lambda k: np.asarray(inputs[k])[l:l + 1]
    m = {}
    for k in ("w_out", "ffn_w_up", "ffn_w_down", "ffn_conv_b", "attn_sink", "ssm_d", "ssm_w_glu", "ssm_b_glu", "w_ada", "b_ada",
              "g_mix", "g_ffn", "gmlp_ln_g", "gmlp_ln_b"):
        m[k] = g(k)
    cw = g("ffn_conv_w")
    m["ffn_conv_w"] = cw[:, ::-1] if s == 1 else cw
    for k in ("ssm_lambda_re", "ssm_lambda_im", "ssm_log_dt", "ssm_b_re", "ssm_b_im", "ssm_c_re", "ssm_c_im"):
        a = g(k)
        m[k] = a[:, ::-1] if s == 1 else a
    ws = g("gmlp_w_s")
    bs = g("gmlp_b_s")
    if s == 1:
        ws = ws[:, :, ::-1, ::-1]
        bs = bs[:, :, ::-1]
    m["gmlp_wsT"] = ws.transpose(0, 1, 3, 2)
    m["gmlp_b_s"] = bs
    return {k: np.ascontiguousarray(v, dtype=np.float32) for k, v in m.items()}


def run_launch(build_fn, ins, outs, core_maps):
    nc = bass.Bass("TRN2", target_bir_lowering=False)
    R = Reg(nc, ins=ins, outs=outs, per_layer=True)
    P = Prog(nc)
    Ck = Ctx(P)
    K = build_consts(P, Ck)
    build_fn(P, R, K)
    P.finish()
    in_maps = [{k: m[k] for k in ins} for m in core_maps]
    res = run_bass_kernel_spmd(nc, in_maps, core_ids=list(range(len(core_maps))))
    return res.results


W_A = ["w_inx", "g_mix", "gmlp_ln_g", "gmlp_ln_b", "gmlp_wsT", "gmlp_b_s"]
W_B = ["ssm_lambda_re", "ssm_lambda_im", "ssm_log_dt", "ssm_b_re", "ssm_b_im", "ssm_c_re", "ssm_c_im"]
W_B2 = ["ssm_d", "ssm_w_glu", "ssm_b_glu"]


def kernel_unfused(inputs):
    import ml_dtypes
    bf = ml_dtypes.bfloat16
    pq, pk = _qsw_perm()
    st = []
    for (b, s) in CORES:
        m = prep_core(inputs, {}, b, s)
        st.append({"x": m["x_in"], "rowcol": m["rowcol"], "cvec": m["cvec"]})
    for l in range(DEPTH):
        w_in = np.asarray(inputs["w_in"])[l:l + 1]
        q = w_in[:, :, 0:1024]
        k = w_in[:, :, 1024:1152]
        w_inx = np.ascontiguousarray(np.concatenate([q, q[:, :, pq], k, k[:, :, pk], w_in[:, :, 1152:]], axis=2), dtype=np.float32)
        lw = [prep_core_layer(inputs, l, s) for s in range(2)]
        maps = []
        for ci, (b, s) in enumerate(CORES):
            m = dict(lw[s])
            m["w_inx"] = w_inx
            m.update({"x_l": st[ci]["x"], "rowcol": st[ci]["rowcol"], "cvec": st[ci]["cvec"]})
            maps.append(m)
        ins1 = ["cvec", "w_ada", "b_ada", "rowcol", "x_l"] + W_A + W_B
        outs1 = ["mod", "qT", "kT", "v", "uT", "oT_g", "ssm_send", "hinit0"]

        def b1(P, R, K, l=l):
            phase_mod(P, R, [l])
            phase_rope(P, R)
            phase_A(P, R, K, l, "x_l")
            phase_B(P, R, K, l, "a")
        r1 = run_launch(b1, ins1, outs1, maps)
        for ci in range(8):
            pr = ci ^ 1
            m = maps[ci]
            for kk in ("mod", "qT", "uT", "oT_g", "hinit0"):
                m[kk] = r1[ci][kk]
            kT = np.array(r1[ci]["kT"])
            kT[:, NTOK:] = np.asarray(r1[pr]["kT"])[:, NTOK - 128:NTOK]
            vv = np.array(r1[ci]["v"])
            vv[NTOK:] = np.asarray(r1[pr]["v"])[NTOK - 128:NTOK]
            m["kT"], m["v"] = kT, vv
            m["ssm_recv"] = np.asarray(r1[pr]["ssm_send"])
        ins2 = ["mod", "x_l", "qT", "kT", "v", "uT", "oT_g", "hinit0", "ssm_recv", "attn_sink", "w_out"] + W_B + W_B2
        outs2 = ["x_mid"]

        def b2(P, R, K, l=l):
            phase_B(P, R, K, l, "b")
            phase_C(P, R, K, l)
            phase_D(P, R, K, l, "x_l", "x_mid")
        r2 = run_launch(b2, ins2, outs2, maps)
        for ci in range(8):
            pr = ci ^ 1
            maps[ci]["x_mid"] = r2[ci]["x_mid"]
            maps[ci]["xb_recv"] = np.ascontiguousarray(np.asarray(r2[pr]["x_mid"])[NTOK - 1:NTOK])
        ins3 = ["mod", "x_mid", "xb_recv", "g_ffn", "ffn_w_up", "ffn_w_down", "ffn_conv_w", "ffn_conv_b"]
        last = l == DEPTH - 1
        outs3 = ["y_out"] if last else ["x_next"]
        if last:
            ins3 = ins3 + ["g_final"]
            for m in maps:
                m["g_final"] = np.ascontiguousarray(inputs["g_final"], dtype=np.float32)

        def b3(P, R, K, l=l, last=last):
            phase_E(P, R, K, l, "x_mid", "x_next")
            if last:
                phase_F(P, R, K, "x_next")
        r3 = run_launch(b3, ins3, outs3, maps)
        if not last:
            for ci in range(8):
                st[ci]["x"] = r3[ci]["x_next"]
    out = np.zeros((4, 4096, D), np.float32)
    for ci, (b, s) in enumerate(CORES):
        y = np.asarray(r3[ci]["y_out"])
        if s == 0:
            out[b, :NLAT] = y
        else:
            out[b, NLAT:] = y[::-1]
    return out


PAIRS = [[0, 1], [2, 3], [4, 5], [6, 7]]


def load_sel(P, C, R):
    psel = R.get("psel", [2])
    SEL = C.sb([128, 2], F32, "SEL")
    t = Tok()
    P.dma("sp", SEL[:], psel.partition_broadcast(128), w=[t])
    return SEL, t


def phase_X1(P, R, K):
    kT = R.get("kT", [128, NTOK + 128], BF16)
    vv = R.get("v", [NTOK + 128, 128], BF16)
    send = R.get("ssm_send", [64, 2, 32])
    recv = R.get("ssm_recv", [64, 2, 32])
    s1 = R.get("send1", [128, 320])
    ag1 = R.get("ag1", [256, 320])
    C = Ctx(P)
    SEL, t_sel = load_sel(P, C, R)
    kb = C.sb([128, 256], BF16, "x1kb")
    PK = C.sb([128, 320], F32, "x1pk")
    t_kb, t_pk = Tok(), Tok()
    P.dma("sp", kb[:, 0:128], kT[:, NTOK - 128:NTOK], w=[t_kb])
    P.dma("sp", kb[:, 128:256], vv[NTOK - 128:NTOK, :], w=[t_kb])
    P.op("pool", lambda g: g.memset(PK[:, 256:320], 0.0), w=[t_pk])
    P.op("dve", lambda v: v.tensor_copy(out=PK[:, 0:256], in_=kb[:]), r=[t_kb], w=[t_pk])
    P.dma("sp", PK[0:64, 256:320], send.rearrange("p r g -> p (r g)"), r=[t_pk], w=[t_pk])
    t_s1, t_ag = Tok(), Tok()
    P.dma("sp", s1, PK[:], r=[t_pk], w=[t_s1])
    P.collective(s1, ag1, PAIRS, r=[t_s1], w=[t_ag])
    G = C.sb([128, 2, 320], F32, "x1g")
    t_G = Tok()
    P.dma("sp", G[:], ag1.rearrange("(r p) c -> p r c", p=128), r=[t_ag], w=[t_G])
    R1 = C.sb([128, 320], F32, "x1r")
    kb2 = C.sb([128, 256], BF16, "x1kb2")
    t_R1 = Tok()
    P.op("dve", lambda v: v.tensor_scalar(out=R1[:], in0=G[:, 0, :], scalar1=SEL[:, 0:1], scalar2=None, op0=ALU.mult), r=[t_G, t_sel], w=[t_R1])
    P.op("dve", lambda v: v.scalar_tensor_tensor(out=R1[:], in0=G[:, 1, :], scalar=SEL[:, 1:2], in1=R1[:], op0=ALU.mult, op1=ALU.add), r=[t_G, t_sel, t_R1], w=[t_R1])
    P.op("dve", lambda v: v.tensor_copy(out=kb2[:], in_=R1[:, 0:256]), r=[t_R1], w=[t_R1])
    P.dma("sp", kT[:, NTOK:NTOK + 128], kb2[:, 0:128], r=[t_R1], w=[R.tk("kT", "halo")])
    P.dma("sp", vv[NTOK:NTOK + 128, :], kb2[:, 128:256], r=[t_R1], w=[R.tk("v", "halo")])
    P.dma("sp", recv.rearrange("p r g -> p (r g)"), R1[0:64, 256:320], r=[t_R1], w=[R.tk("ssm_recv", 0)])
    C.close()


def phase_X2(P, R, K, xname):
    x = R.get(xname, [NTOK, D])
    xb = R.get("xb_recv", [1, D])
    s2 = R.get("send2", [1, D])
    ag2 = R.get("ag2", [2, D])
    C = Ctx(P)
    SEL, t_sel = load_sel(P, C, R)
    row = C.sb([1, D], F32, "x2row")
    t_row, t_s2, t_ag = Tok(), Tok(), Tok()
    P.dma("sp", row[:], x[NTOK - 1:NTOK, :], w=[t_row])
    P.dma("sp", s2, row[:], r=[t_row], w=[t_s2])
    P.collective(s2, ag2, PAIRS, r=[t_s2], w=[t_ag])
    G = C.sb([1, 2, D], F32, "x2g")
    t_G = Tok()
    P.dma("sp", G[:], ag2.rearrange("(o r) c -> o r c", o=1), r=[t_ag], w=[t_G])
    P.op("dve", lambda v: v.tensor_scalar(out=row[:], in0=G[:, 0, :], scalar1=SEL[0:1, 0:1], scalar2=None, op0=ALU.mult), r=[t_G, t_sel, t_row], w=[t_row])
    P.op("dve", lambda v: v.scalar_tensor_tensor(out=row[:], in0=G[:, 1, :], scalar=SEL[0:1, 1:2], in1=row[:], op0=ALU.mult, op1=ALU.add), r=[t_G, t_sel, t_row], w=[t_row])
    P.dma("sp", xb, row[:], r=[t_row], w=[R.tk("xb_recv")])
    C.close()


FUSED_INS = ["x_in", "rowcol", "cvec", "psel", "w_ada", "b_ada", "g_mix", "g_ffn", "w_inx", "w_out", "attn_sink",
             "ssm_lambda_re", "ssm_lambda_im", "ssm_log_dt", "ssm_b_re", "ssm_b_im", "ssm_c_re", "ssm_c_im", "ssm_d", "ssm_w_glu", "ssm_b_glu",
             "gmlp_ln_g", "gmlp_ln_b", "gmlp_wsT", "gmlp_b_s", "ffn_w_up", "ffn_conv_w", "ffn_conv_b", "ffn_w_down", "g_final"]


def build_fused(nlayers=DEPTH):
    nc = bass.Bass("TRN2", target_bir_lowering=False)
    R = Reg(nc, ins=FUSED_INS, outs=["y_out"], per_layer=False)
    P = Prog(nc)
    Ck = Ctx(P)
    K = build_consts(P, Ck)
    phase_mod(P, R, [0])
    phase_rope(P, R)
    xcur = "x_in"
    for l in range(nlayers):
        xnext = "x_res%d" % (l % 2)
        phase_A(P, R, K, l, xcur)
        Cq = Ctx(P)
        LLq = Cq.sb([64, 2, 64, NCH], F32, "LLq")
        S = ssm_prep(P, R, K, Cq, l)
        S.LL, S.t_Ld = LLq, [Tok(), Tok()]
        phase_B(P, R, K, l, "a", S=S)
        phase_X1(P, R, K)
        phase_B(P, R, K, l, "b", S=S)
        Cq.close()
        phase_C(P, R, K, l)
        if l + 1 < nlayers:
            Cm = Ctx(P)
            side = phase_mod_gen(P, R, [l + 1], Cm)
            next(side)
            phase_D(P, R, K, l, xcur, "x_mid", side=side)
            Cm.close()
        else:
            phase_D(P, R, K, l, xcur, "x_mid")
        phase_X2(P, R, K, "x_mid")
        phase_E(P, R, K, l, "x_mid", xnext)
        xcur = xnext
    phase_F(P, R, K, xcur)
    P.finish()
    return nc, P


def fused_inputs(inputs):
    pq, pk = _qsw_perm()
    w_in = np.asarray(inputs["w_in"])
    q = w_in[:, :, 0:1024]
    k = w_in[:, :, 1024:1152]
    w_inx = np.ascontiguousarray(np.concatenate([q, q[:, :, pq], k, k[:, :, pk], w_in[:, :, 1152:]], axis=2), dtype=np.float32)
    f32 = lambda a: np.ascontiguousarray(a, dtype=np.float32)
    shared = {"w_inx": w_inx}
    for k_ in ("w_ada", "b_ada", "g_mix", "g_ffn", "w_out", "attn_sink", "ssm_d", "ssm_w_glu", "ssm_b_glu", "gmlp_ln_g", "gmlp_ln_b",
               "ffn_w_up", "ffn_conv_b", "ffn_w_down", "g_final"):
        shared[k_] = f32(inputs[k_])
    half = []
    for s in range(2):
        m = {}
        cw = np.asarray(inputs["ffn_conv_w"])
        m["ffn_conv_w"] = f32(cw[:, ::-1] if s == 1 else cw)
        for k_ in ("ssm_lambda_re", "ssm_lambda_im", "ssm_log_dt", "ssm_b_re", "ssm_b_im", "ssm_c_re", "ssm_c_im"):
            a = np.asarray(inputs[k_])
            m[k_] = f32(a[:, ::-1] if s == 1 else a)
        ws = np.asarray(inputs["gmlp_w_s"])
        bs = np.asarray(inputs["gmlp_b_s"])
        if s == 1:
            ws = ws[:, :, ::-1, ::-1]
            bs = bs[:, :, ::-1]
        m["gmlp_wsT"] = f32(ws.transpose(0, 1, 3, 2))
        m["gmlp_b_s"] = f32(bs)
        m["psel"] = np.array([0.0, 1.0] if s == 0 else [1.0, 0.0], np.float32)
        half.append(m)
    maps = []
    for (b, s) in CORES:
        m = dict(shared)
        m.update(half[s])
        pc = prep_core(inputs, {}, b, s)
        m["x_in"], m["rowcol"], m["cvec"] = pc["x_in"], pc["rowcol"], pc["cvec"]
        maps.append({k_: m[k_] for k_ in FUSED_INS})
    return maps


def kernel(**inputs):
    nc, P = build_fused()
    maps = fused_inputs(inputs)
    res = run_bass_kernel_spmd(nc, maps, core_ids=list(range(8)))
    out = np.zeros((4, 4096, D), np.float32)
    for ci, (b, s) in enumerate(CORES):
        y = np.asarray(res.results[ci]["y_out"])
        if s == 0:
            out[b, :NLAT] = y
        else:
            out[b, NLAT:] = y[::-1]
    return out
```

```python
import numpy as np
from contextlib import ExitStack
import concourse.bass as bass
import concourse.mybir as mybir
from concourse.bass_utils import run_bass_kernel_spmd

F32 = mybir.dt.float32
BF16 = mybir.dt.bfloat16
I32 = mybir.dt.int32
AF = mybir.ActivationFunctionType
ALU = mybir.AluOpType
AX = mybir.AxisListType

D = 2048
NLAT = 2048
NCTX = 256
NTOK = NLAT + NCTX
NT = NTOK // 128
DEPTH = 4
DFF = 5632
NFF = DFF // 128
EPS = 1e-6
TWO_PI = 6.283185307179586
PI = 3.141592653589793
BLOCKS = [(0, 256)] + [(256 + 512 * i, 512) for i in range(4)]


class Tok:
    __slots__ = ("w", "r", "name")

    def __init__(self, name=""):
        self.w = None
        self.r = []
        self.name = name


class Eng:
    def __init__(self, name, h, selfsync):
        self.name = name
        self.h = h
        self.sem = None
        self.cnt = 0
        self.seen = {}
        self.selfsync = selfsync


class Prog:
    SEM_EPOCH = 12000
    NDMA = 12

    def __init__(self, nc):
        self.nc = nc
        self.stack = ExitStack()
        self.nsem = 0
        self.eng = {
            "pe": Eng("pe", nc.tensor, False),
            "act": Eng("act", nc.scalar, True),
            "dve": Eng("dve", nc.vector, True),
            "pool": Eng("pool", nc.gpsimd, True),
            "sp": Eng("sp", nc.sync, False),
        }
        for e in self.eng.values():
            self._new_sem(e)
        self.dq = {}
        for q in ("sp", "pool", "act"):
            sems = [self._sem("d%s%d" % (q, i)) for i in range(self.NDMA)]
            self.dq[q] = {"sems": sems, "cnt": [0] * self.NDMA, "i": 0}
        self.out_stamps = []

    def _sem(self, name):
        self.nsem += 1
        return self.stack.enter_context(self.nc.semaphore("%s_%d" % (name, self.nsem)))

    def _new_sem(self, e):
        e.sem = self._sem("e" + e.name)
        e.cnt = 0

    def close(self):
        self.stack.close()

    def _wait(self, e, stamp):
        sem, val = stamp
        key = id(sem)
        if e.seen.get(key, 0) >= val:
            return
        e.h.wait_ge(sem, val)
        e.seen[key] = val

    def _deps(self, e, r, w):
        for t in r:
            if t.w is not None:
                if t.w[0] is e.sem and not e.selfsync:
                    continue
                self._wait(e, t.w)
        for t in w:
            if t.w is not None:
                if not (t.w[0] is e.sem and not e.selfsync):
                    self._wait(e, t.w)
            for st in t.r:
                if st[0] is e.sem:
                    continue
                self._wait(e, st)

    def _stamp(self, st, r, w):
        for t in r:
            t.r = [s for s in t.r if s[0] is not st[0]] + [st]
        for t in w:
            t.w = st
            t.r = []

    def op(self, eng, fn, r=(), w=()):
        e = self.eng[eng]
        self._deps(e, r, w)
        ins = fn(e.h)
        if e.cnt >= self.SEM_EPOCH:
            self._new_sem(e)
        e.cnt += 1
        ins.then_inc(e.sem, 1)
        self._stamp((e.sem, e.cnt), r, w)

    def mm(self, fns, r=(), w=()):
        e = self.eng["pe"]
        self._deps(e, r, w)
        ins = None
        for fn in fns:
            ins = fn(e.h)
        if e.cnt >= self.SEM_EPOCH:
            self._new_sem(e)
        e.cnt += 1
        ins.then_inc(e.sem, 1)
        self._stamp((e.sem, e.cnt), r, w)

    def dma(self, q, out, in_, r=(), w=(), is_out=False, **kw):
        e = self.eng[q]
        dq = self.dq[q]
        slot = dq["i"] % self.NDMA
        dq["i"] += 1
        sem = dq["sems"][slot]
        prev = dq["cnt"][slot]
        if prev > 0:
            self._wait(e, (sem, prev))
        self._deps(e, r, w)
        ins = e.h.dma_start(out=out, in_=in_, **kw)
        ins.then_inc(sem, 16)
        dq["cnt"][slot] = prev + 16
        st = (sem, prev + 16)
        self._stamp(st, r, w)
        if is_out:
            self.out_stamps.append(st)

    def collective(self, ins_ap, outs_ap, groups, r=(), w=()):
        e = self.eng["pool"]
        if not hasattr(self, "cc_sem"):
            self.cc_sem = self._sem("cc")
            self.cc_cnt = 0
        self._deps(e, r, w)
        ins = self.nc.gpsimd.collective_compute("AllGather", ALU.bypass, replica_groups=groups, ins=[ins_ap], outs=[outs_ap])
        self.cc_cnt += 1
        ins.then_inc(self.cc_sem, 1)
        self._stamp((self.cc_sem, self.cc_cnt), r, w)

    def barrier(self):
        for e in self.eng.values():
            for o in self.eng.values():
                if o is e or o.cnt == 0:
                    continue
                self._wait(e, (o.sem, o.cnt))
            for q, dq in self.dq.items():
                for s_, c in zip(dq["sems"], dq["cnt"]):
                    if c > 0:
                        self._wait(e, (s_, c))
            if getattr(self, "cc_cnt", 0) > 0:
                self._wait(e, (self.cc_sem, self.cc_cnt))

    def finish(self):
        self.barrier()
        e = self.eng["sp"]
        for st in self.out_stamps:
            self._wait(e, st)
        for q, dq in self.dq.items():
            for s, c in zip(dq["sems"], dq["cnt"]):
                if c > 0:
                    self._wait(e, (s, c))


class Ctx:
    def __init__(self, P):
        self.P = P
        self.nc = P.nc
        self.stack = ExitStack()
        self.n = 0

    UID = [0]

    def sb(self, shape, dt, name="t"):
        Ctx.UID[0] += 1
        return self.stack.enter_context(self.nc.sbuf_tensor("%s_%d" % (name, Ctx.UID[0]), list(shape), dt))

    def ps(self, shape, dt, name="p"):
        Ctx.UID[0] += 1
        return self.stack.enter_context(self.nc.psum_tensor("%s_%d" % (name, Ctx.UID[0]), list(shape), dt))

    def close(self):
        self.P.barrier()
        self.stack.close()


def bcast_rows(ap1d, nrows):
    return ap1d.rearrange("(o n) -> o n", o=1).broadcast(0, nrows)


class Reg:
    def __init__(self, nc, ins=(), outs=(), per_layer=False):
        self.nc = nc
        self.nl = 1 if per_layer else DEPTH
        self.per_layer = per_layer
        self.ins = set(ins)
        self.outs = set(outs)
        self.t = {}
        self.tok = {}

    def get(self, name, shape=None, dt=F32):
        if name not in self.t:
            kind = "ExternalInput" if name in self.ins else ("ExternalOutput" if name in self.outs else "Internal")
            self.t[name] = self.nc.dram_tensor(name, list(shape), dt, kind=kind).ap()
            self.tok[name] = Tok(name)
        return self.t[name]


def build_consts(P, C):
    nc = P.nc
    K = {}
    idf = C.sb([128, 128], F32, "idf")
    idb = C.sb([128, 128], BF16, "idb")
    io = C.sb([128, 128], I32, "io")
    t_id = Tok("ident")
    P.op("pool", lambda g: g.iota(io[:], [[1, 128]], base=0, channel_multiplier=-1), w=[t_id])
    P.op("dve", lambda v: v.tensor_copy(out=idf[:], in_=io[:]), r=[t_id], w=[t_id])
    P.op("dve", lambda v: v.tensor_scalar(out=idf[:], in0=idf[:], scalar1=0.0, scalar2=None, op0=ALU.is_equal), r=[t_id], w=[t_id])
    P.op("dve", lambda v: v.tensor_copy(out=idb[:], in_=idf[:]), r=[t_id], w=[t_id])
    K["idf"] = idf
    K["idb"] = idb
    K["tok"] = t_id
    return K


def reg_li(R, l):
    return 0 if R.per_layer else l


Reg.li = reg_li


def reg_tk(R, name, i=0):
    key = (name, i)
    if key not in R.tok:
        R.tok[key] = Tok("%s[%s]" % (name, i))
    return R.tok[key]


Reg.tk = reg_tk


def phase_mod(P, R, layers):
    C = Ctx(P)
    for _ in phase_mod_gen(P, R, layers, C):
        pass
    C.close()


def phase_mod_gen(P, R, layers, C):
    cvec = R.get("cvec", [2, D])
    w_ada = R.get("w_ada", [R.nl, D, 6 * D])
    b_ada = R.get("b_ada", [R.nl, 6 * D])
    mod = R.get("mod", [DEPTH, 2, 6 * D])
    cc = C.sb([128, 16, 2], F32, "cc")
    t_cc = Tok()
    for j in range(2):
        P.dma("sp", cc[:, :, j], cvec[j].rearrange("(k p) -> p k", p=128), w=[t_cc], allow_slow_non_contiguous=True)
    act = C.sb([128, 16, 2], F32, "act")
    P.op("act", lambda a: a.activation(out=act[:], in_=cc[:], func=AF.Silu), r=[t_cc], w=[t_cc])
    NB = 2
    wb = [C.sb([128, 16, 512], BF16, "wada") for _ in range(NB)]
    actb = C.sb([128, 16, 2], BF16, "actb")
    P.op("dve", lambda v: v.tensor_copy(out=actb[:], in_=act[:]), r=[t_cc], w=[t_cc])
    t_wb = [Tok() for _ in range(NB)]
    bb = [C.sb([2, 512], F32, "bada") for _ in range(NB)]
    ob = [C.sb([2, 512], F32, "oada") for _ in range(NB)]
    t_ob = [Tok() for _ in range(NB)]
    ps = [C.ps([2, 512], F32, "psada") for _ in range(NB)]
    t_ps = [Tok() for _ in range(NB)]
    it = 0
    for l in layers:
        wv = w_ada[R.li(l)].rearrange("(k p) c -> p k c", p=128)
        for cb in range(24):
            i = it % NB
            it += 1
            cs = slice(cb * 512, (cb + 1) * 512)
            P.dma("pool", wb[i][:], wv[:, :, cs], w=[t_wb[i]])
            P.dma("sp", bb[i][:], b_ada[R.li(l), cs].partition_broadcast(2), w=[t_wb[i]])
            P.mm([(lambda pe, k=k, i=i: pe.matmul(ps[i][:], actb[:, k, :], wb[i][:, k, :], start=(k == 0), stop=(k == 15)))
                  for k in range(16)], r=[t_wb[i], t_cc], w=[t_ps[i]])
            P.op("dve", lambda v, i=i: v.tensor_tensor(out=ob[i][:], in0=ps[i][:], in1=bb[i][:], op=ALU.add),
                 r=[t_ps[i], t_wb[i]], w=[t_ob[i]])
            P.dma("sp", mod[l, :, cs], ob[i][:], r=[t_ob[i]], w=[R.tk("mod", l)])
            yield


NQ = 1024
OFFX_Q, OFFX_QS, OFFX_K, OFFX_KS, OFFX_V, OFFX_U, OFFX_GU, OFFX_GV = 0, 1024, 2048, 2176, 2304, 2432, 2944, 3456
NINX = 3968


def range_reduce_sin(P, C, out, ang, shape, tk, extra=0.0):
    ti = C.sb(shape, I32, "rr_i")
    tf = C.sb(shape, F32, "rr_f")
    tr = C.sb(shape, F32, "rr_r")
    t = Tok()
    P.op("dve", lambda v: v.tensor_scalar(out=ti[:], in0=ang, scalar1=extra, scalar2=1.0 / TWO_PI, op0=ALU.add, op1=ALU.mult), r=[tk], w=[t])
    P.op("dve", lambda v: v.tensor_copy(out=tf[:], in_=ti[:]), r=[t], w=[t])
    P.op("dve", lambda v: v.scalar_tensor_tensor(out=tr[:], in0=tf[:], scalar=-TWO_PI, in1=ang, op0=ALU.mult, op1=ALU.add), r=[t, tk], w=[t])
    if extra != 0.0:
        P.op("dve", lambda v: v.tensor_scalar(out=tr[:], in0=tr[:], scalar1=extra, scalar2=None, op0=ALU.add), r=[t], w=[t])
    P.op("dve", lambda v: v.tensor_scalar(out=tf[:], in0=tr[:], scalar1=PI, scalar2=-TWO_PI, op0=ALU.is_gt, op1=ALU.mult), r=[t], w=[t])
    P.op("dve", lambda v: v.tensor_tensor(out=tr[:], in0=tr[:], in1=tf[:], op=ALU.add), r=[t], w=[t])
    P.op("dve", lambda v: v.tensor_scalar(out=tf[:], in0=tr[:], scalar1=-PI, scalar2=TWO_PI, op0=ALU.is_lt, op1=ALU.mult), r=[t], w=[t])
    P.op("dve", lambda v: v.tensor_tensor(out=tr[:], in0=tr[:], in1=tf[:], op=ALU.add), r=[t], w=[t])
    P.op("act", lambda a: a.activation(out=out, in_=tr[:], func=AF.Sin), r=[t], w=[tk])


def phase_rope(P, R):
    C = Ctx(P)
    rowcol = R.get("rowcol", [2, NLAT])
    ropeT = R.get("ropeT", [2, 128, NTOK])
    pos = C.sb([128, NLAT], F32, "pos")
    t = Tok()
    for base in (0, 64):
        P.dma("sp", pos[base:base + 32, :], rowcol[0].partition_broadcast(32), w=[t])
        P.dma("sp", pos[base + 32:base + 64, :], rowcol[1].partition_broadcast(32), w=[t])
    pi_ = C.sb([128, 1], I32, "pidx")
    pf = C.sb([128, 4], F32, "pf")
    tp = Tok()
    P.op("pool", lambda g: g.iota(pi_[:], [[0, 1]], base=0, channel_multiplier=1), w=[tp])
    P.op("dve", lambda v: v.tensor_copy(out=pf[:, 0:1], in_=pi_[:]), r=[tp], w=[tp])
    pj = C.sb([128, 1], I32, "pj")
    P.op("dve", lambda v: v.tensor_scalar(out=pj[:], in0=pi_[:], scalar1=15, scalar2=None, op0=ALU.bitwise_and), r=[tp], w=[tp])
    P.op("dve", lambda v: v.tensor_copy(out=pf[:, 1:2], in_=pj[:]), r=[tp], w=[tp])
    P.op("act", lambda a: a.activation(out=pf[:, 2:3], in_=pf[:, 1:2], func=AF.Exp, scale=-float(np.log(10000.0)) / 16.0), r=[tp], w=[tp])
    P.op("dve", lambda v: v.tensor_scalar(out=pj[:], in0=pi_[:], scalar1=16, scalar2=None, op0=ALU.bitwise_and), r=[tp], w=[tp])
    P.op("dve", lambda v: v.tensor_copy(out=pf[:, 3:4], in_=pj[:]), r=[tp], w=[tp])
    P.op("dve", lambda v: v.tensor_scalar(out=pf[:, 3:4], in0=pf[:, 3:4], scalar1=1.0 / 8.0, scalar2=-1.0, op0=ALU.mult, op1=ALU.add), r=[tp], w=[tp])
    ang = C.sb([128, NLAT], F32, "ang")
    P.op("dve", lambda v: v.tensor_scalar(out=ang[:], in0=pos[:], scalar1=pf[:, 2:3], scalar2=None, op0=ALU.mult), r=[t, tp], w=[t])
    cs = C.sb([128, NTOK], F32, "cs")
    sn = C.sb([128, NTOK], F32, "sn")
    tc_, ts_ = Tok(), Tok()
    P.op("pool", lambda g: g.memset(cs[:, 0:NCTX], 1.0), w=[tc_])
    P.op("pool", lambda g: g.memset(sn[:, 0:NCTX], 0.0), w=[ts_])
    C2 = Ctx(P)
    range_reduce_sin(P, C2, sn[:, NCTX:], ang[:], [128, NLAT], t)
    P.op("dve", lambda v: v.tensor_scalar(out=sn[:, NCTX:], in0=sn[:, NCTX:], scalar1=pf[:, 3:4], scalar2=None, op0=ALU.mult), r=[t, tp], w=[t])
    P.dma("sp", ropeT[1], sn[:], r=[t, ts_], w=[R.tk("ropeT", 1)])
    t2 = Tok()
    t2.w = t.w
    range_reduce_sin(P, C2, cs[:, NCTX:], ang[:], [128, NLAT], t, extra=PI / 2)
    P.dma("sp", ropeT[0], cs[:], r=[t, tc_], w=[R.tk("ropeT", 0)])
    C2.close()
    C.close()


def load_bcast_mod(P, C, R, l, r, sec, name):
    mod = R.get("mod", [DEPTH, 2, 6 * D])
    t = C.sb([128, D], F32, name)
    tk = Tok(name)
    P.dma("sp", t[:], mod[l, r, sec * D:(sec + 1) * D].partition_broadcast(128), r=[R.tk("mod", l)], w=[tk])
    return t, tk


def norm_mod_transpose(P, R, K, C, l, xname, gname, sec_shift, sec_scale, hT_all, dst_fn=None, tok_fn=None, after=None, extra_row=None):
    x = R.get(xname, [NTOK, D])
    gvec = R.get(gname, [R.nl, D])
    C1 = Ctx(P)
    gb = C1.sb([128, D], F32, "gb")
    t_gb = Tok()
    P.dma("sp", gb[:], gvec[R.li(l)].partition_broadcast(128), w=[t_gb])
    GS, SH, tGS, tSH = [], [], [], []
    for r in range(2):
        sc, tsc = load_bcast_mod(P, C1, R, l, r, sec_scale, "gs%d" % r)
        sh, tsh = load_bcast_mod(P, C1, R, l, r, sec_shift, "sh%d" % r)
        P.op("dve", lambda v, sc=sc: v.scalar_tensor_tensor(out=sc[:], in0=sc[:], scalar=1.0, in1=gb[:], op0=ALU.add, op1=ALU.mult),
             r=[tsc, t_gb], w=[tsc])
        GS.append(sc); SH.append(sh); tGS.append(tsc); tSH.append(tsh)
    NB = 3
    xt = [C1.sb([128, D], F32, "xt") for _ in range(NB)]
    t_xt = [Tok() for _ in range(NB)]
    junk = C1.sb([128, D], BF16, "junk")
    t_junk = Tok()
    st = [C1.sb([128, 4], F32, "st") for _ in range(NB)]
    t_st = [Tok() for _ in range(NB)]
    t1 = [C1.sb([128, D], F32, "t1") for _ in range(NB)]
    t_t1 = [Tok() for _ in range(NB)]
    hb = [C1.sb([128, D], BF16, "hb") for _ in range(NB)]
    t_hb = [Tok() for _ in range(NB)]
    pT = [C1.ps([128, 1024], BF16, "pT") for _ in range(4)]
    t_pT = [Tok() for _ in range(4)]
    t_h = [(Tok("hTa%d" % i), Tok("hTb%d" % i)) for i in range(NT + 1)]
    if tok_fn is not None:
        t_h = [tok_fn(i) for i in range(NT + 1)]
    ntile = NT + (1 if extra_row is not None else 0)

    def stage_a(ti):
        i = ti % NB
        r = 1 if ti < NCTX // 128 else 0
        if ti < NT:
            P.dma("sp", xt[i][:], x[ti * 128:(ti + 1) * 128, :], r=[R.tk(xname, ti)], w=[t_xt[i]])
        else:
            P.op("pool", lambda g, i=i: g.memset(xt[i][:], 1.0), w=[t_xt[i]])
            P.dma("sp", xt[i][0:1, :], extra_row[0], r=[extra_row[1]], w=[t_xt[i]])
        P.op("act", lambda a, i=i: a.activation(out=junk[:], in_=xt[i][:], func=AF.Square, accum_out=st[i][:, 0:1]),
             r=[t_xt[i]], w=[t_junk, t_st[i]])
        P.op("act", lambda a, i=i: a.activation(out=st[i][:, 1:2], in_=st[i][:, 0:1], func=AF.Sqrt, scale=1.0 / D, bias=EPS),
             r=[t_st[i]], w=[t_st[i]])
        P.op("dve", lambda v, i=i: v.reciprocal(out=st[i][:, 2:3], in_=st[i][:, 1:2]), r=[t_st[i]], w=[t_st[i]])
        P.op("pool", lambda g, i=i, r=r: g.tensor_tensor(out=t1[i][:], in0=xt[i][:], in1=GS[r][:], op=ALU.mult),
             r=[t_xt[i], tGS[r]], w=[t_t1[i]])
        P.op("dve", lambda v, i=i, r=r: v.scalar_tensor_tensor(out=hb[i][:], in0=t1[i][:], scalar=st[i][:, 2:3], in1=SH[r][:],
                                                              op0=ALU.mult, op1=ALU.add),
             r=[t_t1[i], t_st[i], tSH[r]], w=[t_hb[i]])
        for h in range(2):
            pb_ = (ti % 2) * 2 + h
            P.mm([(lambda pe, k=k, pb_=pb_, i=i: pe.transpose(pT[pb_][:, (k % 8) * 128:(k % 8 + 1) * 128], hb[i][:, k * 128:(k + 1) * 128], K["idb"][:]))
                  for k in range(8 * h, 8 * h + 8)], r=[t_hb[i], K["tok"]], w=[t_pT[pb_]])

    def stage_b(ti):
        for h in range(2):
            pb_ = (ti % 2) * 2 + h
            dst = hT_all[:, 8 * h:8 * h + 8, ti * 128:(ti + 1) * 128] if dst_fn is None else dst_fn(ti, h)
            src = pT[pb_][:].rearrange("p (k t) -> p k t", k=8)
            if h == 0:
                P.op("act", lambda a, dst=dst, src=src: a.copy(out=dst, in_=src), r=[t_pT[pb_]], w=[t_h[ti][h]])
            else:
                P.op("dve", lambda v, dst=dst, src=src: v.tensor_copy(out=dst, in_=src), r=[t_pT[pb_]], w=[t_h[ti][h]])
        if after is not None:
            after(ti, t_h[ti])

    for ti in range(ntile + 1):
        if ti < ntile:
            stage_a(ti)
        if ti >= 1:
            stage_b(ti - 1)
    C1.close()
    return t_h


def toks_for(t_h, t0, n):
    out = []
    for ti in range(t0 // 128, (t0 + n + 127) // 128):
        out += list(t_h[ti])
    return out


def load_w_cols(P, C, wsrc, col_ranges, buf, tk, q="pool"):
    wv = wsrc.rearrange("(k p) c -> p k c", p=128)
    o = 0
    for (c0, n) in col_ranges:
        P.dma(q, buf[:, :, o:o + n], wv[:, :, c0:c0 + n], w=[tk])
        o += n


def phase_A(P, R, K, l, xname, ctx_full=True, parts="abc"):
    w_in = R.get("w_inx", [R.nl, D, NINX])[R.li(l):R.li(l) + 1]
    qT = R.get("qT", [NQ, NTOK], BF16)
    kT = R.get("kT", [128, NTOK + 128], BF16)
    vv = R.get("v", [NTOK + 128, 128], BF16)
    uT = R.get("uT", [512, NTOK])
    oT = R.get("oT_g", [512, NTOK], BF16)
    ropeT = R.get("ropeT", [2, 128, NTOK])
    C = Ctx(P)
    hT = C.sb([128, 16, NTOK], BF16, "hT_all")
    t_h = norm_mod_transpose(P, R, K, C, l, xname, "g_mix", 0, 1, hT)

    if "a" not in parts:
        C.close(); return
    Ca = Ctx(P)
    cs = Ca.sb([128, NTOK], F32, "cs")
    sn = Ca.sb([128, NTOK], F32, "sn")
    t_cs = Tok()
    P.dma("sp", cs[:], ropeT[0], r=[R.tk("ropeT", 0)], w=[t_cs])
    P.dma("sp", sn[:], ropeT[1], r=[R.tk("ropeT", 1)], w=[t_cs])
    NB = 2
    wq = [Ca.sb([128, 16, 1024], BF16, "wq") for _ in range(NB)]
    t_wq = [Tok() for _ in range(NB)]
    psA = [Ca.ps([128, 512], F32, "psA") for _ in range(4)]
    t_psA = [Tok() for _ in range(4)]
    r1 = [Ca.sb([128, 512], F32, "r1") for _ in range(2)]
    r2 = [Ca.sb([128, 512], F32, "r2") for _ in range(2)]
    ro = [Ca.sb([128, 512], BF16, "ro") for _ in range(2)]
    t_r1 = [Tok() for _ in range(2)]
    t_r2 = [Tok() for _ in range(2)]
    t_ro = [Tok() for _ in range(2)]
    it = 0
    wvA = w_in[0].rearrange("(k p) c -> p k c", p=128)
    for j in range(9):
        wi = (j // 4) % NB
        fo = (j % 4) * 128
        if j % 4 == 0 and j < 8:
            P.dma("pool", wq[wi][:, :, 0:512], wvA[:, :, OFFX_Q + j * 128:OFFX_Q + j * 128 + 512], w=[t_wq[wi]])
            P.dma("pool", wq[wi][:, :, 512:1024], wvA[:, :, OFFX_QS + j * 128:OFFX_QS + j * 128 + 512], w=[t_wq[wi]])
        elif j == 8:
            P.dma("pool", wq[wi][:, :, 0:128], wvA[:, :, OFFX_K:OFFX_K + 128], w=[t_wq[wi]])
            P.dma("pool", wq[wi][:, :, 512:640], wvA[:, :, OFFX_KS:OFFX_KS + 128], w=[t_wq[wi]])
        for (t0, n) in BLOCKS:
            i = it % 2
            it += 1
            pa, pb = psA[2 * i], psA[2 * i + 1]
            hs = toks_for(t_h, t0, n)
            P.mm([(lambda pe, k=k, pa=pa, wi=wi, t0=t0, n=n, fo=fo: pe.matmul(pa[:, :n], wq[wi][:, k, fo:fo + 128], hT[:, k, t0:t0 + n], start=(k == 0), stop=(k == 15)))
                  for k in range(16)], r=[t_wq[wi]] + hs, w=[t_psA[2 * i]])
            P.mm([(lambda pe, k=k, pb=pb, wi=wi, t0=t0, n=n, fo=fo: pe.matmul(pb[:, :n], wq[wi][:, k, 512 + fo:512 + fo + 128], hT[:, k, t0:t0 + n], start=(k == 0), stop=(k == 15)))
                  for k in range(16)], r=[t_wq[wi]] + hs, w=[t_psA[2 * i + 1]])
            P.op("dve", lambda v, i=i, pa=pa, t0=t0, n=n: v.tensor_tensor(out=r1[i][:, :n], in0=pa[:, :n], in1=cs[:, t0:t0 + n], op=ALU.mult),
                 r=[t_psA[2 * i], t_cs], w=[t_r1[i]])
            P.op("dve", lambda v, i=i, pb=pb, t0=t0, n=n: v.tensor_tensor(out=r2[i][:, :n], in0=pb[:, :n], in1=sn[:, t0:t0 + n], op=ALU.mult),
                 r=[t_psA[2 * i + 1], t_cs], w=[t_r2[i]])
            P.op("pool", lambda g, i=i, n=n: g.tensor_tensor(out=ro[i][:, :n], in0=r1[i][:, :n], in1=r2[i][:, :n], op=ALU.add),
                 r=[t_r1[i], t_r2[i]], w=[t_ro[i]])
            if j < 8:
                P.dma("sp", qT[j * 128:(j + 1) * 128, t0:t0 + n], ro[i][:, :n], r=[t_ro[i]], w=[R.tk("qT", (j, t0))])
            else:
                P.dma("sp", kT[:, t0:t0 + n], ro[i][:, :n], r=[t_ro[i]], w=[R.tk("kT", t0)])
    Ca.close()

    if "b" not in parts:
        C.close(); return
    Cb = Ctx(P)
    wuA = Cb.sb([128, 16, 512], BF16, "wu")
    t_wuA = Tok()
    load_w_cols(P, Cb, w_in[0], [(OFFX_U, 512)], wuA, t_wuA)
    psB = [Cb.ps([128, 512], F32, "psB") for _ in range(2)]
    t_psB = [Tok() for _ in range(2)]
    ub = [Cb.sb([128, 512], F32, "ub") for _ in range(2)]
    t_ub = [Tok() for _ in range(2)]
    it = 0
    for j in range(4):
        for (t0, n) in BLOCKS:
            i = it % 2
            it += 1
            hs = toks_for(t_h, t0, n)
            P.mm([(lambda pe, k=k, i=i, j=j, t0=t0, n=n: pe.matmul(psB[i][:, :n], wuA[:, k, j * 128:(j + 1) * 128], hT[:, k, t0:t0 + n], start=(k == 0), stop=(k == 15)))
                  for k in range(16)], r=[t_wuA] + hs, w=[t_psB[i]])
            P.op("act", lambda a, i=i, n=n: a.copy(out=ub[i][:, :n], in_=psB[i][:, :n]), r=[t_psB[i]], w=[t_ub[i]])
            P.dma("sp", uT[j * 128:(j + 1) * 128, t0:t0 + n], ub[i][:, :n], r=[t_ub[i]], w=[R.tk("uT", (j, t0))])
    wv_ = Cb.sb([128, 16, 128], BF16, "wv")
    t_wv = Tok()
    load_w_cols(P, Cb, w_in[0], [(OFFX_V, 128)], wv_, t_wv)
    vb = [Cb.sb([128, 128], BF16, "vb") for _ in range(2)]
    t_vb = [Tok() for _ in range(2)]
    for ti in range(NT):
        i = ti % 2
        P.mm([(lambda pe, k=k, i=i, ti=ti: pe.matmul(psB[i][:, :128], hT[:, k, ti * 128:(ti + 1) * 128], wv_[:, k, :], start=(k == 0), stop=(k == 15)))
              for k in range(16)], r=[t_wv] + list(t_h[ti]), w=[t_psB[i]])
        P.op("act", lambda a, i=i: a.copy(out=vb[i][:], in_=psB[i][:, :128]), r=[t_psB[i]], w=[t_vb[i]])
        P.dma("sp", vv[ti * 128:(ti + 1) * 128, :], vb[i][:], r=[t_vb[i]], w=[R.tk("v", ti)])
    Cb.close()

    if "c" not in parts:
        C.close(); return
    Cc = Ctx(P)
    lng = R.get("gmlp_ln_g", [R.nl, 512])[R.li(l):R.li(l) + 1]
    lnb = R.get("gmlp_ln_b", [R.nl, 512])[R.li(l):R.li(l) + 1]
    wsT = R.get("gmlp_wsT", [R.nl, 4, 128, 128])[R.li(l):R.li(l) + 1]
    bs = R.get("gmlp_b_s", [R.nl, 4, 128])[R.li(l):R.li(l) + 1]
    tiles = list(range(NT)) if ctx_full else list(range(NCTX // 128, NT))
    blocks = BLOCKS if ctx_full else BLOCKS[1:]
    guT = Cc.sb([128, 4, NTOK], F32, "guT")
    t_gu = {}
    wgu = Cc.sb([128, 16, 512], BF16, "wgu")
    t_wgu = Tok()
    load_w_cols(P, Cc, w_in[0], [(OFFX_GU, 512)], wgu, t_wgu)
    psC = [Cc.ps([128, 512], F32, "psC") for _ in range(2)]
    t_psC = [Tok() for _ in range(2)]
    it = 0
    for j in range(4):
        for (t0, n) in blocks:
            i = it % 2
            it += 1
            hs = toks_for(t_h, t0, n)
            P.mm([(lambda pe, k=k, i=i, j=j, t0=t0, n=n: pe.matmul(psC[i][:, :n], wgu[:, k, j * 128:(j + 1) * 128], hT[:, k, t0:t0 + n], start=(k == 0), stop=(k == 15)))
                  for k in range(16)], r=[t_wgu] + hs, w=[t_psC[i]])
            t_gu[(j, t0)] = Tok()
            P.op("act", lambda a, i=i, j=j, t0=t0, n=n: a.activation(out=guT[:, j, t0:t0 + n], in_=psC[i][:, :n], func=AF.Gelu_apprx_tanh),
                 r=[t_psC[i]], w=[t_gu[(j, t0)]])
    wgv = Cc.sb([128, 16, 512], BF16, "wgv")
    t_wgv = Tok()
    load_w_cols(P, Cc, w_in[0], [(OFFX_GV, 512)], wgv, t_wgv)
    LNG = Cc.sb([128, 512], F32, "LNG")
    LNB = Cc.sb([128, 512], F32, "LNB")
    BS = Cc.sb([128, 4, 128], F32, "BS")
    WS = Cc.sb([128, 4, 128], BF16, "WS")
    t_par = Tok()
    P.dma("sp", LNG[:], lng[0].partition_broadcast(128), w=[t_par])
    P.dma("sp", LNB[:], lnb[0].partition_broadcast(128), w=[t_par])
    P.dma("sp", BS[:].rearrange("p g i -> p (g i)"), bs[0].rearrange("g i -> (g i)").partition_broadcast(128), w=[t_par])
    P.dma("pool", WS[:], wsT[0].rearrange("g j i -> j g i"), w=[t_par])
    gg = [Cc.sb([128, 512], F32, "gg") for _ in range(2)]
    t_gg = [Tok() for _ in range(2)]
    sq = Cc.sb([128, 512], BF16, "sq")
    t_sq = Tok()
    stt = [Cc.sb([128, 8], F32, "stt") for _ in range(2)]
    t_stt = [Tok() for _ in range(2)]
    vn = [Cc.sb([128, 512], BF16, "vn") for _ in range(2)]
    t_vn = [Tok() for _ in range(2)]
    pm = [Cc.ps([128, 512], F32, "pm") for _ in range(2)]
    t_pm = [Tok() for _ in range(2)]
    mm_ = [Cc.sb([128, 512], F32, "mm") for _ in range(2)]
    t_mm = [Tok() for _ in range(2)]
    og = [Cc.sb([128, 4, 128], BF16, "og") for _ in range(2)]
    t_og = [Tok() for _ in range(2)]
    def g_stage1(n_, ti):
        i = n_ % 2
        s = stt[i]
        P.mm([(lambda pe, k=k, i=i, ti=ti: pe.matmul(psC[i][:], hT[:, k, ti * 128:(ti + 1) * 128], wgv[:, k, :], start=(k == 0), stop=(k == 15)))
              for k in range(16)], r=[t_wgv] + list(t_h[ti]), w=[t_psC[i]])
        P.op("act", lambda a, i=i, s=s: a.activation(out=gg[i][:], in_=psC[i][:], func=AF.Gelu_apprx_tanh, accum_out=s[:, 0:1]),
             r=[t_psC[i]], w=[t_gg[i], t_stt[i]])
        P.op("act", lambda a, i=i, s=s: a.activation(out=sq[:], in_=gg[i][:], func=AF.Square, accum_out=s[:, 1:2]),
             r=[t_gg[i]], w=[t_sq, t_stt[i]])
        P.op("dve", lambda v, s=s: v.tensor_scalar(out=s[:, 2:3], in0=s[:, 0:1], scalar1=1.0 / 512, scalar2=None, op0=ALU.mult), r=[t_stt[i]], w=[t_stt[i]])
        P.op("dve", lambda v, s=s: v.tensor_tensor(out=s[:, 3:4], in0=s[:, 2:3], in1=s[:, 2:3], op=ALU.mult), r=[t_stt[i]], w=[t_stt[i]])
        P.op("dve", lambda v, s=s: v.scalar_tensor_tensor(out=s[:, 4:5], in0=s[:, 1:2], scalar=1.0 / 512, in1=s[:, 3:4], op0=ALU.mult, op1=ALU.subtract),
             r=[t_stt[i]], w=[t_stt[i]])
        P.op("act", lambda a, s=s: a.activation(out=s[:, 5:6], in_=s[:, 4:5], func=AF.Sqrt, bias=EPS), r=[t_stt[i]], w=[t_stt[i]])
        P.op("dve", lambda v, s=s: v.reciprocal(out=s[:, 6:7], in_=s[:, 5:6]), r=[t_stt[i]], w=[t_stt[i]])
        P.op("dve", lambda v, i=i, s=s: v.tensor_scalar(out=gg[i][:], in0=gg[i][:], scalar1=s[:, 2:3], scalar2=s[:, 6:7], op0=ALU.subtract, op1=ALU.mult),
             r=[t_stt[i], t_gg[i]], w=[t_gg[i]])
        P.op("pool", lambda g, i=i: g.tensor_tensor(out=gg[i][:], in0=gg[i][:], in1=LNG[:], op=ALU.mult), r=[t_gg[i], t_par], w=[t_gg[i]])
        P.op("pool", lambda g, i=i: g.tensor_tensor(out=vn[i][:], in0=gg[i][:], in1=LNB[:], op=ALU.add), r=[t_gg[i], t_par], w=[t_vn[i]])
        P.mm([(lambda pe, g=g, i=i: pe.matmul(pm[i][:, g * 128:(g + 1) * 128], vn[i][:, g * 128:(g + 1) * 128], WS[:, g, :], start=True, stop=True))
              for g in range(4)], r=[t_vn[i], t_par], w=[t_pm[i]])

    def g_stage2(n_, ti):
        i = n_ % 2
        P.op("dve", lambda v, i=i: v.tensor_tensor(out=mm_[i][:], in0=pm[i][:], in1=BS[:].rearrange("p g i -> p (g i)"), op=ALU.add),
             r=[t_pm[i], t_par], w=[t_mm[i]])
        blk0 = [b for b in BLOCKS if b[0] <= ti * 128 < b[0] + b[1]][0][0]
        P.op("pool", lambda g, i=i, ti=ti: g.tensor_tensor(out=og[i][:], in0=mm_[i][:].rearrange("p (g i) -> p g i", g=4),
                                                           in1=guT[:, :, ti * 128:(ti + 1) * 128], op=ALU.mult),
             r=[t_mm[i]] + [t_gu[(j, blk0)] for j in range(4)], w=[t_og[i]])
        P.dma("sp", oT[:, ti * 128:(ti + 1) * 128].rearrange("(g p) t -> p g t", p=128), og[i][:], r=[t_og[i]], w=[R.tk("oT_g", ti)])

    for n_ in range(len(tiles) + 1):
        if n_ < len(tiles):
            g_stage1(n_, tiles[n_])
        if n_ >= 1:
            g_stage2(n_ - 1, tiles[n_ - 1])
    Cc.close()
    C.close()


def _qsw_perm():
    def perm(nheads):
        idx = np.arange(nheads * 64).reshape(nheads, 2, 2, 16)
        return idx[:, :, ::-1, :].reshape(-1)
    return perm(16), perm(2)


def prep_shared(inputs):
    w_in = np.asarray(inputs["w_in"])
    pq, pk = _qsw_perm()
    q = w_in[:, :, 0:1024]
    k = w_in[:, :, 1024:1152]
    w_inx = np.concatenate([q, q[:, :, pq], k, k[:, :, pk], w_in[:, :, 1152:]], axis=2)
    sh = {"w_inx": np.ascontiguousarray(w_inx, dtype=np.float32)}
    for name in ("w_ada", "b_ada", "g_mix", "g_ffn", "gmlp_ln_g", "gmlp_ln_b"):
        sh[name] = np.ascontiguousarray(inputs[name], dtype=np.float32)
    return sh


def prep_core(inputs, sh, b, s):
    x = np.asarray(inputs["x"])[b]
    ctx = np.asarray(inputs["ctx"])[b]
    pos = np.arange(4096)
    if s == 0:
        xl = x[:NLAT]
        pl = pos[:NLAT]
        cl = ctx
    else:
        xl = x[NLAT:][::-1]
        pl = pos[NLAT:][::-1]
        cl = ctx[::-1]
    m = dict(sh)
    m["x_in"] = np.ascontiguousarray(np.concatenate([cl, xl], axis=0), dtype=np.float32)
    m["rowcol"] = np.stack([pl // 64, pl % 64]).astype(np.float32)
    m["cvec"] = np.stack([np.asarray(inputs["c"])[b], np.asarray(inputs["c_ctx"])]).astype(np.float32)
    ws = np.asarray(inputs["gmlp_w_s"])
    bs = np.asarray(inputs["gmlp_b_s"])
    if s == 1:
        ws = ws[:, :, ::-1, ::-1]
        bs = bs[:, :, ::-1]
    m["gmlp_wsT"] = np.ascontiguousarray(ws.transpose(0, 1, 3, 2), dtype=np.float32)
    m["gmlp_b_s"] = np.ascontiguousarray(bs, dtype=np.float32)
    return m


def phase_D(P, R, K, l, xname, xout, ctx_full=True, side=None):
    w_out = R.get("w_out", [R.nl, D, D])[R.li(l):R.li(l) + 1]
    oTa = R.get("oT_a", [NQ, NTOK], BF16)
    oTs = R.get("oT_s", [512, NTOK], BF16)
    oTg = R.get("oT_g", [512, NTOK], BF16)
    x = R.get(xname, [NTOK, D])
    xo = R.get(xout, [NTOK, D])
    C = Ctx(P)
    W = C.sb([128, 16, D], BF16, "wout")
    t_W = Tok()
    wv = w_out[0].rearrange("(k p) c -> p k c", p=128)
    for k4 in range(4):
        P.dma("pool", W[:, 4 * k4:4 * k4 + 4, :], wv[:, 4 * k4:4 * k4 + 4, :], w=[t_W])
    GA, tGA = [], []
    for r in range(2):
        g, tg = load_bcast_mod(P, C, R, l, r, 2, "ga%d" % r)
        GA.append(g); tGA.append(tg)
    NB = 2
    ot = [C.sb([128, 16, 512], BF16, "ot") for _ in range(NB)]
    t_ot = [Tok() for _ in range(NB)]
    xt = [C.sb([128, D], F32, "xt") for _ in range(NB)]
    t_xt = [Tok() for _ in range(NB)]
    tmp = [C.sb([128, 512], F32, "tmp") for _ in range(2)]
    t_tmp = [Tok() for _ in range(2)]
    ps = [C.ps([128, 512], F32, "psD") for _ in range(4)]
    t_ps = [Tok() for _ in range(4)]
    tiles = list(range(NT)) if ctx_full else list(range(NCTX // 128, NT))
    it = 0
    for n_, ti in enumerate(tiles):
        i = n_ % NB
        r = 1 if ti < NCTX // 128 else 0
        gi_ = (ti + 2) // 4
        oi = gi_ % NB
        g_lo = max(0, gi_ * 4 - 2)
        g_n = min(NT, gi_ * 4 + 2) - g_lo
        oo = (ti - g_lo) * 128
        if ti == g_lo or n_ == 0:
            tsl = slice(g_lo * 128, (g_lo + g_n) * 128)
            P.dma("sp", ot[oi][:, 0:8, 0:128 * g_n], oTa[:, tsl].rearrange("(k p) t -> p k t", p=128), w=[t_ot[oi]])
            P.dma("sp", ot[oi][:, 8:12, 0:128 * g_n], oTs[:, tsl].rearrange("(k p) t -> p k t", p=128), w=[t_ot[oi]])
            P.dma("sp", ot[oi][:, 12:16, 0:128 * g_n], oTg[:, tsl].rearrange("(k p) t -> p k t", p=128), w=[t_ot[oi]])
        P.dma("act", xt[i][:], x[ti * 128:(ti + 1) * 128, :], r=[R.tk(xname, ti)], w=[t_xt[i]])
        for cb in range(4):
            j = it % 4
            it += 1
            cs = slice(cb * 512, (cb + 1) * 512)
            P.mm([(lambda pe, k=k, j=j, oi=oi, oo=oo, cs=cs: pe.matmul(ps[j][:], ot[oi][:, k, oo:oo + 128], W[:, k, cs], start=(k == 0), stop=(k == 15)))
                  for k in range(16)], r=[t_ot[oi], t_W], w=[t_ps[j]])
            P.op("dve", lambda v, j=j, r=r, cs=cs: v.tensor_tensor(out=tmp[j % 2][:], in0=ps[j][:], in1=GA[r][:, cs], op=ALU.mult),
                 r=[t_ps[j], tGA[r]], w=[t_tmp[j % 2]])
            P.op("dve", lambda g, j=j, i=i, cs=cs: g.tensor_tensor(out=xt[i][:, cs], in0=xt[i][:, cs], in1=tmp[j % 2][:], op=ALU.add),
                 r=[t_tmp[j % 2], t_xt[i]], w=[t_xt[i]])
            if side is not None and cb % 2 == 1:
                next(side, None)
        P.dma("sp", xo[ti * 128:(ti + 1) * 128, :], xt[i][:], r=[t_xt[i]], w=[R.tk(xout, ti)])
    if side is not None:
        for _ in side:
            pass
    C.close()


H2C = NTOK + 3
FFN_BLOCKS = [(1, 769, [0, 1, 2, 3, 4, 5]), (770, 768, [6, 7, 8, 9, 10, 11]), (1538, 768, [12, 13, 14, 15, 16, 17])]


def tile_col(ti):
    return 1 + ti * 128 if ti < 2 else 258 + (ti - 2) * 128


def phase_E(P, R, K, l, xname, xout, dbg=""):
    w_up = R.get("ffn_w_up", [R.nl, D, 2 * DFF])[R.li(l):R.li(l) + 1]
    w_dn = R.get("ffn_w_down", [R.nl, DFF, D])[R.li(l):R.li(l) + 1]
    cw = R.get("ffn_conv_w", [R.nl, 3, DFF])[R.li(l):R.li(l) + 1]
    cb_ = R.get("ffn_conv_b", [R.nl, DFF])[R.li(l):R.li(l) + 1]
    h2T = R.get("h2T", [D, H2C], BF16)
    xb = R.get("xb_recv", [1, D])
    x = R.get(xname, [NTOK, D])
    xo = R.get(xout, [NTOK, D])
    h2v = h2T.rearrange("(k p) c -> p k c", p=128)
    C0 = Ctx(P)
    groups = [[0, 1], [2, 3, 4, 5], [6, 7, 8, 9], [10, 11, 12, 13], [14, 15, 16, 17], [NT]]
    gof = {}
    for gi, gl in enumerate(groups):
        for k_, ti in enumerate(gl):
            gof[ti] = (gi, k_, len(gl))
    stage = [C0.sb([128, 16, 512], BF16, "stage") for _ in range(2)]
    t_stage = [(Tok(), Tok()) for _ in range(2)]
    zt = C0.sb([128, 16, 1], BF16, "zt")
    t_z = Tok()
    P.op("pool", lambda g: g.memset(zt[:], 0.0), w=[t_z])
    P.dma("sp", h2v[:, :, 0:1], zt[:], r=[t_z], w=[R.tk("h2T", "z0")], allow_slow_non_contiguous=True)
    P.dma("sp", h2v[:, :, 257:258], zt[:], r=[t_z], w=[R.tk("h2T", "z1")], allow_slow_non_contiguous=True)

    def after(ti, toks):
        gi, k_, gn = gof[ti]
        if k_ != gn - 1:
            return
        if ti < NT:
            c0 = tile_col(groups[gi][0])
            P.dma("sp", h2v[:, :, c0:c0 + 128 * gn], stage[gi % 2][:, :, 0:128 * gn], r=list(toks), w=[R.tk("h2T", ti)])
        else:
            P.dma("sp", h2v[:, :, H2C - 1:H2C], stage[gi % 2][:, :, 0:1], r=list(toks), w=[R.tk("h2T", ti)], allow_slow_non_contiguous=True)

    norm_mod_transpose(P, R, K, C0, l, xname, "g_ffn", 3, 4, None,
                       dst_fn=lambda ti, h: stage[gof[ti][0] % 2][:, 8 * h:8 * h + 8, 128 * gof[ti][1]:128 * gof[ti][1] + 128],
                       tok_fn=lambda ti: t_stage[gof[ti][0] % 2], after=after, extra_row=(xb, R.tk("xb_recv")))
    C0.close()
    if dbg == "e0":
        return
    h2_toks = [R.tk("h2T", k) for k in ["z0", "z1"] + list(range(NT + 1))]

    C = Ctx(P)
    CW = C.sb([128, NFF, 3], F32, "CW")
    CB = C.sb([128, NFF], F32, "CB")
    t_cw = Tok()
    for j in range(3):
        P.dma("sp", CW[:, :, j], cw[0, j].rearrange("(f p) -> p f", p=128), w=[t_cw], allow_slow_non_contiguous=True)
    P.dma("sp", CB[:], cb_[0].rearrange("(f p) -> p f", p=128), w=[t_cw], allow_slow_non_contiguous=True)
    aT = C.sb([128, NFF, 769], BF16, "aT")
    wuv = w_up[0].rearrange("(k p) c -> p k c", p=128)
    wdv = w_dn[0].rearrange("(f p) c -> p f c", p=128)
    for (a, n, tiles) in FFN_BLOCKS:
        Cu = Ctx(P)
        hb = Cu.sb([128, 16, 771], BF16, "h2blk")
        t_hb = Tok()
        P.dma("sp", hb[:, 0:8, 0:n + 2], h2v[:, 0:8, a - 1:a + n + 1], r=h2_toks, w=[t_hb])
        P.dma("act", hb[:, 8:16, 0:n + 2], h2v[:, 8:16, a - 1:a + n + 1], r=h2_toks, w=[t_hb])
        wu = [Cu.sb([128, 16, 1024], BF16, "wup") for _ in range(2)]
        t_wu = [Tok() for _ in range(2)]
        G = [[Cu.ps([128, 512], F32, "G") for _ in range(2)] for _ in range(2)]
        V = [[Cu.ps([128, 512], F32, "V") for _ in range(2)] for _ in range(2)]
        t_G = [[Tok() for _ in range(2)] for _ in range(2)]
        t_V = [[Tok() for _ in range(2)] for _ in range(2)]
        g1 = [Cu.sb([128, 385], F32, "g1") for _ in range(2)]
        t_g1 = [Tok() for _ in range(2)]
        sl = [Cu.sb([128, 385], F32, "sl") for _ in range(2)]
        t_sl = [Tok() for _ in range(2)]
        n0 = n - 384
        chunks = [(0, n0), (n0, 384)]
        t_aT = Tok()
        for f in range(NFF):
            wi = (f // 4) % 2
            pb = f % 2
            fo = (f % 4) * 128
            if f % 4 == 0:
                P.dma("pool", wu[wi][:, :, 0:512], wuv[:, :, f * 128:f * 128 + 512], w=[t_wu[wi]])
                P.dma("pool", wu[wi][:, :, 512:1024], wuv[:, :, DFF + f * 128:DFF + f * 128 + 512], w=[t_wu[wi]])
            for c, (s0, ln) in enumerate(chunks):
                Gp, Vp = G[pb][c], V[pb][c]
                P.mm([(lambda pe, k=k, Gp=Gp, wi=wi, s0=s0, ln=ln, fo=fo: pe.matmul(Gp[:, 0:ln + 2], wu[wi][:, k, fo:fo + 128], hb[:, k, s0:s0 + ln + 2], start=(k == 0), stop=(k == 15)))
                      for k in range(16)], r=[t_wu[wi], t_hb], w=[t_G[pb][c]])
                P.mm([(lambda pe, k=k, Vp=Vp, wi=wi, s0=s0, ln=ln, fo=fo: pe.matmul(Vp[:, 0:ln], wu[wi][:, k, 512 + fo:512 + fo + 128], hb[:, k, s0 + 1:s0 + ln + 1], start=(k == 0), stop=(k == 15)))
                      for k in range(16)], r=[t_wu[wi], t_hb], w=[t_V[pb][c]])
                P.op("act", lambda a_, Gp=Gp, c=c, f=f, ln=ln: a_.activation(out=g1[c][:, 0:ln], in_=Gp[:, 1:ln + 1], func=AF.Identity,
                                                                           scale=CW[:, f, 1:2], bias=CB[:, f:f + 1]),
                     r=[t_G[pb][c], t_cw], w=[t_g1[c]])
                P.op("dve", lambda v, Gp=Gp, c=c, f=f, ln=ln: v.scalar_tensor_tensor(out=g1[c][:, 0:ln], in0=Gp[:, 0:ln], scalar=CW[:, f, 0:1], in1=g1[c][:, 0:ln],
                                                                                  op0=ALU.mult, op1=ALU.add),
                     r=[t_G[pb][c], t_cw, t_g1[c]], w=[t_g1[c]])
                P.op("dve", lambda v, Gp=Gp, c=c, f=f, ln=ln: v.scalar_tensor_tensor(out=g1[c][:, 0:ln], in0=Gp[:, 2:ln + 2], scalar=CW[:, f, 2:3], in1=g1[c][:, 0:ln],
                                                                                  op0=ALU.mult, op1=ALU.add),
                     r=[t_G[pb][c], t_cw, t_g1[c]], w=[t_g1[c]])
                P.op("act", lambda a_, c=c, ln=ln: a_.activation(out=sl[c][:, 0:ln], in_=g1[c][:, 0:ln], func=AF.Silu), r=[t_g1[c]], w=[t_sl[c]])
                P.op("dve", lambda v, Vp=Vp, c=c, f=f, s0=s0, ln=ln: v.tensor_tensor(out=aT[:, f, s0:s0 + ln], in0=sl[c][:, 0:ln], in1=Vp[:, 0:ln], op=ALU.mult),
                     r=[t_sl[c], t_V[pb][c]], w=[t_aT])
        Cu.close()
        if dbg == "up":
            continue
        Cd = Ctx(P)
        GF, tGF = [], []
        for r in range(2):
            g, tg = load_bcast_mod(P, Cd, R, l, r, 5, "gf%d" % r)
            GF.append(g); tGF.append(tg)
        wd = [Cd.sb([128, 11, 512], BF16, "wd") for _ in range(8)]
        t_wd = [Tok() for _ in range(8)]
        ps = [Cd.ps([128, 512], F32, "psd") for _ in range(6)]
        t_ps = [Tok() for _ in range(6)]
        xt = [Cd.sb([128, 512], F32, "xt") for _ in range(4)]
        t_xt = [Tok() for _ in range(4)]
        tmp = [Cd.sb([128, 512], F32, "tmp") for _ in range(2)]
        t_tmp = [Tok() for _ in range(2)]
        x_it = 0
        for cb in range(4):
            cs = slice(cb * 512, (cb + 1) * 512)
            for fc in range(4):
                wi = (cb % 2) * 4 + fc
                P.dma("pool", wd[wi][:], wdv[:, fc * 11:(fc + 1) * 11, cs], w=[t_wd[wi]])
            for hf in range(2):
                tl = list(enumerate(tiles))[3 * hf:3 * hf + 3]
                for fc in range(4):
                    wi = (cb % 2) * 4 + fc
                    fns = []
                    for ff in range(11):
                        f = fc * 11 + ff
                        for q, ti in tl:
                            lc = tile_col(ti) - a
                            fns.append(lambda pe, q=q, f=f, ff=ff, lc=lc, wi=wi: pe.matmul(ps[q][:], aT[:, f, lc:lc + 128], wd[wi][:, ff, :], start=(f == 0), stop=(f == NFF - 1)))
                    P.mm(fns, r=[t_wd[wi], t_aT], w=[t_ps[q] for q, _ in tl])
                for q, ti in tl:
                    r = 1 if ti < 2 else 0
                    xi = x_it % 4
                    x_it += 1
                    P.dma("sp", xt[xi][:], x[ti * 128:(ti + 1) * 128, cs], r=[R.tk(xname, ti)], w=[t_xt[xi]])
                    P.op("dve", lambda v, q=q, r=r, cs=cs, xi=xi: v.tensor_tensor(out=tmp[xi % 2][:], in0=ps[q][:], in1=GF[r][:, cs], op=ALU.mult),
                         r=[t_ps[q], tGF[r]], w=[t_tmp[xi % 2]])
                    P.op("dve", lambda v, xi=xi: v.tensor_tensor(out=xt[xi][:], in0=xt[xi][:], in1=tmp[xi % 2][:], op=ALU.add),
                         r=[t_tmp[xi % 2], t_xt[xi]], w=[t_xt[xi]])
                    P.dma("sp", xo[ti * 128:(ti + 1) * 128, cs], xt[xi][:], r=[t_xt[xi]], w=[R.tk(xout, (ti, cb))])
        Cd.close()
    C.close()


NEG = -30000.0


def phase_C(P, R, K, l, ctx_full=True, dbg_qtiles=None, dbg_skip=()):
    qT = R.get("qT", [NQ, NTOK], BF16)
    kT = R.get("kT", [128, NTOK + 128], BF16)
    vv = R.get("v", [NTOK + 128, 128], BF16)
    oT = R.get("oT_a", [NQ, NTOK], BF16)
    sink = R.get("attn_sink", [R.nl, 16])[R.li(l):R.li(l) + 1]
    C = Ctx(P)
    io = C.sb([128, 128], I32, "mio")
    mf = C.sb([128, 128], F32, "mf")
    masks = {}
    t_m = Tok()
    for name, cm, op, thr in (("prev", -1, ALU.is_le, 0.0), ("next", -1, ALU.is_ge, 0.0), ("halo", 1, ALU.is_ge, 127.0)):
        mk = C.sb([128, 8, 128], BF16, "mask_" + name)
        P.op("pool", lambda g, cm=cm: g.iota(io[:], [[1, 128]], base=0, channel_multiplier=cm), r=[t_m], w=[t_m])
        P.op("dve", lambda v: v.tensor_copy(out=mf[:], in_=io[:]), r=[t_m], w=[t_m])
        for h8 in range(8):
            P.op("dve", lambda v, mk=mk, op=op, thr=thr, h8=h8: v.tensor_scalar(out=mk[:, h8, :], in0=mf[:], scalar1=thr, scalar2=None, op0=op), r=[t_m], w=[t_m])
        masks[name] = mk
    onesA = C.sb([128, 128], BF16, "onesA")
    onesB = C.sb([128, 128], BF16, "onesB")
    P.op("pool", lambda g: g.memset(onesA[:], 0.0), w=[t_m])
    P.op("pool", lambda g: g.memset(onesB[:], 0.0), w=[t_m])
    P.op("pool", lambda g: g.memset(onesA[:, 0:64], 1.0), r=[t_m], w=[t_m])
    P.op("pool", lambda g: g.memset(onesB[:, 64:128], 1.0), r=[t_m], w=[t_m])
    es = C.sb([128, 8], F32, "esink")
    t_es = Tok()
    sv = sink[0].rearrange("(m h) -> h m", h=2)
    P.dma("sp", es[0:64, :], sv[0].partition_broadcast(64), w=[t_es], allow_slow_non_contiguous=True)
    P.dma("sp", es[64:128, :], sv[1].partition_broadcast(64), w=[t_es], allow_slow_non_contiguous=True)
    P.op("act", lambda a: a.activation(out=es[:], in_=es[:], func=AF.Exp), r=[t_es], w=[t_es])
    NKB = NT + 1
    KT2 = [C.sb([128, NTOK + 128], BF16, "KT2_%d" % j) for j in range(2)]
    t_k = Tok()
    for j in range(2):
        P.dma("sp", KT2[j][0:64, :], kT[64 * j:64 * j + 64, :], w=[t_k])
        P.dma("act", KT2[j][64:128, :], kT[64 * j:64 * j + 64, :], w=[t_k])
    VP = C.sb([128, NKB, 2, 256], BF16, "VP")
    t_v = Tok()
    P.op("pool", lambda g: g.memset(VP[:], 1.0), w=[t_v])
    v3 = vv.rearrange("(kb p) c -> p kb c", p=128)
    for j in range(2):
        P.dma("sp", VP[:, :, j, 0:64], v3[:, :, 64 * j:64 * j + 64], r=[t_v], w=[t_v])
        P.dma("act", VP[:, :, j, 192:256], v3[:, :, 64 * j:64 * j + 64], r=[t_v], w=[t_v])
    qs = [C.sb([128, 8, 128], BF16, "qs") for _ in range(2)]
    t_qs = [Tok() for _ in range(2)]
    S = [C.ps([128, 1024], F32, "S") for _ in range(2)]
    t_S = [Tok() for _ in range(2)]
    PT = [C.sb([128, 5, 1024], BF16, "PT") for _ in range(2)]
    t_PT = [Tok() for _ in range(2)]
    Op = [C.ps([128, 1024], F32, "Op") for _ in range(2)]
    t_O = [Tok() for _ in range(2)]
    dsb = [C.sb([128, 512], F32, "dsb") for _ in range(2)]
    t_dsb = [Tok() for _ in range(2)]
    osb = [C.sb([128, 4, 128], BF16, "osb") for _ in range(2)]
    t_osb = [Tok() for _ in range(2)]
    qtiles = list(range(NT)) if ctx_full else list(range(2, NT))
    if dbg_qtiles is not None:
        qtiles = dbg_qtiles
    s_it = 0
    pj = 0
    for n_, qt in enumerate(qtiles):
        qi = n_ % 2
        P.dma("sp", qs[qi][:], qT[:, qt * 128:(qt + 1) * 128].rearrange("(m p) t -> p m t", p=128), w=[t_qs[qi]])
        if qt < 2:
            kbs = [(0, None), (1, None)]
        else:
            i = qt - 2
            kbs = [(0, None), (1, None), (qt, None)]
            if i > 0:
                kbs.append((qt - 1, "prev"))
            if i < 15:
                kbs.append((qt + 1, "next"))
            else:
                kbs.append((NT, "halo"))
        for j in range(2):
            pi = pj % 2
            pj += 1
            for kbi, (kb, mname) in enumerate(kbs):
                si = s_it % 2
                s_it += 1
                fns = []
                for hh in range(8):
                    m, e = 4 * j + hh // 2, hh % 2
                    cbk = e * 4 + hh // 2
                    fns.append(lambda pe, si=si, hh=cbk, m=m, e=e, kb=kb, qi=qi, j=j, mname=mname: pe.matmul(
                        S[si][:, hh * 128:(hh + 1) * 128], KT2[j][64 * e:64 * e + 64, kb * 128:(kb + 1) * 128],
                        qs[qi][64 * e:64 * e + 64, m, :], start=True, stop=True))
                P.mm(fns, r=[t_k, t_qs[qi]], w=[t_S[si]])
                P.op("act", lambda a, si=si, pi=pi, kbi=kbi: a.activation(out=PT[pi][:, kbi, :], in_=S[si][:], func=AF.Exp, scale=0.125),
                     r=[t_S[si]], w=[t_PT[pi]])
                if mname is not None:
                    P.op("pool", lambda g, pi=pi, kbi=kbi, mname=mname: g.tensor_tensor(out=PT[pi][:, kbi, :], in0=PT[pi][:, kbi, :],
                                                                                       in1=masks[mname][:].rearrange("p h q -> p (h q)"), op=ALU.mult),
                         r=[t_PT[pi], t_m], w=[t_PT[pi]])
            if "pv" in dbg_skip:
                continue
            fns = []
            nk = len(kbs)
            for hh in range(8):
                mm, e = hh // 2, hh % 2
                for kbi, (kb, _) in enumerate(kbs):
                    fns.append(lambda pe, pi=pi, hh=hh, mm=mm, kbi=kbi, kb=kb, e=e, j=j, nk=nk: pe.matmul(
                        Op[pi][:, hh * 128:(hh + 1) * 128], VP[:, kb, j, 128 * e:128 * e + 128],
                        PT[pi][:, kbi, (4 * e + mm) * 128:(4 * e + mm + 1) * 128], start=(kbi == 0), stop=(kbi == nk - 1)))
            P.mm(fns, r=[t_v, t_PT[pi]], w=[t_O[pi]])
            Ov = Op[pi][:].rearrange("p (m e q) -> p m e q", e=2, q=128)
            dv = dsb[pi][:].rearrange("p (m q) -> p m q", q=128)
            P.op("act", lambda a, Ov=Ov, dv=dv: a.copy(out=dv[0:64], in_=Ov[64:128, :, 0, :]), r=[t_O[pi]], w=[t_dsb[pi]])
            P.op("act", lambda a, Ov=Ov, dv=dv: a.copy(out=dv[64:128], in_=Ov[0:64, :, 1, :]), r=[t_O[pi]], w=[t_dsb[pi]])
            for mm in range(4):
                m = 4 * j + mm
                P.op("dve", lambda v, pi=pi, mm=mm, m=m: v.tensor_scalar(out=dsb[pi][:, mm * 128:(mm + 1) * 128], in0=dsb[pi][:, mm * 128:(mm + 1) * 128],
                                                                      scalar1=es[:, m:m + 1], scalar2=None, op0=ALU.add),
                     r=[t_dsb[pi], t_es], w=[t_dsb[pi]])
            P.op("dve", lambda v, pi=pi: v.reciprocal(out=dsb[pi][:], in_=dsb[pi][:]), r=[t_dsb[pi]], w=[t_dsb[pi]])
            P.op("dve", lambda v, pi=pi, Ov=Ov, dv=dv: v.tensor_tensor(out=osb[pi][0:64], in0=Ov[0:64, :, 0, :], in1=dv[0:64], op=ALU.mult),
                 r=[t_O[pi], t_dsb[pi]], w=[t_osb[pi]])
            P.op("dve", lambda v, pi=pi, Ov=Ov, dv=dv: v.tensor_tensor(out=osb[pi][64:128], in0=Ov[64:128, :, 1, :], in1=dv[64:128], op=ALU.mult),
                 r=[t_O[pi], t_dsb[pi]], w=[t_osb[pi]])
            P.dma("sp", oT[512 * j:512 * j + 512, qt * 128:(qt + 1) * 128].rearrange("(m p) t -> p m t", p=128), osb[pi][:],
                  r=[t_osb[pi]], w=[R.tk("oT_a", (qt, j))])
    C.close()


TCH = 16
NCH = NTOK // TCH
NCC = NCTX // TCH


class SSMState:
    pass


def ssm_prep(P, R, K, C, l):
    S = SSMState()
    li = R.li(l)
    lam_re = R.get("ssm_lambda_re", [R.nl, 2, 32, 64])
    lam_im = R.get("ssm_lambda_im", [R.nl, 2, 32, 64])
    log_dt = R.get("ssm_log_dt", [R.nl, 2, 32])
    b_re = R.get("ssm_b_re", [R.nl, 2, 32, 64, 16])
    b_im = R.get("ssm_b_im", [R.nl, 2, 32, 64, 16])
    c_re = R.get("ssm_c_re", [R.nl, 2, 32, 16, 64])
    c_im = R.get("ssm_c_im", [R.nl, 2, 32, 16, 64])
    pf = C.sb([128, 16], F32, "pf")
    ARE = C.sb([128, 64], F32, "ARE")
    AIM = C.sb([128, 64], F32, "AIM")
    LB = C.sb([128, 64, 128], BF16, "LB")
    CW = C.sb([128, 64, 16], F32, "CW")
    CL = C.sb([128, 64], F32, "CL")
    CR = C.sb([128, 64], F32, "CR")
    ID2 = C.sb([128, 64], F32, "ID2")
    ATR = C.sb([64, 64], F32, "ATR")
    ATI = C.sb([64, 64], F32, "ATI")
    Ct = Ctx(P)
    t = Tok()
    LR = Ct.sb([128, 64], F32, "LR")
    LI = Ct.sb([128, 64], F32, "LI")
    DT = Ct.sb([128, 64], F32, "DT")
    for base in (0, 64):
        P.dma("sp", LR[base:base + 64, :], lam_re[li].rearrange("d g p -> p (d g)"), w=[t], allow_slow_non_contiguous=True)
        P.dma("sp", LI[base:base + 64, :], lam_im[li].rearrange("d g p -> p (d g)"), w=[t], allow_slow_non_contiguous=True)
    P.dma("sp", DT[:], log_dt[li].rearrange("d g -> (d g)").partition_broadcast(128), w=[t])
    pi_ = Ct.sb([128, 1], I32, "pi")
    pj_ = Ct.sb([128, 1], I32, "pj")
    tp = Tok()
    P.op("pool", lambda g: g.iota(pi_[:], [[0, 1]], base=0, channel_multiplier=1), w=[tp])
    P.op("dve", lambda v: v.tensor_scalar(out=pj_[:], in0=pi_[:], scalar1=6, scalar2=None, op0=ALU.logical_shift_right), r=[tp], w=[tp])
    P.op("dve", lambda v: v.tensor_copy(out=pf[:, 1:2], in_=pj_[:]), r=[tp], w=[tp])
    P.op("dve", lambda v: v.tensor_scalar(out=pf[:, 0:1], in0=pf[:, 1:2], scalar1=-1.0, scalar2=1.0, op0=ALU.mult, op1=ALU.add), r=[tp], w=[tp])
    P.op("dve", lambda v: v.tensor_scalar(out=pf[:, 2:3], in0=pf[:, 1:2], scalar1=2.0, scalar2=-1.0, op0=ALU.mult, op1=ALU.add), r=[tp], w=[tp])
    P.op("dve", lambda v: v.tensor_scalar(out=pf[:, 3:4], in0=pf[:, 1:2], scalar1=-1.0, scalar2=None, op0=ALU.mult), r=[tp], w=[tp])
    P.op("dve", lambda v: v.tensor_scalar(out=pj_[:], in0=pi_[:], scalar1=4, scalar2=None, op0=ALU.logical_shift_right), r=[tp], w=[tp])
    P.op("dve", lambda v: v.tensor_copy(out=pf[:, 12:13], in_=pj_[:]), r=[tp], w=[tp])
    for i in range(8):
        P.op("dve", lambda v, i=i: v.tensor_scalar(out=pf[:, 4 + i:5 + i], in0=pf[:, 12:13], scalar1=float(i), scalar2=None, op0=ALU.is_equal), r=[tp], w=[tp])
    S.pf, S.t_pf = pf, tp
    P.op("act", lambda a: a.activation(out=DT[:], in_=DT[:], func=AF.Exp), r=[t], w=[t])
    MAG = Ct.sb([128, 64], F32, "MAG")
    ANG = Ct.sb([128, 64], F32, "ANG")
    P.op("dve", lambda v: v.tensor_tensor(out=MAG[:], in0=LR[:], in1=DT[:], op=ALU.mult), r=[t], w=[t])
    P.op("act", lambda a: a.activation(out=MAG[:], in_=MAG[:], func=AF.Exp), r=[t], w=[t])
    P.op("dve", lambda v: v.tensor_tensor(out=ANG[:], in0=LI[:], in1=DT[:], op=ALU.mult), r=[t], w=[t])
    SN = Ct.sb([128, 64], F32, "SN")
    CS = Ct.sb([128, 64], F32, "CS")
    Cr = Ctx(P)
    tsn = Tok(); tsn.w = t.w
    range_reduce_sin(P, Cr, SN[:], ANG[:], [128, 64], t)
    range_reduce_sin(P, Cr, CS[:], ANG[:], [128, 64], t, extra=PI / 2)
    Cr.close()
    t_a = Tok()
    P.op("dve", lambda v: v.tensor_tensor(out=ARE[:], in0=MAG[:], in1=CS[:], op=ALU.mult), r=[t], w=[t_a])
    P.op("dve", lambda v: v.tensor_tensor(out=AIM[:], in0=MAG[:], in1=SN[:], op=ALU.mult), r=[t], w=[t_a])
    S.ARE, S.AIM, S.t_a = ARE, AIM, t_a
    DEN = Ct.sb([128, 64], F32, "DEN")
    T1 = Ct.sb([128, 64], F32, "T1")
    NRE = Ct.sb([128, 64], F32, "NRE")
    FRE = Ct.sb([128, 64], F32, "FRE")
    FIM = Ct.sb([128, 64], F32, "FIM")
    tf = Tok()
    P.op("dve", lambda v: v.tensor_tensor(out=DEN[:], in0=LR[:], in1=LR[:], op=ALU.mult), r=[t], w=[tf])
    P.op("dve", lambda v: v.tensor_tensor(out=T1[:], in0=LI[:], in1=LI[:], op=ALU.mult), r=[t], w=[tf])
    P.op("dve", lambda v: v.tensor_tensor(out=DEN[:], in0=DEN[:], in1=T1[:], op=ALU.add), r=[tf], w=[tf])
    P.op("dve", lambda v: v.reciprocal(out=DEN[:], in_=DEN[:]), r=[tf], w=[tf])
    P.op("dve", lambda v: v.tensor_scalar(out=NRE[:], in0=ARE[:], scalar1=-1.0, scalar2=None, op0=ALU.add), r=[t_a], w=[tf])
    P.op("dve", lambda v: v.tensor_tensor(out=FRE[:], in0=NRE[:], in1=LR[:], op=ALU.mult), r=[tf, t], w=[tf])
    P.op("dve", lambda v: v.tensor_tensor(out=T1[:], in0=AIM[:], in1=LI[:], op=ALU.mult), r=[tf, t, t_a], w=[tf])
    P.op("dve", lambda v: v.tensor_tensor(out=FRE[:], in0=FRE[:], in1=T1[:], op=ALU.add), r=[tf], w=[tf])
    P.op("dve", lambda v: v.tensor_tensor(out=FRE[:], in0=FRE[:], in1=DEN[:], op=ALU.mult), r=[tf], w=[tf])
    P.op("dve", lambda v: v.tensor_tensor(out=FIM[:], in0=AIM[:], in1=LR[:], op=ALU.mult), r=[tf, t, t_a], w=[tf])
    P.op("dve", lambda v: v.tensor_tensor(out=T1[:], in0=NRE[:], in1=LI[:], op=ALU.mult), r=[tf, t], w=[tf])
    P.op("dve", lambda v: v.tensor_tensor(out=FIM[:], in0=FIM[:], in1=T1[:], op=ALU.subtract), r=[tf], w=[tf])
    P.op("dve", lambda v: v.tensor_tensor(out=FIM[:], in0=FIM[:], in1=DEN[:], op=ALU.mult), r=[tf], w=[tf])
    P.op("dve", lambda v: v.tensor_scalar(out=FIM[:], in0=FIM[:], scalar1=pf[:, 2:3], scalar2=None, op0=ALU.mult), r=[tf, tp], w=[tf])
    BX = Ct.sb([128, 64, 16], F32, "BX")
    BY = Ct.sb([128, 64, 16], F32, "BY")
    tb = Tok()
    brv = b_re[li].rearrange("d g p h -> p (d g) h")
    biv = b_im[li].rearrange("d g p h -> p (d g) h")
    P.dma("sp", BX[0:64], brv, w=[tb])
    P.dma("act", BX[64:128], biv, w=[tb])
    P.dma("sp", BY[0:64], biv, w=[tb])
    P.dma("act", BY[64:128], brv, w=[tb])
    for h in range(16):
        P.op("dve", lambda v, h=h: v.tensor_tensor(out=BX[:, :, h], in0=BX[:, :, h], in1=FRE[:], op=ALU.mult), r=[tb, tf], w=[tb])
        P.op("pool", lambda g, h=h: g.tensor_tensor(out=BY[:, :, h], in0=BY[:, :, h], in1=FIM[:], op=ALU.mult), r=[tb, tf], w=[tb])
    P.op("dve", lambda v: v.tensor_tensor(out=BX[:], in0=BX[:], in1=BY[:], op=ALU.add), r=[tb], w=[tb])
    t_LB = Tok()
    psT = Ct.ps([128, 128], F32, "psT")
    t_psT = Tok()
    for b8 in range(8):
        P.mm([lambda pe, b8=b8: pe.transpose(psT[:], BX[:, b8 * 8:(b8 + 1) * 8, :].rearrange("p g h -> p (g h)"), K["idf"][:])],
             r=[tb, K["tok"]], w=[t_psT])
        for i in range(8):
            P.op("dve", lambda v, b8=b8, i=i: v.tensor_scalar(out=LB[:, b8 * 8 + i, :], in0=psT[:], scalar1=pf[:, 4 + i:5 + i], scalar2=None, op0=ALU.mult),
                 r=[t_psT, tp], w=[t_LB])
    S.LB, S.t_LB = LB, t_LB
    t_CW = Tok()
    crv = c_re[li].rearrange("d g h p -> p (d g) h")
    civ = c_im[li].rearrange("d g h p -> p (d g) h")
    for q4 in range(0, 64, 4):
        P.dma("sp", CW[0:64, q4:q4 + 4, :], crv[:, q4:q4 + 4, :], w=[t_CW], allow_slow_non_contiguous=True)
        P.dma("act", CW[64:128, q4:q4 + 4, :], civ[:, q4:q4 + 4, :], w=[t_CW], allow_slow_non_contiguous=True)
    P.op("dve", lambda v: v.tensor_scalar(out=CW[:].rearrange("p g h -> p (g h)"), in0=CW[:].rearrange("p g h -> p (g h)"), scalar1=pf[:, 2:3], scalar2=-1.0,
                                          op0=ALU.mult, op1=ALU.mult), r=[t_CW, tp], w=[t_CW])
    S.CW, S.t_CW = CW, t_CW
    t_c = Tok()
    P.op("dve", lambda v: v.tensor_scalar(out=CL[:], in0=ARE[:], scalar1=pf[:, 0:1], scalar2=None, op0=ALU.mult), r=[t_a, tp], w=[t_c])
    P.op("dve", lambda v: v.scalar_tensor_tensor(out=CL[:], in0=AIM[:], scalar=pf[:, 3:4], in1=CL[:], op0=ALU.mult, op1=ALU.add), r=[t_a, tp, t_c], w=[t_c])
    P.op("dve", lambda v: v.tensor_scalar(out=CR[:], in0=AIM[:], scalar1=pf[:, 0:1], scalar2=None, op0=ALU.mult), r=[t_a, tp], w=[t_c])
    P.op("dve", lambda v: v.scalar_tensor_tensor(out=CR[:], in0=ARE[:], scalar=pf[:, 1:2], in1=CR[:], op0=ALU.mult, op1=ALU.add), r=[t_a, tp, t_c], w=[t_c])
    S.CL, S.CR, S.t_c = CL, CR, t_c
    t_id2 = Tok()
    P.op("dve", lambda v: v.tensor_copy(out=ID2[0:64, :], in_=K["idf"][0:64, 0:64]), r=[K["tok"]], w=[t_id2])
    P.op("dve", lambda v: v.tensor_copy(out=ID2[64:128, :], in_=K["idf"][64:128, 64:128]), r=[K["tok"]], w=[t_id2])
    S.ID2, S.t_id2 = ID2, t_id2
    q1 = Ct.sb([64, 64], F32, "q1")
    q2 = Ct.sb([64, 64], F32, "q2")
    t_at = Tok()
    P.op("dve", lambda v: v.tensor_copy(out=ATR[:], in_=ARE[0:64, :]), r=[t_a], w=[t_at])
    P.op("dve", lambda v: v.tensor_copy(out=ATI[:], in_=AIM[0:64, :]), r=[t_a], w=[t_at])
    for _ in range(4):
        P.op("dve", lambda v: v.tensor_tensor(out=q1[:], in0=ATR[:], in1=ATR[:], op=ALU.mult), r=[t_at], w=[t_at])
        P.op("dve", lambda v: v.tensor_tensor(out=q2[:], in0=ATI[:], in1=ATI[:], op=ALU.mult), r=[t_at], w=[t_at])
        P.op("dve", lambda v: v.tensor_tensor(out=ATI[:], in0=ATR[:], in1=ATI[:], op=ALU.mult), r=[t_at], w=[t_at])
        P.op("dve", lambda v: v.tensor_scalar(out=ATI[:], in0=ATI[:], scalar1=2.0, scalar2=None, op0=ALU.mult), r=[t_at], w=[t_at])
        P.op("dve", lambda v: v.tensor_tensor(out=ATR[:], in0=q1[:], in1=q2[:], op=ALU.subtract), r=[t_at], w=[t_at])
    S.ATR, S.ATI, S.t_at = ATR, ATI, t_at
    Ct.close()
    return S


def ssm_rvm(P, C, S, UT, t_UT, gds, Hs, t_Hs, ps, t_ps, la, t_la, hinit=None, jmap=lambda j: j, act_only=False):
    le = "pool" if act_only else "dve"
    for i, gd in enumerate(gds):
        P.op(le, lambda v, i=i, gd=gd: v.tensor_scalar(out=la[i][:, 0:64], in0=S.ID2[:], scalar1=S.CL[:, gd:gd + 1], scalar2=None, op0=ALU.mult),
             r=[S.t_id2, S.t_c], w=[t_la[i]])
        P.op(le, lambda v, i=i, gd=gd: v.tensor_scalar(out=la[i][:, 64:128], in0=S.ID2[:], scalar1=S.CR[:, gd:gd + 1], scalar2=None, op0=ALU.mult),
             r=[S.t_id2, S.t_c], w=[t_la[i]])
    for s in range(TCH):
        for i, gd in enumerate(gds):
            d, g = gd // 32, gd % 32
            j = s if d == 0 else TCH - 1 - s
            jprev = j - 1 if d == 0 else j + 1
            u_cols = UT[:, g // 8, :].rearrange("p (c j) -> p j c", j=TCH)[:, j, :]
            fns = []
            has2 = (s > 0) or (hinit is not None)
            fns.append(lambda pe, i=i, gd=gd, u_cols=u_cols, has2=has2: pe.matmul(ps[i], S.LB[:, gd, :], u_cols, start=True, stop=(not has2)))
            rd = [t_UT, S.t_LB, t_la[i]]
            if s > 0:
                fns.append(lambda pe, i=i, jprev=jprev: pe.matmul(ps[i], la[i][:], Hs[i][:, :, jmap(jprev)], start=False, stop=True))
                rd.append(t_Hs[i])
            elif hinit is not None:
                fns.append(lambda pe, i=i: pe.matmul(ps[i], la[i][:], hinit[i][0], start=False, stop=True))
                rd.append(hinit[i][1])
            P.mm(fns, r=rd, w=[t_ps[i]])
            if act_only or (i + s) % 2 == 0:
                P.op("act", lambda a, i=i, j=j: a.copy(out=Hs[i][:, :, jmap(j)], in_=ps[i]), r=[t_ps[i]], w=[t_Hs[i]])
            else:
                P.op("dve", lambda v, i=i, j=j: v.tensor_copy(out=Hs[i][:, :, jmap(j)], in_=ps[i]), r=[t_ps[i]], w=[t_Hs[i]])


def ssm_chain(P, C, S, LL, t_L, d, c_list, init, eng="dve"):
    g0 = d * 32
    HH = [C.sb([64, 3, 32], F32, "chH") for _ in range(2)]
    T1 = C.sb([64, 2, 32], F32, "chT1")
    T2 = C.sb([64, 2, 32], F32, "chT2")
    A1 = C.sb([64, 2, 32], F32, "chA1")
    A2 = C.sb([64, 2, 32], F32, "chA2")
    tH = Tok()

    def op(fn, r, w):
        P.op(eng, fn, r=r, w=w)
    op(lambda v: v.tensor_copy(out=A1[:, 0, :], in_=S.ATR[:, g0:g0 + 32]), [S.t_at], [tH])
    op(lambda v: v.tensor_copy(out=A1[:, 1, :], in_=S.ATR[:, g0:g0 + 32]), [S.t_at], [tH])
    op(lambda v: v.tensor_scalar(out=A2[:, 0, :], in0=S.ATI[:, g0:g0 + 32], scalar1=-1.0, scalar2=None, op0=ALU.mult), [S.t_at], [tH])
    op(lambda v: v.tensor_copy(out=A2[:, 1, :], in_=S.ATI[:, g0:g0 + 32]), [S.t_at], [tH])
    if init is None:
        op(lambda v: v.memset(HH[0][:], 0.0), [], [tH])
    else:
        op(lambda v: v.tensor_copy(out=HH[0][:, 0:2, :], in_=init[0]), [init[1]], [tH])
        op(lambda v: v.tensor_copy(out=HH[0][:, 2, :], in_=init[0][:, 0, :]), [init[1]], [tH])
    cur = 0
    for c in c_list:
        h, hn = HH[cur], HH[1 - cur]
        lc = LL[:, :, g0:g0 + 32, c]
        rr = [tH, t_L]
        op(lambda v, h=h: v.tensor_tensor(out=T1[:], in0=A1[:], in1=h[:, 0:2, :], op=ALU.mult), rr, [tH])
        op(lambda v, h=h: v.tensor_tensor(out=T2[:], in0=A2[:], in1=h[:, 1:3, :], op=ALU.mult), rr, [tH])
        op(lambda v: v.tensor_tensor(out=T1[:], in0=T1[:], in1=T2[:], op=ALU.add), rr, [tH])
        op(lambda v, hn=hn, lc=lc: v.tensor_tensor(out=hn[:, 0:2, :], in0=T1[:], in1=lc, op=ALU.add), rr, [tH])
        op(lambda v, hn=hn: v.tensor_copy(out=hn[:, 2, :], in_=hn[:, 0, :]), rr, [tH])
        op(lambda v, h=h, lc=lc: v.tensor_copy(out=lc, in_=h[:, 0:2, :]), rr, [tH, t_L])
        cur = 1 - cur
    return HH[cur], tH


def phase_B(P, R, K, l, part, ctx_full=True, dbg="", S=None):
    li = R.li(l)
    uT = R.get("uT", [512, NTOK])
    send = R.get("ssm_send", [64, 2, 32])
    recv = R.get("ssm_recv", [64, 2, 32])
    hinit0 = R.get("hinit0", [64, 2, 32, NCH])
    oTs = R.get("oT_s", [512, NTOK], BF16)
    C = Ctx(P)
    if S is None:
        S = ssm_prep(P, R, K, C, l)
    if dbg == "prep":
        C.close(); return
    yTM = C.sb([128, NT, 512], F32, "yTM") if part == "b" else None
    t_y = Tok()
    Cs = Ctx(P)
    UT = Cs.sb([128, 4, NTOK], BF16, "UT")
    t_UT = Tok()
    P.dma("pool", UT[:], uT.rearrange("(c p) t -> p c t", p=128), w=[t_UT])
    shared = getattr(S, "LL", None) is not None
    if shared:
        LL, t_Ld = S.LL, S.t_Ld
    else:
        LL = Cs.sb([64, 2, 64, NCH], F32, "LL")
        t_Ld = [Tok(), Tok()]
    LRE = LL[:, 0]
    LIM = LL[:, 1]
    NG = 4
    NG1 = 6
    Hs = [Cs.sb([128, NCH, TCH], F32, "Hs") for _ in range(NG)] if part == "b" else []
    t_Hs = [Tok() for _ in range(NG)]
    Hs1 = [Cs.sb([128, NCH, 2], F32, "Hs1") for _ in range(NG1)]
    t_Hs1 = [Tok() for _ in range(NG1)]
    psb = [Cs.ps([128, 512], F32, "psR") for _ in range(NG1)]
    ps = [psb[i][:, 0:NCH] for i in range(NG1)]
    t_ps = [Tok() for _ in range(NG1)]
    la = [Cs.sb([128, 128], F32, "la") for _ in range(NG1)]
    t_la = [Tok() for _ in range(NG1)]

    def pass1(d, act_only=False):
        for g4 in range(0, 32, NG1):
            gds = [d * 32 + g for g in range(g4, min(32, g4 + NG1))]
            ssm_rvm(P, C, S, UT, t_UT, gds, Hs1, t_Hs1, ps, t_ps, la, t_la, jmap=lambda j: j % 2, act_only=act_only)
            jl = (TCH - 1 if d == 0 else 0) % 2
            for i, gd in enumerate(gds):
                P.op("act", lambda a, i=i, gd=gd: a.copy(out=LRE[:, gd, :], in_=Hs1[i][0:64, :, jl]), r=[t_Hs1[i]], w=[t_Ld[d]])
                if act_only:
                    P.op("act", lambda a, i=i, gd=gd: a.copy(out=LIM[:, gd, :], in_=Hs1[i][64:128, :, jl]), r=[t_Hs1[i]], w=[t_Ld[d]])
                else:
                    P.op("dve", lambda v, i=i, gd=gd: v.tensor_copy(out=LIM[:, gd, :], in_=Hs1[i][64:128, :, jl]), r=[t_Hs1[i]], w=[t_Ld[d]])

    if part == "a":
        pass1(0)
        if dbg == "nochain":
            Cs.close(); C.close(); return
        hh, tH = ssm_chain(P, Cs, S, LL, t_Ld[0], 0, list(range(NCH)), None, eng="dve")
        P.dma("sp", send, hh[:, 0:2, :], r=[tH], w=[R.tk("ssm_send", 0)])
        if shared:
            pass1(1, act_only=True)
        else:
            P.dma("sp", hinit0[:, 0], LRE[:, 0:32, :], r=[t_Ld[0], tH], w=[R.tk("hinit0", 0)])
            P.dma("sp", hinit0[:, 1], LIM[:, 0:32, :], r=[t_Ld[0], tH], w=[R.tk("hinit0", 1)])
        Cs.close()
        C.close()
        return
    if not shared:
        pass1(1)
    if not getattr(S, "chain1_done", False):
        rin = Cs.sb([64, 2, 32], F32, "rin")
        t_rin = Tok()
        P.dma("sp", rin[:], recv, r=[R.tk("ssm_recv", 0)], w=[t_rin])
        ssm_chain(P, Cs, S, LL, t_Ld[1], 1, list(range(NCH - 1, NCC - 1, -1)), (rin[:], t_rin))
        ssm_chain(P, Cs, S, LL, t_Ld[1], 1, list(range(NCC - 1, -1, -1)), None, eng=("pool" if shared else "dve"))
    if not shared:
        P.dma("sp", LRE[:, 0:32, :], hinit0[:, 0], r=[R.tk("hinit0", 0)], w=[t_Ld[0]])
        P.dma("sp", LIM[:, 0:32, :], hinit0[:, 1], r=[R.tk("hinit0", 1)], w=[t_Ld[0]])
    hin = [Cs.sb([128, NCH], F32, "hin") for _ in range(NG)]
    t_hin = [Tok() for _ in range(NG)]
    yps = [Cs.ps([128, 512], F32, "yps") for _ in range(2)]
    t_yps = [Tok() for _ in range(2)]
    for g2 in range(0, 32, 2):
        gds = [g2, 32 + g2, g2 + 1, 32 + g2 + 1]
        hinit = []
        for i, gd in enumerate(gds):
            P.op("act", lambda a, i=i, gd=gd: a.copy(out=hin[i][0:64, :], in_=LRE[:, gd, :]), r=[t_Ld[gd // 32]], w=[t_hin[i]])
            P.op("dve", lambda v, i=i, gd=gd: v.tensor_copy(out=hin[i][64:128, :], in_=LIM[:, gd, :]), r=[t_Ld[gd // 32]], w=[t_hin[i]])
            hinit.append((hin[i][:], t_hin[i]))
        ssm_rvm(P, C, S, UT, t_UT, gds, Hs, t_Hs, ps, t_ps, la, t_la, hinit=hinit)
        for k2 in range(2):
            g = g2 + k2
            yp = yps[k2]
            fns = []
            for ti in range(NT):
                for dd in range(2):
                    i = 2 * k2 + dd
                    gd = gds[i]
                    lhsT = Hs[i][:, ti * 8:(ti + 1) * 8, :].rearrange("p c j -> p (c j)")
                    fns.append(lambda pe, yp=yp, ti=ti, lhsT=lhsT, gd=gd, dd=dd: pe.matmul(yp[:, ti * 16:(ti + 1) * 16], lhsT, S.CW[:, gd, :], start=(dd == 0), stop=(dd == 1)))
            P.mm(fns, r=[t_Hs[2 * k2], t_Hs[2 * k2 + 1], S.t_CW], w=[t_yps[k2]])
            P.op("act" if k2 == 0 else "dve",
                 (lambda a, g=g, yp=yp: a.copy(out=yTM[:, :, 16 * g:16 * g + 16], in_=yp[:, 0:NT * 16].rearrange("p (t h) -> p t h", h=16))) if k2 == 0 else
                 (lambda v, g=g, yp=yp: v.tensor_copy(out=yTM[:, :, 16 * g:16 * g + 16], in_=yp[:, 0:NT * 16].rearrange("p (t h) -> p t h", h=16))),
                 r=[t_yps[k2]], w=[t_y])
    Cs.close()
    Cg = Ctx(P)
    dsk = R.get("ssm_d", [R.nl, 512])
    wglu = R.get("ssm_w_glu", [R.nl, 512, 512])
    bglu = R.get("ssm_b_glu", [R.nl, 512])
    DS = Cg.sb([128, 4], F32, "DS")
    BG = Cg.sb([128, 4], F32, "BG")
    WG = Cg.sb([128, 4, 512], BF16, "WG")
    t_g = Tok()
    P.dma("sp", DS[:], dsk[li].rearrange("(c p) -> p c", p=128), w=[t_g], allow_slow_non_contiguous=True)
    P.dma("sp", BG[:], bglu[li].rearrange("(c p) -> p c", p=128), w=[t_g], allow_slow_non_contiguous=True)
    P.dma("pool", WG[:], wglu[li].rearrange("(c p) n -> p c n", p=128), w=[t_g])
    uf = [Cg.sb([128, 4, 128], F32, "uf") for _ in range(2)]
    t_uf = [Tok() for _ in range(2)]
    pt = [Cg.ps([128, 512], F32, "ptr") for _ in range(2)]
    t_pt = [Tok() for _ in range(2)]
    y2 = [Cg.sb([128, 4, 128], F32, "y2") for _ in range(2)]
    t_y2 = [Tok() for _ in range(2)]
    gf = [Cg.sb([128, 4, 128], F32, "gf") for _ in range(2)]
    gb = [Cg.sb([128, 4, 128], BF16, "gb") for _ in range(2)]
    t_gf = [Tok() for _ in range(2)]
    pg = [Cg.ps([128, 512], F32, "pg") for _ in range(2)]
    t_pg = [Tok() for _ in range(2)]
    sg = [Cg.sb([128, 4, 128], F32, "sg") for _ in range(2)]
    t_sg = [Tok() for _ in range(2)]
    ob = [Cg.sb([128, 4, 128], BF16, "ob") for _ in range(2)]
    t_ob = [Tok() for _ in range(2)]
    tiles = list(range(NT)) if ctx_full else list(range(2, NT))
    def u_stage1(n_, ti):
        i = n_ % 2
        tsl = slice(ti * 128, (ti + 1) * 128)
        P.dma("sp", uf[i][:], uT[:, tsl].rearrange("(c p) t -> p c t", p=128), w=[t_uf[i]])
        P.mm([(lambda pe, c=c, i=i, ti=ti: pe.transpose(pt[i][:, c * 128:(c + 1) * 128], yTM[:, ti, c * 128:(c + 1) * 128], K["idf"][:])) for c in range(4)],
             r=[t_y, K["tok"]], w=[t_pt[i]])
        for c in range(4):
            P.op("dve", lambda v, c=c, i=i: v.scalar_tensor_tensor(out=y2[i][:, c, :], in0=uf[i][:, c, :], scalar=DS[:, c:c + 1], in1=pt[i][:, c * 128:(c + 1) * 128],
                                                                  op0=ALU.mult, op1=ALU.add), r=[t_uf[i], t_pt[i], t_g], w=[t_y2[i]])
        P.op("act", lambda a, i=i: a.activation(out=gf[i][:], in_=y2[i][:], func=AF.Gelu_apprx_tanh), r=[t_y2[i]], w=[t_gf[i]])
        P.op("pool", lambda g_, i=i: g_.tensor_copy(out=gb[i][:], in_=gf[i][:]), r=[t_gf[i]], w=[t_gf[i]])
        fns = []
        for co in range(4):
            for c in range(4):
                fns.append(lambda pe, co=co, c=c, i=i: pe.matmul(pg[i][:, co * 128:(co + 1) * 128], WG[:, c, co * 128:(co + 1) * 128], gb[i][:, c, :], start=(c == 0), stop=(c == 3)))
        P.mm(fns, r=[t_g, t_gf[i]], w=[t_pg[i]])

    def u_stage2(n_, ti):
        i = n_ % 2
        tsl = slice(ti * 128, (ti + 1) * 128)
        for co in range(4):
            P.op("act", lambda a, co=co, i=i: a.activation(out=sg[i][:, co, :], in_=pg[i][:, co * 128:(co + 1) * 128], func=AF.Sigmoid, bias=BG[:, co:co + 1]),
                 r=[t_pg[i], t_g], w=[t_sg[i]])
        P.op("dve", lambda v, i=i: v.tensor_tensor(out=ob[i][:], in0=gf[i][:], in1=sg[i][:], op=ALU.mult), r=[t_gf[i], t_sg[i]], w=[t_ob[i]])
        P.dma("sp", oTs[:, tsl].rearrange("(c p) t -> p c t", p=128), ob[i][:], r=[t_ob[i]], w=[R.tk("oT_s", ti)])

    for n_ in range(len(tiles) + 1):
        if n_ < len(tiles):
            u_stage1(n_, tiles[n_])
        if n_ >= 1:
            u_stage2(n_ - 1, tiles[n_ - 1])
    Cg.close()
    C.close()


def ssm_chain1_async(P, R, Cq, S):
    recv = R.get("ssm_recv", [64, 2, 32])
    rin = Cq.sb([64, 2, 32], F32, "rin")
    t_rin = Tok()
    P.dma("sp", rin[:], recv, r=[R.tk("ssm_recv", 0)], w=[t_rin])
    ssm_chain(P, Cq, S, S.LL, S.t_Ld[1], 1, list(range(NCH - 1, NCC - 1, -1)), (rin[:], t_rin), eng="pool")
    ssm_chain(P, Cq, S, S.LL, S.t_Ld[1], 1, list(range(NCC - 1, -1, -1)), None, eng="pool")
    S.chain1_done = True

def phase_F(P, R, K, xname):
    x = R.get(xname, [NTOK, D])
    y = R.get("y_out", [NLAT, D])
    gf = R.get("g_final", [D])
    C = Ctx(P)
    G = C.sb([128, D], F32, "gfin")
    t_G = Tok()
    P.dma("sp", G[:], gf.partition_broadcast(128), w=[t_G])
    xt = [C.sb([128, D], F32, "xt") for _ in range(2)]
    t_xt = [Tok() for _ in range(2)]
    junk = C.sb([128, D], BF16, "junk")
    t_j = Tok()
    st = [C.sb([128, 4], F32, "st") for _ in range(2)]
    t_st = [Tok() for _ in range(2)]
    yo = [C.sb([128, D], F32, "yo") for _ in range(2)]
    t_yo = [Tok() for _ in range(2)]
    for n_, ti in enumerate(range(2, NT)):
        i = n_ % 2
        P.dma("sp", xt[i][:], x[ti * 128:(ti + 1) * 128, :], r=[R.tk(xname, ti)], w=[t_xt[i]])
        P.op("act", lambda a, i=i: a.activation(out=junk[:], in_=xt[i][:], func=AF.Square, accum_out=st[i][:, 0:1]), r=[t_xt[i]], w=[t_j, t_st[i]])
        P.op("act", lambda a, i=i: a.activation(out=st[i][:, 1:2], in_=st[i][:, 0:1], func=AF.Sqrt, scale=1.0 / D, bias=EPS), r=[t_st[i]], w=[t_st[i]])
        P.op("dve", lambda v, i=i: v.reciprocal(out=st[i][:, 2:3], in_=st[i][:, 1:2]), r=[t_st[i]], w=[t_st[i]])
        P.op("dve", lambda v, i=i: v.scalar_tensor_tensor(out=yo[i][:], in0=xt[i][:], scalar=st[i][:, 2:3], in1=G[:], op0=ALU.mult, op1=ALU.mult),
             r=[t_xt[i], t_st[i], t_G], w=[t_yo[i]])
        P.dma("sp", y[(ti - 2) * 128:(ti - 1) * 128, :], yo[i][:], r=[t_yo[i]], w=[R.tk("y_out", ti)], is_out=True)
    C.close()


CORES = [(b, s) for b in range(4) for s in range(2)]


def prep_core_layer(inputs, l, s):
    g = lambda k: np.asarray(inputs[k])[l:l + 1]
    m = {}
    for k in ("w_out", "ffn_w_up", "ffn_w_down", "ffn_conv_b", "attn_sink", "ssm_d", "ssm_w_glu", "ssm_b_glu", "w_ada", "b_ada",
              "g_mix", "g_ffn", "gmlp_ln_g", "gmlp_ln_b"):
        m[k] = g(k)
    cw = g("ffn_conv_w")
    m["ffn_conv_w"] = cw[:, ::-1] if s == 1 else cw
    for k in ("ssm_lambda_re", "ssm_lambda_im", "ssm_log_dt", "ssm_b_re", "ssm_b_im", "ssm_c_re", "ssm_c_im"):
        a = g(k)
        m[k] = a[:, ::-1] if s == 1 else a
    ws = g("gmlp_w_s")
    bs = g("gmlp_b_s")
    if s == 1:
        ws = ws[:, :, ::-1, ::-1]
        bs = bs[:, :, ::-1]
    m["gmlp_wsT"] = ws.transpose(0, 1, 3, 2)
    m["gmlp_b_s"] = bs
    return {k: np.ascontiguousarray(v, dtype=np.float32) for k, v in m.items()}


def run_launch(build_fn, ins, outs, core_maps):
    nc = bass.Bass("TRN2", target_bir_lowering=False)
    R = Reg(nc, ins=ins, outs=outs, per_layer=True)
    P = Prog(nc)
    Ck = Ctx(P)
    K = build_consts(P, Ck)
    build_fn(P, R, K)
    P.finish()
    in_maps = [{k: m[k] for k in ins} for m in core_maps]
    res = run_bass_kernel_spmd(nc, in_maps, core_ids=list(range(len(core_maps))))
    return res.results


W_A = ["w_inx", "g_mix", "gmlp_ln_g", "gmlp_ln_b", "gmlp_wsT", "gmlp_b_s"]
W_B = ["ssm_lambda_re", "ssm_lambda_im", "ssm_log_dt", "ssm_b_re", "ssm_b_im", "ssm_c_re", "ssm_c_im"]
W_B2 = ["ssm_d", "ssm_w_glu", "ssm_b_glu"]


def kernel_unfused(inputs):
    import ml_dtypes
    bf = ml_dtypes.bfloat16
    pq, pk = _qsw_perm()
    st = []
    for (b, s) in CORES:
        m = prep_core(inputs, {}, b, s)
        st.append({"x": m["x_in"], "rowcol": m["rowcol"], "cvec": m["cvec"]})
    for l in range(DEPTH):
        w_in = np.asarray(inputs["w_in"])[l:l + 1]
        q = w_in[:, :, 0:1024]
        k = w_in[:, :, 1024:1152]
        w_inx = np.ascontiguousarray(np.concatenate([q, q[:, :, pq], k, k[:, :, pk], w_in[:, :, 1152:]], axis=2), dtype=np.float32)
        lw = [prep_core_layer(inputs, l, s) for s in range(2)]
        maps = []
        for ci, (b, s) in enumerate(CORES):
            m = dict(lw[s])
            m["w_inx"] = w_inx
            m.update({"x_l": st[ci]["x"], "rowcol": st[ci]["rowcol"], "cvec": st[ci]["cvec"]})
            maps.append(m)
        ins1 = ["cvec", "w_ada", "b_ada", "rowcol", "x_l"] + W_A + W_B
        outs1 = ["mod", "qT", "kT", "v", "uT", "oT_g", "ssm_send", "hinit0"]

        def b1(P, R, K, l=l):
            phase_mod(P, R, [l])
            phase_rope(P, R)
            phase_A(P, R, K, l, "x_l")
            phase_B(P, R, K, l, "a")
        r1 = run_launch(b1, ins1, outs1, maps)
        for ci in range(8):
            pr = ci ^ 1
            m = maps[ci]
            for kk in ("mod", "qT", "uT", "oT_g", "hinit0"):
                m[kk] = r1[ci][kk]
            kT = np.array(r1[ci]["kT"])
            kT[:, NTOK:] = np.asarray(r1[pr]["kT"])[:, NTOK - 128:NTOK]
            vv = np.array(r1[ci]["v"])
            vv[NTOK:] = np.asarray(r1[pr]["v"])[NTOK - 128:NTOK]
            m["kT"], m["v"] = kT, vv
            m["ssm_recv"] = np.asarray(r1[pr]["ssm_send"])
        ins2 = ["mod", "x_l", "qT", "kT", "v", "uT", "oT_g", "hinit0", "ssm_recv", "attn_sink", "w_out"] + W_B + W_B2
        outs2 = ["x_mid"]

        def b2(P, R, K, l=l):
            phase_B(P, R, K, l, "b")
            phase_C(P, R, K, l)
            phase_D(P, R, K, l, "x_l", "x_mid")
        r2 = run_launch(b2, ins2, outs2, maps)
        for ci in range(8):
            pr = ci ^ 1
            maps[ci]["x_mid"] = r2[ci]["x_mid"]
            maps[ci]["xb_recv"] = np.ascontiguousarray(np.asarray(r2[pr]["x_mid"])[NTOK - 1:NTOK])
        ins3 = ["mod", "x_mid", "xb_recv", "g_ffn", "ffn_w_up", "ffn_w_down", "ffn_conv_w", "ffn_conv_b"]
        last = l == DEPTH - 1
        outs3 = ["y_out"] if last else ["x_next"]
        if last:
            ins3 = ins3 + ["g_final"]
            for m in maps:
                m["g_final"] = np.ascontiguousarray(inputs["g_final"], dtype=np.float32)

        def b3(P, R, K, l=l, last=last):
            phase_E(P, R, K, l, "x_mid", "x_next")
            if last:
                phase_F(P, R, K, "x_next")
        r3 = run_launch(b3, ins3, outs3, maps)
        if not last:
            for ci in range(8):
                st[ci]["x"] = r3[ci]["x_next"]
    out = np.zeros((4, 4096, D), np.float32)
    for ci, (b, s) in enumerate(CORES):
        y = np.asarray(r3[ci]["y_out"])
        if s == 0:
            out[b, :NLAT] = y
        else:
            out[b, NLAT:] = y[::-1]
    return out


PAIRS = [[0, 1], [2, 3], [4, 5], [6, 7]]


def load_sel(P, C, R):
    psel = R.get("psel", [2])
    SEL = C.sb([128, 2], F32, "SEL")
    t = Tok()
    P.dma("sp", SEL[:], psel.partition_broadcast(128), w=[t])
    return SEL, t


def phase_X1(P, R, K):
    kT = R.get("kT", [128, NTOK + 128], BF16)
    vv = R.get("v", [NTOK + 128, 128], BF16)
    send = R.get("ssm_send", [64, 2, 32])
    recv = R.get("ssm_recv", [64, 2, 32])
    s1 = R.get("send1", [128, 320])
    ag1 = R.get("ag1", [256, 320])
    C = Ctx(P)
    SEL, t_sel = load_sel(P, C, R)
    kb = C.sb([128, 256], BF16, "x1kb")
    PK = C.sb([128, 320], F32, "x1pk")
    t_kb, t_pk = Tok(), Tok()
    P.dma("sp", kb[:, 0:128], kT[:, NTOK - 128:NTOK], w=[t_kb])
    P.dma("sp", kb[:, 128:256], vv[NTOK - 128:NTOK, :], w=[t_kb])
    P.op("pool", lambda g: g.memset(PK[:, 256:320], 0.0), w=[t_pk])
    P.op("dve", lambda v: v.tensor_copy(out=PK[:, 0:256], in_=kb[:]), r=[t_kb], w=[t_pk])
    P.dma("sp", PK[0:64, 256:320], send.rearrange("p r g -> p (r g)"), r=[t_pk], w=[t_pk])
    t_s1, t_ag = Tok(), Tok()
    P.dma("sp", s1, PK[:], r=[t_pk], w=[t_s1])
    P.collective(s1, ag1, PAIRS, r=[t_s1], w=[t_ag])
    G = C.sb([128, 2, 320], F32, "x1g")
    t_G = Tok()
    P.dma("sp", G[:], ag1.rearrange("(r p) c -> p r c", p=128), r=[t_ag], w=[t_G])
    R1 = C.sb([128, 320], F32, "x1r")
    kb2 = C.sb([128, 256], BF16, "x1kb2")
    t_R1 = Tok()
    P.op("dve", lambda v: v.tensor_scalar(out=R1[:], in0=G[:, 0, :], scalar1=SEL[:, 0:1], scalar2=None, op0=ALU.mult), r=[t_G, t_sel], w=[t_R1])
    P.op("dve", lambda v: v.scalar_tensor_tensor(out=R1[:], in0=G[:, 1, :], scalar=SEL[:, 1:2], in1=R1[:], op0=ALU.mult, op1=ALU.add), r=[t_G, t_sel, t_R1], w=[t_R1])
    P.op("dve", lambda v: v.tensor_copy(out=kb2[:], in_=R1[:, 0:256]), r=[t_R1], w=[t_R1])
    P.dma("sp", kT[:, NTOK:NTOK + 128], kb2[:, 0:128], r=[t_R1], w=[R.tk("kT", "halo")])
    P.dma("sp", vv[NTOK:NTOK + 128, :], kb2[:, 128:256], r=[t_R1], w=[R.tk("v", "halo")])
    P.dma("sp", recv.rearrange("p r g -> p (r g)"), R1[0:64, 256:320], r=[t_R1], w=[R.tk("ssm_recv", 0)])
    C.close()


def phase_X2(P, R, K, xname):
    x = R.get(xname, [NTOK, D])
    xb = R.get("xb_recv", [1, D])
    s2 = R.get("send2", [1, D])
    ag2 = R.get("ag2", [2, D])
    C = Ctx(P)
    SEL, t_sel = load_sel(P, C, R)
    row = C.sb([1, D], F32, "x2row")
    t_row, t_s2, t_ag = Tok(), Tok(), Tok()
    P.dma("sp", row[:], x[NTOK - 1:NTOK, :], w=[t_row])
    P.dma("sp", s2, row[:], r=[t_row], w=[t_s2])
    P.collective(s2, ag2, PAIRS, r=[t_s2], w=[t_ag])
    G = C.sb([1, 2, D], F32, "x2g")
    t_G = Tok()
    P.dma("sp", G[:], ag2.rearrange("(o r) c -> o r c", o=1), r=[t_ag], w=[t_G])
    P.op("dve", lambda v: v.tensor_scalar(out=row[:], in0=G[:, 0, :], scalar1=SEL[0:1, 0:1], scalar2=None, op0=ALU.mult), r=[t_G, t_sel, t_row], w=[t_row])
    P.op("dve", lambda v: v.scalar_tensor_tensor(out=row[:], in0=G[:, 1, :], scalar=SEL[0:1, 1:2], in1=row[:], op0=ALU.mult, op1=ALU.add), r=[t_G, t_sel, t_row], w=[t_row])
    P.dma("sp", xb, row[:], r=[t_row], w=[R.tk("xb_recv")])
    C.close()


FUSED_INS = ["x_in", "rowcol", "cvec", "psel", "w_ada", "b_ada", "g_mix", "g_ffn", "w_inx", "w_out", "attn_sink",
             "ssm_lambda_re", "ssm_lambda_im", "ssm_log_dt", "ssm_b_re", "ssm_b_im", "ssm_c_re", "ssm_c_im", "ssm_d", "ssm_w_glu", "ssm_b_glu",
             "gmlp_ln_g", "gmlp_ln_b", "gmlp_wsT", "gmlp_b_s", "ffn_w_up", "ffn_conv_w", "ffn_conv_b", "ffn_w_down", "g_final"]


def build_fused(nlayers=DEPTH):
    nc = bass.Bass("TRN2", target_bir_lowering=False)
    R = Reg(nc, ins=FUSED_INS, outs=["y_out"], per_layer=False)
    P = Prog(nc)
    Ck = Ctx(P)
    K = build_consts(P, Ck)
    phase_mod(P, R, [0])
    phase_rope(P, R)
    xcur = "x_in"
    for l in range(nlayers):
        xnext = "x_res%d" % (l % 2)
        phase_A(P, R, K, l, xcur)
        Cq = Ctx(P)
        LLq = Cq.sb([64, 2, 64, NCH], F32, "LLq")
        S = ssm_prep(P, R, K, Cq, l)
        S.LL, S.t_Ld = LLq, [Tok(), Tok()]
        phase_B(P, R, K, l, "a", S=S)
        phase_X1(P, R, K)
        phase_B(P, R, K, l, "b", S=S)
        Cq.close()
        phase_C(P, R, K, l)
        if l + 1 < nlayers:
            Cm = Ctx(P)
            side = phase_mod_gen(P, R, [l + 1], Cm)
            next(side)
            phase_D(P, R, K, l, xcur, "x_mid", side=side)
            Cm.close()
        else:
            phase_D(P, R, K, l, xcur, "x_mid")
        phase_X2(P, R, K, "x_mid")
        phase_E(P, R, K, l, "x_mid", xnext)
        xcur = xnext
    phase_F(P, R, K, xcur)
    P.finish()
    return nc, P


def fused_inputs(inputs):
    pq, pk = _qsw_perm()
    w_in = np.asarray(inputs["w_in"])
    q = w_in[:, :, 0:1024]
    k = w_in[:, :, 1024:1152]
    w_inx = np.ascontiguousarray(np.concatenate([q, q[:, :, pq], k, k[:, :, pk], w_in[:, :, 1152:]], axis=2), dtype=np.float32)
    f32 = lambda a: np.ascontiguousarray(a, dtype=np.float32)
    shared = {"w_inx": w_inx}
    for k_ in ("w_ada", "b_ada", "g_mix", "g_ffn", "w_out", "attn_sink", "ssm_d", "ssm_w_glu", "ssm_b_glu", "gmlp_ln_g", "gmlp_ln_b",
               "ffn_w_up", "ffn_conv_b", "ffn_w_down", "g_final"):
        shared[k_] = f32(inputs[k_])
    half = []
    for s in range(2):
        m = {}
        cw = np.asarray(inputs["ffn_conv_w"])
        m["ffn_conv_w"] = f32(cw[:, ::-1] if s == 1 else cw)
        for k_ in ("ssm_lambda_re", "ssm_lambda_im", "ssm_log_dt", "ssm_b_re", "ssm_b_im", "ssm_c_re", "ssm_c_im"):
            a = np.asarray(inputs[k_])
            m[k_] = f32(a[:, ::-1] if s == 1 else a)
        ws = np.asarray(inputs["gmlp_w_s"])
        bs = np.asarray(inputs["gmlp_b_s"])
        if s == 1:
            ws = ws[:, :, ::-1, ::-1]
            bs = bs[:, :, ::-1]
        m["gmlp_wsT"] = f32(ws.transpose(0, 1, 3, 2))
        m["gmlp_b_s"] = f32(bs)
        m["psel"] = np.array([0.0, 1.0] if s == 0 else [1.0, 0.0], np.float32)
        half.append(m)
    maps = []
    for (b, s) in CORES:
        m = dict(shared)
        m.update(half[s])
        pc = prep_core(inputs, {}, b, s)
        m["x_in"], m["rowcol"], m["cvec"] = pc["x_in"], pc["rowcol"], pc["cvec"]
        maps.append({k_: m[k_] for k_ in FUSED_INS})
    return maps


def kernel(**inputs):
    nc, P = build_fused()
    maps = fused_inputs(inputs)
    res = run_bass_kernel_spmd(nc, maps, core_ids=list(range(8)))
    out = np.zeros((4, 4096, D), np.float32)
    for ci, (b, s) in enumerate(CORES):
        y = np.asarray(res.results[ci]["y_out"])
        if s == 0:
            out[b, :NLAT] = y
        else:
            out[b, NLAT:] = y[::-1]
    return out
```

```python
import numpy as np
from contextlib import ExitStack
import concourse.bass as bass
import concourse.mybir as mybir
from concourse.bass_utils import run_bass_kernel_spmd

F32 = mybir.dt.float32
BF16 = mybir.dt.bfloat16
I32 = mybir.dt.int32
AF = mybir.ActivationFunctionType
ALU = mybir.AluOpType
AX = mybir.AxisListType

D = 2048
NLAT = 2048
NCTX = 256
NTOK = NLAT + NCTX
NT = NTOK // 128
DEPTH = 4
DFF = 5632
NFF = DFF // 128
EPS = 1e-6
TWO_PI = 6.283185307179586
PI = 3.141592653589793
BLOCKS = [(0, 256)] + [(256 + 512 * i, 512) for i in range(4)]


class Tok:
    __slots__ = ("w", "r", "name")

    def __init__(self, name=""):
        self.w = None
        self.r = []
        self.name = name


class Eng:
    def __init__(self, name, h, selfsync):
        self.name = name
        self.h = h
        self.sem = None
        self.cnt = 0
        self.seen = {}
        self.selfsync = selfsync


class Prog:
    SEM_EPOCH = 12000
    NDMA = 12

    def __init__(self, nc):
        self.nc = nc
        self.stack = ExitStack()
        self.nsem = 0
        self.eng = {
            "pe": Eng("pe", nc.tensor, False),
            "act": Eng("act", nc.scalar, True),
            "dve": Eng("dve", nc.vector, True),
            "pool": Eng("pool", nc.gpsimd, True),
            "sp": Eng("sp", nc.sync, False),
        }
        for e in self.eng.values():
            self._new_sem(e)
        self.dq = {}
        for q in ("sp", "pool", "act"):
            sems = [self._sem("d%s%d" % (q, i)) for i in range(self.NDMA)]
            self.dq[q] = {"sems": sems, "cnt": [0] * self.NDMA, "i": 0}
        self.out_stamps = []

    def _sem(self, name):
        self.nsem += 1
        return self.stack.enter_context(self.nc.semaphore("%s_%d" % (name, self.nsem)))

    def _new_sem(self, e):
        e.sem = self._sem("e" + e.name)
        e.cnt = 0

    def close(self):
        self.stack.close()

    def _wait(self, e, stamp):
        sem, val = stamp
        key = id(sem)
        if e.seen.get(key, 0) >= val:
            return
        e.h.wait_ge(sem, val)
        e.seen[key] = val

    def _deps(self, e, r, w):
        for t in r:
            if t.w is not None:
                if t.w[0] is e.sem and not e.selfsync:
                    continue
                self._wait(e, t.w)
        for t in w:
            if t.w is not None:
                if not (t.w[0] is e.sem and not e.selfsync):
                    self._wait(e, t.w)
            for st in t.r:
                if st[0] is e.sem:
                    continue
                self._wait(e, st)

    def _stamp(self, st, r, w):
        for t in r:
            t.r = [s for s in t.r if s[0] is not st[0]] + [st]
        for t in w:
            t.w = st
            t.r = []

    def op(self, eng, fn, r=(), w=()):
        e = self.eng[eng]
        self._deps(e, r, w)
        ins = fn(e.h)
        if e.cnt >= self.SEM_EPOCH:
            self._new_sem(e)
        e.cnt += 1
        ins.then_inc(e.sem, 1)
        self._stamp((e.sem, e.cnt), r, w)

    def mm(self, fns, r=(), w=()):
        e = self.eng["pe"]
        self._deps(e, r, w)
        ins = None
        for fn in fns:
            ins = fn(e.h)
        if e.cnt >= self.SEM_EPOCH:
            self._new_sem(e)
        e.cnt += 1
        ins.then_inc(e.sem, 1)
        self._stamp((e.sem, e.cnt), r, w)

    def dma(self, q, out, in_, r=(), w=(), is_out=False, **kw):
        e = self.eng[q]
        dq = self.dq[q]
        slot = dq["i"] % self.NDMA
        dq["i"] += 1
        sem = dq["sems"][slot]
        prev = dq["cnt"][slot]
        if prev > 0:
            self._wait(e, (sem, prev))
        self._deps(e, r, w)
        ins = e.h.dma_start(out=out, in_=in_, **kw)
        ins.then_inc(sem, 16)
        dq["cnt"][slot] = prev + 16
        st = (sem, prev + 16)
        self._stamp(st, r, w)
        if is_out:
            self.out_stamps.append(st)

    def collective(self, ins_ap, outs_ap, groups, r=(), w=()):
        e = self.eng["pool"]
        if not hasattr(self, "cc_sem"):
            self.cc_sem = self._sem("cc")
            self.cc_cnt = 0
        self._deps(e, r, w)
        ins = self.nc.gpsimd.collective_compute("AllGather", ALU.bypass, replica_groups=groups, ins=[ins_ap], outs=[outs_ap])
        self.cc_cnt += 1
        ins.then_inc(self.cc_sem, 1)
        self._stamp((self.cc_sem, self.cc_cnt), r, w)

    def barrier(self):
        for e in self.eng.values():
            for o in self.eng.values():
                if o is e or o.cnt == 0:
                    continue
                self._wait(e, (o.sem, o.cnt))
            for q, dq in self.dq.items():
                for s_, c in zip(dq["sems"], dq["cnt"]):
                    if c > 0:
                        self._wait(e, (s_, c))
            if getattr(self, "cc_cnt", 0) > 0:
                self._wait(e, (self.cc_sem, self.cc_cnt))

    def finish(self):
        self.barrier()
        e = self.eng["sp"]
        for st in self.out_stamps:
            self._wait(e, st)
        for q, dq in self.dq.items():
            for s, c in zip(dq["sems"], dq["cnt"]):
                if c > 0:
                    self._wait(e, (s, c))


class Ctx:
    def __init__(self, P):
        self.P = P
        self.nc = P.nc
        self.stack = ExitStack()
        self.n = 0

    UID = [0]

    def sb(self, shape, dt, name="t"):
        Ctx.UID[0] += 1
        return self.stack.enter_context(self.nc.sbuf_tensor("%s_%d" % (name, Ctx.UID[0]), list(shape), dt))

    def ps(self, shape, dt, name="p"):
        Ctx.UID[0] += 1
        return self.stack.enter_context(self.nc.psum_tensor("%s_%d" % (name, Ctx.UID[0]), list(shape), dt))

    def close(self):
        self.P.barrier()
        self.stack.close()


def bcast_rows(ap1d, nrows):
    return ap1d.rearrange("(o n) -> o n", o=1).broadcast(0, nrows)


class Reg:
    def __init__(self, nc, ins=(), outs=(), per_layer=False):
        self.nc = nc
        self.nl = 1 if per_layer else DEPTH
        self.per_layer = per_layer
        self.ins = set(ins)
        self.outs = set(outs)
        self.t = {}
        self.tok = {}

    def get(self, name, shape=None, dt=F32):
        if name not in self.t:
            kind = "ExternalInput" if name in self.ins else ("ExternalOutput" if name in self.outs else "Internal")
            self.t[name] = self.nc.dram_tensor(name, list(shape), dt, kind=kind).ap()
            self.tok[name] = Tok(name)
        return self.t[name]


def build_consts(P, C):
    nc = P.nc
    K = {}
    idf = C.sb([128, 128], F32, "idf")
    idb = C.sb([128, 128], BF16, "idb")
    io = C.sb([128, 128], I32, "io")
    t_id = Tok("ident")
    P.op("pool", lambda g: g.iota(io[:], [[1, 128]], base=0, channel_multiplier=-1), w=[t_id])
    P.op("dve", lambda v: v.tensor_copy(out=idf[:], in_=io[:]), r=[t_id], w=[t_id])
    P.op("dve", lambda v: v.tensor_scalar(out=idf[:], in0=idf[:], scalar1=0.0, scalar2=None, op0=ALU.is_equal), r=[t_id], w=[t_id])
    P.op("dve", lambda v: v.tensor_copy(out=idb[:], in_=idf[:]), r=[t_id], w=[t_id])
    K["idf"] = idf
    K["idb"] = idb
    K["tok"] = t_id
    return K


def reg_li(R, l):
    return 0 if R.per_layer else l


Reg.li = reg_li


def reg_tk(R, name, i=0):
    key = (name, i)
    if key not in R.tok:
        R.tok[key] = Tok("%s[%s]" % (name, i))
    return R.tok[key]


Reg.tk = reg_tk


def phase_mod(P, R, layers):
    C = Ctx(P)
    for _ in phase_mod_gen(P, R, layers, C):
        pass
    C.close()


def phase_mod_gen(P, R, layers, C):
    cvec = R.get("cvec", [2, D])
    w_ada = R.get("w_ada", [R.nl, D, 6 * D])
    b_ada = R.get("b_ada", [R.nl, 6 * D])
    mod = R.get("mod", [DEPTH, 2, 6 * D])
    cc = C.sb([128, 16, 2], F32, "cc")
    t_cc = Tok()
    for j in range(2):
        P.dma("sp", cc[:, :, j], cvec[j].rearrange("(k p) -> p k", p=128), w=[t_cc], allow_slow_non_contiguous=True)
    act = C.sb([128, 16, 2], F32, "act")
    P.op("act", lambda a: a.activation(out=act[:], in_=cc[:], func=AF.Silu), r=[t_cc], w=[t_cc])
    NB = 2
    wb = [C.sb([128, 16, 512], BF16, "wada") for _ in range(NB)]
    actb = C.sb([128, 16, 2], BF16, "actb")
    P.op("dve", lambda v: v.tensor_copy(out=actb[:], in_=act[:]), r=[t_cc], w=[t_cc])
    t_wb = [Tok() for _ in range(NB)]
    bb = [C.sb([2, 512], F32, "bada") for _ in range(NB)]
    ob = [C.sb([2, 512], F32, "oada") for _ in range(NB)]
    t_ob = [Tok() for _ in range(NB)]
    ps = [C.ps([2, 512], F32, "psada") for _ in range(NB)]
    t_ps = [Tok() for _ in range(NB)]
    it = 0
    for l in layers:
        wv = w_ada[R.li(l)].rearrange("(k p) c -> p k c", p=128)
        for cb in range(24):
            i = it % NB
            it += 1
            cs = slice(cb * 512, (cb + 1) * 512)
            P.dma("pool", wb[i][:], wv[:, :, cs], w=[t_wb[i]])
            P.dma("sp", bb[i][:], b_ada[R.li(l), cs].partition_broadcast(2), w=[t_wb[i]])
            P.mm([(lambda pe, k=k, i=i: pe.matmul(ps[i][:], actb[:, k, :], wb[i][:, k, :], start=(k == 0), stop=(k == 15)))
                  for k in range(16)], r=[t_wb[i], t_cc], w=[t_ps[i]])
            P.op("dve", lambda v, i=i: v.tensor_tensor(out=ob[i][:], in0=ps[i][:], in1=bb[i][:], op=ALU.add),
                 r=[t_ps[i], t_wb[i]], w=[t_ob[i]])
            P.dma("sp", mod[l, :, cs], ob[i][:], r=[t_ob[i]], w=[R.tk("mod", l)])
            yield


NQ = 1024
OFFX_Q, OFFX_QS, OFFX_K, OFFX_KS, OFFX_V, OFFX_U, OFFX_GU, OFFX_GV = 0, 1024, 2048, 2176, 2304, 2432, 2944, 3456
NINX = 3968


def range_reduce_sin(P, C, out, ang, shape, tk, extra=0.0):
    ti = C.sb(shape, I32, "rr_i")
    tf = C.sb(shape, F32, "rr_f")
    tr = C.sb(shape, F32, "rr_r")
    t = Tok()
    P.op("dve", lambda v: v.tensor_scalar(out=ti[:], in0=ang, scalar1=extra, scalar2=1.0 / TWO_PI, op0=ALU.add, op1=ALU.mult), r=[tk], w=[t])
    P.op("dve", lambda v: v.tensor_copy(out=tf[:], in_=ti[:]), r=[t], w=[t])
    P.op("dve", lambda v: v.scalar_tensor_tensor(out=tr[:], in0=tf[:], scalar=-TWO_PI, in1=ang, op0=ALU.mult, op1=ALU.add), r=[t, tk], w=[t])
    if extra != 0.0:
        P.op("dve", lambda v: v.tensor_scalar(out=tr[:], in0=tr[:], scalar1=extra, scalar2=None, op0=ALU.add), r=[t], w=[t])
    P.op("dve", lambda v: v.tensor_scalar(out=tf[:], in0=tr[:], scalar1=PI, scalar2=-TWO_PI, op0=ALU.is_gt, op1=ALU.mult), r=[t], w=[t])
    P.op("dve", lambda v: v.tensor_tensor(out=tr[:], in0=tr[:], in1=tf[:], op=ALU.add), r=[t], w=[t])
    P.op("dve", lambda v: v.tensor_scalar(out=tf[:], in0=tr[:], scalar1=-PI, scalar2=TWO_PI, op0=ALU.is_lt, op1=ALU.mult), r=[t], w=[t])
    P.op("dve", lambda v: v.tensor_tensor(out=tr[:], in0=tr[:], in1=tf[:], op=ALU.add), r=[t], w=[t])
    P.op("act", lambda a: a.activation(out=out, in_=tr[:], func=AF.Sin), r=[t], w=[tk])


def phase_rope(P, R):
    C = Ctx(P)
    rowcol = R.get("rowcol", [2, NLAT])
    ropeT = R.get("ropeT", [2, 128, NTOK])
    pos = C.sb([128, NLAT], F32, "pos")
    t = Tok()
    for base in (0, 64):
        P.dma("sp", pos[base:base + 32, :], rowcol[0].partition_broadcast(32), w=[t])
        P.dma("sp", pos[base + 32:base + 64, :], rowcol[1].partition_broadcast(32), w=[t])
    pi_ = C.sb([128, 1], I32, "pidx")
    pf = C.sb([128, 4], F32, "pf")
    tp = Tok()
    P.op("pool", lambda g: g.iota(pi_[:], [[0, 1]], base=0, channel_multiplier=1), w=[tp])
    P.op("dve", lambda v: v.tensor_copy(out=pf[:, 0:1], in_=pi_[:]), r=[tp], w=[tp])
    pj = C.sb([128, 1], I32, "pj")
    P.op("dve", lambda v: v.tensor_scalar(out=pj[:], in0=pi_[:], scalar1=15, scalar2=None, op0=ALU.bitwise_and), r=[tp], w=[tp])
    P.op("dve", lambda v: v.tensor_copy(out=pf[:, 1:2], in_=pj[:]), r=[tp], w=[tp])
    P.op("act", lambda a: a.activation(out=pf[:, 2:3], in_=pf[:, 1:2], func=AF.Exp, scale=-float(np.log(10000.0)) / 16.0), r=[tp], w=[tp])
    P.op("dve", lambda v: v.tensor_scalar(out=pj[:], in0=pi_[:], scalar1=16, scalar2=None, op0=ALU.bitwise_and), r=[tp], w=[tp])
    P.op("dve", lambda v: v.tensor_copy(out=pf[:, 3:4], in_=pj[:]), r=[tp], w=[tp])
    P.op("dve", lambda v: v.tensor_scalar(out=pf[:, 3:4], in0=pf[:, 3:4], scalar1=1.0 / 8.0, scalar2=-1.0, op0=ALU.mult, op1=ALU.add), r=[tp], w=[tp])
    ang = C.sb([128, NLAT], F32, "ang")
    P.op("dve", lambda v: v.tensor_scalar(out=ang[:], in0=pos[:], scalar1=pf[:, 2:3], scalar2=None, op0=ALU.mult), r=[t, tp], w=[t])
    cs = C.sb([128, NTOK], F32, "cs")
    sn = C.sb([128, NTOK], F32, "sn")
    tc_, ts_ = Tok(), Tok()
    P.op("pool", lambda g: g.memset(cs[:, 0:NCTX], 1.0), w=[tc_])
    P.op("pool", lambda g: g.memset(sn[:, 0:NCTX], 0.0), w=[ts_])
    C2 = Ctx(P)
    range_reduce_sin(P, C2, sn[:, NCTX:], ang[:], [128, NLAT], t)
    P.op("dve", lambda v: v.tensor_scalar(out=sn[:, NCTX:], in0=sn[:, NCTX:], scalar1=pf[:, 3:4], scalar2=None, op0=ALU.mult), r=[t, tp], w=[t])
    P.dma("sp", ropeT[1], sn[:], r=[t, ts_], w=[R.tk("ropeT", 1)])
    t2 = Tok()
    t2.w = t.w
    range_reduce_sin(P, C2, cs[:, NCTX:], ang[:], [128, NLAT], t, extra=PI / 2)
    P.dma("sp", ropeT[0], cs[:], r=[t, tc_], w=[R.tk("ropeT", 0)])
    C2.close()
    C.close()


def load_bcast_mod(P, C, R, l, r, sec, name):
    mod = R.get("mod", [DEPTH, 2, 6 * D])
    t = C.sb([128, D], F32, name)
    tk = Tok(name)
    P.dma("sp", t[:], mod[l, r, sec * D:(sec + 1) * D].partition_broadcast(128), r=[R.tk("mod", l)], w=[tk])
    return t, tk


def norm_mod_transpose(P, R, K, C, l, xname, gname, sec_shift, sec_scale, hT_all, dst_fn=None, tok_fn=None, after=None, extra_row=None):
    x = R.get(xname, [NTOK, D])
    gvec = R.get(gname, [R.nl, D])
    C1 = Ctx(P)
    gb = C1.sb([128, D], F32, "gb")
    t_gb = Tok()
    P.dma("sp", gb[:], gvec[R.li(l)].partition_broadcast(128), w=[t_gb])
    GS, SH, tGS, tSH = [], [], [], []
    for r in range(2):
        sc, tsc = load_bcast_mod(P, C1, R, l, r, sec_scale, "gs%d" % r)
        sh, tsh = load_bcast_mod(P, C1, R, l, r, sec_shift, "sh%d" % r)
        P.op("dve", lambda v, sc=sc: v.scalar_tensor_tensor(out=sc[:], in0=sc[:], scalar=1.0, in1=gb[:], op0=ALU.add, op1=ALU.mult),
             r=[tsc, t_gb], w=[tsc])
        GS.append(sc); SH.append(sh); tGS.append(tsc); tSH.append(tsh)
    NB = 3
    xt = [C1.sb([128, D], F32, "xt") for _ in range(NB)]
    t_xt = [Tok() for _ in range(NB)]
    junk = C1.sb([128, D], BF16, "junk")
    t_junk = Tok()
    st = [C1.sb([128, 4], F32, "st") for _ in range(NB)]
    t_st = [Tok() for _ in range(NB)]
    t1 = [C1.sb([128, D], F32, "t1") for _ in range(NB)]
    t_t1 = [Tok() for _ in range(NB)]
    hb = [C1.sb([128, D], BF16, "hb") for _ in range(NB)]
    t_hb = [Tok() for _ in range(NB)]
    pT = [C1.ps([128, 1024], BF16, "pT") for _ in range(4)]
    t_pT = [Tok() for _ in range(4)]
    t_h = [(Tok("hTa%d" % i), Tok("hTb%d" % i)) for i in range(NT + 1)]
    if tok_fn is not None:
        t_h = [tok_fn(i) for i in range(NT + 1)]
    ntile = NT + (1 if extra_row is not None else 0)

    def stage_a(ti):
        i = ti % NB
        r = 1 if ti < NCTX // 128 else 0
        if ti < NT:
            P.dma("sp", xt[i][:], x[ti * 128:(ti + 1) * 128, :], r=[R.tk(xname, ti)], w=[t_xt[i]])
        else:
            P.op("pool", lambda g, i=i: g.memset(xt[i][:], 1.0), w=[t_xt[i]])
            P.dma("sp", xt[i][0:1, :], extra_row[0], r=[extra_row[1]], w=[t_xt[i]])
        P.op("act", lambda a, i=i: a.activation(out=junk[:], in_=xt[i][:], func=AF.Square, accum_out=st[i][:, 0:1]),
             r=[t_xt[i]], w=[t_junk, t_st[i]])
        P.op("act", lambda a, i=i: a.activation(out=st[i][:, 1:2], in_=st[i][:, 0:1], func=AF.Sqrt, scale=1.0 / D, bias=EPS),
             r=[t_st[i]], w=[t_st[i]])
        P.op("dve", lambda v, i=i: v.reciprocal(out=st[i][:, 2:3], in_=st[i][:, 1:2]), r=[t_st[i]], w=[t_st[i]])
        P.op("pool", lambda g, i=i, r=r: g.tensor_tensor(out=t1[i][:], in0=xt[i][:], in1=GS[r][:], op=ALU.mult),
             r=[t_xt[i], tGS[r]], w=[t_t1[i]])
        P.op("dve", lambda v, i=i, r=r: v.scalar_tensor_tensor(out=hb[i][:], in0=t1[i][:], scalar=st[i][:, 2:3], in1=SH[r][:],
                                                              op0=ALU.mult, op1=ALU.add),
             r=[t_t1[i], t_st[i], tSH[r]], w=[t_hb[i]])
        for h in range(2):
            pb_ = (ti % 2) * 2 + h
            P.mm([(lambda pe, k=k, pb_=pb_, i=i: pe.transpose(pT[pb_][:, (k % 8) * 128:(k % 8 + 1) * 128], hb[i][:, k * 128:(k + 1) * 128], K["idb"][:]))
                  for k in range(8 * h, 8 * h + 8)], r=[t_hb[i], K["tok"]], w=[t_pT[pb_]])

    def stage_b(ti):
        for h in range(2):
            pb_ = (ti % 2) * 2 + h
            dst = hT_all[:, 8 * h:8 * h + 8, ti * 128:(ti + 1) * 128] if dst_fn is None else dst_fn(ti, h)
            src = pT[pb_][:].rearrange("p (k t) -> p k t", k=8)
            if h == 0:
                P.op("act", lambda a, dst=dst, src=src: a.copy(out=dst, in_=src), r=[t_pT[pb_]], w=[t_h[ti][h]])
            else:
                P.op("dve", lambda v, dst=dst, src=src: v.tensor_copy(out=dst, in_=src), r=[t_pT[pb_]], w=[t_h[ti][h]])
        if after is not None:
            after(ti, t_h[ti])

    for ti in range(ntile + 1):
        if ti < ntile:
            stage_a(ti)
        if ti >= 1:
            stage_b(ti - 1)
    C1.close()
    return t_h


def toks_for(t_h, t0, n):
    out = []
    for ti in range(t0 // 128, (t0 + n + 127) // 128):
        out += list(t_h[ti])
    return out


def load_w_cols(P, C, wsrc, col_ranges, buf, tk, q="pool"):
    wv = wsrc.rearrange("(k p) c -> p k c", p=128)
    o = 0
    for (c0, n) in col_ranges:
        P.dma(q, buf[:, :, o:o + n], wv[:, :, c0:c0 + n], w=[tk])
        o += n


def phase_A(P, R, K, l, xname, ctx_full=True, parts="abc"):
    w_in = R.get("w_inx", [R.nl, D, NINX])[R.li(l):R.li(l) + 1]
    qT = R.get("qT", [NQ, NTOK], BF16)
    kT = R.get("kT", [128, NTOK + 128], BF16)
    vv = R.get("v", [NTOK + 128, 128], BF16)
    uT = R.get("uT", [512, NTOK])
    oT = R.get("oT_g", [512, NTOK], BF16)
    ropeT = R.get("ropeT", [2, 128, NTOK])
    C = Ctx(P)
    hT = C.sb([128, 16, NTOK], BF16, "hT_all")
    t_h = norm_mod_transpose(P, R, K, C, l, xname, "g_mix", 0, 1, hT)

    if "a" not in parts:
        C.close(); return
    Ca = Ctx(P)
    cs = Ca.sb([128, NTOK], F32, "cs")
    sn = Ca.sb([128, NTOK], F32, "sn")
    t_cs = Tok()
    P.dma("sp", cs[:], ropeT[0], r=[R.tk("ropeT", 0)], w=[t_cs])
    P.dma("sp", sn[:], ropeT[1], r=[R.tk("ropeT", 1)], w=[t_cs])
    NB = 2
    wq = [Ca.sb([128, 16, 1024], BF16, "wq") for _ in range(NB)]
    t_wq = [Tok() for _ in range(NB)]
    psA = [Ca.ps([128, 512], F32, "psA") for _ in range(4)]
    t_psA = [Tok() for _ in range(4)]
    r1 = [Ca.sb([128, 512], F32, "r1") for _ in range(2)]
    r2 = [Ca.sb([128, 512], F32, "r2") for _ in range(2)]
    ro = [Ca.sb([128, 512], BF16, "ro") for _ in range(2)]
    t_r1 = [Tok() for _ in range(2)]
    t_r2 = [Tok() for _ in range(2)]
    t_ro = [Tok() for _ in range(2)]
    it = 0
    wvA = w_in[0].rearrange("(k p) c -> p k c", p=128)
    for j in range(9):
        wi = (j // 4) % NB
        fo = (j % 4) * 128
        if j % 4 == 0 and j < 8:
            P.dma("pool", wq[wi][:, :, 0:512], wvA[:, :, OFFX_Q + j * 128:OFFX_Q + j * 128 + 512], w=[t_wq[wi]])
            P.dma("pool", wq[wi][:, :, 512:1024], wvA[:, :, OFFX_QS + j * 128:OFFX_QS + j * 128 + 512], w=[t_wq[wi]])
        elif j == 8:
            P.dma("pool", wq[wi][:, :, 0:128], wvA[:, :, OFFX_K:OFFX_K + 128], w=[t_wq[wi]])
            P.dma("pool", wq[wi][:, :, 512:640], wvA[:, :, OFFX_KS:OFFX_KS + 128], w=[t_wq[wi]])
        for (t0, n) in BLOCKS:
            i = it % 2
            it += 1
            pa, pb = psA[2 * i], psA[2 * i + 1]
            hs = toks_for(t_h, t0, n)
            P.mm([(lambda pe, k=k, pa=pa, wi=wi, t0=t0, n=n, fo=fo: pe.matmul(pa[:, :n], wq[wi][:, k, fo:fo + 128], hT[:, k, t0:t0 + n], start=(k == 0), stop=(k == 15)))
                  for k in range(16)], r=[t_wq[wi]] + hs, w=[t_psA[2 * i]])
            P.mm([(lambda pe, k=k, pb=pb, wi=wi, t0=t0, n=n, fo=fo: pe.matmul(pb[:, :n], wq[wi][:, k, 512 + fo:512 + fo + 128], hT[:, k, t0:t0 + n], start=(k == 0), stop=(k == 15)))
                  for k in range(16)], r=[t_wq[wi]] + hs, w=[t_psA[2 * i + 1]])
            P.op("dve", lambda v, i=i, pa=pa, t0=t0, n=n: v.tensor_tensor(out=r1[i][:, :n], in0=pa[:, :n], in1=cs[:, t0:t0 + n], op=ALU.mult),
                 r=[t_psA[2 * i], t_cs], w=[t_r1[i]])
            P.op("dve", lambda v, i=i, pb=pb, t0=t0, n=n: v.tensor_tensor(out=r2[i][:, :n], in0=pb[:, :n], in1=sn[:, t0:t0 + n], op=ALU.mult),
                 r=[t_psA[2 * i + 1], t_cs], w=[t_r2[i]])
            P.op("pool", lambda g, i=i, n=n: g.tensor_tensor(out=ro[i][:, :n], in0=r1[i][:, :n], in1=r2[i][:, :n], op=ALU.add),
                 r=[t_r1[i], t_r2[i]], w=[t_ro[i]])
            if j < 8:
                P.dma("sp", qT[j * 128:(j + 1) * 128, t0:t0 + n], ro[i][:, :n], r=[t_ro[i]], w=[R.tk("qT", (j, t0))])
            else:
                P.dma("sp", kT[:, t0:t0 + n], ro[i][:, :n], r=[t_ro[i]], w=[R.tk("kT", t0)])
    Ca.close()

    if "b" not in parts:
        C.close(); return
    Cb = Ctx(P)
    wuA = Cb.sb([128, 16, 512], BF16, "wu")
    t_wuA = Tok()
    load_w_cols(P, Cb, w_in[0], [(OFFX_U, 512)], wuA, t_wuA)
    psB = [Cb.ps([128, 512], F32, "psB") for _ in range(2)]
    t_psB = [Tok() for _ in range(2)]
    ub = [Cb.sb([128, 512], F32, "ub") for _ in range(2)]
    t_ub = [Tok() for _ in range(2)]
    it = 0
    for j in range(4):
        for (t0, n) in BLOCKS:
            i = it % 2
            it += 1
            hs = toks_for(t_h, t0, n)
            P.mm([(lambda pe, k=k, i=i, j=j, t0=t0, n=n: pe.matmul(psB[i][:, :n], wuA[:, k, j * 128:(j + 1) * 128], hT[:, k, t0:t0 + n], start=(k == 0), stop=(k == 15)))
                  for k in range(16)], r=[t_wuA] + hs, w=[t_psB[i]])
            P.op("act", lambda a, i=i, n=n: a.copy(out=ub[i][:, :n], in_=psB[i][:, :n]), r=[t_psB[i]], w=[t_ub[i]])
            P.dma("sp", uT[j * 128:(j + 1) * 128, t0:t0 + n], ub[i][:, :n], r=[t_ub[i]], w=[R.tk("uT", (j, t0))])
    wv_ = Cb.sb([128, 16, 128], BF16, "wv")
    t_wv = Tok()
    load_w_cols(P, Cb, w_in[0], [(OFFX_V, 128)], wv_, t_wv)
    vb = [Cb.sb([128, 128], BF16, "vb") for _ in range(2)]
    t_vb = [Tok() for _ in range(2)]
    for ti in range(NT):
        i = ti % 2
        P.mm([(lambda pe, k=k, i=i, ti=ti: pe.matmul(psB[i][:, :128], hT[:, k, ti * 128:(ti + 1) * 128], wv_[:, k, :], start=(k == 0), stop=(k == 15)))
              for k in range(16)], r=[t_wv] + list(t_h[ti]), w=[t_psB[i]])
        P.op("act", lambda a, i=i: a.copy(out=vb[i][:], in_=psB[i][:, :128]), r=[t_psB[i]], w=[t_vb[i]])
        P.dma("sp", vv[ti * 128:(ti + 1) * 128, :], vb[i][:], r=[t_vb[i]], w=[R.tk("v", ti)])
    Cb.close()

    if "c" not in parts:
        C.close(); return
    Cc = Ctx(P)
    lng = R.get("gmlp_ln_g", [R.nl, 512])[R.li(l):R.li(l) + 1]
    lnb = R.get("gmlp_ln_b", [R.nl, 512])[R.li(l):R.li(l) + 1]
    wsT = R.get("gmlp_wsT", [R.nl, 4, 128, 128])[R.li(l):R.li(l) + 1]
    bs = R.get("gmlp_b_s", [R.nl, 4, 128])[R.li(l):R.li(l) + 1]
    tiles = list(range(NT)) if ctx_full else list(range(NCTX // 128, NT))
    blocks = BLOCKS if ctx_full else BLOCKS[1:]
    guT = Cc.sb([128, 4, NTOK], F32, "guT")
    t_gu = {}
    wgu = Cc.sb([128, 16, 512], BF16, "wgu")
    t_wgu = Tok()
    load_w_cols(P, Cc, w_in[0], [(OFFX_GU, 512)], wgu, t_wgu)
    psC = [Cc.ps([128, 512], F32, "psC") for _ in range(2)]
    t_psC = [Tok() for _ in range(2)]
    it = 0
    for j in range(4):
        for (t0, n) in blocks:
            i = it % 2
            it += 1
            hs = toks_for(t_h, t0, n)
            P.mm([(lambda pe, k=k, i=i, j=j, t0=t0, n=n: pe.matmul(psC[i][:, :n], wgu[:, k, j * 128:(j + 1) * 128], hT[:, k, t0:t0 + n], start=(k == 0), stop=(k == 15)))
                  for k in range(16)], r=[t_wgu] + hs, w=[t_psC[i]])
            t_gu[(j, t0)] = Tok()
            P.op("act", lambda a, i=i, j=j, t0=t0, n=n: a.activation(out=guT[:, j, t0:t0 + n], in_=psC[i][:, :n], func=AF.Gelu_apprx_tanh),
                 r=[t_psC[i]], w=[t_gu[(j, t0)]])
    wgv = Cc.sb([128, 16, 512], BF16, "wgv")
    t_wgv = Tok()
    load_w_cols(P, Cc, w_in[0], [(OFFX_GV, 512)], wgv, t_wgv)
    LNG = Cc.sb([128, 512], F32, "LNG")
    LNB = Cc.sb([128, 512], F32, "LNB")
    BS = Cc.sb([128, 4, 128], F32, "BS")
    WS = Cc.sb([128, 4, 128], BF16, "WS")
    t_par = Tok()
    P.dma("sp", LNG[:], lng[0].partition_broadcast(128), w=[t_par])
    P.dma("sp", LNB[:], lnb[0].partition_broadcast(128), w=[t_par])
    P.dma("sp", BS[:].rearrange("p g i -> p (g i)"), bs[0].rearrange("g i -> (g i)").partition_broadcast(128), w=[t_par])
    P.dma("pool", WS[:], wsT[0].rearrange("g j i -> j g i"), w=[t_par])
    gg = [Cc.sb([128, 512], F32, "gg") for _ in range(2)]
    t_gg = [Tok() for _ in range(2)]
    sq = Cc.sb([128, 512], BF16, "sq")
    t_sq = Tok()
    stt = [Cc.sb([128, 8], F32, "stt") for _ in range(2)]
    t_stt = [Tok() for _ in range(2)]
    vn = [Cc.sb([128, 512], BF16, "vn") for _ in range(2)]
    t_vn = [Tok() for _ in range(2)]
    pm = [Cc.ps([128, 512], F32, "pm") for _ in range(2)]
    t_pm = [Tok() for _ in range(2)]
    mm_ = [Cc.sb([128, 512], F32, "mm") for _ in range(2)]
    t_mm = [Tok() for _ in range(2)]
    og = [Cc.sb([128, 4, 128], BF16, "og") for _ in range(2)]
    t_og = [Tok() for _ in range(2)]
    def g_stage1(n_, ti):
        i = n_ % 2
        s = stt[i]
        P.mm([(lambda pe, k=k, i=i, ti=ti: pe.matmul(psC[i][:], hT[:, k, ti * 128:(ti + 1) * 128], wgv[:, k, :], start=(k == 0), stop=(k == 15)))
              for k in range(16)], r=[t_wgv] + list(t_h[ti]), w=[t_psC[i]])
        P.op("act", lambda a, i=i, s=s: a.activation(out=gg[i][:], in_=psC[i][:], func=AF.Gelu_apprx_tanh, accum_out=s[:, 0:1]),
             r=[t_psC[i]], w=[t_gg[i], t_stt[i]])
        P.op("act", lambda a, i=i, s=s: a.activation(out=sq[:], in_=gg[i][:], func=AF.Square, accum_out=s[:, 1:2]),
             r=[t_gg[i]], w=[t_sq, t_stt[i]])
        P.op("dve", lambda v, s=s: v.tensor_scalar(out=s[:, 2:3], in0=s[:, 0:1], scalar1=1.0 / 512, scalar2=None, op0=ALU.mult), r=[t_stt[i]], w=[t_stt[i]])
        P.op("dve", lambda v, s=s: v.tensor_tensor(out=s[:, 3:4], in0=s[:, 2:3], in1=s[:, 2:3], op=ALU.mult), r=[t_stt[i]], w=[t_stt[i]])
        P.op("dve", lambda v, s=s: v.scalar_tensor_tensor(out=s[:, 4:5], in0=s[:, 1:2], scalar=1.0 / 512, in1=s[:, 3:4], op0=ALU.mult, op1=ALU.subtract),
             r=[t_stt[i]], w=[t_stt[i]])
        P.op("act", lambda a, s=s: a.activation(out=s[:, 5:6], in_=s[:, 4:5], func=AF.Sqrt, bias=EPS), r=[t_stt[i]], w=[t_stt[i]])
        P.op("dve", lambda v, s=s: v.reciprocal(out=s[:, 6:7], in_=s[:, 5:6]), r=[t_stt[i]], w=[t_stt[i]])
        P.op("dve", lambda v, i=i, s=s: v.tensor_scalar(out=gg[i][:], in0=gg[i][:], scalar1=s[:, 2:3], scalar2=s[:, 6:7], op0=ALU.subtract, op1=ALU.mult),
             r=[t_stt[i], t_gg[i]], w=[t_gg[i]])
        P.op("pool", lambda g, i=i: g.tensor_tensor(out=gg[i][:], in0=gg[i][:], in1=LNG[:], op=ALU.mult), r=[t_gg[i], t_par], w=[t_gg[i]])
        P.op("pool", lambda g, i=i: g.tensor_tensor(out=vn[i][:], in0=gg[i][:], in1=LNB[:], op=ALU.add), r=[t_gg[i], t_par], w=[t_vn[i]])
        P.mm([(lambda pe, g=g, i=i: pe.matmul(pm[i][:, g * 128:(g + 1) * 128], vn[i][:, g * 128:(g + 1) * 128], WS[:, g, :], start=True, stop=True))
              for g in range(4)], r=[t_vn[i], t_par], w=[t_pm[i]])

    def g_stage2(n_, ti):
        i = n_ % 2
        P.op("dve", lambda v, i=i: v.tensor_tensor(out=mm_[i][:], in0=pm[i][:], in1=BS[:].rearrange("p g i -> p (g i)"), op=ALU.add),
             r=[t_pm[i], t_par], w=[t_mm[i]])
        blk0 = [b for b in BLOCKS if b[0] <= ti * 128 < b[0] + b[1]][0][0]
        P.op("pool", lambda g, i=i, ti=ti: g.tensor_tensor(out=og[i][:], in0=mm_[i][:].rearrange("p (g i) -> p g i", g=4),
                                                           in1=guT[:, :, ti * 128:(ti + 1) * 128], op=ALU.mult),
             r=[t_mm[i]] + [t_gu[(j, blk0)] for j in range(4)], w=[t_og[i]])
        P.dma("sp", oT[:, ti * 128:(ti + 1) * 128].rearrange("(g p) t -> p g t", p=128), og[i][:], r=[t_og[i]], w=[R.tk("oT_g", ti)])

    for n_ in range(len(tiles) + 1):
        if n_ < len(tiles):
            g_stage1(n_, tiles[n_])
        if n_ >= 1:
            g_stage2(n_ - 1, tiles[n_ - 1])
    Cc.close()
    C.close()


def _qsw_perm():
    def perm(nheads):
        idx = np.arange(nheads * 64).reshape(nheads, 2, 2, 16)
        return idx[:, :, ::-1, :].reshape(-1)
    return perm(16), perm(2)


def prep_shared(inputs):
    w_in = np.asarray(inputs["w_in"])
    pq, pk = _qsw_perm()
    q = w_in[:, :, 0:1024]
    k = w_in[:, :, 1024:1152]
    w_inx = np.concatenate([q, q[:, :, pq], k, k[:, :, pk], w_in[:, :, 1152:]], axis=2)
    sh = {"w_inx": np.ascontiguousarray(w_inx, dtype=np.float32)}
    for name in ("w_ada", "b_ada", "g_mix", "g_ffn", "gmlp_ln_g", "gmlp_ln_b"):
        sh[name] = np.ascontiguousarray(inputs[name], dtype=np.float32)
    return sh


def prep_core(inputs, sh, b, s):
    x = np.asarray(inputs["x"])[b]
    ctx = np.asarray(inputs["ctx"])[b]
    pos = np.arange(4096)
    if s == 0:
        xl = x[:NLAT]
        pl = pos[:NLAT]
        cl = ctx
    else:
        xl = x[NLAT:][::-1]
        pl = pos[NLAT:][::-1]
        cl = ctx[::-1]
    m = dict(sh)
    m["x_in"] = np.ascontiguousarray(np.concatenate([cl, xl], axis=0), dtype=np.float32)
    m["rowcol"] = np.stack([pl // 64, pl % 64]).astype(np.float32)
    m["cvec"] = np.stack([np.asarray(inputs["c"])[b], np.asarray(inputs["c_ctx"])]).astype(np.float32)
    ws = np.asarray(inputs["gmlp_w_s"])
    bs = np.asarray(inputs["gmlp_b_s"])
    if s == 1:
        ws = ws[:, :, ::-1, ::-1]
        bs = bs[:, :, ::-1]
    m["gmlp_wsT"] = np.ascontiguousarray(ws.transpose(0, 1, 3, 2), dtype=np.float32)
    m["gmlp_b_s"] = np.ascontiguousarray(bs, dtype=np.float32)
    return m


def phase_D(P, R, K, l, xname, xout, ctx_full=True, side=None):
    w_out = R.get("w_out", [R.nl, D, D])[R.li(l):R.li(l) + 1]
    oTa = R.get("oT_a", [NQ, NTOK], BF16)
    oTs = R.get("oT_s", [512, NTOK], BF16)
    oTg = R.get("oT_g", [512, NTOK], BF16)
    x = R.get(xname, [NTOK, D])
    xo = R.get(xout, [NTOK, D])
    C = Ctx(P)
    W = C.sb([128, 16, D], BF16, "wout")
    t_W = Tok()
    wv = w_out[0].rearrange("(k p) c -> p k c", p=128)
    for k4 in range(4):
        P.dma("pool", W[:, 4 * k4:4 * k4 + 4, :], wv[:, 4 * k4:4 * k4 + 4, :], w=[t_W])
    GA, tGA = [], []
    for r in range(2):
        g, tg = load_bcast_mod(P, C, R, l, r, 2, "ga%d" % r)
        GA.append(g); tGA.append(tg)
    NB = 2
    ot = [C.sb([128, 16, 512], BF16, "ot") for _ in range(NB)]
    t_ot = [Tok() for _ in range(NB)]
    xt = [C.sb([128, D], F32, "xt") for _ in range(NB)]
    t_xt = [Tok() for _ in range(NB)]
    tmp = [C.sb([128, 512], F32, "tmp") for _ in range(2)]
    t_tmp = [Tok() for _ in range(2)]
    ps = [C.ps([128, 512], F32, "psD") for _ in range(4)]
    t_ps = [Tok() for _ in range(4)]
    tiles = list(range(NT)) if ctx_full else list(range(NCTX // 128, NT))
    it = 0
    for n_, ti in enumerate(tiles):
        i = n_ % NB
        r = 1 if ti < NCTX // 128 else 0
        gi_ = (ti + 2) // 4
        oi = gi_ % NB
        g_lo = max(0, gi_ * 4 - 2)
        g_n = min(NT, gi_ * 4 + 2) - g_lo
        oo = (ti - g_lo) * 128
        if ti == g_lo or n_ == 0:
            tsl = slice(g_lo * 128, (g_lo + g_n) * 128)
            P.dma("sp", ot[oi][:, 0:8, 0:128 * g_n], oTa[:, tsl].rearrange("(k p) t -> p k t", p=128), w=[t_ot[oi]])
            P.dma("sp", ot[oi][:, 8:12, 0:128 * g_n], oTs[:, tsl].rearrange("(k p) t -> p k t", p=128), w=[t_ot[oi]])
            P.dma("sp", ot[oi][:, 12:16, 0:128 * g_n], oTg[:, tsl].rearrange("(k p) t -> p k t", p=128), w=[t_ot[oi]])
        P.dma("act", xt[i][:], x[ti * 128:(ti + 1) * 128, :], r=[R.tk(xname, ti)], w=[t_xt[i]])
        for cb in range(4):
            j = it % 4
            it += 1
            cs = slice(cb * 512, (cb + 1) * 512)
            P.mm([(lambda pe, k=k, j=j, oi=oi, oo=oo, cs=cs: pe.matmul(ps[j][:], ot[oi][:, k, oo:oo + 128], W[:, k, cs], start=(k == 0), stop=(k == 15)))
                  for k in range(16)], r=[t_ot[oi], t_W], w=[t_ps[j]])
            P.op("dve", lambda v, j=j, r=r, cs=cs: v.tensor_tensor(out=tmp[j % 2][:], in0=ps[j][:], in1=GA[r][:, cs], op=ALU.mult),
                 r=[t_ps[j], tGA[r]], w=[t_tmp[j % 2]])
            P.op("dve", lambda g, j=j, i=i, cs=cs: g.tensor_tensor(out=xt[i][:, cs], in0=xt[i][:, cs], in1=tmp[j % 2][:], op=ALU.add),
                 r=[t_tmp[j % 2], t_xt[i]], w=[t_xt[i]])
            if side is not None and cb % 2 == 1:
                next(side, None)
        P.dma("sp", xo[ti * 128:(ti + 1) * 128, :], xt[i][:], r=[t_xt[i]], w=[R.tk(xout, ti)])
    if side is not None:
        for _ in side:
            pass
    C.close()


H2C = NTOK + 3
FFN_BLOCKS = [(1, 769, [0, 1, 2, 3, 4, 5]), (770, 768, [6, 7, 8, 9, 10, 11]), (1538, 768, [12, 13, 14, 15, 16, 17])]


def tile_col(ti):
    return 1 + ti * 128 if ti < 2 else 258 + (ti - 2) * 128


def phase_E(P, R, K, l, xname, xout, dbg=""):
    w_up = R.get("ffn_w_up", [R.nl, D, 2 * DFF])[R.li(l):R.li(l) + 1]
    w_dn = R.get("ffn_w_down", [R.nl, DFF, D])[R.li(l):R.li(l) + 1]
    cw = R.get("ffn_conv_w", [R.nl, 3, DFF])[R.li(l):R.li(l) + 1]
    cb_ = R.get("ffn_conv_b", [R.nl, DFF])[R.li(l):R.li(l) + 1]
    h2T = R.get("h2T", [D, H2C], BF16)
    xb = R.get("xb_recv", [1, D])
    x = R.get(xname, [NTOK, D])
    xo = R.get(xout, [NTOK, D])
    h2v = h2T.rearrange("(k p) c -> p k c", p=128)
    C0 = Ctx(P)
    groups = [[0, 1], [2, 3, 4, 5], [6, 7, 8, 9], [10, 11, 12, 13], [14, 15, 16, 17], [NT]]
    gof = {}
    for gi, gl in enumerate(groups):
        for k_, ti in enumerate(gl):
            gof[ti] = (gi, k_, len(gl))
    stage = [C0.sb([128, 16, 512], BF16, "stage") for _ in range(2)]
    t_stage = [(Tok(), Tok()) for _ in range(2)]
    zt = C0.sb([128, 16, 1], BF16, "zt")
    t_z = Tok()
    P.op("pool", lambda g: g.memset(zt[:], 0.0), w=[t_z])
    P.dma("sp", h2v[:, :, 0:1], zt[:], r=[t_z], w=[R.tk("h2T", "z0")], allow_slow_non_contiguous=True)
    P.dma("sp", h2v[:, :, 257:258], zt[:], r=[t_z], w=[R.tk("h2T", "z1")], allow_slow_non_contiguous=True)

    def after(ti, toks):
        gi, k_, gn = gof[ti]
        if k_ != gn - 1:
            return
        if ti < NT:
            c0 = tile_col(groups[gi][0])
            P.dma("sp", h2v[:, :, c0:c0 + 128 * gn], stage[gi % 2][:, :, 0:128 * gn], r=list(toks), w=[R.tk("h2T", ti)])
        else:
            P.dma("sp", h2v[:, :, H2C - 1:H2C], stage[gi % 2][:, :, 0:1], r=list(toks), w=[R.tk("h2T", ti)], allow_slow_non_contiguous=True)

    norm_mod_transpose(P, R, K, C0, l, xname, "g_ffn", 3, 4, None,
                       dst_fn=lambda ti, h: stage[gof[ti][0] % 2][:, 8 * h:8 * h + 8, 128 * gof[ti][1]:128 * gof[ti][1] + 128],
                       tok_fn=lambda ti: t_stage[gof[ti][0] % 2], after=after, extra_row=(xb, R.tk("xb_recv")))
    C0.close()
    if dbg == "e0":
        return
    h2_toks = [R.tk("h2T", k) for k in ["z0", "z1"] + list(range(NT + 1))]

    C = Ctx(P)
    CW = C.sb([128, NFF, 3], F32, "CW")
    CB = C.sb([128, NFF], F32, "CB")
    t_cw = Tok()
    for j in range(3):
        P.dma("sp", CW[:, :, j], cw[0, j].rearrange("(f p) -> p f", p=128), w=[t_cw], allow_slow_non_contiguous=True)
    P.dma("sp", CB[:], cb_[0].rearrange("(f p) -> p f", p=128), w=[t_cw], allow_slow_non_contiguous=True)
    aT = C.sb([128, NFF, 769], BF16, "aT")
    wuv = w_up[0].rearrange("(k p) c -> p k c", p=128)
    wdv = w_dn[0].rearrange("(f p) c -> p f c", p=128)
    for (a, n, tiles) in FFN_BLOCKS:
        Cu = Ctx(P)
        hb = Cu.sb([128, 16, 771], BF16, "h2blk")
        t_hb = Tok()
        P.dma("sp", hb[:, 0:8, 0:n + 2], h2v[:, 0:8, a - 1:a + n + 1], r=h2_toks, w=[t_hb])
        P.dma("act", hb[:, 8:16, 0:n + 2], h2v[:, 8:16, a - 1:a + n + 1], r=h2_toks, w=[t_hb])
        wu = [Cu.sb([128, 16, 1024], BF16, "wup") for _ in range(2)]
        t_wu = [Tok() for _ in range(2)]
        G = [[Cu.ps([128, 512], F32, "G") for _ in range(2)] for _ in range(2)]
        V = [[Cu.ps([128, 512], F32, "V") for _ in range(2)] for _ in range(2)]
        t_G = [[Tok() for _ in range(2)] for _ in range(2)]
        t_V = [[Tok() for _ in range(2)] for _ in range(2)]
        g1 = [Cu.sb([128, 385], F32, "g1") for _ in range(2)]
        t_g1 = [Tok() for _ in range(2)]
        sl = [Cu.sb([128, 385], F32, "sl") for _ in range(2)]
        t_sl = [Tok() for _ in range(2)]
        n0 = n - 384
        chunks = [(0, n0), (n0, 384)]
        t_aT = Tok()
        for f in range(NFF):
            wi = (f // 4) % 2
            pb = f % 2
            fo = (f % 4) * 128
            if f % 4 == 0:
                P.dma("pool", wu[wi][:, :, 0:512], wuv[:, :, f * 128:f * 128 + 512], w=[t_wu[wi]])
                P.dma("pool", wu[wi][:, :, 512:1024], wuv[:, :, DFF + f * 128:DFF + f * 128 + 512], w=[t_wu[wi]])
            for c, (s0, ln) in enumerate(chunks):
                Gp, Vp = G[pb][c], V[pb][c]
                P.mm([(lambda pe, k=k, Gp=Gp, wi=wi, s0=s0, ln=ln, fo=fo: pe.matmul(Gp[:, 0:ln + 2], wu[wi][:, k, fo:fo + 128], hb[:, k, s0:s0 + ln + 2], start=(k == 0), stop=(k == 15)))
                      for k in range(16)], r=[t_wu[wi], t_hb], w=[t_G[pb][c]])
                P.mm([(lambda pe, k=k, Vp=Vp, wi=wi, s0=s0, ln=ln, fo=fo: pe.matmul(Vp[:, 0:ln], wu[wi][:, k, 512 + fo:512 + fo + 128], hb[:, k, s0 + 1:s0 + ln + 1], start=(k == 0), stop=(k == 15)))
                      for k in range(16)], r=[t_wu[wi], t_hb], w=[t_V[pb][c]])
                P.op("act", lambda a_, Gp=Gp, c=c, f=f, ln=ln: a_.activation(out=g1[c][:, 0:ln], in_=Gp[:, 1:ln + 1], func=AF.Identity,
                                                                           scale=CW[:, f, 1:2], bias=CB[:, f:f + 1]),
                     r=[t_G[pb][c], t_cw], w=[t_g1[c]])
                P.op("dve", lambda v, Gp=Gp, c=c, f=f, ln=ln: v.scalar_tensor_tensor(out=g1[c][:, 0:ln], in0=Gp[:, 0:ln], scalar=CW[:, f, 0:1], in1=g1[c][:, 0:ln],
                                                                                  op0=ALU.mult, op1=ALU.add),
                     r=[t_G[pb][c], t_cw, t_g1[c]], w=[t_g1[c]])
                P.op("dve", lambda v, Gp=Gp, c=c, f=f, ln=ln: v.scalar_tensor_tensor(out=g1[c][:, 0:ln], in0=Gp[:, 2:ln + 2], scalar=CW[:, f, 2:3], in1=g1[c][:, 0:ln],
                                                                                  op0=ALU.mult, op1=ALU.add),
                     r=[t_G[pb][c], t_cw, t_g1[c]], w=[t_g1[c]])
                P.op("act", lambda a_, c=c, ln=ln: a_.activation(out=sl[c][:, 0:ln], in_=g1[c][:, 0:ln], func=AF.Silu), r=[t_g1[c]], w=[t_sl[c]])
                P.op("dve", lambda v, Vp=Vp, c=c, f=f, s0=s0, ln=ln: v.tensor_tensor(out=aT[:, f, s0:s0 + ln], in0=sl[c][:, 0:ln], in1=Vp[:, 0:ln], op=ALU.mult),
                     r=[t_sl[c], t_V[pb][c]], w=[t_aT])
        Cu.close()
        if dbg == "up":
            continue
        Cd = Ctx(P)
        GF, tGF = [], []
        for r in range(2):
            g, tg = load_bcast_mod(P, Cd, R, l, r, 5, "gf%d" % r)
            GF.append(g); tGF.append(tg)
        wd = [Cd.sb([128, 11, 512], BF16, "wd") for _ in range(8)]
        t_wd = [Tok() for _ in range(8)]
        ps = [Cd.ps([128, 512], F32, "psd") for _ in range(6)]
        t_ps = [Tok() for _ in range(6)]
        xt = [Cd.sb([128, 512], F32, "xt") for _ in range(4)]
        t_xt = [Tok() for _ in range(4)]
        tmp = [Cd.sb([128, 512], F32, "tmp") for _ in range(2)]
        t_tmp = [Tok() for _ in range(2)]
        x_it = 0
        for cb in range(4):
            cs = slice(cb * 512, (cb + 1) * 512)
            for fc in range(4):
                wi = (cb % 2) * 4 + fc
                P.dma("pool", wd[wi][:], wdv[:, fc * 11:(fc + 1) * 11, cs], w=[t_wd[wi]])
            for hf in range(2):
                tl = list(enumerate(tiles))[3 * hf:3 * hf + 3]
                for fc in range(4):
                    wi = (cb % 2) * 4 + fc
                    fns = []
                    for ff in range(11):
                        f = fc * 11 + ff
                        for q, ti in tl:
                            lc = tile_col(ti) - a
                            fns.append(lambda pe, q=q, f=f, ff=ff, lc=lc, wi=wi: pe.matmul(ps[q][:], aT[:, f, lc:lc + 128], wd[wi][:, ff, :], start=(f == 0), stop=(f == NFF - 1)))
                    P.mm(fns, r=[t_wd[wi], t_aT], w=[t_ps[q] for q, _ in tl])
                for q, ti in tl:
                    r = 1 if ti < 2 else 0
                    xi = x_it % 4
                    x_it += 1
                    P.dma("sp", xt[xi][:], x[ti * 128:(ti + 1) * 128, cs], r=[R.tk(xname, ti)], w=[t_xt[xi]])
                    P.op("dve", lambda v, q=q, r=r, cs=cs, xi=xi: v.tensor_tensor(out=tmp[xi % 2][:], in0=ps[q][:], in1=GF[r][:, cs], op=ALU.mult),
                         r=[t_ps[q], tGF[r]], w=[t_tmp[xi % 2]])
                    P.op("dve", lambda v, xi=xi: v.tensor_tensor(out=xt[xi][:], in0=xt[xi][:], in1=tmp[xi % 2][:], op=ALU.add),
                         r=[t_tmp[xi % 2], t_xt[xi]], w=[t_xt[xi]])
                    P.dma("sp", xo[ti * 128:(ti + 1) * 128, cs], xt[xi][:], r=[t_xt[xi]], w=[R.tk(xout, (ti, cb))])
        Cd.close()
    C.close()


NEG = -30000.0


def phase_C(P, R, K, l, ctx_full=True, dbg_qtiles=None, dbg_skip=()):
    qT = R.get("qT", [NQ, NTOK], BF16)
    kT = R.get("kT", [128, NTOK + 128], BF16)
    vv = R.get("v", [NTOK + 128, 128], BF16)
    oT = R.get("oT_a", [NQ, NTOK], BF16)
    sink = R.get("attn_sink", [R.nl, 16])[R.li(l):R.li(l) + 1]
    C = Ctx(P)
    io = C.sb([128, 128], I32, "mio")
    mf = C.sb([128, 128], F32, "mf")
    masks = {}
    t_m = Tok()
    for name, cm, op, thr in (("prev", -1, ALU.is_le, 0.0), ("next", -1, ALU.is_ge, 0.0), ("halo", 1, ALU.is_ge, 127.0)):
        mk = C.sb([128, 8, 128], BF16, "mask_" + name)
        P.op("pool", lambda g, cm=cm: g.iota(io[:], [[1, 128]], base=0, channel_multiplier=cm), r=[t_m], w=[t_m])
        P.op("dve", lambda v: v.tensor_copy(out=mf[:], in_=io[:]), r=[t_m], w=[t_m])
        for h8 in range(8):
            P.op("dve", lambda v, mk=mk, op=op, thr=thr, h8=h8: v.tensor_scalar(out=mk[:, h8, :], in0=mf[:], scalar1=thr, scalar2=None, op0=op), r=[t_m], w=[t_m])
        masks[name] = mk
    onesA = C.sb([128, 128], BF16, "onesA")
    onesB = C.sb([128, 128], BF16, "onesB")
    P.op("pool", lambda g: g.memset(onesA[:], 0.0), w=[t_m])
    P.op("pool", lambda g: g.memset(onesB[:], 0.0), w=[t_m])
    P.op("pool", lambda g: g.memset(onesA[:, 0:64], 1.0), r=[t_m], w=[t_m])
    P.op("pool", lambda g: g.memset(onesB[:, 64:128], 1.0), r=[t_m], w=[t_m])
    es = C.sb([128, 8], F32, "esink")
    t_es = Tok()
    sv = sink[0].rearrange("(m h) -> h m", h=2)
    P.dma("sp", es[0:64, :], sv[0].partition_broadcast(64), w=[t_es], allow_slow_non_contiguous=True)
    P.dma("sp", es[64:128, :], sv[1].partition_broadcast(64), w=[t_es], allow_slow_non_contiguous=True)
    P.op("act", lambda a: a.activation(out=es[:], in_=es[:], func=AF.Exp), r=[t_es], w=[t_es])
    NKB = NT + 1
    KT2 = [C.sb([128, NTOK + 128], BF16, "KT2_%d" % j) for j in range(2)]
    t_k = Tok()
    for j in range(2):
        P.dma("sp", KT2[j][0:64, :], kT[64 * j:64 * j + 64, :], w=[t_k])
        P.dma("act", KT2[j][64:128, :], kT[64 * j:64 * j + 64, :], w=[t_k])
    VP = C.sb([128, NKB, 2, 256], BF16, "VP")
    t_v = Tok()
    P.op("pool", lambda g: g.memset(VP[:], 1.0), w=[t_v])
    v3 = vv.rearrange("(kb p) c -> p kb c", p=128)
    for j in range(2):
        P.dma("sp", VP[:, :, j, 0:64], v3[:, :, 64 * j:64 * j + 64], r=[t_v], w=[t_v])
        P.dma("act", VP[:, :, j, 192:256], v3[:, :, 64 * j:64 * j + 64], r=[t_v], w=[t_v])
    qs = [C.sb([128, 8, 128], BF16, "qs") for _ in range(2)]
    t_qs = [Tok() for _ in range(2)]
    S = [C.ps([128, 1024], F32, "S") for _ in range(2)]
    t_S = [Tok() for _ in range(2)]
    PT = [C.sb([128, 5, 1024], BF16, "PT") for _ in range(2)]
    t_PT = [Tok() for _ in range(2)]
    Op = [C.ps([128, 1024], F32, "Op") for _ in range(2)]
    t_O = [Tok() for _ in range(2)]
    dsb = [C.sb([128, 512], F32, "dsb") for _ in range(2)]
    t_dsb = [Tok() for _ in range(2)]
    osb = [C.sb([128, 4, 128], BF16, "osb") for _ in range(2)]
    t_osb = [Tok() for _ in range(2)]
    qtiles = list(range(NT)) if ctx_full else list(range(2, NT))
    if dbg_qtiles is not None:
        qtiles = dbg_qtiles
    s_it = [0]
    items = []
    pj = 0
    for n_, qt in enumerate(qtiles):
        qi = n_ % 2
        if qt < 2:
            kbs = [(0, None), (1, None)]
        else:
            i = qt - 2
            kbs = [(0, None), (1, None), (qt, None)]
            if i > 0:
                kbs.append((qt - 1, "prev"))
            if i < 15:
                kbs.append((qt + 1, "next"))
            else:
                kbs.append((NT, "halo"))
        for j in range(2):
            items.append((qt, qi, j, kbs, pj % 2, j == 0))
            pj += 1

    def c_stage1(qt, qi, j, kbs, pi, first):
        if first:
            P.dma("sp", qs[qi][:], qT[:, qt * 128:(qt + 1) * 128].rearrange("(m p) t -> p m t", p=128), w=[t_qs[qi]])
        for kbi, (kb, mname) in enumerate(kbs):
            si = s_it[0] % 2
            s_it[0] += 1
            fns = []
            for hh in range(8):
                m, e = 4 * j + hh // 2, hh % 2
                cbk = e * 4 + hh // 2
                fns.append(lambda pe, si=si, hh=cbk, m=m, e=e, kb=kb, qi=qi, j=j: pe.matmul(
                    S[si][:, hh * 128:(hh + 1) * 128], KT2[j][64 * e:64 * e + 64, kb * 128:(kb + 1) * 128],
                    qs[qi][64 * e:64 * e + 64, m, :], start=True, stop=True))
            P.mm(fns, r=[t_k, t_qs[qi]], w=[t_S[si]])
            P.op("act", lambda a, si=si, pi=pi, kbi=kbi: a.activation(out=PT[pi][:, kbi, :], in_=S[si][:], func=AF.Exp, scale=0.125),
                 r=[t_S[si]], w=[t_PT[pi]])
            if mname is not None:
                P.op("pool", lambda g, pi=pi, kbi=kbi, mname=mname: g.tensor_tensor(out=PT[pi][:, kbi, :], in0=PT[pi][:, kbi, :],
                                                                                   in1=masks[mname][:].rearrange("p h q -> p (h q)"), op=ALU.mult),
                     r=[t_PT[pi], t_m], w=[t_PT[pi]])
        fns = []
        nk = len(kbs)
        for hh in range(8):
            mm, e = hh // 2, hh % 2
            for kbi, (kb, _) in enumerate(kbs):
                fns.append(lambda pe, pi=pi, hh=hh, mm=mm, kbi=kbi, kb=kb, e=e, j=j, nk=nk: pe.matmul(
                    Op[pi][:, hh * 128:(hh + 1) * 128], VP[:, kb, j, 128 * e:128 * e + 128],
                    PT[pi][:, kbi, (4 * e + mm) * 128:(4 * e + mm + 1) * 128], start=(kbi == 0), stop=(kbi == nk - 1)))
        P.mm(fns, r=[t_v, t_PT[pi]], w=[t_O[pi]])

    def c_stage2(qt, qi, j, kbs, pi, first):
        Ov = Op[pi][:].rearrange("p (m e q) -> p m e q", e=2, q=128)
        dv = dsb[pi][:].rearrange("p (m q) -> p m q", q=128)
        P.op("act", lambda a, Ov=Ov, dv=dv: a.copy(out=dv[0:64], in_=Ov[64:128, :, 0, :]), r=[t_O[pi]], w=[t_dsb[pi]])
        P.op("act", lambda a, Ov=Ov, dv=dv: a.copy(out=dv[64:128], in_=Ov[0:64, :, 1, :]), r=[t_O[pi]], w=[t_dsb[pi]])
        for mm in range(4):
            m = 4 * j + mm
            P.op("dve", lambda v, pi=pi, mm=mm, m=m: v.tensor_scalar(out=dsb[pi][:, mm * 128:(mm + 1) * 128], in0=dsb[pi][:, mm * 128:(mm + 1) * 128],
                                                                  scalar1=es[:, m:m + 1], scalar2=None, op0=ALU.add),
                 r=[t_dsb[pi], t_es], w=[t_dsb[pi]])
        P.op("dve", lambda v, pi=pi: v.reciprocal(out=dsb[pi][:], in_=dsb[pi][:]), r=[t_dsb[pi]], w=[t_dsb[pi]])
        P.op("dve", lambda v, pi=pi, Ov=Ov, dv=dv: v.tensor_tensor(out=osb[pi][0:64], in0=Ov[0:64, :, 0, :], in1=dv[0:64], op=ALU.mult),
             r=[t_O[pi], t_dsb[pi]], w=[t_osb[pi]])
        P.op("dve", lambda v, pi=pi, Ov=Ov, dv=dv: v.tensor_tensor(out=osb[pi][64:128], in0=Ov[64:128, :, 1, :], in1=dv[64:128], op=ALU.mult),
             r=[t_O[pi], t_dsb[pi]], w=[t_osb[pi]])
        P.dma("sp", oT[512 * j:512 * j + 512, qt * 128:(qt + 1) * 128].rearrange("(m p) t -> p m t", p=128), osb[pi][:],
              r=[t_osb[pi]], w=[R.tk("oT_a", (qt, j))])

    for n_ in range(len(items) + 1):
        if n_ < len(items):
            c_stage1(*items[n_])
        if n_ >= 1:
            c_stage2(*items[n_ - 1])
    C.close()


TCH = 16
NCH = NTOK // TCH
NCC = NCTX // TCH


class SSMState:
    pass


def ssm_prep(P, R, K, C, l):
    S = SSMState()
    li = R.li(l)
    lam_re = R.get("ssm_lambda_re", [R.nl, 2, 32, 64])
    lam_im = R.get("ssm_lambda_im", [R.nl, 2, 32, 64])
    log_dt = R.get("ssm_log_dt", [R.nl, 2, 32])
    b_re = R.get("ssm_b_re", [R.nl, 2, 32, 64, 16])
    b_im = R.get("ssm_b_im", [R.nl, 2, 32, 64, 16])
    c_re = R.get("ssm_c_re", [R.nl, 2, 32, 16, 64])
    c_im = R.get("ssm_c_im", [R.nl, 2, 32, 16, 64])
    pf = C.sb([128, 16], F32, "pf")
    ARE = C.sb([128, 64], F32, "ARE")
    AIM = C.sb([128, 64], F32, "AIM")
    LB = C.sb([128, 64, 128], BF16, "LB")
    CW = C.sb([128, 64, 16], F32, "CW")
    CL = C.sb([128, 64], F32, "CL")
    CR = C.sb([128, 64], F32, "CR")
    ID2 = C.sb([128, 64], F32, "ID2")
    ATR = C.sb([64, 64], F32, "ATR")
    ATI = C.sb([64, 64], F32, "ATI")
    Ct = Ctx(P)
    t = Tok()
    LR = Ct.sb([128, 64], F32, "LR")
    LI = Ct.sb([128, 64], F32, "LI")
    DT = Ct.sb([128, 64], F32, "DT")
    for base in (0, 64):
        P.dma("sp", LR[base:base + 64, :], lam_re[li].rearrange("d g p -> p (d g)"), w=[t], allow_slow_non_contiguous=True)
        P.dma("sp", LI[base:base + 64, :], lam_im[li].rearrange("d g p -> p (d g)"), w=[t], allow_slow_non_contiguous=True)
    P.dma("sp", DT[:], log_dt[li].rearrange("d g -> (d g)").partition_broadcast(128), w=[t])
    pi_ = Ct.sb([128, 1], I32, "pi")
    pj_ = Ct.sb([128, 1], I32, "pj")
    tp = Tok()
    P.op("pool", lambda g: g.iota(pi_[:], [[0, 1]], base=0, channel_multiplier=1), w=[tp])
    P.op("dve", lambda v: v.tensor_scalar(out=pj_[:], in0=pi_[:], scalar1=6, scalar2=None, op0=ALU.logical_shift_right), r=[tp], w=[tp])
    P.op("dve", lambda v: v.tensor_copy(out=pf[:, 1:2], in_=pj_[:]), r=[tp], w=[tp])
    P.op("dve", lambda v: v.tensor_scalar(out=pf[:, 0:1], in0=pf[:, 1:2], scalar1=-1.0, scalar2=1.0, op0=ALU.mult, op1=ALU.add), r=[tp], w=[tp])
    P.op("dve", lambda v: v.tensor_scalar(out=pf[:, 2:3], in0=pf[:, 1:2], scalar1=2.0, scalar2=-1.0, op0=ALU.mult, op1=ALU.add), r=[tp], w=[tp])
    P.op("dve", lambda v: v.tensor_scalar(out=pf[:, 3:4], in0=pf[:, 1:2], scalar1=-1.0, scalar2=None, op0=ALU.mult), r=[tp], w=[tp])
    P.op("dve", lambda v: v.tensor_scalar(out=pj_[:], in0=pi_[:], scalar1=4, scalar2=None, op0=ALU.logical_shift_right), r=[tp], w=[tp])
    P.op("dve", lambda v: v.tensor_copy(out=pf[:, 12:13], in_=pj_[:]), r=[tp], w=[tp])
    for i in range(8):
        P.op("dve", lambda v, i=i: v.tensor_scalar(out=pf[:, 4 + i:5 + i], in0=pf[:, 12:13], scalar1=float(i), scalar2=None, op0=ALU.is_equal), r=[tp], w=[tp])
    S.pf, S.t_pf = pf, tp
    P.op("act", lambda a: a.activation(out=DT[:], in_=DT[:], func=AF.Exp), r=[t], w=[t])
    MAG = Ct.sb([128, 64], F32, "MAG")
    ANG = Ct.sb([128, 64], F32, "ANG")
    P.op("dve", lambda v: v.tensor_tensor(out=MAG[:], in0=LR[:], in1=DT[:], op=ALU.mult), r=[t], w=[t])
    P.op("act", lambda a: a.activation(out=MAG[:], in_=MAG[:], func=AF.Exp), r=[t], w=[t])
    P.op("dve", lambda v: v.tensor_tensor(out=ANG[:], in0=LI[:], in1=DT[:], op=ALU.mult), r=[t], w=[t])
    SN = Ct.sb([128, 64], F32, "SN")
    CS = Ct.sb([128, 64], F32, "CS")
    Cr = Ctx(P)
    tsn = Tok(); tsn.w = t.w
    range_reduce_sin(P, Cr, SN[:], ANG[:], [128, 64], t)
    range_reduce_sin(P, Cr, CS[:], ANG[:], [128, 64], t, extra=PI / 2)
    Cr.close()
    t_a = Tok()
    P.op("dve", lambda v: v.tensor_tensor(out=ARE[:], in0=MAG[:], in1=CS[:], op=ALU.mult), r=[t], w=[t_a])
    P.op("dve", lambda v: v.tensor_tensor(out=AIM[:], in0=MAG[:], in1=SN[:], op=ALU.mult), r=[t], w=[t_a])
    S.ARE, S.AIM, S.t_a = ARE, AIM, t_a
    DEN = Ct.sb([128, 64], F32, "DEN")
    T1 = Ct.sb([128, 64], F32, "T1")
    NRE = Ct.sb([128, 64], F32, "NRE")
    FRE = Ct.sb([128, 64], F32, "FRE")
    FIM = Ct.sb([128, 64], F32, "FIM")
    tf = Tok()
    P.op("dve", lambda v: v.tensor_tensor(out=DEN[:], in0=LR[:], in1=LR[:], op=ALU.mult), r=[t], w=[tf])
    P.op("dve", lambda v: v.tensor_tensor(out=T1[:], in0=LI[:], in1=LI[:], op=ALU.mult), r=[t], w=[tf])
    P.op("dve", lambda v: v.tensor_tensor(out=DEN[:], in0=DEN[:], in1=T1[:], op=ALU.add), r=[tf], w=[tf])
    P.op("dve", lambda v: v.reciprocal(out=DEN[:], in_=DEN[:]), r=[tf], w=[tf])
    P.op("dve", lambda v: v.tensor_scalar(out=NRE[:], in0=ARE[:], scalar1=-1.0, scalar2=None, op0=ALU.add), r=[t_a], w=[tf])
    P.op("dve", lambda v: v.tensor_tensor(out=FRE[:], in0=NRE[:], in1=LR[:], op=ALU.mult), r=[tf, t], w=[tf])
    P.op("dve", lambda v: v.tensor_tensor(out=T1[:], in0=AIM[:], in1=LI[:], op=ALU.mult), r=[tf, t, t_a], w=[tf])
    P.op("dve", lambda v: v.tensor_tensor(out=FRE[:], in0=FRE[:], in1=T1[:], op=ALU.add), r=[tf], w=[tf])
    P.op("dve", lambda v: v.tensor_tensor(out=FRE[:], in0=FRE[:], in1=DEN[:], op=ALU.mult), r=[tf], w=[tf])
    P.op("dve", lambda v: v.tensor_tensor(out=FIM[:], in0=AIM[:], in1=LR[:], op=ALU.mult), r=[tf, t, t_a], w=[tf])
    P.op("dve", lambda v: v.tensor_tensor(out=T1[:], in0=NRE[:], in1=LI[:], op=ALU.mult), r=[tf, t], w=[tf])
    P.op("dve", lambda v: v.tensor_tensor(out=FIM[:], in0=FIM[:], in1=T1[:], op=ALU.subtract), r=[tf], w=[tf])
    P.op("dve", lambda v: v.tensor_tensor(out=FIM[:], in0=FIM[:], in1=DEN[:], op=ALU.mult), r=[tf], w=[tf])
    P.op("dve", lambda v: v.tensor_scalar(out=FIM[:], in0=FIM[:], scalar1=pf[:, 2:3], scalar2=None, op0=ALU.mult), r=[tf, tp], w=[tf])
    BX = Ct.sb([128, 64, 16], F32, "BX")
    BY = Ct.sb([128, 64, 16], F32, "BY")
    tb = Tok()
    brv = b_re[li].rearrange("d g p h -> p (d g) h")
    biv = b_im[li].rearrange("d g p h -> p (d g) h")
    P.dma("sp", BX[0:64], brv, w=[tb])
    P.dma("act", BX[64:128], biv, w=[tb])
    P.dma("sp", BY[0:64], biv, w=[tb])
    P.dma("act", BY[64:128], brv, w=[tb])
    for h in range(16):
        P.op("dve", lambda v, h=h: v.tensor_tensor(out=BX[:, :, h], in0=BX[:, :, h], in1=FRE[:], op=ALU.mult), r=[tb, tf], w=[tb])
        P.op("pool", lambda g, h=h: g.tensor_tensor(out=BY[:, :, h], in0=BY[:, :, h], in1=FIM[:], op=ALU.mult), r=[tb, tf], w=[tb])
    P.op("dve", lambda v: v.tensor_tensor(out=BX[:], in0=BX[:], in1=BY[:], op=ALU.add), r=[tb], w=[tb])
    t_LB = Tok()
    psT = Ct.ps([128, 128], F32, "psT")
    t_psT = Tok()
    for b8 in range(8):
        P.mm([lambda pe, b8=b8: pe.transpose(psT[:], BX[:, b8 * 8:(b8 + 1) * 8, :].rearrange("p g h -> p (g h)"), K["idf"][:])],
             r=[tb, K["tok"]], w=[t_psT])
        for i in range(8):
            P.op("dve", lambda v, b8=b8, i=i: v.tensor_scalar(out=LB[:, b8 * 8 + i, :], in0=psT[:], scalar1=pf[:, 4 + i:5 + i], scalar2=None, op0=ALU.mult),
                 r=[t_psT, tp], w=[t_LB])
    S.LB, S.t_LB = LB, t_LB
    t_CW = Tok()
    crv = c_re[li].rearrange("d g h p -> p (d g) h")
    civ = c_im[li].rearrange("d g h p -> p (d g) h")
    for q4 in range(0, 64, 4):
        P.dma("sp", CW[0:64, q4:q4 + 4, :], crv[:, q4:q4 + 4, :], w=[t_CW], allow_slow_non_contiguous=True)
        P.dma("act", CW[64:128, q4:q4 + 4, :], civ[:, q4:q4 + 4, :], w=[t_CW], allow_slow_non_contiguous=True)
    P.op("dve", lambda v: v.tensor_scalar(out=CW[:].rearrange("p g h -> p (g h)"), in0=CW[:].rearrange("p g h -> p (g h)"), scalar1=pf[:, 2:3], scalar2=-1.0,
                                          op0=ALU.mult, op1=ALU.mult), r=[t_CW, tp], w=[t_CW])
    S.CW, S.t_CW = CW, t_CW
    t_c = Tok()
    P.op("dve", lambda v: v.tensor_scalar(out=CL[:], in0=ARE[:], scalar1=pf[:, 0:1], scalar2=None, op0=ALU.mult), r=[t_a, tp], w=[t_c])
    P.op("dve", lambda v: v.scalar_tensor_tensor(out=CL[:], in0=AIM[:], scalar=pf[:, 3:4], in1=CL[:], op0=ALU.mult, op1=ALU.add), r=[t_a, tp, t_c], w=[t_c])
    P.op("dve", lambda v: v.tensor_scalar(out=CR[:], in0=AIM[:], scalar1=pf[:, 0:1], scalar2=None, op0=ALU.mult), r=[t_a, tp], w=[t_c])
    P.op("dve", lambda v: v.scalar_tensor_tensor(out=CR[:], in0=ARE[:], scalar=pf[:, 1:2], in1=CR[:], op0=ALU.mult, op1=ALU.add), r=[t_a, tp, t_c], w=[t_c])
    S.CL, S.CR, S.t_c = CL, CR, t_c
    t_id2 = Tok()
    P.op("dve", lambda v: v.tensor_copy(out=ID2[0:64, :], in_=K["idf"][0:64, 0:64]), r=[K["tok"]], w=[t_id2])
    P.op("dve", lambda v: v.tensor_copy(out=ID2[64:128, :], in_=K["idf"][64:128, 64:128]), r=[K["tok"]], w=[t_id2])
    S.ID2, S.t_id2 = ID2, t_id2
    q1 = Ct.sb([64, 64], F32, "q1")
    q2 = Ct.sb([64, 64], F32, "q2")
    t_at = Tok()
    P.op("dve", lambda v: v.tensor_copy(out=ATR[:], in_=ARE[0:64, :]), r=[t_a], w=[t_at])
    P.op("dve", lambda v: v.tensor_copy(out=ATI[:], in_=AIM[0:64, :]), r=[t_a], w=[t_at])
    for _ in range(4):
        P.op("dve", lambda v: v.tensor_tensor(out=q1[:], in0=ATR[:], in1=ATR[:], op=ALU.mult), r=[t_at], w=[t_at])
        P.op("dve", lambda v: v.tensor_tensor(out=q2[:], in0=ATI[:], in1=ATI[:], op=ALU.mult), r=[t_at], w=[t_at])
        P.op("dve", lambda v: v.tensor_tensor(out=ATI[:], in0=ATR[:], in1=ATI[:], op=ALU.mult), r=[t_at], w=[t_at])
        P.op("dve", lambda v: v.tensor_scalar(out=ATI[:], in0=ATI[:], scalar1=2.0, scalar2=None, op0=ALU.mult), r=[t_at], w=[t_at])
        P.op("dve", lambda v: v.tensor_tensor(out=ATR[:], in0=q1[:], in1=q2[:], op=ALU.subtract), r=[t_at], w=[t_at])
    S.ATR, S.ATI, S.t_at = ATR, ATI, t_at
    Ct.close()
    return S


def ssm_rvm(P, C, S, UT, t_UT, gds, Hs, t_Hs, ps, t_ps, la, t_la, hinit=None, jmap=lambda j: j, act_only=False):
    le = "pool" if act_only else "dve"
    for i, gd in enumerate(gds):
        P.op(le, lambda v, i=i, gd=gd: v.tensor_scalar(out=la[i][:, 0:64], in0=S.ID2[:], scalar1=S.CL[:, gd:gd + 1], scalar2=None, op0=ALU.mult),
             r=[S.t_id2, S.t_c], w=[t_la[i]])
        P.op(le, lambda v, i=i, gd=gd: v.tensor_scalar(out=la[i][:, 64:128], in0=S.ID2[:], scalar1=S.CR[:, gd:gd + 1], scalar2=None, op0=ALU.mult),
             r=[S.t_id2, S.t_c], w=[t_la[i]])
    for s in range(TCH):
        for i, gd in enumerate(gds):
            d, g = gd // 32, gd % 32
            j = s if d == 0 else TCH - 1 - s
            jprev = j - 1 if d == 0 else j + 1
            u_cols = UT[:, g // 8, :].rearrange("p (c j) -> p j c", j=TCH)[:, j, :]
            fns = []
            has2 = (s > 0) or (hinit is not None)
            fns.append(lambda pe, i=i, gd=gd, u_cols=u_cols, has2=has2: pe.matmul(ps[i], S.LB[:, gd, :], u_cols, start=True, stop=(not has2)))
            rd = [t_UT, S.t_LB, t_la[i]]
            if s > 0:
                fns.append(lambda pe, i=i, jprev=jprev: pe.matmul(ps[i], la[i][:], Hs[i][:, :, jmap(jprev)], start=False, stop=True))
                rd.append(t_Hs[i])
            elif hinit is not None:
                fns.append(lambda pe, i=i: pe.matmul(ps[i], la[i][:], hinit[i][0], start=False, stop=True))
                rd.append(hinit[i][1])
            P.mm(fns, r=rd, w=[t_ps[i]])
            if act_only or (i + s) % 2 == 0:
                P.op("act", lambda a, i=i, j=j: a.copy(out=Hs[i][:, :, jmap(j)], in_=ps[i]), r=[t_ps[i]], w=[t_Hs[i]])
            else:
                P.op("dve", lambda v, i=i, j=j: v.tensor_copy(out=Hs[i][:, :, jmap(j)], in_=ps[i]), r=[t_ps[i]], w=[t_Hs[i]])


def ssm_chain(P, C, S, LL, t_L, d, c_list, init, eng="dve"):
    g0 = d * 32
    HH = [C.sb([64, 3, 32], F32, "chH") for _ in range(2)]
    T1 = C.sb([64, 2, 32], F32, "chT1")
    T2 = C.sb([64, 2, 32], F32, "chT2")
    A1 = C.sb([64, 2, 32], F32, "chA1")
    A2 = C.sb([64, 2, 32], F32, "chA2")
    tH = Tok()

    def op(fn, r, w):
        P.op(eng, fn, r=r, w=w)
    op(lambda v: v.tensor_copy(out=A1[:, 0, :], in_=S.ATR[:, g0:g0 + 32]), [S.t_at], [tH])
    op(lambda v: v.tensor_copy(out=A1[:, 1, :], in_=S.ATR[:, g0:g0 + 32]), [S.t_at], [tH])
    op(lambda v: v.tensor_scalar(out=A2[:, 0, :], in0=S.ATI[:, g0:g0 + 32], scalar1=-1.0, scalar2=None, op0=ALU.mult), [S.t_at], [tH])
    op(lambda v: v.tensor_copy(out=A2[:, 1, :], in_=S.ATI[:, g0:g0 + 32]), [S.t_at], [tH])
    if init is None:
        op(lambda v: v.memset(HH[0][:], 0.0), [], [tH])
    else:
        op(lambda v: v.tensor_copy(out=HH[0][:, 0:2, :], in_=init[0]), [init[1]], [tH])
        op(lambda v: v.tensor_copy(out=HH[0][:, 2, :], in_=init[0][:, 0, :]), [init[1]], [tH])
    cur = 0
    for c in c_list:
        h, hn = HH[cur], HH[1 - cur]
        lc = LL[:, :, g0:g0 + 32, c]
        rr = [tH, t_L]
        op(lambda v, h=h: v.tensor_tensor(out=T1[:], in0=A1[:], in1=h[:, 0:2, :], op=ALU.mult), rr, [tH])
        op(lambda v, h=h: v.tensor_tensor(out=T2[:], in0=A2[:], in1=h[:, 1:3, :], op=ALU.mult), rr, [tH])
        op(lambda v: v.tensor_tensor(out=T1[:], in0=T1[:], in1=T2[:], op=ALU.add), rr, [tH])
        op(lambda v, hn=hn, lc=lc: v.tensor_tensor(out=hn[:, 0:2, :], in0=T1[:], in1=lc, op=ALU.add), rr, [tH])
        op(lambda v, hn=hn: v.tensor_copy(out=hn[:, 2, :], in_=hn[:, 0, :]), rr, [tH])
        op(lambda v, h=h, lc=lc: v.tensor_copy(out=lc, in_=h[:, 0:2, :]), rr, [tH, t_L])
        cur = 1 - cur
    return HH[cur], tH


def phase_B(P, R, K, l, part, ctx_full=True, dbg="", S=None):
    li = R.li(l)
    uT = R.get("uT", [512, NTOK])
    send = R.get("ssm_send", [64, 2, 32])
    recv = R.get("ssm_recv", [64, 2, 32])
    hinit0 = R.get("hinit0", [64, 2, 32, NCH])
    oTs = R.get("oT_s", [512, NTOK], BF16)
    C = Ctx(P)
    if S is None:
        S = ssm_prep(P, R, K, C, l)
    if dbg == "prep":
        C.close(); return
    yTM = C.sb([128, NT, 512], F32, "yTM") if part == "b" else None
    t_y = Tok()
    Cs = Ctx(P)
    UT = Cs.sb([128, 4, NTOK], BF16, "UT")
    t_UT = Tok()
    P.dma("pool", UT[:], uT.rearrange("(c p) t -> p c t", p=128), w=[t_UT])
    shared = getattr(S, "LL", None) is not None
    if shared:
        LL, t_Ld = S.LL, S.t_Ld
    else:
        LL = Cs.sb([64, 2, 64, NCH], F32, "LL")
        t_Ld = [Tok(), Tok()]
    LRE = LL[:, 0]
    LIM = LL[:, 1]
    NG = 4
    NG1 = 6
    Hs = [Cs.sb([128, NCH, TCH], F32, "Hs") for _ in range(NG)] if part == "b" else []
    t_Hs = [Tok() for _ in range(NG)]
    Hs1 = [Cs.sb([128, NCH, 2], F32, "Hs1") for _ in range(NG1)]
    t_Hs1 = [Tok() for _ in range(NG1)]
    psb = [Cs.ps([128, 512], F32, "psR") for _ in range(NG1)]
    ps = [psb[i][:, 0:NCH] for i in range(NG1)]
    t_ps = [Tok() for _ in range(NG1)]
    la = [Cs.sb([128, 128], F32, "la") for _ in range(NG1)]
    t_la = [Tok() for _ in range(NG1)]

    def pass1(d, act_only=False):
        for g4 in range(0, 32, NG1):
            gds = [d * 32 + g for g in range(g4, min(32, g4 + NG1))]
            ssm_rvm(P, C, S, UT, t_UT, gds, Hs1, t_Hs1, ps, t_ps, la, t_la, jmap=lambda j: j % 2, act_only=act_only)
            jl = (TCH - 1 if d == 0 else 0) % 2
            for i, gd in enumerate(gds):
                P.op("act", lambda a, i=i, gd=gd: a.copy(out=LRE[:, gd, :], in_=Hs1[i][0:64, :, jl]), r=[t_Hs1[i]], w=[t_Ld[d]])
                if act_only:
                    P.op("act", lambda a, i=i, gd=gd: a.copy(out=LIM[:, gd, :], in_=Hs1[i][64:128, :, jl]), r=[t_Hs1[i]], w=[t_Ld[d]])
                else:
                    P.op("dve", lambda v, i=i, gd=gd: v.tensor_copy(out=LIM[:, gd, :], in_=Hs1[i][64:128, :, jl]), r=[t_Hs1[i]], w=[t_Ld[d]])

    if part == "a":
        pass1(0)
        if dbg == "nochain":
            Cs.close(); C.close(); return
        hh, tH = ssm_chain(P, Cs, S, LL, t_Ld[0], 0, list(range(NCH)), None, eng="dve")
        P.dma("sp", send, hh[:, 0:2, :], r=[tH], w=[R.tk("ssm_send", 0)])
        if shared:
            pass1(1, act_only=True)
        else:
            P.dma("sp", hinit0[:, 0], LRE[:, 0:32, :], r=[t_Ld[0], tH], w=[R.tk("hinit0", 0)])
            P.dma("sp", hinit0[:, 1], LIM[:, 0:32, :], r=[t_Ld[0], tH], w=[R.tk("hinit0", 1)])
        Cs.close()
        C.close()
        return
    if not shared:
        pass1(1)
    if not getattr(S, "chain1_done", False):
        rin = Cs.sb([64, 2, 32], F32, "rin")
        t_rin = Tok()
        P.dma("sp", rin[:], recv, r=[R.tk("ssm_recv", 0)], w=[t_rin])
        ssm_chain(P, Cs, S, LL, t_Ld[1], 1, list(range(NCH - 1, NCC - 1, -1)), (rin[:], t_rin))
        ssm_chain(P, Cs, S, LL, t_Ld[1], 1, list(range(NCC - 1, -1, -1)), None, eng=("pool" if shared else "dve"))
    if not shared:
        P.dma("sp", LRE[:, 0:32, :], hinit0[:, 0], r=[R.tk("hinit0", 0)], w=[t_Ld[0]])
        P.dma("sp", LIM[:, 0:32, :], hinit0[:, 1], r=[R.tk("hinit0", 1)], w=[t_Ld[0]])
    hin = [Cs.sb([128, NCH], F32, "hin") for _ in range(NG)]
    t_hin = [Tok() for _ in range(NG)]
    yps = [Cs.ps([128, 512], F32, "yps") for _ in range(2)]
    t_yps = [Tok() for _ in range(2)]
    for g2 in range(0, 32, 2):
        gds = [g2, 32 + g2, g2 + 1, 32 + g2 + 1]
        hinit = []
        for i, gd in enumerate(gds):
            P.op("act", lambda a, i=i, gd=gd: a.copy(out=hin[i][0:64, :], in_=LRE[:, gd, :]), r=[t_Ld[gd // 32]], w=[t_hin[i]])
            P.op("dve", lambda v, i=i, gd=gd: v.tensor_copy(out=hin[i][64:128, :], in_=LIM[:, gd, :]), r=[t_Ld[gd // 32]], w=[t_hin[i]])
            hinit.append((hin[i][:], t_hin[i]))
        ssm_rvm(P, C, S, UT, t_UT, gds, Hs, t_Hs, ps, t_ps, la, t_la, hinit=hinit)
        for k2 in range(2):
            g = g2 + k2
            yp = yps[k2]
            fns = []
            for ti in range(NT):
                for dd in range(2):
                    i = 2 * k2 + dd
                    gd = gds[i]
                    lhsT = Hs[i][:, ti * 8:(ti + 1) * 8, :].rearrange("p c j -> p (c j)")
                    fns.append(lambda pe, yp=yp, ti=ti, lhsT=lhsT, gd=gd, dd=dd: pe.matmul(yp[:, ti * 16:(ti + 1) * 16], lhsT, S.CW[:, gd, :], start=(dd == 0), stop=(dd == 1)))
            P.mm(fns, r=[t_Hs[2 * k2], t_Hs[2 * k2 + 1], S.t_CW], w=[t_yps[k2]])
            P.op("act" if k2 == 0 else "dve",
                 (lambda a, g=g, yp=yp: a.copy(out=yTM[:, :, 16 * g:16 * g + 16], in_=yp[:, 0:NT * 16].rearrange("p (t h) -> p t h", h=16))) if k2 == 0 else
                 (lambda v, g=g, yp=yp: v.tensor_copy(out=yTM[:, :, 16 * g:16 * g + 16], in_=yp[:, 0:NT * 16].rearrange("p (t h) -> p t h", h=16))),
                 r=[t_yps[k2]], w=[t_y])
    Cs.close()
    Cg = Ctx(P)
    dsk = R.get("ssm_d", [R.nl, 512])
    wglu = R.get("ssm_w_glu", [R.nl, 512, 512])
    bglu = R.get("ssm_b_glu", [R.nl, 512])
    DS = Cg.sb([128, 4], F32, "DS")
    BG = Cg.sb([128, 4], F32, "BG")
    WG = Cg.sb([128, 4, 512], BF16, "WG")
    t_g = Tok()
    P.dma("sp", DS[:], dsk[li].rearrange("(c p) -> p c", p=128), w=[t_g], allow_slow_non_contiguous=True)
    P.dma("sp", BG[:], bglu[li].rearrange("(c p) -> p c", p=128), w=[t_g], allow_slow_non_contiguous=True)
    P.dma("pool", WG[:], wglu[li].rearrange("(c p) n -> p c n", p=128), w=[t_g])
    uf = [Cg.sb([128, 4, 128], F32, "uf") for _ in range(2)]
    t_uf = [Tok() for _ in range(2)]
    pt = [Cg.ps([128, 512], F32, "ptr") for _ in range(2)]
    t_pt = [Tok() for _ in range(2)]
    y2 = [Cg.sb([128, 4, 128], F32, "y2") for _ in range(2)]
    t_y2 = [Tok() for _ in range(2)]
    gf = [Cg.sb([128, 4, 128], F32, "gf") for _ in range(2)]
    gb = [Cg.sb([128, 4, 128], BF16, "gb") for _ in range(2)]
    t_gf = [Tok() for _ in range(2)]
    pg = [Cg.ps([128, 512], F32, "pg") for _ in range(2)]
    t_pg = [Tok() for _ in range(2)]
    sg = [Cg.sb([128, 4, 128], F32, "sg") for _ in range(2)]
    t_sg = [Tok() for _ in range(2)]
    ob = [Cg.sb([128, 4, 128], BF16, "ob") for _ in range(2)]
    t_ob = [Tok() for _ in range(2)]
    tiles = list(range(NT)) if ctx_full else list(range(2, NT))
    def u_stage1(n_, ti):
        i = n_ % 2
        tsl = slice(ti * 128, (ti + 1) * 128)
        P.dma("sp", uf[i][:], uT[:, tsl].rearrange("(c p) t -> p c t", p=128), w=[t_uf[i]])
        P.mm([(lambda pe, c=c, i=i, ti=ti: pe.transpose(pt[i][:, c * 128:(c + 1) * 128], yTM[:, ti, c * 128:(c + 1) * 128], K["idf"][:])) for c in range(4)],
             r=[t_y, K["tok"]], w=[t_pt[i]])
        for c in range(4):
            P.op("dve", lambda v, c=c, i=i: v.scalar_tensor_tensor(out=y2[i][:, c, :], in0=uf[i][:, c, :], scalar=DS[:, c:c + 1], in1=pt[i][:, c * 128:(c + 1) * 128],
                                                                  op0=ALU.mult, op1=ALU.add), r=[t_uf[i], t_pt[i], t_g], w=[t_y2[i]])
        P.op("act", lambda a, i=i: a.activation(out=gf[i][:], in_=y2[i][:], func=AF.Gelu_apprx_tanh), r=[t_y2[i]], w=[t_gf[i]])
        P.op("pool", lambda g_, i=i: g_.tensor_copy(out=gb[i][:], in_=gf[i][:]), r=[t_gf[i]], w=[t_gf[i]])
        fns = []
        for co in range(4):
            for c in range(4):
                fns.append(lambda pe, co=co, c=c, i=i: pe.matmul(pg[i][:, co * 128:(co + 1) * 128], WG[:, c, co * 128:(co + 1) * 128], gb[i][:, c, :], start=(c == 0), stop=(c == 3)))
        P.mm(fns, r=[t_g, t_gf[i]], w=[t_pg[i]])

    def u_stage2(n_, ti):
        i = n_ % 2
        tsl = slice(ti * 128, (ti + 1) * 128)
        for co in range(4):
            P.op("act", lambda a, co=co, i=i: a.activation(out=sg[i][:, co, :], in_=pg[i][:, co * 128:(co + 1) * 128], func=AF.Sigmoid, bias=BG[:, co:co + 1]),
                 r=[t_pg[i], t_g], w=[t_sg[i]])
        P.op("dve", lambda v, i=i: v.tensor_tensor(out=ob[i][:], in0=gf[i][:], in1=sg[i][:], op=ALU.mult), r=[t_gf[i], t_sg[i]], w=[t_ob[i]])
        P.dma("sp", oTs[:, tsl].rearrange("(c p) t -> p c t", p=128), ob[i][:], r=[t_ob[i]], w=[R.tk("oT_s", ti)])

    for n_ in range(len(tiles) + 1):
        if n_ < len(tiles):
            u_stage1(n_, tiles[n_])
        if n_ >= 1:
            u_stage2(n_ - 1, tiles[n_ - 1])
    Cg.close()
    C.close()


def ssm_chain1_async(P, R, Cq, S):
    recv = R.get("ssm_recv", [64, 2, 32])
    rin = Cq.sb([64, 2, 32], F32, "rin")
    t_rin = Tok()
    P.dma("sp", rin[:], recv, r=[R.tk("ssm_recv", 0)], w=[t_rin])
    ssm_chain(P, Cq, S, S.LL, S.t_Ld[1], 1, list(range(NCH - 1, NCC - 1, -1)), (rin[:], t_rin), eng="pool")
    ssm_chain(P, Cq, S, S.LL, S.t_Ld[1], 1, list(range(NCC - 1, -1, -1)), None, eng="pool")
    S.chain1_done = True

def phase_F(P, R, K, xname):
    x = R.get(xname, [NTOK, D])
    y = R.get("y_out", [NLAT, D])
    gf = R.get("g_final", [D])
    C = Ctx(P)
    G = C.sb([128, D], F32, "gfin")
    t_G = Tok()
    P.dma("sp", G[:], gf.partition_broadcast(128), w=[t_G])
    xt = [C.sb([128, D], F32, "xt") for _ in range(2)]
    t_xt = [Tok() for _ in range(2)]
    junk = C.sb([128, D], BF16, "junk")
    t_j = Tok()
    st = [C.sb([128, 4], F32, "st") for _ in range(2)]
    t_st = [Tok() for _ in range(2)]
    yo = [C.sb([128, D], F32, "yo") for _ in range(2)]
    t_yo = [Tok() for _ in range(2)]
    for n_, ti in enumerate(range(2, NT)):
        i = n_ % 2
        P.dma("sp", xt[i][:], x[ti * 128:(ti + 1) * 128, :], r=[R.tk(xname, ti)], w=[t_xt[i]])
        P.op("act", lambda a, i=i: a.activation(out=junk[:], in_=xt[i][:], func=AF.Square, accum_out=st[i][:, 0:1]), r=[t_xt[i]], w=[t_j, t_st[i]])
        P.op("act", lambda a, i=i: a.activation(out=st[i][:, 1:2], in_=st[i][:, 0:1], func=AF.Sqrt, scale=1.0 / D, bias=EPS), r=[t_st[i]], w=[t_st[i]])
        P.op("dve", lambda v, i=i: v.reciprocal(out=st[i][:, 2:3], in_=st[i][:, 1:2]), r=[t_st[i]], w=[t_st[i]])
        P.op("dve", lambda v, i=i: v.scalar_tensor_tensor(out=yo[i][:], in0=xt[i][:], scalar=st[i][:, 2:3], in1=G[:], op0=ALU.mult, op1=ALU.mult),
             r=[t_xt[i], t_st[i], t_G], w=[t_yo[i]])
        P.dma("sp", y[(ti - 2) * 128:(ti - 1) * 128, :], yo[i][:], r=[t_yo[i]], w=[R.tk("y_out", ti)], is_out=True)
    C.close()


CORES = [(b, s) for b in range(4) for s in range(2)]


def prep_core_layer(inputs, l, s):
    g = lambda k: np.asarray(inputs[k])[l:l + 1]
    m = {}
    for k in ("w_out", "ffn_w_up", "ffn_w_down", "ffn_conv_b", "attn_sink", "ssm_d", "ssm_w_glu", "ssm_b_glu", "w_ada", "b_ada",
              "g_mix", "g_ffn", "gmlp_ln_g", "gmlp_ln_b"):
        m[k] = g(k)
    cw = g("ffn_conv_w")
    m["ffn_conv_w"] = cw[:, ::-1] if s == 1 else cw
    for k in ("ssm_lambda_re", "ssm_lambda_im", "ssm_log_dt", "ssm_b_re", "ssm_b_im", "ssm_c_re", "ssm_c_im"):
        a = g(k)
        m[k] = a[:, ::-1] if s == 1 else a
    ws = g("gmlp_w_s")
    bs = g("gmlp_b_s")
    if s == 1:
        ws = ws[:, :, ::-1, ::-1]
        bs = bs[:, :, ::-1]
    m["gmlp_wsT"] = ws.transpose(0, 1, 3, 2)
    m["gmlp_b_s"] = bs
    return {k: np.ascontiguousarray(v, dtype=np.float32) for k, v in m.items()}


def run_launch(build_fn, ins, outs, core_maps):
    nc = bass.Bass("TRN2", target_bir_lowering=False)
    R = Reg(nc, ins=ins, outs=outs, per_layer=True)
    P = Prog(nc)
    Ck = Ctx(P)
    K = build_consts(P, Ck)
    build_fn(P, R, K)
    P.finish()
    in_maps = [{k: m[k] for k in ins} for m in core_maps]
    res = run_bass_kernel_spmd(nc, in_maps, core_ids=list(range(len(core_maps))))
    return res.results


W_A = ["w_inx", "g_mix", "gmlp_ln_g", "gmlp_ln_b", "gmlp_wsT", "gmlp_b_s"]
W_B = ["ssm_lambda_re", "ssm_lambda_im", "ssm_log_dt", "ssm_b_re", "ssm_b_im", "ssm_c_re", "ssm_c_im"]
W_B2 = ["ssm_d", "ssm_w_glu", "ssm_b_glu"]


def kernel_unfused(inputs):
    import ml_dtypes
    bf = ml_dtypes.bfloat16
    pq, pk = _qsw_perm()
    st = []
    for (b, s) in CORES:
        m = prep_core(inputs, {}, b, s)
        st.append({"x": m["x_in"], "rowcol": m["rowcol"], "cvec": m["cvec"]})
    for l in range(DEPTH):
        w_in = np.asarray(inputs["w_in"])[l:l + 1]
        q = w_in[:, :, 0:1024]
        k = w_in[:, :, 1024:1152]
        w_inx = np.ascontiguousarray(np.concatenate([q, q[:, :, pq], k, k[:, :, pk], w_in[:, :, 1152:]], axis=2), dtype=np.float32)
        lw = [prep_core_layer(inputs, l, s) for s in range(2)]
        maps = []
        for ci, (b, s) in enumerate(CORES):
            m = dict(lw[s])
            m["w_inx"] = w_inx
            m.update({"x_l": st[ci]["x"], "rowcol": st[ci]["rowcol"], "cvec": st[ci]["cvec"]})
            maps.append(m)
        ins1 = ["cvec", "w_ada", "b_ada", "rowcol", "x_l"] + W_A + W_B
        outs1 = ["mod", "qT", "kT", "v", "uT", "oT_g", "ssm_send", "hinit0"]

        def b1(P, R, K, l=l):
            phase_mod(P, R, [l])
            phase_rope(P, R)
            phase_A(P, R, K, l, "x_l")
            phase_B(P, R, K, l, "a")
        r1 = run_launch(b1, ins1, outs1, maps)
        for ci in range(8):
            pr = ci ^ 1
            m = maps[ci]
            for kk in ("mod", "qT", "uT", "oT_g", "hinit0"):
                m[kk] = r1[ci][kk]
            kT = np.array(r1[ci]["kT"])
            kT[:, NTOK:] = np.asarray(r1[pr]["kT"])[:, NTOK - 128:NTOK]
            vv = np.array(r1[ci]["v"])
            vv[NTOK:] = np.asarray(r1[pr]["v"])[NTOK - 128:NTOK]
            m["kT"], m["v"] = kT, vv
            m["ssm_recv"] = np.asarray(r1[pr]["ssm_send"])
        ins2 = ["mod", "x_l", "qT", "kT", "v", "uT", "oT_g", "hinit0", "ssm_recv", "attn_sink", "w_out"] + W_B + W_B2
        outs2 = ["x_mid"]

        def b2(P, R, K, l=l):
            phase_B(P, R, K, l, "b")
            phase_C(P, R, K, l)
            phase_D(P, R, K, l, "x_l", "x_mid")
        r2 = run_launch(b2, ins2, outs2, maps)
        for ci in range(8):
            pr = ci ^ 1
            maps[ci]["x_mid"] = r2[ci]["x_mid"]
            maps[ci]["xb_recv"] = np.ascontiguousarray(np.asarray(r2[pr]["x_mid"])[NTOK - 1:NTOK])
        ins3 = ["mod", "x_mid", "xb_recv", "g_ffn", "ffn_w_up", "ffn_w_down", "ffn_conv_w", "ffn_conv_b"]
        last = l == DEPTH - 1
        outs3 = ["y_out"] if last else ["x_next"]
        if last:
            ins3 = ins3 + ["g_final"]
            for m in maps:
                m["g_final"] = np.ascontiguousarray(inputs["g_final"], dtype=np.float32)

        def b3(P, R, K, l=l, last=last):
            phase_E(P, R, K, l, "x_mid", "x_next")
            if last:
                phase_F(P, R, K, "x_next")
        r3 = run_launch(b3, ins3, outs3, maps)
        if not last:
            for ci in range(8):
                st[ci]["x"] = r3[ci]["x_next"]
    out = np.zeros((4, 4096, D), np.float32)
    for ci, (b, s) in enumerate(CORES):
        y = np.asarray(r3[ci]["y_out"])
        if s == 0:
            out[b, :NLAT] = y
        else:
            out[b, NLAT:] = y[::-1]
    return out


PAIRS = [[0, 1], [2, 3], [4, 5], [6, 7]]


def load_sel(P, C, R):
    psel = R.get("psel", [2])
    SEL = C.sb([128, 2], F32, "SEL")
    t = Tok()
    P.dma("sp", SEL[:], psel.partition_broadcast(128), w=[t])
    return SEL, t


def phase_X1(P, R, K):
    kT = R.get("kT", [128, NTOK + 128], BF16)
    vv = R.get("v", [NTOK + 128, 128], BF16)
    send = R.get("ssm_send", [64, 2, 32])
    recv = R.get("ssm_recv", [64, 2, 32])
    s1 = R.get("send1", [128, 320])
    ag1 = R.get("ag1", [256, 320])
    C = Ctx(P)
    SEL, t_sel = load_sel(P, C, R)
    kb = C.sb([128, 256], BF16, "x1kb")
    PK = C.sb([128, 320], F32, "x1pk")
    t_kb, t_pk = Tok(), Tok()
    P.dma("sp", kb[:, 0:128], kT[:, NTOK - 128:NTOK], w=[t_kb])
    P.dma("sp", kb[:, 128:256], vv[NTOK - 128:NTOK, :], w=[t_kb])
    P.op("pool", lambda g: g.memset(PK[:, 256:320], 0.0), w=[t_pk])
    P.op("dve", lambda v: v.tensor_copy(out=PK[:, 0:256], in_=kb[:]), r=[t_kb], w=[t_pk])
    P.dma("sp", PK[0:64, 256:320], send.rearrange("p r g -> p (r g)"), r=[t_pk], w=[t_pk])
    t_s1, t_ag = Tok(), Tok()
    P.dma("sp", s1, PK[:], r=[t_pk], w=[t_s1])
    P.collective(s1, ag1, PAIRS, r=[t_s1], w=[t_ag])
    G = C.sb([128, 2, 320], F32, "x1g")
    t_G = Tok()
    P.dma("sp", G[:], ag1.rearrange("(r p) c -> p r c", p=128), r=[t_ag], w=[t_G])
    R1 = C.sb([128, 320], F32, "x1r")
    kb2 = C.sb([128, 256], BF16, "x1kb2")
    t_R1 = Tok()
    P.op("dve", lambda v: v.tensor_scalar(out=R1[:], in0=G[:, 0, :], scalar1=SEL[:, 0:1], scalar2=None, op0=ALU.mult), r=[t_G, t_sel], w=[t_R1])
    P.op("dve", lambda v: v.scalar_tensor_tensor(out=R1[:], in0=G[:, 1, :], scalar=SEL[:, 1:2], in1=R1[:], op0=ALU.mult, op1=ALU.add), r=[t_G, t_sel, t_R1], w=[t_R1])
    P.op("dve", lambda v: v.tensor_copy(out=kb2[:], in_=R1[:, 0:256]), r=[t_R1], w=[t_R1])
    P.dma("sp", kT[:, NTOK:NTOK + 128], kb2[:, 0:128], r=[t_R1], w=[R.tk("kT", "halo")])
    P.dma("sp", vv[NTOK:NTOK + 128, :], kb2[:, 128:256], r=[t_R1], w=[R.tk("v", "halo")])
    P.dma("sp", recv.rearrange("p r g -> p (r g)"), R1[0:64, 256:320], r=[t_R1], w=[R.tk("ssm_recv", 0)])
    C.close()


def phase_X2(P, R, K, xname):
    x = R.get(xname, [NTOK, D])
    xb = R.get("xb_recv", [1, D])
    s2 = R.get("send2", [1, D])
    ag2 = R.get("ag2", [2, D])
    C = Ctx(P)
    SEL, t_sel = load_sel(P, C, R)
    row = C.sb([1, D], F32, "x2row")
    t_row, t_s2, t_ag = Tok(), Tok(), Tok()
    P.dma("sp", row[:], x[NTOK - 1:NTOK, :], w=[t_row])
    P.dma("sp", s2, row[:], r=[t_row], w=[t_s2])
    P.collective(s2, ag2, PAIRS, r=[t_s2], w=[t_ag])
    G = C.sb([1, 2, D], F32, "x2g")
    t_G = Tok()
    P.dma("sp", G[:], ag2.rearrange("(o r) c -> o r c", o=1), r=[t_ag], w=[t_G])
    P.op("dve", lambda v: v.tensor_scalar(out=row[:], in0=G[:, 0, :], scalar1=SEL[0:1, 0:1], scalar2=None, op0=ALU.mult), r=[t_G, t_sel, t_row], w=[t_row])
    P.op("dve", lambda v: v.scalar_tensor_tensor(out=row[:], in0=G[:, 1, :], scalar=SEL[0:1, 1:2], in1=row[:], op0=ALU.mult, op1=ALU.add), r=[t_G, t_sel, t_row], w=[t_row])
    P.dma("sp", xb, row[:], r=[t_row], w=[R.tk("xb_recv")])
    C.close()


FUSED_INS = ["x_in", "rowcol", "cvec", "psel", "w_ada", "b_ada", "g_mix", "g_ffn", "w_inx", "w_out", "attn_sink",
             "ssm_lambda_re", "ssm_lambda_im", "ssm_log_dt", "ssm_b_re", "ssm_b_im", "ssm_c_re", "ssm_c_im", "ssm_d", "ssm_w_glu", "ssm_b_glu",
             "gmlp_ln_g", "gmlp_ln_b", "gmlp_wsT", "gmlp_b_s", "ffn_w_up", "ffn_conv_w", "ffn_conv_b", "ffn_w_down", "g_final"]


def build_fused(nlayers=DEPTH):
    nc = bass.Bass("TRN2", target_bir_lowering=False)
    R = Reg(nc, ins=FUSED_INS, outs=["y_out"], per_layer=False)
    P = Prog(nc)
    Ck = Ctx(P)
    K = build_consts(P, Ck)
    phase_mod(P, R, [0])
    phase_rope(P, R)
    xcur = "x_in"
    for l in range(nlayers):
        xnext = "x_res%d" % (l % 2)
        phase_A(P, R, K, l, xcur)
        Cq = Ctx(P)
        LLq = Cq.sb([64, 2, 64, NCH], F32, "LLq")
        S = ssm_prep(P, R, K, Cq, l)
        S.LL, S.t_Ld = LLq, [Tok(), Tok()]
        phase_B(P, R, K, l, "a", S=S)
        phase_X1(P, R, K)
        phase_B(P, R, K, l, "b", S=S)
        Cq.close()
        phase_C(P, R, K, l)
        if l + 1 < nlayers:
            Cm = Ctx(P)
            side = phase_mod_gen(P, R, [l + 1], Cm)
            next(side)
            phase_D(P, R, K, l, xcur, "x_mid", side=side)
            Cm.close()
        else:
            phase_D(P, R, K, l, xcur, "x_mid")
        phase_X2(P, R, K, "x_mid")
        phase_E(P, R, K, l, "x_mid", xnext)
        xcur = xnext
    phase_F(P, R, K, xcur)
    P.finish()
    return nc, P


def fused_inputs(inputs):
    pq, pk = _qsw_perm()
    w_in = np.asarray(inputs["w_in"])
    q = w_in[:, :, 0:1024]
    k = w_in[:, :, 1024:1152]
    w_inx = np.ascontiguousarray(np.concatenate([q, q[:, :, pq], k, k[:, :, pk], w_in[:, :, 1152:]], axis=2), dtype=np.float32)
    f32 = lambda a: np.ascontiguousarray(a, dtype=np.float32)
    shared = {"w_inx": w_inx}
    for k_ in ("w_ada", "b_ada", "g_mix", "g_ffn", "w_out", "attn_sink", "ssm_d", "ssm_w_glu", "ssm_b_glu", "gmlp_ln_g", "gmlp_ln_b",
               "ffn_w_up", "ffn_conv_b", "ffn_w_down", "g_final"):
        shared[k_] = f32(inputs[k_])
    half = []
    for s in range(2):
        m = {}
        cw = np.asarray(inputs["ffn_conv_w"])
        m["ffn_conv_w"] = f32(cw[:, ::-1] if s == 1 else cw)
        for k_ in ("ssm_lambda_re", "ssm_lambda_im", "ssm_log_dt", "ssm_b_re", "ssm_b_im", "ssm_c_re", "ssm_c_im"):
            a = np.asarray(inputs[k_])
            m[k_] = f32(a[:, ::-1] if s == 1 else a)
        ws = np.asarray(inputs["gmlp_w_s"])
        bs = np.asarray(inputs["gmlp_b_s"])
        if s == 1:
            ws = ws[:, :, ::-1, ::-1]
            bs = bs[:, :, ::-1]
        m["gmlp_wsT"] = f32(ws.transpose(0, 1, 3, 2))
        m["gmlp_b_s"] = f32(bs)
        m["psel"] = np.array([0.0, 1.0] if s == 0 else [1.0, 0.0], np.float32)
        half.append(m)
    maps = []
    for (b, s) in CORES:
        m = dict(shared)
        m.update(half[s])
        pc = prep_core(inputs, {}, b, s)
        m["x_in"], m["rowcol"], m["cvec"] = pc["x_in"], pc["rowcol"], pc["cvec"]
        maps.append({k_: m[k_] for k_ in FUSED_INS})
    return maps


def kernel(**inputs):
    nc, P = build_fused()
    maps = fused_inputs(inputs)
    res = run_bass_kernel_spmd(nc, maps, core_ids=list(range(8)))
    out = np.zeros((4, 4096, D), np.float32)
    for ci, (b, s) in enumerate(CORES):
        y = np.asarray(res.results[ci]["y_out"])
        if s == 0:
            out[b, :NLAT] = y
        else:
            out[b, NLAT:] = y[::-1]
    return out
```

```python
import numpy as np
from contextlib import ExitStack
import concourse.bass as bass
import concourse.mybir as mybir
from concourse.bass_utils import run_bass_kernel_spmd

F32 = mybir.dt.float32
BF16 = mybir.dt.bfloat16
I32 = mybir.dt.int32
AF = mybir.ActivationFunctionType
ALU = mybir.AluOpType
AX = mybir.AxisListType

D = 2048
NLAT = 2048
NCTX = 256
NTOK = NLAT + NCTX
NT = NTOK // 128
DEPTH = 4
DFF = 5632
NFF = DFF // 128
EPS = 1e-6
TWO_PI = 6.283185307179586
PI = 3.141592653589793
BLOCKS = [(0, 256)] + [(256 + 512 * i, 512) for i in range(4)]


class Tok:
    __slots__ = ("w", "r", "name")

    def __init__(self, name=""):
        self.w = None
        self.r = []
        self.name = name


class Eng:
    def __init__(self, name, h, selfsync):
        self.name = name
        self.h = h
        self.sem = None
        self.cnt = 0
        self.seen = {}
        self.selfsync = selfsync


class Prog:
    SEM_EPOCH = 12000
    NDMA = 12

    def __init__(self, nc):
        self.nc = nc
        self.stack = ExitStack()
        self.nsem = 0
        self.eng = {
            "pe": Eng("pe", nc.tensor, False),
            "act": Eng("act", nc.scalar, True),
            "dve": Eng("dve", nc.vector, True),
            "pool": Eng("pool", nc.gpsimd, True),
            "sp": Eng("sp", nc.sync, False),
        }
        for e in self.eng.values():
            self._new_sem(e)
        self.dq = {}
        for q in ("sp", "pool", "act"):
            sems = [self._sem("d%s%d" % (q, i)) for i in range(self.NDMA)]
            self.dq[q] = {"sems": sems, "cnt": [0] * self.NDMA, "i": 0}
        self.out_stamps = []

    def _sem(self, name):
        self.nsem += 1
        return self.stack.enter_context(self.nc.semaphore("%s_%d" % (name, self.nsem)))

    def _new_sem(self, e):
        e.sem = self._sem("e" + e.name)
        e.cnt = 0

    def close(self):
        self.stack.close()

    def _wait(self, e, stamp):
        sem, val = stamp
        key = id(sem)
        if e.seen.get(key, 0) >= val:
            return
        e.h.wait_ge(sem, val)
        e.seen[key] = val

    def _deps(self, e, r, w):
        for t in r:
            if t.w is not None:
                if t.w[0] is e.sem and not e.selfsync:
                    continue
                self._wait(e, t.w)
        for t in w:
            if t.w is not None:
                if not (t.w[0] is e.sem and not e.selfsync):
                    self._wait(e, t.w)
            for st in t.r:
                if st[0] is e.sem:
                    continue
                self._wait(e, st)

    def _stamp(self, st, r, w):
        for t in r:
            t.r = [s for s in t.r if s[0] is not st[0]] + [st]
        for t in w:
            t.w = st
            t.r = []

    def op(self, eng, fn, r=(), w=()):
        e = self.eng[eng]
        self._deps(e, r, w)
        ins = fn(e.h)
        if e.cnt >= self.SEM_EPOCH:
            self._new_sem(e)
        e.cnt += 1
        ins.then_inc(e.sem, 1)
        self._stamp((e.sem, e.cnt), r, w)

    def mm(self, fns, r=(), w=()):
        e = self.eng["pe"]
        self._deps(e, r, w)
        ins = None
        for fn in fns:
            ins = fn(e.h)
        if e.cnt >= self.SEM_EPOCH:
            self._new_sem(e)
        e.cnt += 1
        ins.then_inc(e.sem, 1)
        self._stamp((e.sem, e.cnt), r, w)

    def dma(self, q, out, in_, r=(), w=(), is_out=False, **kw):
        e = self.eng[q]
        dq = self.dq[q]
        slot = dq["i"] % self.NDMA
        dq["i"] += 1
        sem = dq["sems"][slot]
        prev = dq["cnt"][slot]
        if prev > 0:
            self._wait(e, (sem, prev))
        self._deps(e, r, w)
        ins = e.h.dma_start(out=out, in_=in_, **kw)
        ins.then_inc(sem, 16)
        dq["cnt"][slot] = prev + 16
        st = (sem, prev + 16)
        self._stamp(st, r, w)
        if is_out:
            self.out_stamps.append(st)

    def collective(self, ins_ap, outs_ap, groups, r=(), w=()):
        e = self.eng["pool"]
        if not hasattr(self, "cc_sem"):
            self.cc_sem = self._sem("cc")
            self.cc_cnt = 0
        self._deps(e, r, w)
        ins = self.nc.gpsimd.collective_compute("AllGather", ALU.bypass, replica_groups=groups, ins=[ins_ap], outs=[outs_ap])
        self.cc_cnt += 1
        ins.then_inc(self.cc_sem, 1)
        self._stamp((self.cc_sem, self.cc_cnt), r, w)

    def barrier(self):
        for e in self.eng.values():
            for o in self.eng.values():
                if o is e or o.cnt == 0:
                    continue
                self._wait(e, (o.sem, o.cnt))
            for q, dq in self.dq.items():
                for s_, c in zip(dq["sems"], dq["cnt"]):
                    if c > 0:
                        self._wait(e, (s_, c))
            if getattr(self, "cc_cnt", 0) > 0:
                self._wait(e, (self.cc_sem, self.cc_cnt))

    def finish(self):
        self.barrier()
        e = self.eng["sp"]
        for st in self.out_stamps:
            self._wait(e, st)
        for q, dq in self.dq.items():
            for s, c in zip(dq["sems"], dq["cnt"]):
                if c > 0:
                    self._wait(e, (s, c))


class Ctx:
    def __init__(self, P):
        self.P = P
        self.nc = P.nc
        self.stack = ExitStack()
        self.n = 0

    UID = [0]

    def sb(self, shape, dt, name="t"):
        Ctx.UID[0] += 1
        return self.stack.enter_context(self.nc.sbuf_tensor("%s_%d" % (name, Ctx.UID[0]), list(shape), dt))

    def ps(self, shape, dt, name="p"):
        Ctx.UID[0] += 1
        return self.stack.enter_context(self.nc.psum_tensor("%s_%d" % (name, Ctx.UID[0]), list(shape), dt))

    def close(self):
        self.P.barrier()
        self.stack.close()


def bcast_rows(ap1d, nrows):
    return ap1d.rearrange("(o n) -> o n", o=1).broadcast(0, nrows)


class Reg:
    def __init__(self, nc, ins=(), outs=(), per_layer=False):
        self.nc = nc
        self.nl = 1 if per_layer else DEPTH
        self.per_layer = per_layer
        self.ins = set(ins)
        self.outs = set(outs)
        self.t = {}
        self.tok = {}

    def get(self, name, shape=None, dt=F32):
        if name not in self.t:
            kind = "ExternalInput" if name in self.ins else ("ExternalOutput" if name in self.outs else "Internal")
            self.t[name] = self.nc.dram_tensor(name, list(shape), dt, kind=kind).ap()
            self.tok[name] = Tok(name)
        return self.t[name]


def build_consts(P, C):
    nc = P.nc
    K = {}
    idf = C.sb([128, 128], F32, "idf")
    idb = C.sb([128, 128], BF16, "idb")
    io = C.sb([128, 128], I32, "io")
    t_id = Tok("ident")
    P.op("pool", lambda g: g.iota(io[:], [[1, 128]], base=0, channel_multiplier=-1), w=[t_id])
    P.op("dve", lambda v: v.tensor_copy(out=idf[:], in_=io[:]), r=[t_id], w=[t_id])
    P.op("dve", lambda v: v.tensor_scalar(out=idf[:], in0=idf[:], scalar1=0.0, scalar2=None, op0=ALU.is_equal), r=[t_id], w=[t_id])
    P.op("dve", lambda v: v.tensor_copy(out=idb[:], in_=idf[:]), r=[t_id], w=[t_id])
    K["idf"] = idf
    K["idb"] = idb
    K["tok"] = t_id
    return K


def reg_li(R, l):
    return 0 if R.per_layer else l


Reg.li = reg_li


def reg_tk(R, name, i=0):
    key = (name, i)
    if key not in R.tok:
        R.tok[key] = Tok("%s[%s]" % (name, i))
    return R.tok[key]


Reg.tk = reg_tk


def phase_mod(P, R, layers):
    C = Ctx(P)
    for _ in phase_mod_gen(P, R, layers, C):
        pass
    C.close()


def phase_mod_gen(P, R, layers, C):
    cvec = R.get("cvec", [2, D])
    w_ada = R.get("w_ada", [R.nl, D, 6 * D])
    b_ada = R.get("b_ada", [R.nl, 6 * D])
    mod = R.get("mod", [DEPTH, 2, 6 * D])
    cc = C.sb([128, 16, 2], F32, "cc")
    t_cc = Tok()
    for j in range(2):
        P.dma("sp", cc[:, :, j], cvec[j].rearrange("(k p) -> p k", p=128), w=[t_cc], allow_slow_non_contiguous=True)
    act = C.sb([128, 16, 2], F32, "act")
    P.op("act", lambda a: a.activation(out=act[:], in_=cc[:], func=AF.Silu), r=[t_cc], w=[t_cc])
    NB = 2
    wb = [C.sb([128, 16, 512], BF16, "wada") for _ in range(NB)]
    actb = C.sb([128, 16, 2], BF16, "actb")
    P.op("dve", lambda v: v.tensor_copy(out=actb[:], in_=act[:]), r=[t_cc], w=[t_cc])
    t_wb = [Tok() for _ in range(NB)]
    bb = [C.sb([2, 512], F32, "bada") for _ in range(NB)]
    ob = [C.sb([2, 512], F32, "oada") for _ in range(NB)]
    t_ob = [Tok() for _ in range(NB)]
    ps = [C.ps([2, 512], F32, "psada") for _ in range(NB)]
    t_ps = [Tok() for _ in range(NB)]
    it = 0
    for l in layers:
        wv = w_ada[R.li(l)].rearrange("(k p) c -> p k c", p=128)
        for cb in range(24):
            i = it % NB
            it += 1
            cs = slice(cb * 512, (cb + 1) * 512)
            P.dma("pool", wb[i][:], wv[:, :, cs], w=[t_wb[i]])
            P.dma("sp", bb[i][:], b_ada[R.li(l), cs].partition_broadcast(2), w=[t_wb[i]])
            P.mm([(lambda pe, k=k, i=i: pe.matmul(ps[i][:], actb[:, k, :], wb[i][:, k, :], start=(k == 0), stop=(k == 15)))
                  for k in range(16)], r=[t_wb[i], t_cc], w=[t_ps[i]])
            P.op("dve", lambda v, i=i: v.tensor_tensor(out=ob[i][:], in0=ps[i][:], in1=bb[i][:], op=ALU.add),
                 r=[t_ps[i], t_wb[i]], w=[t_ob[i]])
            P.dma("sp", mod[l, :, cs], ob[i][:], r=[t_ob[i]], w=[R.tk("mod", l)])
            yield


NQ = 1024
OFFX_Q, OFFX_QS, OFFX_K, OFFX_KS, OFFX_V, OFFX_U, OFFX_GU, OFFX_GV = 0, 1024, 2048, 2176, 2304, 2432, 2944, 3456
NINX = 3968


def range_reduce_sin(P, C, out, ang, shape, tk, extra=0.0):
    ti = C.sb(shape, I32, "rr_i")
    tf = C.sb(shape, F32, "rr_f")
    tr = C.sb(shape, F32, "rr_r")
    t = Tok()
    P.op("dve", lambda v: v.tensor_scalar(out=ti[:], in0=ang, scalar1=extra, scalar2=1.0 / TWO_PI, op0=ALU.add, op1=ALU.mult), r=[tk], w=[t])
    P.op("dve", lambda v: v.tensor_copy(out=tf[:], in_=ti[:]), r=[t], w=[t])
    P.op("dve", lambda v: v.scalar_tensor_tensor(out=tr[:], in0=tf[:], scalar=-TWO_PI, in1=ang, op0=ALU.mult, op1=ALU.add), r=[t, tk], w=[t])
    if extra != 0.0:
        P.op("dve", lambda v: v.tensor_scalar(out=tr[:], in0=tr[:], scalar1=extra, scalar2=None, op0=ALU.add), r=[t], w=[t])
    P.op("dve", lambda v: v.tensor_scalar(out=tf[:], in0=tr[:], scalar1=PI, scalar2=-TWO_PI, op0=ALU.is_gt, op1=ALU.mult), r=[t], w=[t])
    P.op("dve", lambda v: v.tensor_tensor(out=tr[:], in0=tr[:], in1=tf[:], op=ALU.add), r=[t], w=[t])
    P.op("dve", lambda v: v.tensor_scalar(out=tf[:], in0=tr[:], scalar1=-PI, scalar2=TWO_PI, op0=ALU.is_lt, op1=ALU.mult), r=[t], w=[t])
    P.op("dve", lambda v: v.tensor_tensor(out=tr[:], in0=tr[:], in1=tf[:], op=ALU.add), r=[t], w=[t])
    P.op("act", lambda a: a.activation(out=out, in_=tr[:], func=AF.Sin), r=[t], w=[tk])


def phase_rope(P, R):
    C = Ctx(P)
    rowcol = R.get("rowcol", [2, NLAT])
    ropeT = R.get("ropeT", [2, 128, NTOK])
    pos = C.sb([128, NLAT], F32, "pos")
    t = Tok()
    for base in (0, 64):
        P.dma("sp", pos[base:base + 32, :], rowcol[0].partition_broadcast(32), w=[t])
        P.dma("sp", pos[base + 32:base + 64, :], rowcol[1].partition_broadcast(32), w=[t])
    pi_ = C.sb([128, 1], I32, "pidx")
    pf = C.sb([128, 4], F32, "pf")
    tp = Tok()
    P.op("pool", lambda g: g.iota(pi_[:], [[0, 1]], base=0, channel_multiplier=1), w=[tp])
    P.op("dve", lambda v: v.tensor_copy(out=pf[:, 0:1], in_=pi_[:]), r=[tp], w=[tp])
    pj = C.sb([128, 1], I32, "pj")
    P.op("dve", lambda v: v.tensor_scalar(out=pj[:], in0=pi_[:], scalar1=15, scalar2=None, op0=ALU.bitwise_and), r=[tp], w=[tp])
    P.op("dve", lambda v: v.tensor_copy(out=pf[:, 1:2], in_=pj[:]), r=[tp], w=[tp])
    P.op("act", lambda a: a.activation(out=pf[:, 2:3], in_=pf[:, 1:2], func=AF.Exp, scale=-float(np.log(10000.0)) / 16.0), r=[tp], w=[tp])
    P.op("dve", lambda v: v.tensor_scalar(out=pj[:], in0=pi_[:], scalar1=16, scalar2=None, op0=ALU.bitwise_and), r=[tp], w=[tp])
    P.op("dve", lambda v: v.tensor_copy(out=pf[:, 3:4], in_=pj[:]), r=[tp], w=[tp])
    P.op("dve", lambda v: v.tensor_scalar(out=pf[:, 3:4], in0=pf[:, 3:4], scalar1=1.0 / 8.0, scalar2=-1.0, op0=ALU.mult, op1=ALU.add), r=[tp], w=[tp])
    ang = C.sb([128, NLAT], F32, "ang")
    P.op("dve", lambda v: v.tensor_scalar(out=ang[:], in0=pos[:], scalar1=pf[:, 2:3], scalar2=None, op0=ALU.mult), r=[t, tp], w=[t])
    cs = C.sb([128, NTOK], F32, "cs")
    sn = C.sb([128, NTOK], F32, "sn")
    tc_, ts_ = Tok(), Tok()
    P.op("pool", lambda g: g.memset(cs[:, 0:NCTX], 1.0), w=[tc_])
    P.op("pool", lambda g: g.memset(sn[:, 0:NCTX], 0.0), w=[ts_])
    C2 = Ctx(P)
    range_reduce_sin(P, C2, sn[:, NCTX:], ang[:], [128, NLAT], t)
    P.op("dve", lambda v: v.tensor_scalar(out=sn[:, NCTX:], in0=sn[:, NCTX:], scalar1=pf[:, 3:4], scalar2=None, op0=ALU.mult), r=[t, tp], w=[t])
    P.dma("sp", ropeT[1], sn[:], r=[t, ts_], w=[R.tk("ropeT", 1)])
    t2 = Tok()
    t2.w = t.w
    range_reduce_sin(P, C2, cs[:, NCTX:], ang[:], [128, NLAT], t, extra=PI / 2)
    P.dma("sp", ropeT[0], cs[:], r=[t, tc_], w=[R.tk("ropeT", 0)])
    C2.close()
    C.close()


def load_bcast_mod(P, C, R, l, r, sec, name):
    mod = R.get("mod", [DEPTH, 2, 6 * D])
    t = C.sb([128, D], F32, name)
    tk = Tok(name)
    P.dma("sp", t[:], mod[l, r, sec * D:(sec + 1) * D].partition_broadcast(128), r=[R.tk("mod", l)], w=[tk])
    return t, tk


def norm_mod_transpose(P, R, K, C, l, xname, gname, sec_shift, sec_scale, hT_all, dst_fn=None, tok_fn=None, after=None, extra_row=None):
    x = R.get(xname, [NTOK, D])
    gvec = R.get(gname, [R.nl, D])
    C1 = Ctx(P)
    gb = C1.sb([128, D], F32, "gb")
    t_gb = Tok()
    P.dma("sp", gb[:], gvec[R.li(l)].partition_broadcast(128), w=[t_gb])
    GS, SH, tGS, tSH = [], [], [], []
    for r in range(2):
        sc, tsc = load_bcast_mod(P, C1, R, l, r, sec_scale, "gs%d" % r)
        sh, tsh = load_bcast_mod(P, C1, R, l, r, sec_shift, "sh%d" % r)
        P.op("dve", lambda v, sc=sc: v.scalar_tensor_tensor(out=sc[:], in0=sc[:], scalar=1.0, in1=gb[:], op0=ALU.add, op1=ALU.mult),
             r=[tsc, t_gb], w=[tsc])
        GS.append(sc); SH.append(sh); tGS.append(tsc); tSH.append(tsh)
    NB = 3
    xt = [C1.sb([128, D], F32, "xt") for _ in range(NB)]
    t_xt = [Tok() for _ in range(NB)]
    junk = C1.sb([128, D], BF16, "junk")
    t_junk = Tok()
    st = [C1.sb([128, 4], F32, "st") for _ in range(NB)]
    t_st = [Tok() for _ in range(NB)]
    t1 = [C1.sb([128, D], F32, "t1") for _ in range(NB)]
    t_t1 = [Tok() for _ in range(NB)]
    hb = [C1.sb([128, D], BF16, "hb") for _ in range(NB)]
    t_hb = [Tok() for _ in range(NB)]
    pT = [C1.ps([128, 1024], BF16, "pT") for _ in range(4)]
    t_pT = [Tok() for _ in range(4)]
    t_h = [(Tok("hTa%d" % i), Tok("hTb%d" % i)) for i in range(NT + 1)]
    if tok_fn is not None:
        t_h = [tok_fn(i) for i in range(NT + 1)]
    ntile = NT + (1 if extra_row is not None else 0)

    def stage_a(ti):
        i = ti % NB
        r = 1 if ti < NCTX // 128 else 0
        if ti < NT:
            P.dma("sp", xt[i][:], x[ti * 128:(ti + 1) * 128, :], r=[R.tk(xname, ti)], w=[t_xt[i]])
        else:
            P.op("pool", lambda g, i=i: g.memset(xt[i][:], 1.0), w=[t_xt[i]])
            P.dma("sp", xt[i][0:1, :], extra_row[0], r=[extra_row[1]], w=[t_xt[i]])
        P.op("act", lambda a, i=i: a.activation(out=junk[:], in_=xt[i][:], func=AF.Square, accum_out=st[i][:, 0:1]),
             r=[t_xt[i]], w=[t_junk, t_st[i]])
        P.op("act", lambda a, i=i: a.activation(out=st[i][:, 1:2], in_=st[i][:, 0:1], func=AF.Sqrt, scale=1.0 / D, bias=EPS),
             r=[t_st[i]], w=[t_st[i]])
        P.op("dve", lambda v, i=i: v.reciprocal(out=st[i][:, 2:3], in_=st[i][:, 1:2]), r=[t_st[i]], w=[t_st[i]])
        P.op("pool", lambda g, i=i, r=r: g.tensor_tensor(out=t1[i][:], in0=xt[i][:], in1=GS[r][:], op=ALU.mult),
             r=[t_xt[i], tGS[r]], w=[t_t1[i]])
        P.op("dve", lambda v, i=i, r=r: v.scalar_tensor_tensor(out=hb[i][:], in0=t1[i][:], scalar=st[i][:, 2:3], in1=SH[r][:],
                                                              op0=ALU.mult, op1=ALU.add),
             r=[t_t1[i], t_st[i], tSH[r]], w=[t_hb[i]])
        for h in range(2):
            pb_ = (ti % 2) * 2 + h
            P.mm([(lambda pe, k=k, pb_=pb_, i=i: pe.transpose(pT[pb_][:, (k % 8) * 128:(k % 8 + 1) * 128], hb[i][:, k * 128:(k + 1) * 128], K["idb"][:]))
                  for k in range(8 * h, 8 * h + 8)], r=[t_hb[i], K["tok"]], w=[t_pT[pb_]])

    def stage_b(ti):
        for h in range(2):
            pb_ = (ti % 2) * 2 + h
            dst = hT_all[:, 8 * h:8 * h + 8, ti * 128:(ti + 1) * 128] if dst_fn is None else dst_fn(ti, h)
            src = pT[pb_][:].rearrange("p (k t) -> p k t", k=8)
            if h == 0:
                P.op("act", lambda a, dst=dst, src=src: a.copy(out=dst, in_=src), r=[t_pT[pb_]], w=[t_h[ti][h]])
            else:
                P.op("dve", lambda v, dst=dst, src=src: v.tensor_copy(out=dst, in_=src), r=[t_pT[pb_]], w=[t_h[ti][h]])
        if after is not None:
            after(ti, t_h[ti])

    for ti in range(ntile + 1):
        if ti < ntile:
            stage_a(ti)
        if ti >= 1:
            stage_b(ti - 1)
    C1.close()
    return t_h


def toks_for(t_h, t0, n):
    out = []
    for ti in range(t0 // 128, (t0 + n + 127) // 128):
        out += list(t_h[ti])
    return out


def load_w_cols(P, C, wsrc, col_ranges, buf, tk, q="pool"):
    wv = wsrc.rearrange("(k p) c -> p k c", p=128)
    o = 0
    for (c0, n) in col_ranges:
        P.dma(q, buf[:, :, o:o + n], wv[:, :, c0:c0 + n], w=[tk])
        o += n


def phase_A(P, R, K, l, xname, ctx_full=True, parts="abc"):
    w_in = R.get("w_inx", [R.nl, D, NINX])[R.li(l):R.li(l) + 1]
    qT = R.get("qT", [NQ, NTOK], BF16)
    kT = R.get("kT", [128, NTOK + 128], BF16)
    vv = R.get("v", [NTOK + 128, 128], BF16)
    uT = R.get("uT", [512, NTOK])
    oT = R.get("oT_g", [512, NTOK], BF16)
    ropeT = R.get("ropeT", [2, 128, NTOK])
    C = Ctx(P)
    hT = C.sb([128, 16, NTOK], BF16, "hT_all")
    t_h = norm_mod_transpose(P, R, K, C, l, xname, "g_mix", 0, 1, hT)

    if "a" not in parts:
        C.close(); return
    Ca = Ctx(P)
    cs = Ca.sb([128, NTOK], F32, "cs")
    sn = Ca.sb([128, NTOK], F32, "sn")
    t_cs = Tok()
    P.dma("sp", cs[:], ropeT[0], r=[R.tk("ropeT", 0)], w=[t_cs])
    P.dma("sp", sn[:], ropeT[1], r=[R.tk("ropeT", 1)], w=[t_cs])
    NB = 2
    wq = [Ca.sb([128, 16, 1024], BF16, "wq") for _ in range(NB)]
    t_wq = [Tok() for _ in range(NB)]
    psA = [Ca.ps([128, 512], F32, "psA") for _ in range(4)]
    t_psA = [Tok() for _ in range(4)]
    r1 = [Ca.sb([128, 512], F32, "r1") for _ in range(2)]
    r2 = [Ca.sb([128, 512], F32, "r2") for _ in range(2)]
    ro = [Ca.sb([128, 512], BF16, "ro") for _ in range(2)]
    t_r1 = [Tok() for _ in range(2)]
    t_r2 = [Tok() for _ in range(2)]
    t_ro = [Tok() for _ in range(2)]
    it = 0
    wvA = w_in[0].rearrange("(k p) c -> p k c", p=128)
    for j in range(9):
        wi = (j // 4) % NB
        fo = (j % 4) * 128
        if j % 4 == 0 and j < 8:
            P.dma("pool", wq[wi][:, :, 0:512], wvA[:, :, OFFX_Q + j * 128:OFFX_Q + j * 128 + 512], w=[t_wq[wi]])
            P.dma("pool", wq[wi][:, :, 512:1024], wvA[:, :, OFFX_QS + j * 128:OFFX_QS + j * 128 + 512], w=[t_wq[wi]])
        elif j == 8:
            P.dma("pool", wq[wi][:, :, 0:128], wvA[:, :, OFFX_K:OFFX_K + 128], w=[t_wq[wi]])
            P.dma("pool", wq[wi][:, :, 512:640], wvA[:, :, OFFX_KS:OFFX_KS + 128], w=[t_wq[wi]])
        for (t0, n) in BLOCKS:
            i = it % 2
            it += 1
            pa, pb = psA[2 * i], psA[2 * i + 1]
            hs = toks_for(t_h, t0, n)
            P.mm([(lambda pe, k=k, pa=pa, wi=wi, t0=t0, n=n, fo=fo: pe.matmul(pa[:, :n], wq[wi][:, k, fo:fo + 128], hT[:, k, t0:t0 + n], start=(k == 0), stop=(k == 15)))
                  for k in range(16)], r=[t_wq[wi]] + hs, w=[t_psA[2 * i]])
            P.mm([(lambda pe, k=k, pb=pb, wi=wi, t0=t0, n=n, fo=fo: pe.matmul(pb[:, :n], wq[wi][:, k, 512 + fo:512 + fo + 128], hT[:, k, t0:t0 + n], start=(k == 0), stop=(k == 15)))
                  for k in range(16)], r=[t_wq[wi]] + hs, w=[t_psA[2 * i + 1]])
            P.op("dve", lambda v, i=i, pa=pa, t0=t0, n=n: v.tensor_tensor(out=r1[i][:, :n], in0=pa[:, :n], in1=cs[:, t0:t0 + n], op=ALU.mult),
                 r=[t_psA[2 * i], t_cs], w=[t_r1[i]])
            P.op("dve", lambda v, i=i, pb=pb, t0=t0, n=n: v.tensor_tensor(out=r2[i][:, :n], in0=pb[:, :n], in1=sn[:, t0:t0 + n], op=ALU.mult),
                 r=[t_psA[2 * i + 1], t_cs], w=[t_r2[i]])
            P.op("pool", lambda g, i=i, n=n: g.tensor_tensor(out=ro[i][:, :n], in0=r1[i][:, :n], in1=r2[i][:, :n], op=ALU.add),
                 r=[t_r1[i], t_r2[i]], w=[t_ro[i]])
            if j < 8:
                P.dma("sp", qT[j * 128:(j + 1) * 128, t0:t0 + n], ro[i][:, :n], r=[t_ro[i]], w=[R.tk("qT", (j, t0))])
            else:
                P.dma("sp", kT[:, t0:t0 + n], ro[i][:, :n], r=[t_ro[i]], w=[R.tk("kT", t0)])
    Ca.close()

    if "b" not in parts:
        C.close(); return
    Cb = Ctx(P)
    wuA = Cb.sb([128, 16, 512], BF16, "wu")
    t_wuA = Tok()
    load_w_cols(P, Cb, w_in[0], [(OFFX_U, 512)], wuA, t_wuA)
    psB = [Cb.ps([128, 512], F32, "psB") for _ in range(2)]
    t_psB = [Tok() for _ in range(2)]
    ub = [Cb.sb([128, 512], F32, "ub") for _ in range(2)]
    t_ub = [Tok() for _ in range(2)]
    it = 0
    for j in range(4):
        for (t0, n) in BLOCKS:
            i = it % 2
            it += 1
            hs = toks_for(t_h, t0, n)
            P.mm([(lambda pe, k=k, i=i, j=j, t0=t0, n=n: pe.matmul(psB[i][:, :n], wuA[:, k, j * 128:(j + 1) * 128], hT[:, k, t0:t0 + n], start=(k == 0), stop=(k == 15)))
                  for k in range(16)], r=[t_wuA] + hs, w=[t_psB[i]])
            P.op("act", lambda a, i=i, n=n: a.copy(out=ub[i][:, :n], in_=psB[i][:, :n]), r=[t_psB[i]], w=[t_ub[i]])
            P.dma("sp", uT[j * 128:(j + 1) * 128, t0:t0 + n], ub[i][:, :n], r=[t_ub[i]], w=[R.tk("uT", (j, t0))])
    wv_ = Cb.sb([128, 16, 128], BF16, "wv")
    t_wv = Tok()
    load_w_cols(P, Cb, w_in[0], [(OFFX_V, 128)], wv_, t_wv)
    vb = [Cb.sb([128, 128], BF16, "vb") for _ in range(2)]
    t_vb = [Tok() for _ in range(2)]
    for ti in range(NT):
        i = ti % 2
        P.mm([(lambda pe, k=k, i=i, ti=ti: pe.matmul(psB[i][:, :128], hT[:, k, ti * 128:(ti + 1) * 128], wv_[:, k, :], start=(k == 0), stop=(k == 15)))
              for k in range(16)], r=[t_wv] + list(t_h[ti]), w=[t_psB[i]])
        P.op("act", lambda a, i=i: a.copy(out=vb[i][:], in_=psB[i][:, :128]), r=[t_psB[i]], w=[t_vb[i]])
        P.dma("sp", vv[ti * 128:(ti + 1) * 128, :], vb[i][:], r=[t_vb[i]], w=[R.tk("v", ti)])
    Cb.close()

    if "c" not in parts:
        C.close(); return
    Cc = Ctx(P)
    lng = R.get("gmlp_ln_g", [R.nl, 512])[R.li(l):R.li(l) + 1]
    lnb = R.get("gmlp_ln_b", [R.nl, 512])[R.li(l):R.li(l) + 1]
    wsT = R.get("gmlp_wsT", [R.nl, 4, 128, 128])[R.li(l):R.li(l) + 1]
    bs = R.get("gmlp_b_s", [R.nl, 4, 128])[R.li(l):R.li(l) + 1]
    tiles = list(range(NT)) if ctx_full else list(range(NCTX // 128, NT))
    blocks = BLOCKS if ctx_full else BLOCKS[1:]
    guT = Cc.sb([128, 4, NTOK], F32, "guT")
    t_gu = {}
    wgu = Cc.sb([128, 16, 512], BF16, "wgu")
    t_wgu = Tok()
    load_w_cols(P, Cc, w_in[0], [(OFFX_GU, 512)], wgu, t_wgu)
    psC = [Cc.ps([128, 512], F32, "psC") for _ in range(2)]
    t_psC = [Tok() for _ in range(2)]
    it = 0
    for j in range(4):
        for (t0, n) in blocks:
            i = it % 2
            it += 1
            hs = toks_for(t_h, t0, n)
            P.mm([(lambda pe, k=k, i=i, j=j, t0=t0, n=n: pe.matmul(psC[i][:, :n], wgu[:, k, j * 128:(j + 1) * 128], hT[:, k, t0:t0 + n], start=(k == 0), stop=(k == 15)))
                  for k in range(16)], r=[t_wgu] + hs, w=[t_psC[i]])
            t_gu[(j, t0)] = Tok()
            P.op("act", lambda a, i=i, j=j, t0=t0, n=n: a.activation(out=guT[:, j, t0:t0 + n], in_=psC[i][:, :n], func=AF.Gelu_apprx_tanh),
                 r=[t_psC[i]], w=[t_gu[(j, t0)]])
    wgv = Cc.sb([128, 16, 512], BF16, "wgv")
    t_wgv = Tok()
    load_w_cols(P, Cc, w_in[0], [(OFFX_GV, 512)], wgv, t_wgv)
    LNG = Cc.sb([128, 512], F32, "LNG")
    LNB = Cc.sb([128, 512], F32, "LNB")
    BS = Cc.sb([128, 4, 128], F32, "BS")
    WS = Cc.sb([128, 4, 128], BF16, "WS")
    t_par = Tok()
    P.dma("sp", LNG[:], lng[0].partition_broadcast(128), w=[t_par])
    P.dma("sp", LNB[:], lnb[0].partition_broadcast(128), w=[t_par])
    P.dma("sp", BS[:].rearrange("p g i -> p (g i)"), bs[0].rearrange("g i -> (g i)").partition_broadcast(128), w=[t_par])
    P.dma("pool", WS[:], wsT[0].rearrange("g j i -> j g i"), w=[t_par])
    gg = [Cc.sb([128, 512], F32, "gg") for _ in range(2)]
    t_gg = [Tok() for _ in range(2)]
    sq = Cc.sb([128, 512], BF16, "sq")
    t_sq = Tok()
    stt = [Cc.sb([128, 8], F32, "stt") for _ in range(2)]
    t_stt = [Tok() for _ in range(2)]
    vn = [Cc.sb([128, 512], BF16, "vn") for _ in range(2)]
    t_vn = [Tok() for _ in range(2)]
    pm = [Cc.ps([128, 512], F32, "pm") for _ in range(2)]
    t_pm = [Tok() for _ in range(2)]
    mm_ = [Cc.sb([128, 512], F32, "mm") for _ in range(2)]
    t_mm = [Tok() for _ in range(2)]
    og = [Cc.sb([128, 4, 128], BF16, "og") for _ in range(2)]
    t_og = [Tok() for _ in range(2)]
    def g_stage1(n_, ti):
        i = n_ % 2
        s = stt[i]
        P.mm([(lambda pe, k=k, i=i, ti=ti: pe.matmul(psC[i][:], hT[:, k, ti * 128:(ti + 1) * 128], wgv[:, k, :], start=(k == 0), stop=(k == 15)))
              for k in range(16)], r=[t_wgv] + list(t_h[ti]), w=[t_psC[i]])
        P.op("act", lambda a, i=i, s=s: a.activation(out=gg[i][:], in_=psC[i][:], func=AF.Gelu_apprx_tanh, accum_out=s[:, 0:1]),
             r=[t_psC[i]], w=[t_gg[i], t_stt[i]])
        P.op("act", lambda a, i=i, s=s: a.activation(out=sq[:], in_=gg[i][:], func=AF.Square, accum_out=s[:, 1:2]),
             r=[t_gg[i]], w=[t_sq, t_stt[i]])
        P.op("dve", lambda v, s=s: v.tensor_scalar(out=s[:, 2:3], in0=s[:, 0:1], scalar1=1.0 / 512, scalar2=None, op0=ALU.mult), r=[t_stt[i]], w=[t_stt[i]])
        P.op("dve", lambda v, s=s: v.tensor_tensor(out=s[:, 3:4], in0=s[:, 2:3], in1=s[:, 2:3], op=ALU.mult), r=[t_stt[i]], w=[t_stt[i]])
        P.op("dve", lambda v, s=s: v.scalar_tensor_tensor(out=s[:, 4:5], in0=s[:, 1:2], scalar=1.0 / 512, in1=s[:, 3:4], op0=ALU.mult, op1=ALU.subtract),
             r=[t_stt[i]], w=[t_stt[i]])
        P.op("act", lambda a, s=s: a.activation(out=s[:, 5:6], in_=s[:, 4:5], func=AF.Sqrt, bias=EPS), r=[t_stt[i]], w=[t_stt[i]])
        P.op("dve", lambda v, s=s: v.reciprocal(out=s[:, 6:7], in_=s[:, 5:6]), r=[t_stt[i]], w=[t_stt[i]])
        P.op("dve", lambda v, i=i, s=s: v.tensor_scalar(out=gg[i][:], in0=gg[i][:], scalar1=s[:, 2:3], scalar2=s[:, 6:7], op0=ALU.subtract, op1=ALU.mult),
             r=[t_stt[i], t_gg[i]], w=[t_gg[i]])
        P.op("pool", lambda g, i=i: g.tensor_tensor(out=gg[i][:], in0=gg[i][:], in1=LNG[:], op=ALU.mult), r=[t_gg[i], t_par], w=[t_gg[i]])
        P.op("pool", lambda g, i=i: g.tensor_tensor(out=vn[i][:], in0=gg[i][:], in1=LNB[:], op=ALU.add), r=[t_gg[i], t_par], w=[t_vn[i]])
        P.mm([(lambda pe, g=g, i=i: pe.matmul(pm[i][:, g * 128:(g + 1) * 128], vn[i][:, g * 128:(g + 1) * 128], WS[:, g, :], start=True, stop=True))
              for g in range(4)], r=[t_vn[i], t_par], w=[t_pm[i]])

    def g_stage2(n_, ti):
        i = n_ % 2
        P.op("dve", lambda v, i=i: v.tensor_tensor(out=mm_[i][:], in0=pm[i][:], in1=BS[:].rearrange("p g i -> p (g i)"), op=ALU.add),
             r=[t_pm[i], t_par], w=[t_mm[i]])
        blk0 = [b for b in BLOCKS if b[0] <= ti * 128 < b[0] + b[1]][0][0]
        P.op("pool", lambda g, i=i, ti=ti: g.tensor_tensor(out=og[i][:], in0=mm_[i][:].rearrange("p (g i) -> p g i", g=4),
                                                           in1=guT[:, :, ti * 128:(ti + 1) * 128], op=ALU.mult),
             r=[t_mm[i]] + [t_gu[(j, blk0)] for j in range(4)], w=[t_og[i]])
        P.dma("sp", oT[:, ti * 128:(ti + 1) * 128].rearrange("(g p) t -> p g t", p=128), og[i][:], r=[t_og[i]], w=[R.tk("oT_g", ti)])

    for n_ in range(len(tiles) + 1):
        if n_ < len(tiles):
            g_stage1(n_, tiles[n_])
        if n_ >= 1:
            g_stage2(n_ - 1, tiles[n_ - 1])
    Cc.close()
    C.close()


def _qsw_perm():
    def perm(nheads):
        idx = np.arange(nheads * 64).reshape(nheads, 2, 2, 16)
        return idx[:, :, ::-1, :].reshape(-1)
    return perm(16), perm(2)


def prep_shared(inputs):
    w_in = np.asarray(inputs["w_in"])
    pq, pk = _qsw_perm()
    q = w_in[:, :, 0:1024]
    k = w_in[:, :, 1024:1152]
    w_inx = np.concatenate([q, q[:, :, pq], k, k[:, :, pk], w_in[:, :, 1152:]], axis=2)
    sh = {"w_inx": np.ascontiguousarray(w_inx, dtype=np.float32)}
    for name in ("w_ada", "b_ada", "g_mix", "g_ffn", "gmlp_ln_g", "gmlp_ln_b"):
        sh[name] = np.ascontiguousarray(inputs[name], dtype=np.float32)
    return sh


def prep_core(inputs, sh, b, s):
    x = np.asarray(inputs["x"])[b]
    ctx = np.asarray(inputs["ctx"])[b]
    pos = np.arange(4096)
    if s == 0:
        xl = x[:NLAT]
        pl = pos[:NLAT]
        cl = ctx
    else:
        xl = x[NLAT:][::-1]
        pl = pos[NLAT:][::-1]
        cl = ctx[::-1]
    m = dict(sh)
    m["x_in"] = np.ascontiguousarray(np.concatenate([cl, xl], axis=0), dtype=np.float32)
    m["rowcol"] = np.stack([pl // 64, pl % 64]).astype(np.float32)
    m["cvec"] = np.stack([np.asarray(inputs["c"])[b], np.asarray(inputs["c_ctx"])]).astype(np.float32)
    ws = np.asarray(inputs["gmlp_w_s"])
    bs = np.asarray(inputs["gmlp_b_s"])
    if s == 1:
        ws = ws[:, :, ::-1, ::-1]
        bs = bs[:, :, ::-1]
    m["gmlp_wsT"] = np.ascontiguousarray(ws.transpose(0, 1, 3, 2), dtype=np.float32)
    m["gmlp_b_s"] = np.ascontiguousarray(bs, dtype=np.float32)
    return m


def phase_D(P, R, K, l, xname, xout, ctx_full=True, side=None):
    w_out = R.get("w_out", [R.nl, D, D])[R.li(l):R.li(l) + 1]
    oTa = R.get("oT_a", [NQ, NTOK], BF16)
    oTs = R.get("oT_s", [512, NTOK], BF16)
    oTg = R.get("oT_g", [512, NTOK], BF16)
    x = R.get(xname, [NTOK, D])
    xo = R.get(xout, [NTOK, D])
    C = Ctx(P)
    W = C.sb([128, 16, D], BF16, "wout")
    t_W = Tok()
    wv = w_out[0].rearrange("(k p) c -> p k c", p=128)
    for k4 in range(4):
        P.dma("pool", W[:, 4 * k4:4 * k4 + 4, :], wv[:, 4 * k4:4 * k4 + 4, :], w=[t_W])
    GA, tGA = [], []
    for r in range(2):
        g, tg = load_bcast_mod(P, C, R, l, r, 2, "ga%d" % r)
        GA.append(g); tGA.append(tg)
    NB = 2
    ot = [C.sb([128, 16, 512], BF16, "ot") for _ in range(NB)]
    t_ot = [Tok() for _ in range(NB)]
    xt = [C.sb([128, D], F32, "xt") for _ in range(NB)]
    t_xt = [Tok() for _ in range(NB)]
    tmp = [C.sb([128, 512], F32, "tmp") for _ in range(2)]
    t_tmp = [Tok() for _ in range(2)]
    ps = [C.ps([128, 512], F32, "psD") for _ in range(4)]
    t_ps = [Tok() for _ in range(4)]
    tiles = list(range(NT)) if ctx_full else list(range(NCTX // 128, NT))
    it = 0
    for n_, ti in enumerate(tiles):
        i = n_ % NB
        r = 1 if ti < NCTX // 128 else 0
        gi_ = (ti + 2) // 4
        oi = gi_ % NB
        g_lo = max(0, gi_ * 4 - 2)
        g_n = min(NT, gi_ * 4 + 2) - g_lo
        oo = (ti - g_lo) * 128
        if ti == g_lo or n_ == 0:
            tsl = slice(g_lo * 128, (g_lo + g_n) * 128)
            P.dma("sp", ot[oi][:, 0:8, 0:128 * g_n], oTa[:, tsl].rearrange("(k p) t -> p k t", p=128), w=[t_ot[oi]])
            P.dma("sp", ot[oi][:, 8:12, 0:128 * g_n], oTs[:, tsl].rearrange("(k p) t -> p k t", p=128), w=[t_ot[oi]])
            P.dma("sp", ot[oi][:, 12:16, 0:128 * g_n], oTg[:, tsl].rearrange("(k p) t -> p k t", p=128), w=[t_ot[oi]])
        P.dma("act", xt[i][:], x[ti * 128:(ti + 1) * 128, :], r=[R.tk(xname, ti)], w=[t_xt[i]])
        for cb in range(4):
            j = it % 4
            it += 1
            cs = slice(cb * 512, (cb + 1) * 512)
            P.mm([(lambda pe, k=k, j=j, oi=oi, oo=oo, cs=cs: pe.matmul(ps[j][:], ot[oi][:, k, oo:oo + 128], W[:, k, cs], start=(k == 0), stop=(k == 15)))
                  for k in range(16)], r=[t_ot[oi], t_W], w=[t_ps[j]])
            P.op("dve", lambda v, j=j, r=r, cs=cs: v.tensor_tensor(out=tmp[j % 2][:], in0=ps[j][:], in1=GA[r][:, cs], op=ALU.mult),
                 r=[t_ps[j], tGA[r]], w=[t_tmp[j % 2]])
            P.op("dve", lambda g, j=j, i=i, cs=cs: g.tensor_tensor(out=xt[i][:, cs], in0=xt[i][:, cs], in1=tmp[j % 2][:], op=ALU.add),
                 r=[t_tmp[j % 2], t_xt[i]], w=[t_xt[i]])
            if side is not None and cb % 2 == 1:
                next(side, None)
        P.dma("sp", xo[ti * 128:(ti + 1) * 128, :], xt[i][:], r=[t_xt[i]], w=[R.tk(xout, ti)])
    if side is not None:
        for _ in side:
            pass
    C.close()


H2C = NTOK + 3
FFN_BLOCKS = [(1, 769, [0, 1, 2, 3, 4, 5]), (770, 768, [6, 7, 8, 9, 10, 11]), (1538, 768, [12, 13, 14, 15, 16, 17])]


def tile_col(ti):
    return 1 + ti * 128 if ti < 2 else 258 + (ti - 2) * 128


def phase_E(P, R, K, l, xname, xout, dbg=""):
    w_up = R.get("ffn_w_up", [R.nl, D, 2 * DFF])[R.li(l):R.li(l) + 1]
    w_dn = R.get("ffn_w_down", [R.nl, DFF, D])[R.li(l):R.li(l) + 1]
    cw = R.get("ffn_conv_w", [R.nl, 3, DFF])[R.li(l):R.li(l) + 1]
    cb_ = R.get("ffn_conv_b", [R.nl, DFF])[R.li(l):R.li(l) + 1]
    h2T = R.get("h2T", [D, H2C], BF16)
    xb = R.get("xb_recv", [1, D])
    x = R.get(xname, [NTOK, D])
    xo = R.get(xout, [NTOK, D])
    h2v = h2T.rearrange("(k p) c -> p k c", p=128)
    C0 = Ctx(P)
    groups = [[0, 1], [2, 3, 4, 5], [6, 7, 8, 9], [10, 11, 12, 13], [14, 15, 16, 17], [NT]]
    gof = {}
    for gi, gl in enumerate(groups):
        for k_, ti in enumerate(gl):
            gof[ti] = (gi, k_, len(gl))
    stage = [C0.sb([128, 16, 512], BF16, "stage") for _ in range(2)]
    t_stage = [(Tok(), Tok()) for _ in range(2)]
    zt = C0.sb([128, 16, 1], BF16, "zt")
    t_z = Tok()
    P.op("pool", lambda g: g.memset(zt[:], 0.0), w=[t_z])
    P.dma("sp", h2v[:, :, 0:1], zt[:], r=[t_z], w=[R.tk("h2T", "z0")], allow_slow_non_contiguous=True)
    P.dma("sp", h2v[:, :, 257:258], zt[:], r=[t_z], w=[R.tk("h2T", "z1")], allow_slow_non_contiguous=True)

    def after(ti, toks):
        gi, k_, gn = gof[ti]
        if k_ != gn - 1:
            return
        if ti < NT:
            c0 = tile_col(groups[gi][0])
            P.dma("sp", h2v[:, :, c0:c0 + 128 * gn], stage[gi % 2][:, :, 0:128 * gn], r=list(toks), w=[R.tk("h2T", ti)])
        else:
            P.dma("sp", h2v[:, :, H2C - 1:H2C], stage[gi % 2][:, :, 0:1], r=list(toks), w=[R.tk("h2T", ti)], allow_slow_non_contiguous=True)

    norm_mod_transpose(P, R, K, C0, l, xname, "g_ffn", 3, 4, None,
                       dst_fn=lambda ti, h: stage[gof[ti][0] % 2][:, 8 * h:8 * h + 8, 128 * gof[ti][1]:128 * gof[ti][1] + 128],
                       tok_fn=lambda ti: t_stage[gof[ti][0] % 2], after=after, extra_row=(xb, R.tk("xb_recv")))
    C0.close()
    if dbg == "e0":
        return
    h2_toks = [R.tk("h2T", k) for k in ["z0", "z1"] + list(range(NT + 1))]

    C = Ctx(P)
    CW = C.sb([128, NFF, 3], F32, "CW")
    CB = C.sb([128, NFF], F32, "CB")
    t_cw = Tok()
    for j in range(3):
        P.dma("sp", CW[:, :, j], cw[0, j].rearrange("(f p) -> p f", p=128), w=[t_cw], allow_slow_non_contiguous=True)
    P.dma("sp", CB[:], cb_[0].rearrange("(f p) -> p f", p=128), w=[t_cw], allow_slow_non_contiguous=True)
    aT = C.sb([128, NFF, 769], BF16, "aT")
    wuv = w_up[0].rearrange("(k p) c -> p k c", p=128)
    wdv = w_dn[0].rearrange("(f p) c -> p f c", p=128)
    for (a, n, tiles) in FFN_BLOCKS:
        Cu = Ctx(P)
        hb = Cu.sb([128, 16, 771], BF16, "h2blk")
        t_hb = Tok()
        P.dma("sp", hb[:, 0:8, 0:n + 2], h2v[:, 0:8, a - 1:a + n + 1], r=h2_toks, w=[t_hb])
        P.dma("act", hb[:, 8:16, 0:n + 2], h2v[:, 8:16, a - 1:a + n + 1], r=h2_toks, w=[t_hb])
        wu = [Cu.sb([128, 16, 1024], BF16, "wup") for _ in range(2)]
        t_wu = [Tok() for _ in range(2)]
        G = [[Cu.ps([128, 512], F32, "G") for _ in range(2)] for _ in range(2)]
        V = [[Cu.ps([128, 512], F32, "V") for _ in range(2)] for _ in range(2)]
        t_G = [[Tok() for _ in range(2)] for _ in range(2)]
        t_V = [[Tok() for _ in range(2)] for _ in range(2)]
        g1 = [Cu.sb([128, 385], F32, "g1") for _ in range(2)]
        t_g1 = [Tok() for _ in range(2)]
        sl = [Cu.sb([128, 385], F32, "sl") for _ in range(2)]
        t_sl = [Tok() for _ in range(2)]
        n0 = n - 384
        chunks = [(0, n0), (n0, 384)]
        t_aT = Tok()
        for f in range(NFF):
            wi = (f // 4) % 2
            pb = f % 2
            fo = (f % 4) * 128
            if f % 4 == 0:
                P.dma("pool", wu[wi][:, :, 0:512], wuv[:, :, f * 128:f * 128 + 512], w=[t_wu[wi]])
                P.dma("pool", wu[wi][:, :, 512:1024], wuv[:, :, DFF + f * 128:DFF + f * 128 + 512], w=[t_wu[wi]])
            for c, (s0, ln) in enumerate(chunks):
                Gp, Vp = G[pb][c], V[pb][c]
                P.mm([(lambda pe, k=k, Gp=Gp, wi=wi, s0=s0, ln=ln, fo=fo: pe.matmul(Gp[:, 0:ln + 2], wu[wi][:, k, fo:fo + 128], hb[:, k, s0:s0 + ln + 2], start=(k == 0), stop=(k == 15)))
                      for k in range(16)], r=[t_wu[wi], t_hb], w=[t_G[pb][c]])
                P.mm([(lambda pe, k=k, Vp=Vp, wi=wi, s0=s0, ln=ln, fo=fo: pe.matmul(Vp[:, 0:ln], wu[wi][:, k, 512 + fo:512 + fo + 128], hb[:, k, s0 + 1:s0 + ln + 1], start=(k == 0), stop=(k == 15)))
                      for k in range(16)], r=[t_wu[wi], t_hb], w=[t_V[pb][c]])
                P.op("act", lambda a_, Gp=Gp, c=c, f=f, ln=ln: a_.activation(out=g1[c][:, 0:ln], in_=Gp[:, 1:ln + 1], func=AF.Identity,
                                                                           scale=CW[:, f, 1:2], bias=CB[:, f:f + 1]),
                     r=[t_G[pb][c], t_cw], w=[t_g1[c]])
                P.op("dve", lambda v, Gp=Gp, c=c, f=f, ln=ln: v.scalar_tensor_tensor(out=g1[c][:, 0:ln], in0=Gp[:, 0:ln], scalar=CW[:, f, 0:1], in1=g1[c][:, 0:ln],
                                                                                  op0=ALU.mult, op1=ALU.add),
                     r=[t_G[pb][c], t_cw, t_g1[c]], w=[t_g1[c]])
                P.op("dve", lambda v, Gp=Gp, c=c, f=f, ln=ln: v.scalar_tensor_tensor(out=g1[c][:, 0:ln], in0=Gp[:, 2:ln + 2], scalar=CW[:, f, 2:3], in1=g1[c][:, 0:ln],
                                                                                  op0=ALU.mult, op1=ALU.add),
                     r=[t_G[pb][c], t_cw, t_g1[c]], w=[t_g1[c]])
                P.op("act", lambda a_, c=c, ln=ln: a_.activation(out=sl[c][:, 0:ln], in_=g1[c][:, 0:ln], func=AF.Silu), r=[t_g1[c]], w=[t_sl[c]])
                P.op("dve", lambda v, Vp=Vp, c=c, f=f, s0=s0, ln=ln: v.tensor_tensor(out=aT[:, f, s0:s0 + ln], in0=sl[c][:, 0:ln], in1=Vp[:, 0:ln], op=ALU.mult),
                     r=[t_sl[c], t_V[pb][c]], w=[t_aT])
        Cu.close()
        if dbg == "up":
            continue
        Cd = Ctx(P)
        GF, tGF = [], []
        for r in range(2):
            g, tg = load_bcast_mod(P, Cd, R, l, r, 5, "gf%d" % r)
            GF.append(g); tGF.append(tg)
        wd = [Cd.sb([128, 11, 512], BF16, "wd") for _ in range(8)]
        t_wd = [Tok() for _ in range(8)]
        ps = [Cd.ps([128, 512], F32, "psd") for _ in range(6)]
        t_ps = [Tok() for _ in range(6)]
        xt = [Cd.sb([128, 512], F32, "xt") for _ in range(4)]
        t_xt = [Tok() for _ in range(4)]
        tmp = [Cd.sb([128, 512], F32, "tmp") for _ in range(2)]
        t_tmp = [Tok() for _ in range(2)]
        x_it = 0
        for cb in range(4):
            cs = slice(cb * 512, (cb + 1) * 512)
            for fc in range(4):
                wi = (cb % 2) * 4 + fc
                P.dma("pool", wd[wi][:], wdv[:, fc * 11:(fc + 1) * 11, cs], w=[t_wd[wi]])
            for hf in range(2):
                tl = list(enumerate(tiles))[3 * hf:3 * hf + 3]
                for fc in range(4):
                    wi = (cb % 2) * 4 + fc
                    fns = []
                    for ff in range(11):
                        f = fc * 11 + ff
                        for q, ti in tl:
                            lc = tile_col(ti) - a
                            fns.append(lambda pe, q=q, f=f, ff=ff, lc=lc, wi=wi: pe.matmul(ps[q][:], aT[:, f, lc:lc + 128], wd[wi][:, ff, :], start=(f == 0), stop=(f == NFF - 1)))
                    P.mm(fns, r=[t_wd[wi], t_aT], w=[t_ps[q] for q, _ in tl])
                for q, ti in tl:
                    r = 1 if ti < 2 else 0
                    xi = x_it % 4
                    x_it += 1
                    P.dma("sp", xt[xi][:], x[ti * 128:(ti + 1) * 128, cs], r=[R.tk(xname, ti)], w=[t_xt[xi]])
                    P.op("dve", lambda v, q=q, r=r, cs=cs, xi=xi: v.tensor_tensor(out=tmp[xi % 2][:], in0=ps[q][:], in1=GF[r][:, cs], op=ALU.mult),
                         r=[t_ps[q], tGF[r]], w=[t_tmp[xi % 2]])
                    P.op("dve", lambda v, xi=xi: v.tensor_tensor(out=xt[xi][:], in0=xt[xi][:], in1=tmp[xi % 2][:], op=ALU.add),
                         r=[t_tmp[xi % 2], t_xt[xi]], w=[t_xt[xi]])
                    P.dma("sp", xo[ti * 128:(ti + 1) * 128, cs], xt[xi][:], r=[t_xt[xi]], w=[R.tk(xout, (ti, cb))])
        Cd.close()
    C.close()


NEG = -30000.0


def phase_C(P, R, K, l, ctx_full=True, dbg_qtiles=None, dbg_skip=()):
    qT = R.get("qT", [NQ, NTOK], BF16)
    kT = R.get("kT", [128, NTOK + 128], BF16)
    vv = R.get("v", [NTOK + 128, 128], BF16)
    oT = R.get("oT_a", [NQ, NTOK], BF16)
    sink = R.get("attn_sink", [R.nl, 16])[R.li(l):R.li(l) + 1]
    C = Ctx(P)
    io = C.sb([128, 128], I32, "mio")
    mf = C.sb([128, 128], F32, "mf")
    masks = {}
    t_m = Tok()
    for name, cm, op, thr in (("prev", -1, ALU.is_le, 0.0), ("next", -1, ALU.is_ge, 0.0), ("halo", 1, ALU.is_ge, 127.0)):
        mk = C.sb([128, 8, 128], BF16, "mask_" + name)
        P.op("pool", lambda g, cm=cm: g.iota(io[:], [[1, 128]], base=0, channel_multiplier=cm), r=[t_m], w=[t_m])
        P.op("dve", lambda v: v.tensor_copy(out=mf[:], in_=io[:]), r=[t_m], w=[t_m])
        for h8 in range(8):
            P.op("dve", lambda v, mk=mk, op=op, thr=thr, h8=h8: v.tensor_scalar(out=mk[:, h8, :], in0=mf[:], scalar1=thr, scalar2=None, op0=op), r=[t_m], w=[t_m])
        masks[name] = mk
    onesA = C.sb([128, 128], BF16, "onesA")
    onesB = C.sb([128, 128], BF16, "onesB")
    P.op("pool", lambda g: g.memset(onesA[:], 0.0), w=[t_m])
    P.op("pool", lambda g: g.memset(onesB[:], 0.0), w=[t_m])
    P.op("pool", lambda g: g.memset(onesA[:, 0:64], 1.0), r=[t_m], w=[t_m])
    P.op("pool", lambda g: g.memset(onesB[:, 64:128], 1.0), r=[t_m], w=[t_m])
    es = C.sb([128, 8], F32, "esink")
    t_es = Tok()
    sv = sink[0].rearrange("(m h) -> h m", h=2)
    P.dma("sp", es[0:64, :], sv[0].partition_broadcast(64), w=[t_es], allow_slow_non_contiguous=True)
    P.dma("sp", es[64:128, :], sv[1].partition_broadcast(64), w=[t_es], allow_slow_non_contiguous=True)
    P.op("act", lambda a: a.activation(out=es[:], in_=es[:], func=AF.Exp), r=[t_es], w=[t_es])
    NKB = NT + 1
    KT2 = [C.sb([128, NTOK + 128], BF16, "KT2_%d" % j) for j in range(2)]
    t_k = Tok()
    for j in range(2):
        P.dma("sp", KT2[j][0:64, :], kT[64 * j:64 * j + 64, :], w=[t_k])
        P.dma("act", KT2[j][64:128, :], kT[64 * j:64 * j + 64, :], w=[t_k])
    VP = C.sb([128, NKB, 2, 256], BF16, "VP")
    t_v = Tok()
    P.op("pool", lambda g: g.memset(VP[:], 1.0), w=[t_v])
    v3 = vv.rearrange("(kb p) c -> p kb c", p=128)
    for j in range(2):
        P.dma("sp", VP[:, :, j, 0:64], v3[:, :, 64 * j:64 * j + 64], r=[t_v], w=[t_v])
        P.dma("act", VP[:, :, j, 192:256], v3[:, :, 64 * j:64 * j + 64], r=[t_v], w=[t_v])
    qs = [C.sb([128, 8, 128], BF16, "qs") for _ in range(2)]
    t_qs = [Tok() for _ in range(2)]
    S = [C.ps([128, 1024], F32, "S") for _ in range(2)]
    t_S = [Tok() for _ in range(2)]
    PT = [C.sb([128, 5, 1024], BF16, "PT") for _ in range(2)]
    t_PT = [Tok() for _ in range(2)]
    Op = [C.ps([128, 1024], F32, "Op") for _ in range(2)]
    t_O = [Tok() for _ in range(2)]
    dsb = [C.sb([128, 512], F32, "dsb") for _ in range(2)]
    t_dsb = [Tok() for _ in range(2)]
    osb = [C.sb([128, 4, 128], BF16, "osb") for _ in range(2)]
    t_osb = [Tok() for _ in range(2)]
    qtiles = list(range(NT)) if ctx_full else list(range(2, NT))
    if dbg_qtiles is not None:
        qtiles = dbg_qtiles
    s_it = [0]
    items = []
    pj = 0
    for n_, qt in enumerate(qtiles):
        qi = n_ % 2
        if qt < 2:
            kbs = [(0, None), (1, None)]
        else:
            i = qt - 2
            kbs = [(0, None), (1, None), (qt, None)]
            if i > 0:
                kbs.append((qt - 1, "prev"))
            if i < 15:
                kbs.append((qt + 1, "next"))
            else:
                kbs.append((NT, "halo"))
        for j in range(2):
            items.append((qt, qi, j, kbs, pj % 2, j == 0))
            pj += 1

    def c_stage1(qt, qi, j, kbs, pi, first):
        if first:
            P.dma("sp", qs[qi][:], qT[:, qt * 128:(qt + 1) * 128].rearrange("(m p) t -> p m t", p=128), w=[t_qs[qi]])
        for kbi, (kb, mname) in enumerate(kbs):
            si = s_it[0] % 2
            s_it[0] += 1
            fns = []
            for hh in range(8):
                m, e = 4 * j + hh // 2, hh % 2
                cbk = e * 4 + hh // 2
                fns.append(lambda pe, si=si, hh=cbk, m=m, e=e, kb=kb, qi=qi, j=j: pe.matmul(
                    S[si][:, hh * 128:(hh + 1) * 128], KT2[j][64 * e:64 * e + 64, kb * 128:(kb + 1) * 128],
                    qs[qi][64 * e:64 * e + 64, m, :], start=True, stop=True))
            P.mm(fns, r=[t_k, t_qs[qi]], w=[t_S[si]])
            P.op("act", lambda a, si=si, pi=pi, kbi=kbi: a.activation(out=PT[pi][:, kbi, :], in_=S[si][:], func=AF.Exp, scale=0.125),
                 r=[t_S[si]], w=[t_PT[pi]])
            if mname is not None:
                P.op("pool", lambda g, pi=pi, kbi=kbi, mname=mname: g.tensor_tensor(out=PT[pi][:, kbi, :], in0=PT[pi][:, kbi, :],
                                                                                   in1=masks[mname][:].rearrange("p h q -> p (h q)"), op=ALU.mult),
                     r=[t_PT[pi], t_m], w=[t_PT[pi]])
        fns = []
        nk = len(kbs)
        for hh in range(8):
            mm, e = hh // 2, hh % 2
            for kbi, (kb, _) in enumerate(kbs):
                fns.append(lambda pe, pi=pi, hh=hh, mm=mm, kbi=kbi, kb=kb, e=e, j=j, nk=nk: pe.matmul(
                    Op[pi][:, hh * 128:(hh + 1) * 128], VP[:, kb, j, 128 * e:128 * e + 128],
                    PT[pi][:, kbi, (4 * e + mm) * 128:(4 * e + mm + 1) * 128], start=(kbi == 0), stop=(kbi == nk - 1)))
        P.mm(fns, r=[t_v, t_PT[pi]], w=[t_O[pi]])

    def c_stage2(qt, qi, j, kbs, pi, first):
        Ov = Op[pi][:].rearrange("p (m e q) -> p m e q", e=2, q=128)
        dv = dsb[pi][:].rearrange("p (m q) -> p m q", q=128)
        P.op("act", lambda a, Ov=Ov, dv=dv: a.copy(out=dv[0:64], in_=Ov[64:128, :, 0, :]), r=[t_O[pi]], w=[t_dsb[pi]])
        P.op("act", lambda a, Ov=Ov, dv=dv: a.copy(out=dv[64:128], in_=Ov[0:64, :, 1, :]), r=[t_O[pi]], w=[t_dsb[pi]])
        for mm in range(4):
            m = 4 * j + mm
            P.op("dve", lambda v, pi=pi, mm=mm, m=m: v.tensor_scalar(out=dsb[pi][:, mm * 128:(mm + 1) * 128], in0=dsb[pi][:, mm * 128:(mm + 1) * 128],
                                                                  scalar1=es[:, m:m + 1], scalar2=None, op0=ALU.add),
                 r=[t_dsb[pi], t_es], w=[t_dsb[pi]])
        P.op("dve", lambda v, pi=pi: v.reciprocal(out=dsb[pi][:], in_=dsb[pi][:]), r=[t_dsb[pi]], w=[t_dsb[pi]])
        P.op("dve", lambda v, pi=pi, Ov=Ov, dv=dv: v.tensor_tensor(out=osb[pi][0:64], in0=Ov[0:64, :, 0, :], in1=dv[0:64], op=ALU.mult),
             r=[t_O[pi], t_dsb[pi]], w=[t_osb[pi]])
        P.op("dve", lambda v, pi=pi, Ov=Ov, dv=dv: v.tensor_tensor(out=osb[pi][64:128], in0=Ov[64:128, :, 1, :], in1=dv[64:128], op=ALU.mult),
             r=[t_O[pi], t_dsb[pi]], w=[t_osb[pi]])
        P.dma("sp", oT[512 * j:512 * j + 512, qt * 128:(qt + 1) * 128].rearrange("(m p) t -> p m t", p=128), osb[pi][:],
              r=[t_osb[pi]], w=[R.tk("oT_a", (qt, j))])

    for n_ in range(len(items) + 1):
        if n_ < len(items):
            c_stage1(*items[n_])
        if n_ >= 1:
            c_stage2(*items[n_ - 1])
    C.close()


TCH = 16
NCH = NTOK // TCH
NCC = NCTX // TCH


class SSMState:
    pass


def ssm_prep(P, R, K, C, l):
    S = SSMState()
    li = R.li(l)
    lam_re = R.get("ssm_lambda_re", [R.nl, 2, 32, 64])
    lam_im = R.get("ssm_lambda_im", [R.nl, 2, 32, 64])
    log_dt = R.get("ssm_log_dt", [R.nl, 2, 32])
    b_re = R.get("ssm_b_re", [R.nl, 2, 32, 64, 16])
    b_im = R.get("ssm_b_im", [R.nl, 2, 32, 64, 16])
    c_re = R.get("ssm_c_re", [R.nl, 2, 32, 16, 64])
    c_im = R.get("ssm_c_im", [R.nl, 2, 32, 16, 64])
    pf = C.sb([128, 16], F32, "pf")
    ARE = C.sb([128, 64], F32, "ARE")
    AIM = C.sb([128, 64], F32, "AIM")
    LB = C.sb([128, 64, 128], BF16, "LB")
    CW = C.sb([128, 64, 16], F32, "CW")
    CL = C.sb([128, 64], F32, "CL")
    CR = C.sb([128, 64], F32, "CR")
    ID2 = C.sb([128, 64], F32, "ID2")
    ATR = C.sb([64, 64], F32, "ATR")
    ATI = C.sb([64, 64], F32, "ATI")
    Ct = Ctx(P)
    t = Tok()
    LR = Ct.sb([128, 64], F32, "LR")
    LI = Ct.sb([128, 64], F32, "LI")
    DT = Ct.sb([128, 64], F32, "DT")
    for base in (0, 64):
        P.dma("sp", LR[base:base + 64, :], lam_re[li].rearrange("d g p -> p (d g)"), w=[t], allow_slow_non_contiguous=True)
        P.dma("sp", LI[base:base + 64, :], lam_im[li].rearrange("d g p -> p (d g)"), w=[t], allow_slow_non_contiguous=True)
    P.dma("sp", DT[:], log_dt[li].rearrange("d g -> (d g)").partition_broadcast(128), w=[t])
    pi_ = Ct.sb([128, 1], I32, "pi")
    pj_ = Ct.sb([128, 1], I32, "pj")
    tp = Tok()
    P.op("pool", lambda g: g.iota(pi_[:], [[0, 1]], base=0, channel_multiplier=1), w=[tp])
    P.op("dve", lambda v: v.tensor_scalar(out=pj_[:], in0=pi_[:], scalar1=6, scalar2=None, op0=ALU.logical_shift_right), r=[tp], w=[tp])
    P.op("dve", lambda v: v.tensor_copy(out=pf[:, 1:2], in_=pj_[:]), r=[tp], w=[tp])
    P.op("dve", lambda v: v.tensor_scalar(out=pf[:, 0:1], in0=pf[:, 1:2], scalar1=-1.0, scalar2=1.0, op0=ALU.mult, op1=ALU.add), r=[tp], w=[tp])
    P.op("dve", lambda v: v.tensor_scalar(out=pf[:, 2:3], in0=pf[:, 1:2], scalar1=2.0, scalar2=-1.0, op0=ALU.mult, op1=ALU.add), r=[tp], w=[tp])
    P.op("dve", lambda v: v.tensor_scalar(out=pf[:, 3:4], in0=pf[:, 1:2], scalar1=-1.0, scalar2=None, op0=ALU.mult), r=[tp], w=[tp])
    P.op("dve", lambda v: v.tensor_scalar(out=pj_[:], in0=pi_[:], scalar1=4, scalar2=None, op0=ALU.logical_shift_right), r=[tp], w=[tp])
    P.op("dve", lambda v: v.tensor_copy(out=pf[:, 12:13], in_=pj_[:]), r=[tp], w=[tp])
    for i in range(8):
        P.op("dve", lambda v, i=i: v.tensor_scalar(out=pf[:, 4 + i:5 + i], in0=pf[:, 12:13], scalar1=float(i), scalar2=None, op0=ALU.is_equal), r=[tp], w=[tp])
    S.pf, S.t_pf = pf, tp
    P.op("act", lambda a: a.activation(out=DT[:], in_=DT[:], func=AF.Exp), r=[t], w=[t])
    MAG = Ct.sb([128, 64], F32, "MAG")
    ANG = Ct.sb([128, 64], F32, "ANG")
    P.op("dve", lambda v: v.tensor_tensor(out=MAG[:], in0=LR[:], in1=DT[:], op=ALU.mult), r=[t], w=[t])
    P.op("act", lambda a: a.activation(out=MAG[:], in_=MAG[:], func=AF.Exp), r=[t], w=[t])
    P.op("dve", lambda v: v.tensor_tensor(out=ANG[:], in0=LI[:], in1=DT[:], op=ALU.mult), r=[t], w=[t])
    SN = Ct.sb([128, 64], F32, "SN")
    CS = Ct.sb([128, 64], F32, "CS")
    Cr = Ctx(P)
    tsn = Tok(); tsn.w = t.w
    range_reduce_sin(P, Cr, SN[:], ANG[:], [128, 64], t)
    range_reduce_sin(P, Cr, CS[:], ANG[:], [128, 64], t, extra=PI / 2)
    Cr.close()
    t_a = Tok()
    P.op("dve", lambda v: v.tensor_tensor(out=ARE[:], in0=MAG[:], in1=CS[:], op=ALU.mult), r=[t], w=[t_a])
    P.op("dve", lambda v: v.tensor_tensor(out=AIM[:], in0=MAG[:], in1=SN[:], op=ALU.mult), r=[t], w=[t_a])
    S.ARE, S.AIM, S.t_a = ARE, AIM, t_a
    DEN = Ct.sb([128, 64], F32, "DEN")
    T1 = Ct.sb([128, 64], F32, "T1")
    NRE = Ct.sb([128, 64], F32, "NRE")
    FRE = Ct.sb([128, 64], F32, "FRE")
    FIM = Ct.sb([128, 64], F32, "FIM")
    tf = Tok()
    P.op("dve", lambda v: v.tensor_tensor(out=DEN[:], in0=LR[:], in1=LR[:], op=ALU.mult), r=[t], w=[tf])
    P.op("dve", lambda v: v.tensor_tensor(out=T1[:], in0=LI[:], in1=LI[:], op=ALU.mult), r=[t], w=[tf])
    P.op("dve", lambda v: v.tensor_tensor(out=DEN[:], in0=DEN[:], in1=T1[:], op=ALU.add), r=[tf], w=[tf])
    P.op("dve", lambda v: v.reciprocal(out=DEN[:], in_=DEN[:]), r=[tf], w=[tf])
    P.op("dve", lambda v: v.tensor_scalar(out=NRE[:], in0=ARE[:], scalar1=-1.0, scalar2=None, op0=ALU.add), r=[t_a], w=[tf])
    P.op("dve", lambda v: v.tensor_tensor(out=FRE[:], in0=NRE[:], in1=LR[:], op=ALU.mult), r=[tf, t], w=[tf])
    P.op("dve", lambda v: v.tensor_tensor(out=T1[:], in0=AIM[:], in1=LI[:], op=ALU.mult), r=[tf, t, t_a], w=[tf])
    P.op("dve", lambda v: v.tensor_tensor(out=FRE[:], in0=FRE[:], in1=T1[:], op=ALU.add), r=[tf], w=[tf])
    P.op("dve", lambda v: v.tensor_tensor(out=FRE[:], in0=FRE[:], in1=DEN[:], op=ALU.mult), r=[tf], w=[tf])
    P.op("dve", lambda v: v.tensor_tensor(out=FIM[:], in0=AIM[:], in1=LR[:], op=ALU.mult), r=[tf, t, t_a], w=[tf])
    P.op("dve", lambda v: v.tensor_tensor(out=T1[:], in0=NRE[:], in1=LI[:], op=ALU.mult), r=[tf, t], w=[tf])
    P.op("dve", lambda v: v.tensor_tensor(out=FIM[:], in0=FIM[:], in1=T1[:], op=ALU.subtract), r=[tf], w=[tf])
    P.op("dve", lambda v: v.tensor_tensor(out=FIM[:], in0=FIM[:], in1=DEN[:], op=ALU.mult), r=[tf], w=[tf])
    P.op("dve", lambda v: v.tensor_scalar(out=FIM[:], in0=FIM[:], scalar1=pf[:, 2:3], scalar2=None, op0=ALU.mult), r=[tf, tp], w=[tf])
    BX = Ct.sb([128, 64, 16], F32, "BX")
    BY = Ct.sb([128, 64, 16], F32, "BY")
    tb = Tok()
    brv = b_re[li].rearrange("d g p h -> p (d g) h")
    biv = b_im[li].rearrange("d g p h -> p (d g) h")
    P.dma("sp", BX[0:64], brv, w=[tb])
    P.dma("act", BX[64:128], biv, w=[tb])
    P.dma("sp", BY[0:64], biv, w=[tb])
    P.dma("act", BY[64:128], brv, w=[tb])
    for h in range(16):
        P.op("dve", lambda v, h=h: v.tensor_tensor(out=BX[:, :, h], in0=BX[:, :, h], in1=FRE[:], op=ALU.mult), r=[tb, tf], w=[tb])
        P.op("pool", lambda g, h=h: g.tensor_tensor(out=BY[:, :, h], in0=BY[:, :, h], in1=FIM[:], op=ALU.mult), r=[tb, tf], w=[tb])
    P.op("dve", lambda v: v.tensor_tensor(out=BX[:], in0=BX[:], in1=BY[:], op=ALU.add), r=[tb], w=[tb])
    t_LB = Tok()
    psT = Ct.ps([128, 128], F32, "psT")
    t_psT = Tok()
    for b8 in range(8):
        P.mm([lambda pe, b8=b8: pe.transpose(psT[:], BX[:, b8 * 8:(b8 + 1) * 8, :].rearrange("p g h -> p (g h)"), K["idf"][:])],
             r=[tb, K["tok"]], w=[t_psT])
        for i in range(8):
            P.op("dve", lambda v, b8=b8, i=i: v.tensor_scalar(out=LB[:, b8 * 8 + i, :], in0=psT[:], scalar1=pf[:, 4 + i:5 + i], scalar2=None, op0=ALU.mult),
                 r=[t_psT, tp], w=[t_LB])
    S.LB, S.t_LB = LB, t_LB
    t_CW = Tok()
    crv = c_re[li].rearrange("d g h p -> p (d g) h")
    civ = c_im[li].rearrange("d g h p -> p (d g) h")
    for q4 in range(0, 64, 4):
        P.dma("sp", CW[0:64, q4:q4 + 4, :], crv[:, q4:q4 + 4, :], w=[t_CW], allow_slow_non_contiguous=True)
        P.dma("act", CW[64:128, q4:q4 + 4, :], civ[:, q4:q4 + 4, :], w=[t_CW], allow_slow_non_contiguous=True)
    P.op("dve", lambda v: v.tensor_scalar(out=CW[:].rearrange("p g h -> p (g h)"), in0=CW[:].rearrange("p g h -> p (g h)"), scalar1=pf[:, 2:3], scalar2=-1.0,
                                          op0=ALU.mult, op1=ALU.mult), r=[t_CW, tp], w=[t_CW])
    S.CW, S.t_CW = CW, t_CW
    t_c = Tok()
    P.op("dve", lambda v: v.tensor_scalar(out=CL[:], in0=ARE[:], scalar1=pf[:, 0:1], scalar2=None, op0=ALU.mult), r=[t_a, tp], w=[t_c])
    P.op("dve", lambda v: v.scalar_tensor_tensor(out=CL[:], in0=AIM[:], scalar=pf[:, 3:4], in1=CL[:], op0=ALU.mult, op1=ALU.add), r=[t_a, tp, t_c], w=[t_c])
    P.op("dve", lambda v: v.tensor_scalar(out=CR[:], in0=AIM[:], scalar1=pf[:, 0:1], scalar2=None, op0=ALU.mult), r=[t_a, tp], w=[t_c])
    P.op("dve", lambda v: v.scalar_tensor_tensor(out=CR[:], in0=ARE[:], scalar=pf[:, 1:2], in1=CR[:], op0=ALU.mult, op1=ALU.add), r=[t_a, tp, t_c], w=[t_c])
    S.CL, S.CR, S.t_c = CL, CR, t_c
    t_id2 = Tok()
    P.op("dve", lambda v: v.tensor_copy(out=ID2[0:64, :], in_=K["idf"][0:64, 0:64]), r=[K["tok"]], w=[t_id2])
    P.op("dve", lambda v: v.tensor_copy(out=ID2[64:128, :], in_=K["idf"][64:128, 64:128]), r=[K["tok"]], w=[t_id2])
    S.ID2, S.t_id2 = ID2, t_id2
    q1 = Ct.sb([64, 64], F32, "q1")
    q2 = Ct.sb([64, 64], F32, "q2")
    t_at = Tok()
    P.op("dve", lambda v: v.tensor_copy(out=ATR[:], in_=ARE[0:64, :]), r=[t_a], w=[t_at])
    P.op("dve", lambda v: v.tensor_copy(out=ATI[:], in_=AIM[0:64, :]), r=[t_a], w=[t_at])
    for _ in range(4):
        P.op("dve", lambda v: v.tensor_tensor(out=q1[:], in0=ATR[:], in1=ATR[:], op=ALU.mult), r=[t_at], w=[t_at])
        P.op("dve", lambda v: v.tensor_tensor(out=q2[:], in0=ATI[:], in1=ATI[:], op=ALU.mult), r=[t_at], w=[t_at])
        P.op("dve", lambda v: v.tensor_tensor(out=ATI[:], in0=ATR[:], in1=ATI[:], op=ALU.mult), r=[t_at], w=[t_at])
        P.op("dve", lambda v: v.tensor_scalar(out=ATI[:], in0=ATI[:], scalar1=2.0, scalar2=None, op0=ALU.mult), r=[t_at], w=[t_at])
        P.op("dve", lambda v: v.tensor_tensor(out=ATR[:], in0=q1[:], in1=q2[:], op=ALU.subtract), r=[t_at], w=[t_at])
    S.ATR, S.ATI, S.t_at = ATR, ATI, t_at
    Ct.close()
    return S


def ssm_rvm(P, C, S, UT, t_UT, gds, Hs, t_Hs, ps, t_ps, la, t_la, hinit=None, jmap=lambda j: j, act_only=False, skip_la=False):
    le = "pool" if act_only else "dve"
    for i, gd in enumerate([] if skip_la else gds):
        P.op(le, lambda v, i=i, gd=gd: v.tensor_scalar(out=la[i][:, 0:64], in0=S.ID2[:], scalar1=S.CL[:, gd:gd + 1], scalar2=None, op0=ALU.mult),
             r=[S.t_id2, S.t_c], w=[t_la[i]])
        P.op(le, lambda v, i=i, gd=gd: v.tensor_scalar(out=la[i][:, 64:128], in0=S.ID2[:], scalar1=S.CR[:, gd:gd + 1], scalar2=None, op0=ALU.mult),
             r=[S.t_id2, S.t_c], w=[t_la[i]])
    for s in range(TCH):
        for i, gd in enumerate(gds):
            d, g = gd // 32, gd % 32
            j = s if d == 0 else TCH - 1 - s
            jprev = j - 1 if d == 0 else j + 1
            u_cols = UT[:, g // 8, :].rearrange("p (c j) -> p j c", j=TCH)[:, j, :]
            fns = []
            has2 = (s > 0) or (hinit is not None)
            fns.append(lambda pe, i=i, gd=gd, u_cols=u_cols, has2=has2: pe.matmul(ps[i], S.LB[:, gd, :], u_cols, start=True, stop=(not has2)))
            rd = [t_UT, S.t_LB, t_la[i]]
            if s > 0:
                fns.append(lambda pe, i=i, jprev=jprev: pe.matmul(ps[i], la[i][:], Hs[i][:, :, jmap(jprev)], start=False, stop=True))
                rd.append(t_Hs[i])
            elif hinit is not None:
                fns.append(lambda pe, i=i: pe.matmul(ps[i], la[i][:], hinit[i][0], start=False, stop=True))
                rd.append(hinit[i][1])
            P.mm(fns, r=rd, w=[t_ps[i]])
            if act_only or (i + s) % 2 == 0:
                P.op("act", lambda a, i=i, j=j: a.copy(out=Hs[i][:, :, jmap(j)], in_=ps[i]), r=[t_ps[i]], w=[t_Hs[i]])
            else:
                P.op("dve", lambda v, i=i, j=j: v.tensor_copy(out=Hs[i][:, :, jmap(j)], in_=ps[i]), r=[t_ps[i]], w=[t_Hs[i]])


def ssm_chain(P, C, S, LL, t_L, d, c_list, init, eng="dve"):
    g0 = d * 32
    HH = [C.sb([64, 3, 32], F32, "chH") for _ in range(2)]
    T1 = C.sb([64, 2, 32], F32, "chT1")
    T2 = C.sb([64, 2, 32], F32, "chT2")
    A1 = C.sb([64, 2, 32], F32, "chA1")
    A2 = C.sb([64, 2, 32], F32, "chA2")
    tH = Tok()

    def op(fn, r, w):
        P.op(eng, fn, r=r, w=w)
    op(lambda v: v.tensor_copy(out=A1[:, 0, :], in_=S.ATR[:, g0:g0 + 32]), [S.t_at], [tH])
    op(lambda v: v.tensor_copy(out=A1[:, 1, :], in_=S.ATR[:, g0:g0 + 32]), [S.t_at], [tH])
    op(lambda v: v.tensor_scalar(out=A2[:, 0, :], in0=S.ATI[:, g0:g0 + 32], scalar1=-1.0, scalar2=None, op0=ALU.mult), [S.t_at], [tH])
    op(lambda v: v.tensor_copy(out=A2[:, 1, :], in_=S.ATI[:, g0:g0 + 32]), [S.t_at], [tH])
    if init is None:
        op(lambda v: v.memset(HH[0][:], 0.0), [], [tH])
    else:
        op(lambda v: v.tensor_copy(out=HH[0][:, 0:2, :], in_=init[0]), [init[1]], [tH])
        op(lambda v: v.tensor_copy(out=HH[0][:, 2, :], in_=init[0][:, 0, :]), [init[1]], [tH])
    cur = 0
    for c in c_list:
        h, hn = HH[cur], HH[1 - cur]
        lc = LL[:, :, g0:g0 + 32, c]
        rr = [tH, t_L]
        op(lambda v, h=h: v.tensor_tensor(out=T1[:], in0=A1[:], in1=h[:, 0:2, :], op=ALU.mult), rr, [tH])
        op(lambda v, h=h: v.tensor_tensor(out=T2[:], in0=A2[:], in1=h[:, 1:3, :], op=ALU.mult), rr, [tH])
        op(lambda v: v.tensor_tensor(out=T1[:], in0=T1[:], in1=T2[:], op=ALU.add), rr, [tH])
        op(lambda v, hn=hn, lc=lc: v.tensor_tensor(out=hn[:, 0:2, :], in0=T1[:], in1=lc, op=ALU.add), rr, [tH])
        op(lambda v, hn=hn: v.tensor_copy(out=hn[:, 2, :], in_=hn[:, 0, :]), rr, [tH])
        op(lambda v, h=h, lc=lc: v.tensor_copy(out=lc, in_=h[:, 0:2, :]), rr, [tH, t_L])
        cur = 1 - cur
    return HH[cur], tH


def phase_B(P, R, K, l, part, ctx_full=True, dbg="", S=None):
    li = R.li(l)
    uT = R.get("uT", [512, NTOK])
    send = R.get("ssm_send", [64, 2, 32])
    recv = R.get("ssm_recv", [64, 2, 32])
    hinit0 = R.get("hinit0", [64, 2, 32, NCH])
    oTs = R.get("oT_s", [512, NTOK], BF16)
    C = Ctx(P)
    if S is None:
        S = ssm_prep(P, R, K, C, l)
    if dbg == "prep":
        C.close(); return
    yTM = C.sb([128, NT, 512], F32, "yTM") if part == "b" else None
    t_y = Tok()
    Cs = Ctx(P)
    UT = Cs.sb([128, 4, NTOK], BF16, "UT")
    t_UT = Tok()
    P.dma("pool", UT[:], uT.rearrange("(c p) t -> p c t", p=128), w=[t_UT])
    shared = getattr(S, "LL", None) is not None
    if shared:
        LL, t_Ld = S.LL, S.t_Ld
    else:
        LL = Cs.sb([64, 2, 64, NCH], F32, "LL")
        t_Ld = [Tok(), Tok()]
    LRE = LL[:, 0]
    LIM = LL[:, 1]
    NG = 4
    NG1 = 6
    Hs = [Cs.sb([128, NCH, TCH], F32, "Hs") for _ in range(NG)] if part == "b" else []
    t_Hs = [Tok() for _ in range(NG)]
    Hs1 = [Cs.sb([128, NCH, 2], F32, "Hs1") for _ in range(NG1)]
    t_Hs1 = [Tok() for _ in range(NG1)]
    psb = [Cs.ps([128, 512], F32, "psR") for _ in range(NG1)]
    ps = [psb[i][:, 0:NCH] for i in range(NG1)]
    t_ps = [Tok() for _ in range(NG1)]
    la = [Cs.sb([128, 128], F32, "la") for _ in range(NG1)]
    t_la = [Tok() for _ in range(NG1)]

    def pass1(d, act_only=False):
        for g4 in range(0, 32, NG1):
            gds = [d * 32 + g for g in range(g4, min(32, g4 + NG1))]
            ssm_rvm(P, C, S, UT, t_UT, gds, Hs1, t_Hs1, ps, t_ps, la, t_la, jmap=lambda j: j % 2, act_only=act_only)
            jl = (TCH - 1 if d == 0 else 0) % 2
            for i, gd in enumerate(gds):
                P.op("act", lambda a, i=i, gd=gd: a.copy(out=LRE[:, gd, :], in_=Hs1[i][0:64, :, jl]), r=[t_Hs1[i]], w=[t_Ld[d]])
                if act_only:
                    P.op("act", lambda a, i=i, gd=gd: a.copy(out=LIM[:, gd, :], in_=Hs1[i][64:128, :, jl]), r=[t_Hs1[i]], w=[t_Ld[d]])
                else:
                    P.op("dve", lambda v, i=i, gd=gd: v.tensor_copy(out=LIM[:, gd, :], in_=Hs1[i][64:128, :, jl]), r=[t_Hs1[i]], w=[t_Ld[d]])

    if part == "a":
        pass1(0)
        if dbg == "nochain":
            Cs.close(); C.close(); return
        hh, tH = ssm_chain(P, Cs, S, LL, t_Ld[0], 0, list(range(NCH)), None, eng="dve")
        P.dma("sp", send, hh[:, 0:2, :], r=[tH], w=[R.tk("ssm_send", 0)])
        if shared:
            pass1(1, act_only=True)
        else:
            P.dma("sp", hinit0[:, 0], LRE[:, 0:32, :], r=[t_Ld[0], tH], w=[R.tk("hinit0", 0)])
            P.dma("sp", hinit0[:, 1], LIM[:, 0:32, :], r=[t_Ld[0], tH], w=[R.tk("hinit0", 1)])
        Cs.close()
        C.close()
        return
    if not shared:
        pass1(1)
    if not getattr(S, "chain1_done", False):
        rin = Cs.sb([64, 2, 32], F32, "rin")
        t_rin = Tok()
        P.dma("sp", rin[:], recv, r=[R.tk("ssm_recv", 0)], w=[t_rin])
        ssm_chain(P, Cs, S, LL, t_Ld[1], 1, list(range(NCH - 1, NCC - 1, -1)), (rin[:], t_rin))
        ssm_chain(P, Cs, S, LL, t_Ld[1], 1, list(range(NCC - 1, -1, -1)), None, eng=("pool" if shared else "dve"))
    if not shared:
        P.dma("sp", LRE[:, 0:32, :], hinit0[:, 0], r=[R.tk("hinit0", 0)], w=[t_Ld[0]])
        P.dma("sp", LIM[:, 0:32, :], hinit0[:, 1], r=[R.tk("hinit0", 1)], w=[t_Ld[0]])
    hin = [Cs.sb([128, NCH], F32, "hin") for _ in range(NG)]
    t_hin = [Tok() for _ in range(NG)]
    yps = [Cs.ps([128, 512], F32, "yps") for _ in range(2)]
    t_yps = [Tok() for _ in range(2)]
    gl = [[g2, 32 + g2, g2 + 1, 32 + g2 + 1] for g2 in range(0, 32, 2)]
    hinit = [(hin[i][:], t_hin[i]) for i in range(NG)]

    def p2_prep(gds):
        for i, gd in enumerate(gds):
            P.op("act", lambda a, i=i, gd=gd: a.copy(out=hin[i][0:64, :], in_=LRE[:, gd, :]), r=[t_Ld[gd // 32]], w=[t_hin[i]])
            P.op("dve", lambda v, i=i, gd=gd: v.tensor_copy(out=hin[i][64:128, :], in_=LIM[:, gd, :]), r=[t_Ld[gd // 32]], w=[t_hin[i]])
            P.op("dve", lambda v, i=i, gd=gd: v.tensor_scalar(out=la[i][:, 0:64], in0=S.ID2[:], scalar1=S.CL[:, gd:gd + 1], scalar2=None, op0=ALU.mult),
                 r=[S.t_id2, S.t_c], w=[t_la[i]])
            P.op("dve", lambda v, i=i, gd=gd: v.tensor_scalar(out=la[i][:, 64:128], in0=S.ID2[:], scalar1=S.CR[:, gd:gd + 1], scalar2=None, op0=ALU.mult),
                 r=[S.t_id2, S.t_c], w=[t_la[i]])

    p2_prep(gl[0])
    for n2, gds in enumerate(gl):
        g2 = gds[0]
        ssm_rvm(P, C, S, UT, t_UT, gds, Hs, t_Hs, ps, t_ps, la, t_la, hinit=hinit, skip_la=True)
        if n2 + 1 < len(gl):
            p2_prep(gl[n2 + 1])
        for k2 in range(2):
            g = g2 + k2
            yp = yps[k2]
            fns = []
            for ti in range(NT):
                for dd in range(2):
                    i = 2 * k2 + dd
                    gd = gds[i]
                    lhsT = Hs[i][:, ti * 8:(ti + 1) * 8, :].rearrange("p c j -> p (c j)")
                    fns.append(lambda pe, yp=yp, ti=ti, lhsT=lhsT, gd=gd, dd=dd: pe.matmul(yp[:, ti * 16:(ti + 1) * 16], lhsT, S.CW[:, gd, :], start=(dd == 0), stop=(dd == 1)))
            P.mm(fns, r=[t_Hs[2 * k2], t_Hs[2 * k2 + 1], S.t_CW], w=[t_yps[k2]])
            P.op("act" if k2 == 0 else "dve",
                 (lambda a, g=g, yp=yp: a.copy(out=yTM[:, :, 16 * g:16 * g + 16], in_=yp[:, 0:NT * 16].rearrange("p (t h) -> p t h", h=16))) if k2 == 0 else
                 (lambda v, g=g, yp=yp: v.tensor_copy(out=yTM[:, :, 16 * g:16 * g + 16], in_=yp[:, 0:NT * 16].rearrange("p (t h) -> p t h", h=16))),
                 r=[t_yps[k2]], w=[t_y])
    Cs.close()
    Cg = Ctx(P)
    dsk = R.get("ssm_d", [R.nl, 512])
    wglu = R.get("ssm_w_glu", [R.nl, 512, 512])
    bglu = R.get("ssm_b_glu", [R.nl, 512])
    DS = Cg.sb([128, 4], F32, "DS")
    BG = Cg.sb([128, 4], F32, "BG")
    WG = Cg.sb([128, 4, 512], BF16, "WG")
    t_g = Tok()
    P.dma("sp", DS[:], dsk[li].rearrange("(c p) -> p c", p=128), w=[t_g], allow_slow_non_contiguous=True)
    P.dma("sp", BG[:], bglu[li].rearrange("(c p) -> p c", p=128), w=[t_g], allow_slow_non_contiguous=True)
    P.dma("pool", WG[:], wglu[li].rearrange("(c p) n -> p c n", p=128), w=[t_g])
    uf = [Cg.sb([128, 4, 128], F32, "uf") for _ in range(2)]
    t_uf = [Tok() for _ in range(2)]
    pt = [Cg.ps([128, 512], F32, "ptr") for _ in range(2)]
    t_pt = [Tok() for _ in range(2)]
    y2 = [Cg.sb([128, 4, 128], F32, "y2") for _ in range(2)]
    t_y2 = [Tok() for _ in range(2)]
    gf = [Cg.sb([128, 4, 128], F32, "gf") for _ in range(2)]
    gb = [Cg.sb([128, 4, 128], BF16, "gb") for _ in range(2)]
    t_gf = [Tok() for _ in range(2)]
    pg = [Cg.ps([128, 512], F32, "pg") for _ in range(2)]
    t_pg = [Tok() for _ in range(2)]
    sg = [Cg.sb([128, 4, 128], F32, "sg") for _ in range(2)]
    t_sg = [Tok() for _ in range(2)]
    ob = [Cg.sb([128, 4, 128], BF16, "ob") for _ in range(2)]
    t_ob = [Tok() for _ in range(2)]
    tiles = list(range(NT)) if ctx_full else list(range(2, NT))
    def u_stage1(n_, ti):
        i = n_ % 2
        tsl = slice(ti * 128, (ti + 1) * 128)
        P.dma("sp", uf[i][:], uT[:, tsl].rearrange("(c p) t -> p c t", p=128), w=[t_uf[i]])
        P.mm([(lambda pe, c=c, i=i, ti=ti: pe.transpose(pt[i][:, c * 128:(c + 1) * 128], yTM[:, ti, c * 128:(c + 1) * 128], K["idf"][:])) for c in range(4)],
             r=[t_y, K["tok"]], w=[t_pt[i]])
        for c in range(4):
            P.op("dve", lambda v, c=c, i=i: v.scalar_tensor_tensor(out=y2[i][:, c, :], in0=uf[i][:, c, :], scalar=DS[:, c:c + 1], in1=pt[i][:, c * 128:(c + 1) * 128],
                                                                  op0=ALU.mult, op1=ALU.add), r=[t_uf[i], t_pt[i], t_g], w=[t_y2[i]])
        P.op("act", lambda a, i=i: a.activation(out=gf[i][:], in_=y2[i][:], func=AF.Gelu_apprx_tanh), r=[t_y2[i]], w=[t_gf[i]])
        P.op("pool", lambda g_, i=i: g_.tensor_copy(out=gb[i][:], in_=gf[i][:]), r=[t_gf[i]], w=[t_gf[i]])
        fns = []
        for co in range(4):
            for c in range(4):
                fns.append(lambda pe, co=co, c=c, i=i: pe.matmul(pg[i][:, co * 128:(co + 1) * 128], WG[:, c, co * 128:(co + 1) * 128], gb[i][:, c, :], start=(c == 0), stop=(c == 3)))
        P.mm(fns, r=[t_g, t_gf[i]], w=[t_pg[i]])

    def u_stage2(n_, ti):
        i = n_ % 2
        tsl = slice(ti * 128, (ti + 1) * 128)
        for co in range(4):
            P.op("act", lambda a, co=co, i=i: a.activation(out=sg[i][:, co, :], in_=pg[i][:, co * 128:(co + 1) * 128], func=AF.Sigmoid, bias=BG[:, co:co + 1]),
                 r=[t_pg[i], t_g], w=[t_sg[i]])
        P.op("dve", lambda v, i=i: v.tensor_tensor(out=ob[i][:], in0=gf[i][:], in1=sg[i][:], op=ALU.mult), r=[t_gf[i], t_sg[i]], w=[t_ob[i]])
        P.dma("sp", oTs[:, tsl].rearrange("(c p) t -> p c t", p=128), ob[i][:], r=[t_ob[i]], w=[R.tk("oT_s", ti)])

    for n_ in range(len(tiles) + 1):
        if n_ < len(tiles):
            u_stage1(n_, tiles[n_])
        if n_ >= 1:
            u_stage2(n_ - 1, tiles[n_ - 1])
    Cg.close()
    C.close()


def ssm_chain1_async(P, R, Cq, S):
    recv = R.get("ssm_recv", [64, 2, 32])
    rin = Cq.sb([64, 2, 32], F32, "rin")
    t_rin = Tok()
    P.dma("sp", rin[:], recv, r=[R.tk("ssm_recv", 0)], w=[t_rin])
    ssm_chain(P, Cq, S, S.LL, S.t_Ld[1], 1, list(range(NCH - 1, NCC - 1, -1)), (rin[:], t_rin), eng="pool")
    ssm_chain(P, Cq, S, S.LL, S.t_Ld[1], 1, list(range(NCC - 1, -1, -1)), None, eng="pool")
    S.chain1_done = True

def phase_F(P, R, K, xname):
    x = R.get(xname, [NTOK, D])
    y = R.get("y_out", [NLAT, D])
    gf = R.get("g_final", [D])
    C = Ctx(P)
    G = C.sb([128, D], F32, "gfin")
    t_G = Tok()
    P.dma("sp", G[:], gf.partition_broadcast(128), w=[t_G])
    xt = [C.sb([128, D], F32, "xt") for _ in range(2)]
    t_xt = [Tok() for _ in range(2)]
    junk = C.sb([128, D], BF16, "junk")
    t_j = Tok()
    st = [C.sb([128, 4], F32, "st") for _ in range(2)]
    t_st = [Tok() for _ in range(2)]
    yo = [C.sb([128, D], F32, "yo") for _ in range(2)]
    t_yo = [Tok() for _ in range(2)]
    for n_, ti in enumerate(range(2, NT)):
        i = n_ % 2
        P.dma("sp", xt[i][:], x[ti * 128:(ti + 1) * 128, :], r=[R.tk(xname, ti)], w=[t_xt[i]])
        P.op("act", lambda a, i=i: a.activation(out=junk[:], in_=xt[i][:], func=AF.Square, accum_out=st[i][:, 0:1]), r=[t_xt[i]], w=[t_j, t_st[i]])
        P.op("act", lambda a, i=i: a.activation(out=st[i][:, 1:2], in_=st[i][:, 0:1], func=AF.Sqrt, scale=1.0 / D, bias=EPS), r=[t_st[i]], w=[t_st[i]])
        P.op("dve", lambda v, i=i: v.reciprocal(out=st[i][:, 2:3], in_=st[i][:, 1:2]), r=[t_st[i]], w=[t_st[i]])
        P.op("dve", lambda v, i=i: v.scalar_tensor_tensor(out=yo[i][:], in0=xt[i][:], scalar=st[i][:, 2:3], in1=G[:], op0=ALU.mult, op1=ALU.mult),
             r=[t_xt[i], t_st[i], t_G], w=[t_yo[i]])
        P.dma("sp", y[(ti - 2) * 128:(ti - 1) * 128, :], yo[i][:], r=[t_yo[i]], w=[R.tk("y_out", ti)], is_out=True)
    C.close()


CORES = [(b, s) for b in range(4) for s in range(2)]


def prep_core_layer(inputs, l, s):
    g = lambda k: np.asarray(inputs[k])[l:l + 1]
    m = {}
    for k in ("w_out", "ffn_w_up", "ffn_w_down", "ffn_conv_b", "attn_sink", "ssm_d", "ssm_w_glu", "ssm_b_glu", "w_ada", "b_ada",
              "g_mix", "g_ffn", "gmlp_ln_g", "gmlp_ln_b"):
        m[k] = g(k)
    cw = g("ffn_conv_w")
    m["ffn_conv_w"] = cw[:, ::-1] if s == 1 else cw
    for k in ("ssm_lambda_re", "ssm_lambda_im", "ssm_log_dt", "ssm_b_re", "ssm_b_im", "ssm_c_re", "ssm_c_im"):
        a = g(k)
        m[k] = a[:, ::-1] if s == 1 else a
    ws = g("gmlp_w_s")
    bs = g("gmlp_b_s")
    if s == 1:
        ws = ws[:, :, ::-1, ::-1]
        bs = bs[:, :, ::-1]
    m["gmlp_wsT"] = ws.transpose(0, 1, 3, 2)
    m["gmlp_b_s"] = bs
    return {k: np.ascontiguousarray(v, dtype=np.float32) for k, v in m.items()}


def run_launch(build_fn, ins, outs, core_maps):
    nc = bass.Bass("TRN2", target_bir_lowering=False)
    R = Reg(nc, ins=ins, outs=outs, per_layer=True)
    P = Prog(nc)
    Ck = Ctx(P)
    K = build_consts(P, Ck)
    build_fn(P, R, K)
    P.finish()
    in_maps = [{k: m[k] for k in ins} for m in core_maps]
    res = run_bass_kernel_spmd(nc, in_maps, core_ids=list(range(len(core_maps))))
    return res.results


W_A = ["w_inx", "g_mix", "gmlp_ln_g", "gmlp_ln_b", "gmlp_wsT", "gmlp_b_s"]
W_B = ["ssm_lambda_re", "ssm_lambda_im", "ssm_log_dt", "ssm_b_re", "ssm_b_im", "ssm_c_re", "ssm_c_im"]
W_B2 = ["ssm_d", "ssm_w_glu", "ssm_b_glu"]


def kernel_unfused(inputs):
    import ml_dtypes
    bf = ml_dtypes.bfloat16
    pq, pk = _qsw_perm()
    st = []
    for (b, s) in CORES:
        m = prep_core(inputs, {}, b, s)
        st.append({"x": m["x_in"], "rowcol": m["rowcol"], "cvec": m["cvec"]})
    for l in range(DEPTH):
        w_in = np.asarray(inputs["w_in"])[l:l + 1]
        q = w_in[:, :, 0:1024]
        k = w_in[:, :, 1024:1152]
        w_inx = np.ascontiguousarray(np.concatenate([q, q[:, :, pq], k, k[:, :, pk], w_in[:, :, 1152:]], axis=2), dtype=np.float32)
        lw = [prep_core_layer(inputs, l, s) for s in range(2)]
        maps = []
        for ci, (b, s) in enumerate(CORES):
            m = dict(lw[s])
            m["w_inx"] = w_inx
            m.update({"x_l": st[ci]["x"], "rowcol": st[ci]["rowcol"], "cvec": st[ci]["cvec"]})
            maps.append(m)
        ins1 = ["cvec", "w_ada", "b_ada", "rowcol", "x_l"] + W_A + W_B
        outs1 = ["mod", "qT", "kT", "v", "uT", "oT_g", "ssm_send", "hinit0"]

        def b1(P, R, K, l=l):
            phase_mod(P, R, [l])
            phase_rope(P, R)
            phase_A(P, R, K, l, "x_l")
            phase_B(P, R, K, l, "a")
        r1 = run_launch(b1, ins1, outs1, maps)
        for ci in range(8):
            pr = ci ^ 1
            m = maps[ci]
            for kk in ("mod", "qT", "uT", "oT_g", "hinit0"):
                m[kk] = r1[ci][kk]
            kT = np.array(r1[ci]["kT"])
            kT[:, NTOK:] = np.asarray(r1[pr]["kT"])[:, NTOK - 128:NTOK]
            vv = np.array(r1[ci]["v"])
            vv[NTOK:] = np.asarray(r1[pr]["v"])[NTOK - 128:NTOK]
            m["kT"], m["v"] = kT, vv
            m["ssm_recv"] = np.asarray(r1[pr]["ssm_send"])
        ins2 = ["mod", "x_l", "qT", "kT", "v", "uT", "oT_g", "hinit0", "ssm_recv", "attn_sink", "w_out"] + W_B + W_B2
        outs2 = ["x_mid"]

        def b2(P, R, K, l=l):
            phase_B(P, R, K, l, "b")
            phase_C(P, R, K, l)
            phase_D(P, R, K, l, "x_l", "x_mid")
        r2 = run_launch(b2, ins2, outs2, maps)
        for ci in range(8):
            pr = ci ^ 1
            maps[ci]["x_mid"] = r2[ci]["x_mid"]
            maps[ci]["xb_recv"] = np.ascontiguousarray(np.asarray(r2[pr]["x_mid"])[NTOK - 1:NTOK])
        ins3 = ["mod", "x_mid", "xb_recv", "g_ffn", "ffn_w_up", "ffn_w_down", "ffn_conv_w", "ffn_conv_b"]
        last = l == DEPTH - 1
        outs3 = ["y_out"] if last else ["x_next"]
        if last:
            ins3 = ins3 + ["g_final"]
            for m in maps:
                m["g_final"] = np.ascontiguousarray(inputs["g_final"], dtype=np.float32)

        def b3(P, R, K, l=l, last=last):
            phase_E(P, R, K, l, "x_mid", "x_next")
            if last:
                phase_F(P, R, K, "x_next")
        r3 = run_launch(b3, ins3, outs3, maps)
        if not last:
            for ci in range(8):
                st[ci]["x"] = r3[ci]["x_next"]
    out = np.zeros((4, 4096, D), np.float32)
    for ci, (b, s) in enumerate(CORES):
        y = np.asarray(r3[ci]["y_out"])
        if s == 0:
            out[b, :NLAT] = y
        else:
            out[b, NLAT:] = y[::-1]
    return out


PAIRS = [[0, 1], [2, 3], [4, 5], [6, 7]]


def load_sel(P, C, R):
    psel = R.get("psel", [2])
    SEL = C.sb([128, 2], F32, "SEL")
    t = Tok()
    P.dma("sp", SEL[:], psel.partition_broadcast(128), w=[t])
    return SEL, t


def phase_X1(P, R, K):
    kT = R.get("kT", [128, NTOK + 128], BF16)
    vv = R.get("v", [NTOK + 128, 128], BF16)
    send = R.get("ssm_send", [64, 2, 32])
    recv = R.get("ssm_recv", [64, 2, 32])
    s1 = R.get("send1", [128, 320])
    ag1 = R.get("ag1", [256, 320])
    C = Ctx(P)
    SEL, t_sel = load_sel(P, C, R)
    kb = C.sb([128, 256], BF16, "x1kb")
    PK = C.sb([128, 320], F32, "x1pk")
    t_kb, t_pk = Tok(), Tok()
    P.dma("sp", kb[:, 0:128], kT[:, NTOK - 128:NTOK], w=[t_kb])
    P.dma("sp", kb[:, 128:256], vv[NTOK - 128:NTOK, :], w=[t_kb])
    P.op("pool", lambda g: g.memset(PK[:, 256:320], 0.0), w=[t_pk])
    P.op("dve", lambda v: v.tensor_copy(out=PK[:, 0:256], in_=kb[:]), r=[t_kb], w=[t_pk])
    P.dma("sp", PK[0:64, 256:320], send.rearrange("p r g -> p (r g)"), r=[t_pk], w=[t_pk])
    t_s1, t_ag = Tok(), Tok()
    P.dma("sp", s1, PK[:], r=[t_pk], w=[t_s1])
    P.collective(s1, ag1, PAIRS, r=[t_s1], w=[t_ag])
    G = C.sb([128, 2, 320], F32, "x1g")
    t_G = Tok()
    P.dma("sp", G[:], ag1.rearrange("(r p) c -> p r c", p=128), r=[t_ag], w=[t_G])
    R1 = C.sb([128, 320], F32, "x1r")
    kb2 = C.sb([128, 256], BF16, "x1kb2")
    t_R1 = Tok()
    P.op("dve", lambda v: v.tensor_scalar(out=R1[:], in0=G[:, 0, :], scalar1=SEL[:, 0:1], scalar2=None, op0=ALU.mult), r=[t_G, t_sel], w=[t_R1])
    P.op("dve", lambda v: v.scalar_tensor_tensor(out=R1[:], in0=G[:, 1, :], scalar=SEL[:, 1:2], in1=R1[:], op0=ALU.mult, op1=ALU.add), r=[t_G, t_sel, t_R1], w=[t_R1])
    P.op("dve", lambda v: v.tensor_copy(out=kb2[:], in_=R1[:, 0:256]), r=[t_R1], w=[t_R1])
    P.dma("sp", kT[:, NTOK:NTOK + 128], kb2[:, 0:128], r=[t_R1], w=[R.tk("kT", "halo")])
    P.dma("sp", vv[NTOK:NTOK + 128, :], kb2[:, 128:256], r=[t_R1], w=[R.tk("v", "halo")])
    P.dma("sp", recv.rearrange("p r g -> p (r g)"), R1[0:64, 256:320], r=[t_R1], w=[R.tk("ssm_recv", 0)])
    C.close()


def phase_X2(P, R, K, xname):
    x = R.get(xname, [NTOK, D])
    xb = R.get("xb_recv", [1, D])
    s2 = R.get("send2", [1, D])
    ag2 = R.get("ag2", [2, D])
    C = Ctx(P)
    SEL, t_sel = load_sel(P, C, R)
    row = C.sb([1, D], F32, "x2row")
    t_row, t_s2, t_ag = Tok(), Tok(), Tok()
    P.dma("sp", row[:], x[NTOK - 1:NTOK, :], w=[t_row])
    P.dma("sp", s2, row[:], r=[t_row], w=[t_s2])
    P.collective(s2, ag2, PAIRS, r=[t_s2], w=[t_ag])
    G = C.sb([1, 2, D], F32, "x2g")
    t_G = Tok()
    P.dma("sp", G[:], ag2.rearrange("(o r) c -> o r c", o=1), r=[t_ag], w=[t_G])
    P.op("dve", lambda v: v.tensor_scalar(out=row[:], in0=G[:, 0, :], scalar1=SEL[0:1, 0:1], scalar2=None, op0=ALU.mult), r=[t_G, t_sel, t_row], w=[t_row])
    P.op("dve", lambda v: v.scalar_tensor_tensor(out=row[:], in0=G[:, 1, :], scalar=SEL[0:1, 1:2], in1=row[:], op0=ALU.mult, op1=ALU.add), r=[t_G, t_sel, t_row], w=[t_row])
    P.dma("sp", xb, row[:], r=[t_row], w=[R.tk("xb_recv")])
    C.close()


FUSED_INS = ["x_in", "rowcol", "cvec", "psel", "w_ada", "b_ada", "g_mix", "g_ffn", "w_inx", "w_out", "attn_sink",
             "ssm_lambda_re", "ssm_lambda_im", "ssm_log_dt", "ssm_b_re", "ssm_b_im", "ssm_c_re", "ssm_c_im", "ssm_d", "ssm_w_glu", "ssm_b_glu",
             "gmlp_ln_g", "gmlp_ln_b", "gmlp_wsT", "gmlp_b_s", "ffn_w_up", "ffn_conv_w", "ffn_conv_b", "ffn_w_down", "g_final"]


def build_fused(nlayers=DEPTH):
    nc = bass.Bass("TRN2", target_bir_lowering=False)
    R = Reg(nc, ins=FUSED_INS, outs=["y_out"], per_layer=False)
    P = Prog(nc)
    Ck = Ctx(P)
    K = build_consts(P, Ck)
    phase_mod(P, R, [0])
    phase_rope(P, R)
    xcur = "x_in"
    for l in range(nlayers):
        xnext = "x_res%d" % (l % 2)
        phase_A(P, R, K, l, xcur)
        Cq = Ctx(P)
        LLq = Cq.sb([64, 2, 64, NCH], F32, "LLq")
        S = ssm_prep(P, R, K, Cq, l)
        S.LL, S.t_Ld = LLq, [Tok(), Tok()]
        phase_B(P, R, K, l, "a", S=S)
        phase_X1(P, R, K)
        phase_B(P, R, K, l, "b", S=S)
        Cq.close()
        phase_C(P, R, K, l)
        if l + 1 < nlayers:
            Cm = Ctx(P)
            side = phase_mod_gen(P, R, [l + 1], Cm)
            next(side)
            phase_D(P, R, K, l, xcur, "x_mid", side=side)
            Cm.close()
        else:
            phase_D(P, R, K, l, xcur, "x_mid")
        phase_X2(P, R, K, "x_mid")
        phase_E(P, R, K, l, "x_mid", xnext)
        xcur = xnext
    phase_F(P, R, K, xcur)
    P.finish()
    return nc, P


def fused_inputs(inputs):
    pq, pk = _qsw_perm()
    w_in = np.asarray(inputs["w_in"])
    q = w_in[:, :, 0:1024]
    k = w_in[:, :, 1024:1152]
    w_inx = np.ascontiguousarray(np.concatenate([q, q[:, :, pq], k, k[:, :, pk], w_in[:, :, 1152:]], axis=2), dtype=np.float32)
    f32 = lambda a: np.ascontiguousarray(a, dtype=np.float32)
    shared = {"w_inx": w_inx}
    for k_ in ("w_ada", "b_ada", "g_mix", "g_ffn", "w_out", "attn_sink", "ssm_d", "ssm_w_glu", "ssm_b_glu", "gmlp_ln_g", "gmlp_ln_b",
               "ffn_w_up", "ffn_conv_b", "ffn_w_down", "g_final"):
        shared[k_] = f32(inputs[k_])
    half = []
    for s in range(2):
        m = {}
        cw = np.asarray(inputs["ffn_conv_w"])
        m["ffn_conv_w"] = f32(cw[:, ::-1] if s == 1 else cw)
        for k_ in ("ssm_lambda_re", "ssm_lambda_im", "ssm_log_dt", "ssm_b_re", "ssm_b_im", "ssm_c_re", "ssm_c_im"):
            a = np.asarray(inputs[k_])
            m[k_] = f32(a[:, ::-1] if s == 1 else a)
        ws = np.asarray(inputs["gmlp_w_s"])
        bs = np.asarray(inputs["gmlp_b_s"])
        if s == 1:
            ws = ws[:, :, ::-1, ::-1]
            bs = bs[:, :, ::-1]
        m["gmlp_wsT"] = f32(ws.transpose(0, 1, 3, 2))
        m["gmlp_b_s"] = f32(bs)
        m["psel"] = np.array([0.0, 1.0] if s == 0 else [1.0, 0.0], np.float32)
        half.append(m)
    maps = []
    for (b, s) in CORES:
        m = dict(shared)
        m.update(half[s])
        pc = prep_core(inputs, {}, b, s)
        m["x_in"], m["rowcol"], m["cvec"] = pc["x_in"], pc["rowcol"], pc["cvec"]
        maps.append({k_: m[k_] for k_ in FUSED_INS})
    return maps


def kernel(**inputs):
    nc, P = build_fused()
    maps = fused_inputs(inputs)
    res = run_bass_kernel_spmd(nc, maps, core_ids=list(range(8)))
    out = np.zeros((4, 4096, D), np.float32)
    for ci, (b, s) in enumerate(CORES):
        y = np.asarray(res.results[ci]["y_out"])
        if s == 0:
            out[b, :NLAT] = y
        else:
            out[b, NLAT:] = y[::-1]
    return out
```
